# Optimizing a Trainium2 kernel written in Bass

```python
import jax, jax.numpy as jnp
from jax import lax
import numpy as np

D_MODEL = 1024
BATCH = 2
SEQ = 8192
DEPTH = 4
DEC_BATCH = 128
DEC_SEQ = 1
PAST_LEN = 8192
PAGE_SIZE = 128

N_META = 16
N_A = DEPTH // 2
N_B = DEPTH - N_A
POOL_WINDOWS = (2, 4, 8, 16)
N_POOL_GROUPS = len(POOL_WINDOWS)
POOL_GC = D_MODEL // N_POOL_GROUPS
MAX_POOL = max(POOL_WINDOWS)
POOL_STATE = MAX_POOL - 1
HEAD_DIM = 64
N_HEADS = D_MODEL // HEAD_DIM
N_KV_HEADS = 4
GROUP = N_HEADS // N_KV_HEADS
KV_DIM = N_KV_HEADS * HEAD_DIM
WINDOW = 128
BLOCK = 128
ATTN_SCALE = HEAD_DIM ** -0.5
D_FF = 2816
CONV_W = 3
ALPHA = (2.0 * DEPTH) ** 0.25
BETA = (8.0 * DEPTH) ** -0.25
LN_EPS = 1e-5
NEG = -1e30

kernel_name = "yoco_pool_swa_sink_convffn_step"


def layer_norm(x, g, b):
    xf = x.astype(jnp.float32)
    mu = jnp.mean(xf, axis=-1, keepdims=True)
    var = jnp.mean(jnp.square(xf - mu), axis=-1, keepdims=True)
    y = (xf - mu) * lax.rsqrt(var + LN_EPS) * g.astype(jnp.float32) + b.astype(jnp.float32)
    return y.astype(x.dtype)


def pool_mixer(xs, w_pool, scale):
    L = xs.shape[1]
    xf = xs.astype(jnp.float32)
    cs = jnp.pad(jnp.cumsum(xf, axis=1), ((0, 0), (MAX_POOL, 0), (0, 0)))
    t = jnp.arange(L)
    parts = []
    for gi, w in enumerate(POOL_WINDOWS):
        ch = slice(gi * POOL_GC, (gi + 1) * POOL_GC)
        win = cs[:, MAX_POOL:MAX_POOL + L, ch] - cs[:, MAX_POOL - w:MAX_POOL - w + L, ch]
        cnt = jnp.minimum(w, t + 1).astype(jnp.float32)[None, :, None]
        parts.append(win / cnt - xf[..., ch])
    d = jnp.stack(parts, axis=2).astype(xs.dtype)
    y = jnp.einsum('blgc,gce->blge', d, w_pool).reshape(xs.shape)
    return y * scale


def conv_ffn(xs, prefix, w_in, conv_w, conv_b, w_out):
    S = xs.shape[1]
    gu = xs @ w_in
    g, u = gu[..., :D_FF], gu[..., D_FF:]
    gp = jnp.concatenate([prefix.astype(g.dtype), g], axis=1)
    c = conv_b
    for k in range(CONV_W):
        c = c + gp[:, k:k + S] * conv_w[k]
    h = jax.nn.silu(c) * u
    return h @ w_out, gp[:, -(CONV_W - 1):]


def sink_attention(q, k, v, mask, sinks):
    s = jnp.einsum('...qkgd,...skd->...kgqs', q, k).astype(jnp.float32) * ATTN_SCALE
    s = jnp.where(mask, s, NEG)
    sink = sinks.astype(jnp.float32).reshape(N_KV_HEADS, GROUP, 1, 1)
    m = jnp.maximum(jnp.max(s, axis=-1, keepdims=True), sink)
    e = jnp.exp(s - m)
    den = jnp.sum(e, axis=-1, keepdims=True) + jnp.exp(sink - m)
    p = (e / den).astype(v.dtype)
    return jnp.einsum('...kgqs,...skd->...qkgd', p, v)


def band_blocks(t, pad):
    B = t.shape[0]
    t = jnp.pad(t, ((0, 0), (pad, 0), (0, 0), (0, 0)))
    nb = t.shape[1] // BLOCK
    t = t.reshape(B, nb, BLOCK, N_KV_HEADS, HEAD_DIM)
    prev = jnp.pad(t[:, :-1], ((0, 0), (1, 0), (0, 0), (0, 0), (0, 0)))
    return jnp.concatenate([prev, t], axis=2)


def band_mask(nb, pad):
    n = jnp.arange(nb)[:, None, None]
    i = jnp.arange(BLOCK)[None, :, None]
    j = jnp.arange(2 * BLOCK)[None, None, :]
    qp = n * BLOCK + i
    kp = (n - 1) * BLOCK + j
    return (kp <= qp) & (qp - kp < WINDOW) & (kp >= pad)


def swa_prompt(x, k_band, v_band, mask, pad, w_q, b_q, sinks, w_o, b_o):
    B, L, _ = x.shape
    q = (x @ w_q + b_q).reshape(B, L, N_KV_HEADS, GROUP, HEAD_DIM)
    q = jnp.pad(q, ((0, 0), (pad, 0), (0, 0), (0, 0), (0, 0)))
    Lp = L + pad
    q = q.reshape(B, Lp // BLOCK, BLOCK, N_KV_HEADS, GROUP, HEAD_DIM)
    o = sink_attention(q, k_band, v_band, mask[None, :, None, None], sinks)
    o = o.reshape(B, Lp, N_HEADS * HEAD_DIM)[:, pad:]
    return o @ w_o + b_o


def swa_sample(x, k_all, v_all, mask, w_q, b_q, sinks, w_o, b_o):
    DB, S, _ = x.shape
    q = (x @ w_q + b_q).reshape(DB, S, N_KV_HEADS, GROUP, HEAD_DIM)
    o = sink_attention(q, k_all, v_all, mask[None, None, None], sinks)
    return o.reshape(DB, S, N_HEADS * HEAD_DIM) @ w_o + b_o


def setup_inputs(seed: int = 0) -> dict:
    key = jax.random.key(seed)
    ks = jax.random.split(key, 26)
    nrm = jax.random.normal
    f32 = jnp.float32
    kv_col_scale = jnp.concatenate([jnp.ones((KV_DIM,), f32), jnp.full((KV_DIM,), BETA, f32)])
    return {
        "x_prompt": nrm(ks[0], (BATCH, SEQ, D_MODEL), f32),
        "x_sample": nrm(ks[1], (DEC_BATCH, DEC_SEQ, D_MODEL), f32),
        "state_pool": nrm(ks[2], (N_A, DEC_BATCH, POOL_STATE, D_MODEL), f32),
        "state_conv": nrm(ks[3], (DEPTH, DEC_BATCH, CONV_W - 1, D_FF), f32),
        "state_k_win": nrm(ks[4], (DEC_BATCH, WINDOW, N_KV_HEADS, HEAD_DIM), f32),
        "state_v_win": nrm(ks[5], (DEC_BATCH, WINDOW, N_KV_HEADS, HEAD_DIM), f32),
        "meta_tokens": nrm(ks[6], (N_META, D_MODEL), f32),
        "pool_w": nrm(ks[7], (N_A, N_POOL_GROUPS, POOL_GC, POOL_GC), f32) * (POOL_GC ** -0.5) * BETA,
        "pool_scale": 1.0 + 0.1 * nrm(ks[8], (N_A, D_MODEL), f32),
        "w_kv": nrm(ks[9], (D_MODEL, 2 * KV_DIM), f32) * (D_MODEL ** -0.5) * kv_col_scale,
        "b_kv": 0.02 * nrm(ks[10], (2 * KV_DIM,), f32),
        "attn_w_q": nrm(ks[11], (N_B, D_MODEL, N_HEADS * HEAD_DIM), f32) * (D_MODEL ** -0.5),
        "attn_b_q": 0.02 * nrm(ks[12], (N_B, N_HEADS * HEAD_DIM), f32),
        "attn_sinks": 0.5 * nrm(ks[13], (N_B, N_HEADS), f32),
        "attn_w_o": nrm(ks[14], (N_B, N_HEADS * HEAD_DIM, D_MODEL), f32) * ((N_HEADS * HEAD_DIM) ** -0.5) * BETA,
        "attn_b_o": 0.02 * nrm(ks[15], (N_B, D_MODEL), f32),
        "ffn_w_in": nrm(ks[16], (DEPTH, D_MODEL, 2 * D_FF), f32) * (D_MODEL ** -0.5),
        "ffn_conv_w": nrm(ks[17], (DEPTH, CONV_W, D_FF), f32) * (CONV_W ** -0.5),
        "ffn_conv_b": 0.02 * nrm(ks[18], (DEPTH, D_FF), f32),
        "ffn_w_out": nrm(ks[19], (DEPTH, D_FF, D_MODEL), f32) * (D_FF ** -0.5) * BETA,
        "ln_mix_g": 1.0 + 0.05 * nrm(ks[20], (DEPTH, D_MODEL), f32),
        "ln_mix_b": 0.02 * nrm(ks[21], (DEPTH, D_MODEL), f32),
        "ln_ffn_g": 1.0 + 0.05 * nrm(ks[22], (DEPTH, D_MODEL), f32),
        "ln_ffn_b": 0.02 * nrm(ks[23], (DEPTH, D_MODEL), f32),
    }


def reference(x_prompt, x_sample, state_pool, state_conv, state_k_win, state_v_win,
              meta_tokens, pool_w, pool_scale, w_kv, b_kv,
              attn_w_q, attn_b_q, attn_sinks, attn_w_o, attn_b_o,
              ffn_w_in, ffn_conv_w, ffn_conv_b, ffn_w_out,
              ln_mix_g, ln_mix_b, ln_ffn_g, ln_ffn_b):
    B = x_prompt.shape[0]
    S = x_sample.shape[1]
    meta = jnp.broadcast_to(meta_tokens[None].astype(x_prompt.dtype), (B, N_META, D_MODEL))
    hp = jnp.concatenate([meta, x_prompt], axis=1)
    hs = x_sample
    L = hp.shape[1]
    pad = (-N_META) % BLOCK
    nb = (L + pad) // BLOCK

    new_pool_p, new_pool_s, new_conv_p, new_conv_s = [], [], [], []
    for layer in range(DEPTH):
        if layer < N_A:
            a = layer
            mix_p = pool_mixer(hp, pool_w[a], pool_scale[a])
            ps = jnp.concatenate([state_pool[a].astype(hs.dtype), hs], axis=1)
            mix_s = pool_mixer(ps, pool_w[a], pool_scale[a])[:, -S:]
            new_pool_p.append(hp[:, -POOL_STATE:])
            new_pool_s.append(ps[:, -POOL_STATE:])
        else:
            bi = layer - N_A
            mix_p = swa_prompt(hp, k_band, v_band, mask_p, pad, attn_w_q[bi], attn_b_q[bi],
                               attn_sinks[bi], attn_w_o[bi], attn_b_o[bi])
            mix_s = swa_sample(hs, k_all, v_all, mask_s, attn_w_q[bi], attn_b_q[bi],
                               attn_sinks[bi], attn_w_o[bi], attn_b_o[bi])
        hp = layer_norm(ALPHA * hp + mix_p, ln_mix_g[layer], ln_mix_b[layer])
        hs = layer_norm(ALPHA * hs + mix_s, ln_mix_g[layer], ln_mix_b[layer])

        zero_prefix = jnp.zeros((B, CONV_W - 1, D_FF), hp.dtype)
        f_p, c_p = conv_ffn(hp, zero_prefix, ffn_w_in[layer], ffn_conv_w[layer], ffn_conv_b[layer], ffn_w_out[layer])
        f_s, c_s = conv_ffn(hs, state_conv[layer], ffn_w_in[layer], ffn_conv_w[layer], ffn_conv_b[layer], ffn_w_out[layer])
        new_conv_p.append(c_p)
        new_conv_s.append(c_s)
        hp = layer_norm(ALPHA * hp + f_p, ln_ffn_g[layer], ln_ffn_b[layer])
        hs = layer_norm(ALPHA * hs + f_s, ln_ffn_g[layer], ln_ffn_b[layer])

        if layer == N_A - 1:
            kv_p = (hp @ w_kv + b_kv).reshape(B, L, 2, N_KV_HEADS, HEAD_DIM)
            k_p, v_p = kv_p[:, :, 0], kv_p[:, :, 1]
            new_k_p = k_p[:, -WINDOW:]
            new_v_p = v_p[:, -WINDOW:]
            k_band = band_blocks(k_p, pad)
            v_band = band_blocks(v_p, pad)
            mask_p = band_mask(nb, pad)
            kv_s = (hs @ w_kv + b_kv).reshape(hs.shape[0], S, 2, N_KV_HEADS, HEAD_DIM)
            k_all = jnp.concatenate([state_k_win.astype(hs.dtype), kv_s[:, :, 0]], axis=1)
            v_all = jnp.concatenate([state_v_win.astype(hs.dtype), kv_s[:, :, 1]], axis=1)
            new_k_s = k_all[:, -WINDOW:]
            new_v_s = v_all[:, -WINDOW:]
            qpos = jnp.arange(S)[:, None]
            kpos = (jnp.arange(WINDOW + S) - WINDOW)[None, :]
            mask_s = (kpos <= qpos) & (qpos - kpos < WINDOW)

    y_prompt = hp[:, N_META:]
    y_sample = hs
    return (y_prompt, y_sample,
            jnp.stack(new_pool_p), jnp.stack(new_pool_s),
            jnp.stack(new_conv_p), jnp.stack(new_conv_s),
            new_k_p, new_v_p, new_k_s, new_v_s)
```

```python
import contextlib
import numpy as np
import concourse.bass as bass
import concourse.mybir as mybir
from concourse.bass_utils import run_bass_kernel_spmd

F32 = mybir.dt.float32
BF16 = mybir.dt.bfloat16
AF = mybir.ActivationFunctionType
ALU = mybir.AluOpType
AX = mybir.AxisListType


class Buf:
    __slots__ = ("name", "w", "rs", "dsem", "dcnt", "slot")

    def __init__(self, name):
        self.name = name
        self.w = None
        self.rs = {}
        self.dsem = None
        self.dcnt = 0
        self.slot = self


class Sched:
    ENG = ["pe", "act", "dve", "pool", "sp"]

    def __init__(self, nc):
        self.nc = nc
        self.ops = {e: [] for e in self.ENG}
        self.waited = {e: {} for e in self.ENG}
        self.dbufs = []

    def _deps(self, eng, reads, writes):
        best = {}
        idx = len(self.ops[eng])

        def add(tok):
            if tok is None:
                return
            if tok[0] == "e":
                _, pe, pidx = tok
                if pe == eng and eng == "pe":
                    return
                key = ("e", pe)
                v = pidx
            else:
                _, b, v = tok
                key = ("d", b)
            if best.get(key, -1) < v:
                best[key] = v

        for b in reads:
            add(b.w)
        for b in writes:
            add(b.w)
            for t in b.rs.values():
                add(t)
        waits = []
        for key, v in best.items():
            if self.waited[eng].get(key, -1) >= v:
                continue
            self.waited[eng][key] = v
            waits.append((key, v))
        return waits

    def _commit(self, tok, reads, writes):
        for b in writes:
            b.w = tok
            b.rs = {}
        for b in reads:
            if b in writes:
                continue
            if tok[0] == "e":
                b.rs[("e", tok[1])] = tok
            else:
                b.rs[("d", tok[1])] = tok

    def op(self, eng, fn, reads=(), writes=()):
        waits = self._deps(eng, reads, writes)
        idx = len(self.ops[eng])
        self.ops[eng].append(dict(fn=fn, waits=waits, sig=False, dma=None))
        tok = ("e", eng, idx)
        self._commit(tok, reads, writes)
        return tok

    def dma(self, eng, out, in_, reads=(), writes=(), **kw):
        waits = self._deps(eng, reads, writes)
        pb = (writes[0] if writes else reads[0]).slot
        if pb.dsem is None:
            pb.dsem = True
            self.dbufs.append(pb)
        pb.dcnt += 16
        self.ops[eng].append(dict(
            fn=lambda e: e.dma_start(out=out, in_=in_, **kw), waits=waits, sig=False, dma=pb))
        tok = ("d", pb, pb.dcnt)
        self._commit(tok, reads, writes)
        return tok

    def finalize(self, stack, final_bufs=()):
        nc = self.nc
        waits = self._deps("sp", list(final_bufs), list(final_bufs))
        self.ops["sp"].append(dict(fn=None, waits=waits, sig=False, dma=None))
        for e in self.ENG:
            for rec in self.ops[e]:
                for key, v in rec["waits"]:
                    if key[0] == "e":
                        self.ops[key[1]][v]["sig"] = True
        cum = {}
        for e in self.ENG:
            c = 0
            arr = []
            for rec in self.ops[e]:
                if rec["sig"]:
                    c += 1
                arr.append(c)
            cum[e] = arr
        esem = {e: stack.enter_context(nc.semaphore("s_" + e)) for e in self.ENG}
        for b in self.dbufs:
            b.dsem = stack.enter_context(nc.semaphore("d_" + b.name))
        engobj = {"pe": "tensor", "act": "scalar", "dve": "vector", "pool": "gpsimd", "sp": "sync"}

        def emit(name, e):
            for rec in self.ops[name]:
                for key, v in rec["waits"]:
                    if key[0] == "e":
                        e.wait_ge(esem[key[1]], cum[key[1]][v])
                    else:
                        e.wait_ge(key[1].dsem, v)
                if rec["fn"] is None:
                    continue
                ins = rec["fn"](e)
                if rec["dma"] is not None:
                    ins.then_inc(rec["dma"].dsem, 16)
                elif rec["sig"]:
                    ins.then_inc(esem[name], 1)

        block = stack.enter_context(nc.Block())
        for name in self.ENG:
            getattr(block, engobj[name])(lambda e, name=name: emit(name, e))
        self.stats = {e: len(self.ops[e]) for e in self.ENG}
        self.nsem = 5 + len(self.dbufs)

D = 1024; DFF = 2816; NJ = 22; NL = 4; NT = 10; TP = NT * 128; NS = 16; TC = TP + NS
ALPHA = (2.0 * 4) ** 0.25; EPS = 1e-5; SCALE = 0.125; NEG = -1e30
GROUPS = [(0, 6), (6, 12), (12, 17), (17, 22)]
C_CW = 0; C_CB = 264; C_BQT = 352; C_BQH = 368; C_BKT = 400; C_SINKB = 404; C_SINKS = 436; C_TM = 440; C_NSINKB = 444; NCST = 476
B_BC = 0; B_BP = 1024; B_SEL = 1536; B_CI = 1664; B_MK = 1728; B_IAB = 4800; NCSTB = 5056
U8 = mybir.dt.uint8
PERM = [0, 2, 1, 3]
import os
KVDBG = int(os.environ.get('KVDBG', '0'))
ARENA_BYTES = 97 * 1024


def build_nc(stop=None, dbg=False):
    nc = bass.Bass("TRN2", target_bir_lowering=False)

    def din(name, shape):
        return nc.dram_tensor(name, list(shape), F32, kind="ExternalInput").ap()

    def dout(name, shape):
        return nc.dram_tensor(name, list(shape), F32, kind="ExternalOutput").ap()

    xin = din("xin", [2 * NT * 128, D]); xs = din("xs", [NS, D])
    spool = din("spool", [2, 240, D]); sconv = din("sconv", [NL, 32, DFF])
    skw = din("skw", [NS, 128, 256]); svw = din("svw", [NS, 128, 256])
    pool_w = din("pool_w", [2, 4, 256, 256]); pool_scale = din("pool_scale", [2, D])
    w_kv = din("w_kv", [D, 512])
    w_q = din("w_q", [2, D, D]); w_o = din("w_o", [2, D, D])
    w_in = din("w_in", [NL, D, 2 * DFF]); w_out = din("w_out", [NL, DFF, D])
    lmg = din("lmg", [NL, D]); lmb = din("lmb", [NL, D]); lfg = din("lfg", [NL, D]); lfb = din("lfb", [NL, D])
    cst_d = din("cst", [128, NCST]); cstb_d = din("cstb", [128, NCSTB]); brow_d = din("brow", [1, 2560])

    y_out = dout("y_out", [2 * 8 * 128, D]); ys_out = dout("ys_out", [NS, D])
    npool_p = dout("npool_p", [2, 15, D]); npool_s = dout("npool_s", [2, NS, 15, D])
    nconv_p = dout("nconv_p", [NL, 2, DFF]); nconv_s = dout("nconv_s", [NL, NS, 2, DFF])
    nk_p = dout("nk_p", [128, 256]); nv_p = dout("nv_p", [128, 256])
    nk_s = dout("nk_s", [NS, 128, 256]); nv_s = dout("nv_s", [NS, 128, 256])

    if dbg:
        dbgy = dout("dbgy", [NT + 1, 128, D]); dbgx = dout("dbgx", [128, 8 * TC])
        dbgk = dout("dbgk", [128, 4 * TP]); dbgv = dout("dbgv", [128, NT * 256])
    S = Sched(nc)
    with contextlib.ExitStack() as st:
        def sbt(name, shape, dt):
            return st.enter_context(nc.sbuf_tensor("sb_" + name, list(shape), dt))

        def A(eng, fn, r=(), w=()):
            return S.op(eng, fn, reads=list(r), writes=list(w))

        cst = sbt("cst", [128, NCST], F32); cst_b = Buf("cst")
        cstb = sbt("cstb", [128, NCSTB], BF16); cstb_b = Buf("cstb")
        Y = sbt("Y", [128, NT + 1, D], F32); yb = [Buf(f"y{i}") for i in range(NT + 1)]
        for i_ in range(1, NT):
            yb[i_].slot = yb[0]
        XT = sbt("XT", [128, 8, TC], BF16); xtb = [Buf(f"xt{i}") for i in range(NT + 1)]
        lng = sbt("lng", [128, D], F32); lng_b = Buf("lng")
        lnb = sbt("lnb", [128, D], F32); lnb_b = Buf("lnb")
        KT2 = sbt("KT2", [128, 4, TP], BF16); kt_b = Buf("kt2")
        V = sbt("V", [128, NT, 256], BF16); v_b = [Buf(f"v{i}") for i in range(NT)]
        ident = sbt("ident", [128, 128], F32); ident_b = Buf("ident")
        identb = sbt("identb", [128, 128], BF16); identb_b = Buf("identb")
        ones33 = sbt("ones33", [33, 128], BF16); ones_b = Buf("ones33")
        bb = sbt("bb", [33, 2560], BF16); bb_b = Buf("bb")
        mhalf = sbt("mhalf", [128, 1], F32); mhalf_b = Buf("mhalf")
        NSL = 4
        sm = sbt("sm", [128, NSL, 24], F32); sm_b = [Buf(f"sm{i}") for i in range(NSL)]
        sa = sbt("sa", [128, 4, 48], F32); sa_b = [Buf(f"sa{i}") for i in range(4)]
        kvo = sbt("kvo", [128, 512], F32); kvo_b = Buf("kvo")
        arena = sbt("arena", [128, ARENA_BYTES], U8)
        PS = st.enter_context(nc.psum_tensor("PS", [128, 8, 512], F32))
        pair_b = [Buf(f"pp{i}") for i in range(4)]
        bank_b = [Buf(f"pb{i}") for i in range(8)]

        def pair(p):
            return PS[:, 2 * p:2 * p + 2, :].rearrange("p a b -> p (a b)")

        def bank(bk):
            return PS[:, bk, :]

        def bank_deps(bk):
            return [pair_b[bk // 2], bank_b[bk]]

        nks_b = Buf("nks"); nvs_b = Buf("nvs"); dram_misc_b = Buf("dmisc")

        ar = {"off": 0, "bufs": [], "old": []}
        slots = {}

        def arena_reset():
            ar["old"] = ar["old"][-200:] + ar["bufs"] if False else ar["bufs"]
            ar["bufs"] = []
            ar["off"] = 0

        def take(name, shape, dt, nb=1, at=None, after=()):
            esz = 4 if dt == F32 else 2
            free = 1
            for s_ in shape[1:]:
                free *= s_
            nbytes = free * esz * nb
            nbytes = (nbytes + 63) // 64 * 64
            if at is None:
                assert ar["off"] + nbytes <= ARENA_BYTES, (name, ar["off"], nbytes)
                ar["last_off"] = ar["off"]
                v = arena[:, ar["off"]:ar["off"] + nbytes].bitcast(dt)
                ar["off"] += nbytes
            else:
                v = arena[:, at:at + nbytes].bitcast(dt)
            v = v[:, 0:free * nb]
            bufs = []
            for i in range(nb):
                b = Buf(f"{name}{i}")
                b.slot = slots.setdefault(b.name, b)
                for ob in list(ar["old"]) + list(after):
                    toks = ([ob.w] if ob.w is not None else []) + list(ob.rs.values())
                    for t in toks:
                        key = ("e", t[1]) if t[0] == "e" else ("d", t[1])
                        cur = b.rs.get(key)
                        if cur is None or cur[2] < t[2]:
                            b.rs[key] = t
                bufs.append(b)
                ar["bufs"].append(b)
            return v, bufs

        def inherit(dst, srcs):
            for ob in srcs:
                toks = ([ob.w] if ob.w is not None else []) + list(ob.rs.values())
                for t in toks:
                    key = ("e", t[1]) if t[0] == "e" else ("d", t[1])
                    cur = dst.rs.get(key)
                    if cur is None or cur[2] < t[2]:
                        dst.rs[key] = t

        def view(v, pat, **kw):
            return v.rearrange(pat, **kw)

        S.dma("sp", cst[:], cst_d, writes=[cst_b])
        S.dma("pool", cstb[:], cstb_d, writes=[cstb_b])
        A("dve", lambda e: e.memset(ident[:], 0.0), w=[ident_b])
        A("pool", lambda e: e.affine_select(out=ident[:], in_=ident[:], compare_op=ALU.not_equal, fill=1.0,
                                            base=0, pattern=[[-1, 128]], channel_multiplier=1),
          r=[ident_b], w=[ident_b])
        A("act", lambda e: e.activation(out=identb[:], in_=ident[:], func=AF.Copy), r=[ident_b], w=[identb_b])
        A("dve", lambda e: e.memset(ones33[:], 1.0), w=[ones_b])
        A("dve", lambda e: e.memset(mhalf[:], -0.5), w=[mhalf_b])
        A("dve", lambda e: e.memset(bb[:], 0.0), w=[bb_b])
        arena_reset()
        bst, (bst_b,) = take("bst", [33, 2560], F32)
        bhi, (bhi_b,) = take("bhi", [33, 2560], BF16)
        blo, (blo_b,) = take("blo", [33, 2560], F32)
        A("dve", lambda e: e.memset(bst[0:33, :], 0.0), w=[bst_b])
        S.dma("sp", bst[0:1, :], brow_d, writes=[bst_b])
        S.dma("sp", bst[32:33, :], brow_d, writes=[bst_b])
        A("act", lambda e: e.activation(out=bhi[0:33, :], in_=bst[0:33, :], func=AF.Copy), r=[bst_b], w=[bhi_b])
        A("dve", lambda e: e.tensor_tensor(out=blo[0:33, :], in0=bst[0:33, :], in1=bhi[0:33, :], op=ALU.subtract),
          r=[bst_b, bhi_b], w=[blo_b])
        A("dve", lambda e: e.tensor_copy(out=bb[0:1, :], in_=bhi[0:1, :]), r=[bhi_b, bb_b], w=[bb_b])
        A("dve", lambda e: e.tensor_copy(out=bb[32:33, :], in_=blo[32:33, :]), r=[blo_b, bb_b], w=[bb_b])

        bandc = cstb[:, B_BC:B_BC + 1024].rearrange("p (v g t) -> p v g t", v=2, g=4)
        bandp = cstb[:, B_BP:B_BP + 512].rearrange("p (g t) -> p g t", g=4)
        sel = cstb[:, B_SEL:B_SEL + 128].rearrange("p (t g i) -> p t g i", t=2, g=4)
        coefI = cstb[:, B_CI:B_CI + 64].rearrange("p (g i) -> p g i", g=4)
        mk = cstb[:, B_MK:B_MK + 3072].rearrange("p (v h j) -> p v h j", v=6, h=2)
        iab = cstb[:, B_IAB:B_IAB + 256].rearrange("p (h q) -> p h q", h=2)

        cnt = {"sm": 0, "sa": 0, "pp": 0}

        def ln_A1(it):
            i, rows = it["i"], it["rows"]
            Yi = Y[0:rows, i, :]
            s_ = cnt["sm"] % NSL; cnt["sm"] += 1
            it["s"] = s_
            smb = sm_b[s_]
            t = sm[0:rows, s_, :]
            A("dve", lambda e: e.bn_stats(out=t[:, 0:6], in_=Yi[:, 0:512]), r=[yb[i]], w=[smb])
            A("dve", lambda e: e.bn_stats(out=t[:, 6:12], in_=Yi[:, 512:1024]), r=[yb[i], smb], w=[smb])
            A("dve", lambda e: e.bn_aggr(out=t[:, 12:14], in_=t[:, 0:12]), r=[smb], w=[smb])
            A("dve", lambda e: e.tensor_scalar(out=t[:, 14:15], in0=t[:, 13:14], scalar1=EPS, scalar2=None, op0=ALU.add),
              r=[smb], w=[smb])
            A("pool", lambda e: e.tensor_tensor(out=t[:, 15:16], in0=t[:, 14:15], in1=mhalf[0:rows, :], op=ALU.pow),
              r=[smb, mhalf_b], w=[smb])

        def ln_A2(it):
            i, rows, s_ = it["i"], it["rows"], it["s"]
            Yi = Y[0:rows, i, :]
            smb = sm_b[s_]
            t = sm[0:rows, s_, :]
            A("dve", lambda e: e.scalar_tensor_tensor(out=t[:, 16:17], in0=t[:, 12:13], scalar=-1.0, in1=t[:, 15:16],
                                                      op0=ALU.mult, op1=ALU.mult), r=[smb], w=[smb])
            A("act", lambda e: e.activation(out=Yi, in_=Yi, func=AF.Identity, bias=t[:, 16:17], scale=t[:, 15:16]),
              r=[yb[i], smb], w=[yb[i]])

        def ln_A3(it):
            i, rows, ch = it["i"], it["rows"], it["ch"]
            Yi = Y[0:rows, i, :]
            A("dve", lambda e: e.tensor_tensor(out=Yi, in0=Yi, in1=lng[0:rows, :], op=ALU.mult), r=[yb[i], lng_b], w=[yb[i]])
            A("dve", lambda e: e.tensor_tensor(out=Yi, in0=Yi, in1=lnb[0:rows, :], op=ALU.add), r=[yb[i], lnb_b], w=[yb[i]])
            if i < 2:
                c0 = C_TM + ch * 2 + i
                A("dve", lambda e: e.tensor_scalar(out=Yi, in0=Yi, scalar1=cst[0:rows, c0:c0 + 1], scalar2=None, op0=ALU.mult),
                  r=[yb[i], cst_b], w=[yb[i]])

        def ln_B(it):
            i, rows = it["i"], it["rows"]
            Yi = Y[0:rows, i, :]
            if it["need_xt"]:
                rpair = it["rpair"]() if callable(it["rpair"]) else it["rpair"]
                R = pair(rpair)
                for k in range(8):
                    A("pe", lambda e, k=k: e.transpose(out=R[:, k * 128:k * 128 + rows], in_=Yi[:, k * 128:(k + 1) * 128],
                                                       identity=ident[0:rows, 0:rows]),
                      r=[yb[i], ident_b], w=[pair_b[rpair]])
                cx = i * 128
                A("act", lambda e: e.activation(out=XT[:, :, cx:cx + rows],
                                                in_=R.rearrange("p (k t) -> p k t", k=8)[:, :, 0:rows], func=AF.Copy),
                  r=[pair_b[rpair]], w=[xtb[i]])
            if it.get("post") is not None:
                it["post"]()

        lnq = []

        def ln_push(ch, i, rows, need_xt, rpair, post=None):
            it = dict(ch=ch, i=i, rows=rows, need_xt=need_xt, rpair=rpair, post=post, st=1)
            ln_A1(it)
            lnq.append(it)
            if len(lnq) >= 2 and lnq[-2]["st"] == 1:
                ln_A2(lnq[-2]); lnq[-2]["st"] = 2
            if len(lnq) >= 3 and lnq[-3]["st"] == 2:
                ln_A3(lnq[-3]); lnq[-3]["st"] = 3
            if len(lnq) >= 4:
                o = lnq.pop(0)
                ln_B(o)

        def ln_flush():
            while lnq:
                for o in lnq:
                    if o["st"] == 1:
                        ln_A2(o); o["st"] = 2
                    elif o["st"] == 2:
                        ln_A3(o); o["st"] = 3
                    elif o["st"] == 3:
                        ln_B(o); o["st"] = 4
                while lnq and lnq[0]["st"] == 4:
                    lnq.pop(0)

        def ln_core(ch, i, rows, need_xt, rpair, post=None):
            ln_push(ch, i, rows, need_xt, rpair, post)

        def ln_mix(ch, i, rows, mp, need_xt, rpair):
            Yi = Y[0:rows, i, :]
            A("dve", lambda e: e.scalar_tensor_tensor(out=Yi, in0=Yi, scalar=ALPHA, in1=pair(mp)[0:rows, :],
                                                      op0=ALU.mult, op1=ALU.add), r=[yb[i], pair_b[mp]], w=[yb[i]])
            ln_core(ch, i, rows, need_xt, rpair)

        def load_ln(g_d, b_d, l):
            S.dma("sp", lng[:], g_d[l:l + 1, :].partition_broadcast(128), writes=[lng_b])
            S.dma("sp", lnb[:], b_d[l:l + 1, :].partition_broadcast(128), writes=[lnb_b])

        def pool_layer(ch, a):
            has_s = (ch == 1)
            arena_reset()
            psc, (psc_b,) = take("psc", [128, D], F32)
            wpf, (wpf_b,) = take("wpf", [128, 8, 256], F32)
            wp, (wp_b,) = take("wp", [128, 8, 256], BF16)
            ybf, ybf_b = take("ybf", [128, D], BF16, nb=3)
            dT, dT_b = take("dT", [128, 8, 128], BF16, nb=2)
            spb, (spb_b,) = take("spb", [128, 2, D], BF16)
            xnb, (xnb_b,) = take("xnb", [128, D], BF16)
            wpf4 = wpf.rearrange("p (g k e) -> p g k e", g=4, k=2)
            wp4 = wp.rearrange("p (g k e) -> p g k e", g=4, k=2)
            wp3 = wp.rearrange("p (c e) -> p c e", c=8)
            ybf3 = ybf.rearrange("p (s d) -> p s d", s=3)
            dT4 = dT.rearrange("p (s c t) -> p s c t", s=2, c=8)
            spb3 = spb.rearrange("p (t d) -> p t d", t=2)
            S.dma("sp", psc, pool_scale[a:a + 1, :].partition_broadcast(128), writes=[psc_b])
            S.dma("sp", wpf.rearrange("p (c e) -> p c e", c=8), pool_w[a].rearrange("g (k p) e -> p (g k) e", p=128), writes=[wpf_b])
            for kk in range(2):
                A("dve", lambda e, kk=kk: e.tensor_tensor(out=wp4[:, :, kk, :], in0=wpf4[:, :, kk, :],
                                                          in1=psc.rearrange("p (g e) -> p g e", g=4), op=ALU.mult),
                  r=[wpf_b, psc_b], w=[wp_b])
            load_ln(lmg, lmb, a)
            if has_s:
                for t_ in range(2):
                    S.dma("pool", spb3[0:120, t_, :], spool[a, t_ * 120:(t_ + 1) * 120, :], writes=[spb_b])
            def stageA(i):
                sl = i % 3
                A("act", lambda e, i=i, sl=sl: e.activation(out=ybf3[:, sl, :], in_=Y[:, i, :], func=AF.Copy),
                  r=[yb[i]], w=[ybf_b[sl]])
                dp = i % 2
                P = pair(dp)
                var = 0 if (ch == 0 and i == 1) else 1
                for kc in range(8):
                    g = kc // 2
                    A("pe", lambda e, kc=kc, g=g, sl=sl, var=var, P=P, i=i: e.matmul(
                        P[:, kc * 128:(kc + 1) * 128], lhsT=ybf3[:, sl, kc * 128:(kc + 1) * 128], rhs=bandc[:, var, g, :],
                        start=True, stop=(i == 0)), r=[ybf_b[sl], cstb_b], w=[pair_b[dp]])
                    if i > 0:
                        sp_ = (i - 1) % 3
                        A("pe", lambda e, kc=kc, g=g, sp_=sp_, P=P: e.matmul(
                            P[:, kc * 128:(kc + 1) * 128], lhsT=ybf3[:, sp_, kc * 128:(kc + 1) * 128], rhs=bandp[:, g, :],
                            start=False, stop=True), r=[ybf_b[sp_], cstb_b], w=[pair_b[dp]])
                if ch == 1 and i == NT - 1:
                    S.dma("sp", npool_p[a], Y[113:128, i, :], reads=[yb[i]])

            def stageB(i):
                dp = i % 2
                mp = 2 + (i % 2)
                P = pair(dp)
                ds = i % 2
                A("act", lambda e, ds=ds, P=P: e.activation(out=dT4[:, ds, :, :], in_=P.rearrange("p (c t) -> p c t", c=8),
                                                            func=AF.Copy), r=[pair_b[dp]], w=[dT_b[ds]])
                Q = pair(mp)
                for g in range(4):
                    for kk in range(2):
                        A("pe", lambda e, g=g, kk=kk, ds=ds, Q=Q: e.matmul(
                            Q[:, g * 256:(g + 1) * 256], lhsT=dT4[:, ds, 2 * g + kk, :], rhs=wp3[:, 2 * g + kk, :],
                            start=(kk == 0), stop=(kk == 1)), r=[dT_b[ds], wp_b], w=[pair_b[mp]])
                ln_mix(ch, i, 128, mp, True, mp)

            stageA(0)
            for i in range(NT):
                if i + 1 < NT:
                    stageA(i + 1)
                stageB(i)
            if has_s:
                i = NT
                S.dma("sp", npool_s[a, :, 14, :], Y[0:NS, i, :], reads=[yb[i]])
                S.dma("sp", npool_s[a, :, 0:14, :], spool[a].rearrange("(i r) d -> i r d", r=15)[:, 1:15, :], writes=[dram_misc_b])
                A("act", lambda e: e.activation(out=xnb[0:NS, :], in_=Y[0:NS, NT, :], func=AF.Copy), r=[yb[i]], w=[xnb_b])
                dp, mp = 0, 2
                P = pair(dp)
                for kc in range(8):
                    g = kc // 2
                    for t_ in range(2):
                        A("pe", lambda e, kc=kc, g=g, t_=t_: e.matmul(
                            P[:, kc * 128:kc * 128 + NS], lhsT=spb3[0:120, t_, kc * 128:(kc + 1) * 128], rhs=sel[0:120, t_, g, :],
                            start=(t_ == 0), stop=False), r=[spb_b, cstb_b], w=[pair_b[dp]])
                    A("pe", lambda e, kc=kc, g=g: e.matmul(
                        P[:, kc * 128:kc * 128 + NS], lhsT=xnb[0:NS, kc * 128:(kc + 1) * 128], rhs=coefI[0:NS, g, :],
                        start=False, stop=True), r=[xnb_b, cstb_b], w=[pair_b[dp]])
                A("act", lambda e: e.activation(out=dT4[:, 0, :, 0:NS], in_=P.rearrange("p (c t) -> p c t", c=8)[:, :, 0:NS],
                                                func=AF.Copy), r=[pair_b[dp]], w=[dT_b[0]])
                Q = pair(mp)
                for g in range(4):
                    for kk in range(2):
                        A("pe", lambda e, g=g, kk=kk: e.matmul(
                            Q[0:NS, g * 256:(g + 1) * 256], lhsT=dT4[:, 0, 2 * g + kk, 0:NS], rhs=wp3[:, 2 * g + kk, :],
                            start=(kk == 0), stop=(kk == 1)), r=[dT_b[0], wp_b], w=[pair_b[mp]])
                ln_mix(ch, i, NS, mp, True, dp)
            ln_flush()

        def ffn_layer(ch, l):
            has_s = (ch == 1)
            ntile = NT + (1 if has_s else 0)
            arena_reset()
            HT, _ = take("HT", [128, 6, TC], BF16)
            HT3 = HT.rearrange("p (j t) -> p j t", j=6)
            ht_b = [[Buf(f"ht{j}_{t}") for t in range(3)] for j in range(6)]
            for row in ht_b:
                for b in row:
                    for ob in ar["old"]:
                        toks = ([ob.w] if ob.w is not None else []) + list(ob.rs.values())
                        for t in toks:
                            key = ("e", t[1]) if t[0] == "e" else ("d", t[1])
                            cur = b.rs.get(key)
                            if cur is None or cur[2] < t[2]:
                                b.rs[key] = t
                    ar["bufs"].append(b)
            wout, wout_b = take("wout", [128, 6, D], BF16, nb=2)
            wout4 = wout.rearrange("p (s j d) -> p s j d", s=2, j=6)
            win, win_b = take("win", [128, 8, 256], BF16, nb=4)
            win4 = win.rearrange("p (s k c) -> p s k c", s=4, k=8)
            gsb, gsb_b2 = take("gsb", [128, 2 + TC], F32, nb=2)
            gsb3 = gsb.rearrange("p (s t) -> p s t", s=2)
            gsb_b = [[Buf(f"gs{s_}_{t}") for t in range(4)] for s_ in range(2)]
            for s_ in range(2):
                for b in gsb_b[s_]:
                    b.rs = dict(gsb_b2[s_].rs)
                    ar["bufs"].append(b)
            cc, cc_b = take("cc", [128, 512], F32, nb=2)
            cc3 = cc.rearrange("p (s t) -> p s t", s=2)
            ss, ss_b = take("ss", [128, 512], F32, nb=2)
            ss3 = ss.rearrange("p (s t) -> p s t", s=2)
            us, us_b = take("us", [128, 512], F32, nb=2)
            us3 = us.rearrange("p (s t) -> p s t", s=2)
            if has_s:
                scT, (scT_b,) = take("scT", [128, NJ, 32], F32)
                scT3 = scT.rearrange("p (j c) -> p j c", j=NJ)
                cs, (cs_b,) = take("cs", [128, DFF], F32)
            load_ln(lfg, lfb, l)
            for bk_ in range(4, 8):
                inherit(bank_b[bk_], [pair_b[bk_ // 2]])
            i_lo = 1 if l >= 2 else 0
            c_lo = 128 * i_lo
            for s_ in range(2):
                A("dve", lambda e, s_=s_: e.memset(gsb3[:, s_, c_lo:c_lo + 2], 0.0), w=[gsb_b[s_][0]])
            if has_s:
                S.dma("sp", cs[0:32, :], sconv[l], writes=[cs_b])
                S.dma("sp", nconv_s[l, :, 0, :], sconv[l].rearrange("(i r) f -> i r f", r=2)[:, 1, :], writes=[dram_misc_b])
                for j0 in range(0, NJ, 8):
                    nj = min(8, NJ - j0)
                    pp = (j0 // 8) % 2
                    for jj in range(nj):
                        A("pe", lambda e, j0=j0, jj=jj, pp=pp: e.transpose(
                            out=pair(pp)[:, jj * 32:(jj + 1) * 32], in_=cs[0:32, (j0 + jj) * 128:(j0 + jj + 1) * 128],
                            identity=ident[0:32, 0:32]), r=[cs_b, ident_b], w=[pair_b[pp]])
                    A("dve", lambda e, j0=j0, nj=nj, pp=pp: e.tensor_copy(
                        out=scT3[:, j0:j0 + nj, :], in_=pair(pp)[:, 0:nj * 32].rearrange("p (j c) -> p j c", j=nj)),
                      r=[pair_b[pp]], w=[scT_b])
            if i_lo == 0:
                TT = [(0, 512), (512, 512), (1024, 256 + (NS if has_s else 0))]
            else:
                TT = [(128, 512), (640, 512), (1152, 128 + (NS if has_s else 0))]
            winv = w_in[l].rearrange("(k p) c -> p k c", p=128)
            slab = 0
            u1 = 0
            accs = {"n": 0}
            pend = {"f": None}
            exn = {"n": 0}
            pend2 = {"f": None}
            for gi, (j0, j1) in enumerate(GROUPS):
                wo = gi % 2
                ng = j1 - j0
                def load_wout(wo=wo, ng=ng, j0=j0, j1=j1):
                    S.dma("pool", wout4[:, wo, 0:ng, :], w_out[l, j0 * 128:j1 * 128, :].rearrange("(j p) d -> p j d", p=128),
                          writes=[wout_b[wo]])
                if gi > 0:
                    load_wout()
                for j in range(j0, j1):
                    s_ = slab % 4; slab += 1
                    S.dma("pool", win4[:, s_, :, 0:128], winv[:, :, j * 128:(j + 1) * 128], writes=[win_b[s_]])
                    S.dma("pool", win4[:, s_, :, 128:256], winv[:, :, DFF + j * 128:DFF + (j + 1) * 128], writes=[win_b[s_]])
                    if gi == 0 and j == j0 + 2:
                        load_wout()
                    gs = j % 2
                    cw = C_CW + (l * NJ + j) * 3
                    cbc = C_CB + l * NJ + j
                    for tt, (t0, n) in enumerate(TT):
                        set_ = u1 % 2; u1 += 1
                        bg, bu = 4 + 2 * set_, 5 + 2 * set_
                        pg, pu = bank(bg), bank(bu)
                        xr = [xtb[ii] for ii in range(t0 // 128, min(NT, (t0 + n + 127) // 128))]
                        if has_s and tt == 2:
                            xr.append(xtb[NT])
                        for k in range(8):
                            A("pe", lambda e, k=k, s_=s_, pg=pg, t0=t0, n=n: e.matmul(
                                pg[:, 0:n], lhsT=win4[:, s_, k, 0:128], rhs=XT[:, k, t0:t0 + n], start=(k == 0), stop=(k == 7)),
                              r=[win_b[s_]] + xr, w=[bank_b[bg]])
                        for k in range(8):
                            A("pe", lambda e, k=k, s_=s_, pu=pu, t0=t0, n=n: e.matmul(
                                pu[:, 0:n], lhsT=win4[:, s_, k, 128:256], rhs=XT[:, k, t0:t0 + n], start=(k == 0), stop=(k == 7)),
                              r=[win_b[s_]] + xr, w=[bank_b[bu]])
                        npr = min(n, TP - t0)
                        cs_ = u1 % 2
                        A("act", lambda e, gs=gs, t0=t0, n=n, pg=pg: e.activation(
                            out=gsb3[:, gs, 2 + t0:2 + t0 + n], in_=pg[:, 0:n], func=AF.Copy),
                          r=[bank_b[bg]], w=[gsb_b[gs][1 + tt]])
                        A("act", lambda e, cs_=cs_, n=n, pg=pg, cw=cw, cbc=cbc: e.activation(
                            out=cc3[:, cs_, 0:n], in_=pg[:, 0:n], func=AF.Identity, bias=cst[:, cbc:cbc + 1],
                            scale=cst[:, cw + 2:cw + 3]), r=[bank_b[bg]] + [cst_b], w=[cc_b[cs_]])
                        grd = [gsb_b[gs][tt], gsb_b[gs][1 + tt]]
                        A("dve", lambda e, cs_=cs_, gs=gs, t0=t0, npr=npr, cw=cw: e.scalar_tensor_tensor(
                            out=cc3[:, cs_, 0:npr], in0=gsb3[:, gs, 1 + t0:1 + t0 + npr], scalar=cst[:, cw + 1:cw + 2],
                            in1=cc3[:, cs_, 0:npr], op0=ALU.mult, op1=ALU.add), r=grd + [cc_b[cs_], cst_b], w=[cc_b[cs_]])
                        A("dve", lambda e, cs_=cs_, gs=gs, t0=t0, npr=npr, cw=cw: e.scalar_tensor_tensor(
                            out=cc3[:, cs_, 0:npr], in0=gsb3[:, gs, t0:t0 + npr], scalar=cst[:, cw:cw + 1],
                            in1=cc3[:, cs_, 0:npr], op0=ALU.mult, op1=ALU.add), r=grd + [cc_b[cs_], cst_b], w=[cc_b[cs_]])
                        if n > npr:
                            A("dve", lambda e, cs_=cs_, j=j, npr=npr, n=n, cw=cw: e.scalar_tensor_tensor(
                                out=cc3[:, cs_, npr:n], in0=scT3[:, j, 1:32:2], scalar=cst[:, cw + 1:cw + 2],
                                in1=cc3[:, cs_, npr:n], op0=ALU.mult, op1=ALU.add), r=[scT_b, cc_b[cs_], cst_b], w=[cc_b[cs_]])
                            A("dve", lambda e, cs_=cs_, j=j, npr=npr, n=n, cw=cw: e.scalar_tensor_tensor(
                                out=cc3[:, cs_, npr:n], in0=scT3[:, j, 0:32:2], scalar=cst[:, cw:cw + 1],
                                in1=cc3[:, cs_, npr:n], op0=ALU.mult, op1=ALU.add), r=[scT_b, cc_b[cs_], cst_b], w=[cc_b[cs_]])
                        A("act", lambda e, cs_=cs_, n=n, pu=pu: e.activation(out=us3[:, cs_, 0:n], in_=pu[:, 0:n], func=AF.Copy),
                          r=[bank_b[bu]], w=[us_b[cs_]])

                        def second(cs_=cs_, n=n, j=j, j0=j0, t0=t0, tt=tt):
                            A("act", lambda e: e.activation(out=ss3[:, cs_, 0:n], in_=cc3[:, cs_, 0:n], func=AF.Silu),
                              r=[cc_b[cs_]], w=[ss_b[cs_]])
                            A("dve", lambda e: e.tensor_tensor(
                                out=HT3[:, j - j0, t0:t0 + n], in0=ss3[:, cs_, 0:n], in1=us3[:, cs_, 0:n], op=ALU.mult),
                              r=[ss_b[cs_], us_b[cs_]], w=[ht_b[j - j0][tt]])
                        if pend2["f"] is not None:
                            pend2["f"]()
                        pend2["f"] = second
                        if tt == 0 and pend["f"] is not None:
                            pend["f"](); pend["f"] = None
                    if has_s:
                        def export(j=j, gs=gs):
                            eb = exn["n"] % 4; exn["n"] += 1
                            A("pe", lambda e: e.transpose(out=bank(eb)[0:18, 0:128], in_=gsb3[:, gs, TP:TP + 18],
                                                          identity=ident[:, :]),
                              r=[gsb_b[gs][3], ident_b], w=bank_deps(eb))
                            A("act", lambda e: e.activation(out=cs[0:18, j * 128:(j + 1) * 128], in_=bank(eb)[0:18, 0:128],
                                                            func=AF.Copy), r=bank_deps(eb), w=[cs_b])
                        pend["f"] = export
                if pend2["f"] is not None:
                    pend2["f"](); pend2["f"] = None
                if pend["f"] is not None:
                    pend["f"](); pend["f"] = None
                last = (gi == len(GROUPS) - 1)
                for i in range(i_lo, ntile):
                    rows = 128 if i < NT else NS
                    c0 = i * 128
                    tt = min((i - i_lo) // 4, 2)
                    ap_ = accs["n"] % 2; accs["n"] += 1
                    P = pair(ap_)
                    for jj in range(ng):
                        for half in range(2):
                            A("pe", lambda e, jj=jj, half=half, rows=rows, c0=c0, P=P, wo=wo, ng=ng: e.matmul(
                                P[0:rows, half * 512:(half + 1) * 512], lhsT=HT3[:, jj, c0:c0 + rows],
                                rhs=wout4[:, wo, jj, half * 512:(half + 1) * 512], start=(jj == 0), stop=(jj == ng - 1)),
                              r=[ht_b[jj][tt], wout_b[wo]], w=[pair_b[ap_]])
                    Yi = Y[0:rows, i, :]
                    if gi == 0:
                        A("dve", lambda e, Yi=Yi, P=P, rows=rows: e.scalar_tensor_tensor(
                            out=Yi, in0=Yi, scalar=ALPHA, in1=P[0:rows, :], op0=ALU.mult, op1=ALU.add),
                          r=[yb[i], pair_b[ap_]], w=[yb[i]])
                    else:
                        A("dve", lambda e, Yi=Yi, P=P, rows=rows: e.tensor_tensor(out=Yi, in0=Yi, in1=P[0:rows, :], op=ALU.add),
                          r=[yb[i], pair_b[ap_]], w=[yb[i]])
                    if last:
                        def rp_fn():
                            v = accs["n"] % 2; accs["n"] += 1
                            return v
                        post = None
                        if l == NL - 1 and i == NT:
                            post = lambda: S.dma("sp", ys_out, Y[0:NS, NT, :], reads=[yb[NT]])
                        elif l == NL - 1 and i >= 2:
                            def post(i=i):
                                r0 = (ch * 8 + i - 2) * 128
                                S.dma("sp", y_out[r0:r0 + 128, :], Y[:, i, :], reads=[yb[i]])
                        ln_core(ch, i, rows, l in (1, 2), rp_fn, post)
            ln_flush()
            inherit(pair_b[2], [bank_b[4], bank_b[5]])
            inherit(pair_b[3], [bank_b[6], bank_b[7]])
            if has_s:
                S.dma("sp", nconv_p[l], cs[0:2, :], reads=[cs_b])
                S.dma("sp", nconv_s[l, :, 1, :], cs[2:18, :], reads=[cs_b])

        def kv_proj(ch):
            has_s = (ch == 1)
            arena_reset()
            wkt, (wkt_b,) = take("wkt", [128, 8, 512], BF16)
            wkt3 = wkt.rearrange("p (k c) -> p k c", k=8)
            wk2, (wk2_b,) = take("wk2", [128, 8, 4, 128], BF16)
            wk24 = wk2.rearrange("p (k h c) -> p k h c", k=8, h=4)
            kvs, (kvs_b,) = take("kvs", [128, 512], F32)
            wv = w_kv.rearrange("(k p) c -> p k c", p=128)
            for kh in range(4):
                for dup in range(2):
                    S.dma("pool", wk24[:, :, kh, dup * 64:(dup + 1) * 64], wv[:, :, kh * 64:(kh + 1) * 64], writes=[wk2_b])
            S.dma("pool", wkt3, wv, writes=[wkt_b])
            u = 0
            for kh in range(4):
                for (t0, n) in [(0, 512), (512, 512), (1024, 256)]:
                    bk = 4 + (u % 4); u += 1
                    xr = [xtb[ii] for ii in range(t0 // 128, (t0 + n) // 128)]
                    for k in range(8):
                        A("pe", lambda e, k=k, kh=kh, bk=bk, t0=t0, n=n: e.matmul(
                            bank(bk)[:, 0:n], lhsT=wk24[:, k, kh, :], rhs=XT[:, k, t0:t0 + n], start=(k == 0), stop=(k == 7)),
                          r=[wk2_b] + xr, w=bank_deps(bk))
                    A("act", lambda e, kh=kh, bk=bk, t0=t0, n=n: e.activation(
                        out=KT2[:, kh, t0:t0 + n], in_=bank(bk)[:, 0:n], func=AF.Identity,
                        bias=cst[:, C_BKT + kh:C_BKT + kh + 1], scale=1.0), r=bank_deps(bk) + [cst_b], w=[kt_b])
            ntile = NT + (1 if (has_s and not (KVDBG & 2)) else 0)
            for i in range(ntile):
                rows = 128 if i < NT else NS
                c0 = i * 128
                bk = i % 4
                for k in range(8):
                    A("pe", lambda e, k=k, bk=bk, rows=rows, c0=c0: e.matmul(
                        bank(bk)[0:rows, :], lhsT=XT[:, k, c0:c0 + rows], rhs=wkt3[:, k, :], start=(k == 0), stop=False),
                      r=[wkt_b, xtb[i]], w=bank_deps(bk))
                A("pe", lambda e, bk=bk, rows=rows: e.matmul(
                    bank(bk)[0:128, :], lhsT=ones33[:, 0:128], rhs=bb[:, 2048:2560], start=False, stop=True),
                  r=[ones_b, bb_b], w=bank_deps(bk))
                if i < NT:
                    A("act", lambda e, i=i, bk=bk: e.activation(out=V[:, i, :], in_=bank(bk)[:, 256:512], func=AF.Copy),
                      r=bank_deps(bk), w=[v_b[i]])
                if ch == 1 and i == NT - 1 and not (KVDBG & 4):
                    A("act", lambda e, bk=bk: e.activation(out=kvo[:], in_=bank(bk)[:, :], func=AF.Copy), r=bank_deps(bk), w=[kvo_b])
                    S.dma("sp", nk_p, kvo[:, 0:256], reads=[kvo_b])
                    S.dma("sp", nv_p, kvo[:, 256:512], reads=[kvo_b])
                if i == NT:
                    A("dve", lambda e, bk=bk: e.tensor_copy(out=kvs[0:NS, :], in_=bank(bk)[0:NS, :]), r=bank_deps(bk), w=[kvs_b])
                    S.dma("sp", nk_s[:, 127, :], kvs[0:NS, 0:256], reads=[kvs_b], writes=[nks_b])
                    S.dma("sp", nv_s[:, 127, :], kvs[0:NS, 256:512], reads=[kvs_b], writes=[nvs_b])
                    for (a0, a1) in ([] if (KVDBG & 1) else [(1, 33), (33, 65), (65, 97), (97, 128)]):
                        S.dma("sp", nk_s[:, a0 - 1:a1 - 1, :], skw[:, a0:a1, :], writes=[nks_b])
                        S.dma("sp", nv_s[:, a0 - 1:a1 - 1, :], svw[:, a0:a1, :], writes=[nvs_b])

        def attn_layer(ch, l):
            bi = l - 2
            has_s = (ch == 1)
            arena_reset()
            wq, (wq_b,) = take("wq", [128, 8, D], BF16)
            wq_at = ar["last_off"]
            wq3 = wq.rearrange("p (k c) -> p k c", k=8)
            wo_, (wo_b,) = take("wo", [128, 8, D], BF16)
            wo3 = wo_.rearrange("p (k c) -> p k c", k=8)
            qT, (qT_b,) = take("qTa", [128, 8, TP], BF16)
            qT3 = qT.rearrange("p (m t) -> p m t", m=8)
            en, en_b = take("en", [128, 4, 256], BF16, nb=2)
            en4 = en.rearrange("p (d h s) -> p d h s", d=2, h=4)
            ee, ee_b = take("ee", [128, 4, 256], F32, nb=2)
            ee4 = ee.rearrange("p (d h s) -> p d h s", d=2, h=4)
            PT, PT_b = take("PT", [128, 4, 2, 128], BF16, nb=2)
            PT5 = PT.rearrange("p (d h f q) -> p d h f q", d=2, h=4, f=2)
            oT, (oT_b,) = take("oT", [128, 8, 128], BF16)
            oT3 = oT.rearrange("p (m t) -> p m t", m=8)
            S.dma("pool", wq3, w_q[bi].rearrange("(k p) c -> p k c", p=128), writes=[wq_b])
            S.dma("pool", wo3, w_o[bi].rearrange("(k p) c -> p k c", p=128), writes=[wo_b])
            load_ln(lmg, lmb, l)
            sinkb = cst[:, C_SINKB + bi * 16:C_SINKB + bi * 16 + 16]
            nsinkb = cst[:, C_NSINKB + bi * 16:C_NSINKB + bi * 16 + 16]
            uq = 0
            for m in range(8):
                for (t0, n) in [(128, 512), (640, 512), (1152, 128)]:
                    bk = 4 + (uq % 4); uq += 1
                    xr = [xtb[ii] for ii in range(t0 // 128, (t0 + n) // 128)]
                    for k in range(8):
                        A("pe", lambda e, k=k, m=m, bk=bk, t0=t0, n=n: e.matmul(
                            bank(bk)[:, 0:n], lhsT=wq3[:, k, m * 128:(m + 1) * 128], rhs=XT[:, k, t0:t0 + n],
                            start=(k == 0), stop=(k == 7)), r=[wq_b] + xr, w=bank_deps(bk))
                    cq = C_BQT + bi * 8 + m
                    if uq % 2 == 0:
                        A("act", lambda e, m=m, bk=bk, t0=t0, n=n, cq=cq: e.activation(
                            out=qT3[:, m, t0:t0 + n], in_=bank(bk)[:, 0:n], func=AF.Identity, bias=cst[:, cq:cq + 1], scale=1.0),
                          r=bank_deps(bk) + [cst_b], w=[qT_b])
                    else:
                        A("dve", lambda e, m=m, bk=bk, t0=t0, n=n, cq=cq: e.tensor_scalar(
                            out=qT3[:, m, t0:t0 + n], in0=bank(bk)[:, 0:n], scalar1=cst[:, cq:cq + 1], scalar2=None, op0=ALU.add),
                          r=bank_deps(bk) + [cst_b], w=[qT_b])

            def out_proj(rows, osrc, mp):
                MP = pair(mp)
                for half in range(2):
                    for m in range(8):
                        A("pe", lambda e, half=half, m=m: e.matmul(
                            MP[0:rows, half * 512:(half + 1) * 512], lhsT=osrc(m), rhs=wo3[:, m, half * 512:(half + 1) * 512],
                            start=(m == 0), stop=False), r=[oT_b, wo_b], w=[pair_b[mp]])
                    A("pe", lambda e, half=half: e.matmul(
                        MP[0:rows, half * 512:(half + 1) * 512], lhsT=ones33[:, 0:rows],
                        rhs=bb[:, bi * 1024 + half * 512:bi * 1024 + (half + 1) * 512], start=False, stop=True),
                      r=[ones_b, bb_b], w=[pair_b[mp]])

            def make_tile(i):
                OP = pair(1)
                v_ = 0 if i == 1 else (1 if i == 2 else 2)
                mv = ch * 3 + v_
                SPp = pair(2)
                TPb = pair(3).bitcast(BF16)

                def scores(kh, i=i):
                    for sl4 in range(4):
                        hh = PERM[sl4]
                        h = 4 * kh + hh; m = h // 2; po = 64 * (h % 2)
                        A("pe", lambda e, hh=sl4, m=m, po=po, kh=kh, i=i: e.matmul(
                            SPp[:, hh * 256:(hh + 1) * 256], lhsT=qT3[po:po + 64, m, i * 128:(i + 1) * 128],
                            rhs=KT2[po:po + 64, kh, (i - 1) * 128:(i + 1) * 128], start=True, stop=False),
                          r=[qT_b, kt_b], w=[pair_b[2]])
                        for hf in range(2):
                            A("pe", lambda e, hh=sl4, po=po, hf=hf: e.matmul(
                                SPp[:, hh * 256:(hh + 1) * 256], lhsT=iab[po:po + 64, hf, :],
                                rhs=mk[po:po + 64, mv, hf, :], start=False, stop=(hf == 1)),
                              r=[cstb_b], w=[pair_b[2]])

                def c1(kh):
                    d = kh % 2
                    s_ = cnt["sa"] % 4; cnt["sa"] += 1
                    t = sa[:, s_, :]
                    sab = sa_b[s_]
                    sS = SPp.rearrange("p (h s) -> p h s", h=4); eE = ee4[:, d]
                    A("dve", lambda e: e.tensor_reduce(out=t[:, 0:4], in_=sS, axis=AX.X, op=ALU.max), r=[pair_b[2]], w=[sab])
                    A("dve", lambda e: e.scalar_tensor_tensor(
                        out=t[:, 8:12], in0=t[:, 0:4], scalar=-SCALE, in1=nsinkb[:, 4 * kh:4 * kh + 4], op0=ALU.mult, op1=ALU.min),
                      r=[sab, cst_b], w=[sab])
                    A("dve", lambda e: e.tensor_tensor(out=t[:, 16:20], in0=sinkb[:, 4 * kh:4 * kh + 4], in1=t[:, 8:12],
                                                       op=ALU.add), r=[sab, cst_b], w=[sab])
                    for hh in range(4):
                        A("act", lambda e, hh=hh: e.activation(
                            out=eE[:, hh, :], in_=sS[:, hh, :], func=AF.Exp, bias=t[:, 8 + hh:9 + hh], scale=SCALE,
                            accum_out=t[:, 12 + hh:13 + hh]), r=[pair_b[2], sab], w=[ee_b[d], sab])
                    A("act", lambda e: e.activation(out=t[:, 20:24], in_=t[:, 16:20], func=AF.Exp), r=[sab], w=[sab])
                    return (t, sab)

                def c2(kh, ts):
                    d = kh % 2
                    t, sab = ts
                    eE = ee4[:, d]; eN = en4[:, d]
                    A("dve", lambda e: e.tensor_tensor(out=t[:, 24:28], in0=t[:, 12:16], in1=t[:, 20:24], op=ALU.add),
                      r=[sab], w=[sab])
                    A("dve", lambda e: e.reciprocal(out=t[:, 28:32], in_=t[:, 24:28]), r=[sab], w=[sab])
                    A("dve", lambda e: e.tensor_tensor(
                        out=eN, in0=eE, in1=t[:, 28:32].unsqueeze(2).broadcast_to([128, 4, 256]), op=ALU.mult),
                      r=[ee_b[d], sab], w=[en_b[d]])

                def tp(kh, i=i):
                    d = kh % 2
                    eN = en4[:, d]
                    for hh in range(4):
                        for half in range(2):
                            A("pe", lambda e, hh=hh, half=half: e.transpose(
                                out=TPb[:, (hh * 2 + half) * 128:(hh * 2 + half + 1) * 128],
                                in_=eN[:, hh, half * 128:(half + 1) * 128], identity=identb[:, :]),
                              r=[en_b[d], identb_b], w=[pair_b[3]])
                    A("act", lambda e: e.activation(out=PT5[:, d].rearrange("p h f q -> p (h f q)"), in_=TPb[:, 0:1024], func=AF.Copy),
                      r=[pair_b[3]], w=[PT_b[d]])
                    for sl4 in range(4):
                        hh = PERM[sl4]
                        h = 4 * kh + hh; m = h // 2; po = 64 * (h % 2)
                        for half in range(2):
                            A("pe", lambda e, hh=sl4, half=half, m=m, po=po, kh=kh, i=i: e.matmul(
                                OP[po:po + 64, m * 128:(m + 1) * 128], lhsT=V[:, i - 1 + half, kh * 64:(kh + 1) * 64],
                                rhs=PT5[:, d, hh, half, :], start=(half == 0), stop=(half == 1)),
                              r=[PT_b[d], v_b[i - 1], v_b[i]], w=[pair_b[1]])

                def otcopy():
                    A("act", lambda e: e.activation(out=oT, in_=OP, func=AF.Copy), r=[pair_b[1]], w=[oT_b])

                def outproj():
                    out_proj(128, lambda m: oT3[:, m, :], 0)

                def lnpart():
                    ln_mix(ch, i, 128, 0, True, 3)

                return dict(scores=scores, c1=c1, c2=c2, tp=tp, otcopy=otcopy, outproj=outproj, lnpart=lnpart, st={})

            tls = [make_tile(i) for i in range(1, NT)]
            T0 = tls[0]
            T0["scores"](0); T0["st"][0] = T0["c1"](0)
            T0["scores"](1); T0["st"][1] = T0["c1"](1)
            prev = None
            for ti, T_ in enumerate(tls):
                N_ = tls[ti + 1] if ti + 1 < len(tls) else None
                st_ = T_["st"]
                T_["scores"](2)
                if prev is not None:
                    prev["outproj"]()
                T_["c2"](0, st_[0]); T_["tp"](0)
                st_[2] = T_["c1"](2)
                if prev is not None:
                    prev["lnpart"]()
                T_["scores"](3)
                T_["c2"](1, st_[1]); T_["tp"](1)
                st_[3] = T_["c1"](3)
                if N_ is not None:
                    N_["scores"](0)
                T_["c2"](2, st_[2]); T_["tp"](2)
                if N_ is not None:
                    N_["st"][0] = N_["c1"](0)
                    N_["scores"](1)
                T_["c2"](3, st_[3]); T_["tp"](3)
                T_["otcopy"]()
                if N_ is not None:
                    N_["st"][1] = N_["c1"](1)
                prev = T_
            prev["outproj"]()
            prev["lnpart"]()

            if has_s:
                i = NT
                Ksb, (Ksb_b,) = take("Ksb", [128, NS, 256], BF16)
                Ksb3 = Ksb.rearrange("p (i d) -> p i d", i=NS)
                Vsb, (Vsb_b,) = take("Vsb", [128, NS, 256], BF16)
                Vsb3 = Vsb.rearrange("p (i d) -> p i d", i=NS)
                KsT, (KsT_b,) = take("KsT", [128, NS, 4, 128], BF16, at=wq_at, after=[wq_b])
                KsT4 = KsT.rearrange("p (i h s) -> p i h s", i=NS, h=4)
                qsT, (qsT_b,) = take("qsT", [128, 16, NS], BF16)
                qsT3 = qsT.rearrange("p (h i) -> p h i", h=16)
                STs, (STs_b,) = take("STs", [128, 256], F32)
                es, (es_b,) = take("es", [128, 2, 128], F32)
                es3 = es.rearrange("p (f s) -> p f s", f=2)
                PTs, (PTs_b,) = take("PTs", [128, 256], BF16)
                osT, (osT_b,) = take("osT", [128, 8, NS], BF16)
                osT3 = osT.rearrange("p (m i) -> p m i", m=8)
                S.dma("pool", Ksb3, nk_s.rearrange("i s d -> s i d"), reads=[nks_b], writes=[Ksb_b])
                S.dma("pool", Vsb3, nv_s.rearrange("i s d -> s i d"), reads=[nvs_b], writes=[Vsb_b])
                QS = pair(0)
                for h in range(16):
                    for k in range(8):
                        A("pe", lambda e, h=h, k=k: e.matmul(
                            QS[0:64, h * 16:(h + 1) * 16], lhsT=wq3[:, k, h * 64:(h + 1) * 64], rhs=XT[:, k, TP:TP + NS],
                            start=(k == 0), stop=(k == 7)), r=[wq_b, xtb[NT]], w=[pair_b[0]])
                c0 = C_BQH + bi * 16
                A("dve", lambda e, c0=c0: e.tensor_tensor(
                    out=qsT3[0:64, :, :], in0=QS[0:64, 0:256].rearrange("p (h i) -> p h i", h=16),
                    in1=cst[0:64, c0:c0 + 16].unsqueeze(2).broadcast_to([64, 16, NS]), op=ALU.add),
                  r=[pair_b[0], cst_b], w=[qsT_b])
                for i0 in range(0, NS, 4):
                    pp = 2 + (i0 // 4) % 2
                    Pb = pair(pp).bitcast(BF16)
                    for ii in range(4):
                        for kh in range(4):
                            A("pe", lambda e, ii=ii, kh=kh, i0=i0, Pb=Pb: e.transpose(
                                out=Pb[0:64, (ii * 4 + kh) * 128:(ii * 4 + kh + 1) * 128],
                                in_=Ksb3[:, i0 + ii, kh * 64:(kh + 1) * 64], identity=identb[:, :]),
                              r=[Ksb_b, identb_b], w=[pair_b[pp]])
                    A("act", lambda e, i0=i0, Pb=Pb: e.activation(
                        out=KsT4[0:64, i0:i0 + 4, :, :], in_=Pb[0:64, 0:2048].rearrange("p (i h s) -> p i h s", i=4, h=4),
                        func=AF.Copy), r=[pair_b[pp]], w=[KsT_b])
                ST = pair(1)
                for ii in range(NS):
                    for kh in range(4):
                        A("pe", lambda e, ii=ii, kh=kh: e.matmul(
                            ST[:, ii * 16 + 4 * kh:ii * 16 + 4 * kh + 4], lhsT=KsT4[0:64, ii, kh, :],
                            rhs=qsT3[0:64, 4 * kh:4 * kh + 4, ii], start=True, stop=True),
                          r=[KsT_b, qsT_b], w=[pair_b[1]])
                A("dve", lambda e: e.tensor_copy(out=STs, in_=ST[:, 0:256]), r=[pair_b[1]], w=[STs_b])
                S2 = pair(2)
                for hf in range(2):
                    A("pe", lambda e, hf=hf: e.transpose(out=S2[:, hf * 128:(hf + 1) * 128], in_=STs[:, hf * 128:(hf + 1) * 128],
                                                         identity=ident[:, :]), r=[STs_b, ident_b], w=[pair_b[2]])
                s_ = cnt["sa"] % 4; cnt["sa"] += 1
                t = sa[:, s_, :]
                sab = sa_b[s_]
                sk = cst[:, C_SINKS + bi * 2:C_SINKS + bi * 2 + 2]
                A("dve", lambda e: e.tensor_reduce(out=t[:, 0:2], in_=S2[:, 0:256].rearrange("p (f s) -> p f s", f=2),
                                                   axis=AX.X, op=ALU.max), r=[pair_b[2]], w=[sab])
                A("dve", lambda e: e.scalar_tensor_tensor(out=t[:, 4:6], in0=t[:, 0:2], scalar=SCALE, in1=sk,
                                                          op0=ALU.mult, op1=ALU.max), r=[sab, cst_b], w=[sab])
                A("dve", lambda e: e.tensor_scalar(out=t[:, 8:10], in0=t[:, 4:6], scalar1=-1.0, scalar2=None, op0=ALU.mult),
                  r=[sab], w=[sab])
                for hf in range(2):
                    A("act", lambda e, hf=hf: e.activation(
                        out=es3[:, hf, :], in_=S2[:, hf * 128:(hf + 1) * 128], func=AF.Exp, bias=t[:, 8 + hf:9 + hf], scale=SCALE,
                        accum_out=t[:, 12 + hf:13 + hf]), r=[pair_b[2], sab], w=[es_b, sab])
                A("dve", lambda e: e.tensor_tensor(out=t[:, 16:18], in0=sk, in1=t[:, 4:6], op=ALU.subtract), r=[sab, cst_b], w=[sab])
                A("act", lambda e: e.activation(out=t[:, 20:22], in_=t[:, 16:18], func=AF.Exp), r=[sab], w=[sab])
                A("dve", lambda e: e.tensor_tensor(out=t[:, 24:26], in0=t[:, 12:14], in1=t[:, 20:22], op=ALU.add), r=[sab], w=[sab])
                A("dve", lambda e: e.reciprocal(out=t[:, 28:30], in_=t[:, 24:26]), r=[sab], w=[sab])
                A("dve", lambda e: e.tensor_tensor(out=es3, in0=es3, in1=t[:, 28:30].unsqueeze(2).broadcast_to([128, 2, 128]),
                                                   op=ALU.mult), r=[es_b, sab], w=[es_b])
                P2 = pair(3)
                for hf in range(2):
                    A("pe", lambda e, hf=hf: e.transpose(out=P2[:, hf * 128:(hf + 1) * 128], in_=es3[:, hf, :], identity=ident[:, :]),
                      r=[es_b, ident_b], w=[pair_b[3]])
                A("act", lambda e: e.activation(out=PTs, in_=P2[:, 0:256], func=AF.Copy), r=[pair_b[3]], w=[PTs_b])
                OS = pair(1)
                OS3 = OS[:, 0:128].rearrange("p (m i) -> p m i", m=8)
                for ii in range(NS):
                    for kh in range(4):
                        for par in range(2):
                            A("pe", lambda e, ii=ii, kh=kh, par=par: e.matmul(
                                OS3[64 * par:64 * par + 64, 2 * kh:2 * kh + 2, ii], lhsT=Vsb3[:, ii, kh * 64:(kh + 1) * 64],
                                rhs=PTs[:, ii * 16 + 4 * kh + par:ii * 16 + 4 * kh + 4:2], start=True, stop=True),
                              r=[Vsb_b, PTs_b, STs_b], w=[pair_b[1]])
                A("act", lambda e: e.activation(out=osT, in_=OS[:, 0:128], func=AF.Copy), r=[pair_b[1]], w=[osT_b])
                oT_b_save = oT_b
                MP = pair(0)
                for half in range(2):
                    for m in range(8):
                        A("pe", lambda e, half=half, m=m: e.matmul(
                            MP[0:NS, half * 512:(half + 1) * 512], lhsT=osT3[:, m, :], rhs=wo3[:, m, half * 512:(half + 1) * 512],
                            start=(m == 0), stop=False), r=[osT_b, wo_b], w=[pair_b[0]])
                    A("pe", lambda e, half=half: e.matmul(
                        MP[0:128, half * 512:(half + 1) * 512], lhsT=ones33[:, 0:128],
                        rhs=bb[:, bi * 1024 + half * 512:bi * 1024 + (half + 1) * 512], start=False, stop=True),
                      r=[ones_b, bb_b], w=[pair_b[0]])
                ln_mix(ch, i, NS, 0, True, 3)
            ln_flush()

        stage = {"n": 0}

        def go():
            stage["n"] += 1
            return stop is None or stage["n"] <= stop

        for ch in range(2):
            if not go():
                break
            S.dma("sp", Y[:, 0:NT, :], xin[ch * TP:(ch + 1) * TP, :].rearrange("(t p) d -> p t d", p=128), writes=yb[0:NT])
            if ch == 1:
                S.dma("sp", Y[0:NS, NT, :], xs, writes=[yb[NT]])
            for l in range(NL):
                if go():
                    if l < 2:
                        pool_layer(ch, l)
                    else:
                        attn_layer(ch, l)
                if go():
                    ffn_layer(ch, l)
                if l == 1 and go():
                    kv_proj(ch)
        fin = yb + [kvo_b, nks_b, nvs_b, dram_misc_b] + ar["bufs"]
        if dbg:
            dby_b = Buf("dbgyb")
            S.dma("sp", dbgy.rearrange("t p d -> p t d"), Y[:, :, :], reads=yb, writes=[dby_b])
            S.dma("pool", dbgx, XT[:, :, :].rearrange("p k t -> p (k t)"), reads=xtb, writes=[dby_b])
            S.dma("pool", dbgk, KT2[:, :, :].rearrange("p k t -> p (k t)"), reads=[kt_b], writes=[dby_b])
            S.dma("pool", dbgv, V[:, :, :].rearrange("p k t -> p (k t)"), reads=v_b, writes=[dby_b])
            fin = fin + [dby_b]
        S.finalize(st, fin)
    return nc, S


POOL_WINDOWS = (2, 4, 8, 16)


def _consts_for_core(c, inp):
    qd = c % 4
    f32 = np.float32
    cst = np.zeros((128, NCST), f32)
    cw = np.asarray(inp["ffn_conv_w"], f32)
    cbv = np.asarray(inp["ffn_conv_b"], f32)
    cst[:, C_CW:C_CW + 264] = cw.reshape(NL, 3, NJ, 128).transpose(3, 0, 2, 1).reshape(128, 264)
    cst[:, C_CB:C_CB + 88] = cbv.reshape(NL, NJ, 128).transpose(2, 0, 1).reshape(128, 88)
    bq = np.asarray(inp["attn_b_q"], f32)
    cst[:, C_BQT:C_BQT + 16] = bq.reshape(2, 8, 128).transpose(2, 0, 1).reshape(128, 16)
    cst[0:64, C_BQH:C_BQH + 32] = bq.reshape(2, 16, 64).transpose(2, 0, 1).reshape(64, 32)
    bkv = np.asarray(inp["b_kv"], f32)
    bk = bkv[:256].reshape(4, 64)
    cst[:, C_BKT:C_BKT + 4] = np.concatenate([bk.T, bk.T], axis=0)
    sinks = np.asarray(inp["attn_sinks"], f32)
    sperm = sinks.reshape(2, 4, 4)[:, :, PERM].reshape(1, 32)
    cst[:, C_SINKB:C_SINKB + 32] = np.broadcast_to(sperm, (128, 32))
    cst[:, C_NSINKB:C_NSINKB + 32] = np.broadcast_to(-sperm, (128, 32))
    pidx = np.arange(128)
    for bi in range(2):
        for hf in range(2):
            cst[:, C_SINKS + bi * 2 + hf] = sinks[bi, pidx % 16]

    def real(ch, t, r):
        blk = 16 * qd + 8 * ch - 1 + t
        return (blk * 128 + r) >= 112

    r = np.arange(128)
    for ch in range(2):
        for i in range(2):
            cst[:, C_TM + ch * 2 + i] = real(ch, i, r).astype(f32)
    q = np.arange(128)[:, None]
    j = np.arange(256)[None, :]
    band = (q < j) & (j <= q + 128)
    amask = np.zeros((6, 128, 256), f32)
    for ch in range(2):
        for v in range(3):
            if v == 2:
                ok = band
            else:
                ti = 1 + v
                kr = np.where(j < 128, real(ch, ti - 1, j % 128), real(ch, ti, j % 128))
                ok = band & kr
            amask[ch * 3 + v] = np.where(ok, 0.0, NEG).astype(f32)

    cstb = np.zeros((128, NCSTB), f32)
    s = np.arange(128)[:, None]
    t = np.arange(128)[None, :]
    bc = np.zeros((128, 2, 4, 128), f32)
    bp = np.zeros((128, 4, 128), f32)
    for g, w in enumerate(POOL_WINDOWS):
        inwin = (s > t - w) & (s <= t)
        gen = inwin.astype(f32) / w - (s == t).astype(f32)
        bc[:, 1, g, :] = gen
        if qd == 0:
            tseq = t - 112
            cnt = np.where(tseq >= 0, np.minimum(w, tseq + 1), w).astype(f32)
            bc[:, 0, g, :] = inwin.astype(f32) / cnt - (s == t).astype(f32)
        else:
            bc[:, 0, g, :] = gen
        bp[:, g, :] = ((s > 128 + t - w).astype(f32)) / w
    cstb[:, B_BC:B_BC + 1024] = bc.reshape(128, 1024)
    cstb[:, B_BP:B_BP + 512] = bp.reshape(128, 512)
    sel = np.zeros((128, 2, 4, 16), f32)
    ci = np.zeros((128, 4, 16), f32)
    for g, w in enumerate(POOL_WINDOWS):
        for p in range(120):
            rr = p % 15
            if rr >= 16 - w:
                for t_ in range(2):
                    sel[p, t_, g, t_ * 8 + p // 15] = 1.0 / w
        for p in range(16):
            ci[p, g, p] = 1.0 / w - 1.0
    cstb[:, B_SEL:B_SEL + 128] = sel.reshape(128, 128)
    cstb[:, B_CI:B_CI + 64] = ci.reshape(128, 64)
    mkh = np.zeros((128, 6, 2, 256), f32)
    iab = np.zeros((128, 2, 128), f32)
    for p in range(128):
        for h in range(2):
            mkh[p, :, h, :] = amask[:, h * 64 + p % 64, :]
            iab[p, h, h * 64 + p % 64] = 1.0
    cstb[:, B_MK:B_MK + 3072] = mkh.reshape(128, 3072)
    cstb[:, B_IAB:B_IAB + 256] = iab.reshape(128, 256)
    return cst, cstb


_NC_CACHE = {}


def kernel(**inputs):
    f32 = np.float32
    inp = {k: np.asarray(v) for k, v in inputs.items()}
    xp = inp["x_prompt"].astype(f32, copy=False)
    meta = inp["meta_tokens"].astype(f32, copy=False)
    n = 8
    if "nc" not in _NC_CACHE:
        _NC_CACHE["nc"] = build_nc()[0]
    nc = _NC_CACHE["nc"]
    brow = np.concatenate([inp["attn_b_o"][0], inp["attn_b_o"][1], inp["b_kv"]]).astype(f32).reshape(1, 2560)
    shared = {
        "pool_w": np.ascontiguousarray(inp["pool_w"], f32), "pool_scale": np.ascontiguousarray(inp["pool_scale"], f32),
        "w_kv": np.ascontiguousarray(inp["w_kv"], f32), "w_q": np.ascontiguousarray(inp["attn_w_q"], f32),
        "w_o": np.ascontiguousarray(inp["attn_w_o"], f32), "w_in": np.ascontiguousarray(inp["ffn_w_in"], f32),
        "w_out": np.ascontiguousarray(inp["ffn_w_out"], f32),
        "lmg": np.ascontiguousarray(inp["ln_mix_g"], f32), "lmb": np.ascontiguousarray(inp["ln_mix_b"], f32),
        "lfg": np.ascontiguousarray(inp["ln_ffn_g"], f32), "lfb": np.ascontiguousarray(inp["ln_ffn_b"], f32),
        "brow": brow,
    }
    in_maps = []
    for c in range(n):
        b, qd = c // 4, c % 4
        xin = np.zeros((2, NT, 128, D), f32)
        for ch in range(2):
            for i in range(NT):
                blk = 16 * qd + 8 * ch - 1 + i
                if blk >= 1:
                    xin[ch, i] = xp[b, (blk - 1) * 128:blk * 128]
                elif blk == 0:
                    xin[ch, i, 112:128] = meta
        cst, cstb = _consts_for_core(c, inp)
        sl = slice(NS * c, NS * (c + 1))
        m = dict(shared)
        m.update({
            "xin": xin.reshape(2 * NT * 128, D),
            "xs": np.ascontiguousarray(inp["x_sample"][sl, 0, :], f32),
            "spool": np.ascontiguousarray(inp["state_pool"][:, sl], f32).reshape(2, 240, D),
            "sconv": np.ascontiguousarray(inp["state_conv"][:, sl], f32).reshape(NL, 32, DFF),
            "skw": np.ascontiguousarray(inp["state_k_win"][sl], f32).reshape(NS, 128, 256),
            "svw": np.ascontiguousarray(inp["state_v_win"][sl], f32).reshape(NS, 128, 256),
            "cst": cst, "cstb": cstb,
        })
        in_maps.append(m)
    res = run_bass_kernel_spmd(nc, in_maps, core_ids=list(range(n)))
    R = res.results
    y_prompt = np.zeros((2, 8192, D), f32)
    y_sample = np.zeros((128, 1, D), f32)
    npp = np.zeros((2, 2, 15, D), f32); nps = np.zeros((2, 128, 15, D), f32)
    ncp = np.zeros((NL, 2, 2, DFF), f32); ncs = np.zeros((NL, 128, 2, DFF), f32)
    nkp = np.zeros((2, 128, 4, 64), f32); nvp = np.zeros((2, 128, 4, 64), f32)
    nks = np.zeros((128, 128, 4, 64), f32); nvs = np.zeros((128, 128, 4, 64), f32)
    for c in range(n):
        b, qd = c // 4, c % 4
        r = R[c]
        sl = slice(NS * c, NS * (c + 1))
        y_prompt[b, 2048 * qd:2048 * (qd + 1)] = r["y_out"]
        y_sample[sl, 0] = r["ys_out"]
        nps[:, sl] = r["npool_s"]
        ncs[:, sl] = r["nconv_s"]
        nks[sl] = r["nk_s"].reshape(NS, 128, 4, 64)
        nvs[sl] = r["nv_s"].reshape(NS, 128, 4, 64)
        if qd == 3:
            npp[:, b] = r["npool_p"]
            ncp[:, b] = r["nconv_p"]
            nkp[b] = r["nk_p"].reshape(128, 4, 64)
            nvp[b] = r["nv_p"].reshape(128, 4, 64)
    return (y_prompt, y_sample, npp, nps, ncp, ncs, nkp, nvp, nks, nvs)
```

```python
import contextlib
import numpy as np
import concourse.bass as bass
import concourse.mybir as mybir
from concourse.bass_utils import run_bass_kernel_spmd

F32 = mybir.dt.float32
BF16 = mybir.dt.bfloat16
AF = mybir.ActivationFunctionType
ALU = mybir.AluOpType
AX = mybir.AxisListType


class Buf:
    __slots__ = ("name", "w", "rs", "dsem", "dcnt", "slot")

    def __init__(self, name):
        self.name = name
        self.w = None
        self.rs = {}
        self.dsem = None
        self.dcnt = 0
        self.slot = self


class Sched:
    ENG = ["pe", "act", "dve", "pool", "sp"]

    def __init__(self, nc):
        self.nc = nc
        self.ops = {e: [] for e in self.ENG}
        self.waited = {e: {} for e in self.ENG}
        self.dbufs = []

    def _deps(self, eng, reads, writes):
        best = {}
        idx = len(self.ops[eng])

        def add(tok):
            if tok is None:
                return
            if tok[0] == "e":
                _, pe, pidx = tok
                if pe == eng and eng == "pe":
                    return
                key = ("e", pe)
                v = pidx
            else:
                _, b, v = tok
                key = ("d", b)
            if best.get(key, -1) < v:
                best[key] = v

        for b in reads:
            add(b.w)
        for b in writes:
            add(b.w)
            for t in b.rs.values():
                add(t)
        waits = []
        for key, v in best.items():
            if self.waited[eng].get(key, -1) >= v:
                continue
            self.waited[eng][key] = v
            waits.append((key, v))
        return waits

    def _commit(self, tok, reads, writes):
        for b in writes:
            b.w = tok
            b.rs = {}
        for b in reads:
            if b in writes:
                continue
            if tok[0] == "e":
                b.rs[("e", tok[1])] = tok
            else:
                b.rs[("d", tok[1])] = tok

    def op(self, eng, fn, reads=(), writes=()):
        waits = self._deps(eng, reads, writes)
        idx = len(self.ops[eng])
        self.ops[eng].append(dict(fn=fn, waits=waits, sig=False, dma=None))
        tok = ("e", eng, idx)
        self._commit(tok, reads, writes)
        return tok

    def dma(self, eng, out, in_, reads=(), writes=(), **kw):
        waits = self._deps(eng, reads, writes)
        pb = (writes[0] if writes else reads[0]).slot
        if pb.dsem is None:
            pb.dsem = True
            self.dbufs.append(pb)
        pb.dcnt += 16
        self.ops[eng].append(dict(
            fn=lambda e: e.dma_start(out=out, in_=in_, **kw), waits=waits, sig=False, dma=pb))
        tok = ("d", pb, pb.dcnt)
        self._commit(tok, reads, writes)
        return tok

    def finalize(self, stack, final_bufs=()):
        nc = self.nc
        waits = self._deps("sp", list(final_bufs), list(final_bufs))
        self.ops["sp"].append(dict(fn=None, waits=waits, sig=False, dma=None))
        for e in self.ENG:
            for rec in self.ops[e]:
                for key, v in rec["waits"]:
                    if key[0] == "e":
                        self.ops[key[1]][v]["sig"] = True
        cum = {}
        for e in self.ENG:
            c = 0
            arr = []
            for rec in self.ops[e]:
                if rec["sig"]:
                    c += 1
                arr.append(c)
            cum[e] = arr
        esem = {e: stack.enter_context(nc.semaphore("s_" + e)) for e in self.ENG}
        for b in self.dbufs:
            b.dsem = stack.enter_context(nc.semaphore("d_" + b.name))
        engobj = {"pe": "tensor", "act": "scalar", "dve": "vector", "pool": "gpsimd", "sp": "sync"}

        def emit(name, e):
            for rec in self.ops[name]:
                for key, v in rec["waits"]:
                    if key[0] == "e":
                        e.wait_ge(esem[key[1]], cum[key[1]][v])
                    else:
                        e.wait_ge(key[1].dsem, v)
                if rec["fn"] is None:
                    continue
                ins = rec["fn"](e)
                if rec["dma"] is not None:
                    ins.then_inc(rec["dma"].dsem, 16)
                elif rec["sig"]:
                    ins.then_inc(esem[name], 1)

        block = stack.enter_context(nc.Block())
        for name in self.ENG:
            getattr(block, engobj[name])(lambda e, name=name: emit(name, e))
        self.stats = {e: len(self.ops[e]) for e in self.ENG}
        self.nsem = 5 + len(self.dbufs)

D = 1024; DFF = 2816; NJ = 22; NL = 4; NT = 10; TP = NT * 128; NS = 16; TC = TP + NS
ALPHA = (2.0 * 4) ** 0.25; EPS = 1e-5; SCALE = 0.125; NEG = -1e30
GROUPS = [(0, 6), (6, 12), (12, 17), (17, 22)]
C_CW = 0; C_CB = 264; C_BQT = 352; C_BQH = 368; C_BKT = 400; C_SINKB = 404; C_SINKS = 436; C_TM = 440; C_NSINKB = 444; NCST = 476
B_BC = 0; B_BP = 1024; B_SEL = 1536; B_CI = 1664; B_MK = 1728; B_IAB = 4800; NCSTB = 5056
U8 = mybir.dt.uint8
PERM = [0, 2, 1, 3]
import os
KVDBG = int(os.environ.get('KVDBG', '0'))
ARENA_BYTES = 97 * 1024


def build_nc(stop=None, dbg=False):
    nc = bass.Bass("TRN2", target_bir_lowering=False)

    def din(name, shape):
        return nc.dram_tensor(name, list(shape), F32, kind="ExternalInput").ap()

    def dout(name, shape):
        return nc.dram_tensor(name, list(shape), F32, kind="ExternalOutput").ap()

    xin = din("xin", [2 * NT * 128, D]); xs = din("xs", [NS, D])
    spool = din("spool", [2, 240, D]); sconv = din("sconv", [NL, 32, DFF])
    skw = din("skw", [NS, 128, 256]); svw = din("svw", [NS, 128, 256])
    pool_w = din("pool_w", [2, 4, 256, 256]); pool_scale = din("pool_scale", [2, D])
    w_kv = din("w_kv", [D, 512])
    w_q = din("w_q", [2, D, D]); w_o = din("w_o", [2, D, D])
    w_in = din("w_in", [NL, D, 2 * DFF]); w_out = din("w_out", [NL, DFF, D])
    lmg = din("lmg", [NL, D]); lmb = din("lmb", [NL, D]); lfg = din("lfg", [NL, D]); lfb = din("lfb", [NL, D])
    cst_d = din("cst", [128, NCST]); cstb_d = din("cstb", [128, NCSTB]); brow_d = din("brow", [1, 2560])

    y_out = dout("y_out", [2 * 8 * 128, D]); ys_out = dout("ys_out", [NS, D])
    npool_p = dout("npool_p", [2, 15, D]); npool_s = dout("npool_s", [2, NS, 15, D])
    nconv_p = dout("nconv_p", [NL, 2, DFF]); nconv_s = dout("nconv_s", [NL, NS, 2, DFF])
    nk_p = dout("nk_p", [128, 256]); nv_p = dout("nv_p", [128, 256])
    nk_s = dout("nk_s", [NS, 128, 256]); nv_s = dout("nv_s", [NS, 128, 256])

    if dbg:
        dbgy = dout("dbgy", [NT + 1, 128, D]); dbgx = dout("dbgx", [128, 8 * TC])
        dbgk = dout("dbgk", [128, 4 * TP]); dbgv = dout("dbgv", [128, NT * 256])
    S = Sched(nc)
    with contextlib.ExitStack() as st:
        def sbt(name, shape, dt):
            return st.enter_context(nc.sbuf_tensor("sb_" + name, list(shape), dt))

        def A(eng, fn, r=(), w=()):
            return S.op(eng, fn, reads=list(r), writes=list(w))

        cst = sbt("cst", [128, NCST], F32); cst_b = Buf("cst")
        cstb = sbt("cstb", [128, NCSTB], BF16); cstb_b = Buf("cstb")
        Y = sbt("Y", [128, NT + 1, D], F32); yb = [Buf(f"y{i}") for i in range(NT + 1)]
        for i_ in range(1, NT):
            yb[i_].slot = yb[0]
        XT = sbt("XT", [128, 8, TC], BF16); xtb = [Buf(f"xt{i}") for i in range(NT + 1)]
        lng = sbt("lng", [128, D], F32); lng_b = Buf("lng")
        lnb = sbt("lnb", [128, D], F32); lnb_b = Buf("lnb")
        KT2 = sbt("KT2", [128, 4, TP], BF16); kt_b = Buf("kt2")
        V = sbt("V", [128, NT, 256], BF16); v_b = [Buf(f"v{i}") for i in range(NT)]
        ident = sbt("ident", [128, 128], F32); ident_b = Buf("ident")
        identb = sbt("identb", [128, 128], BF16); identb_b = Buf("identb")
        ones33 = sbt("ones33", [33, 128], BF16); ones_b = Buf("ones33")
        bb = sbt("bb", [33, 2560], BF16); bb_b = Buf("bb")
        mhalf = sbt("mhalf", [128, 1], F32); mhalf_b = Buf("mhalf")
        NSL = 4
        sm = sbt("sm", [128, NSL, 24], F32); sm_b = [Buf(f"sm{i}") for i in range(NSL)]
        sa = sbt("sa", [128, 4, 48], F32); sa_b = [Buf(f"sa{i}") for i in range(4)]
        kvo = sbt("kvo", [128, 512], F32); kvo_b = Buf("kvo")
        arena = sbt("arena", [128, ARENA_BYTES], U8)
        PS = st.enter_context(nc.psum_tensor("PS", [128, 8, 512], F32))
        pair_b = [Buf(f"pp{i}") for i in range(4)]
        bank_b = [Buf(f"pb{i}") for i in range(8)]

        def pair(p):
            return PS[:, 2 * p:2 * p + 2, :].rearrange("p a b -> p (a b)")

        def bank(bk):
            return PS[:, bk, :]

        def bank_deps(bk):
            return [pair_b[bk // 2], bank_b[bk]]

        nks_b = Buf("nks"); nvs_b = Buf("nvs"); dram_misc_b = Buf("dmisc")

        ar = {"off": 0, "bufs": [], "old": []}
        slots = {}

        def arena_reset():
            ar["old"] = ar["old"][-200:] + ar["bufs"] if False else ar["bufs"]
            ar["bufs"] = []
            ar["off"] = 0

        def take(name, shape, dt, nb=1, at=None, after=()):
            esz = 4 if dt == F32 else 2
            free = 1
            for s_ in shape[1:]:
                free *= s_
            nbytes = free * esz * nb
            nbytes = (nbytes + 63) // 64 * 64
            if at is None:
                assert ar["off"] + nbytes <= ARENA_BYTES, (name, ar["off"], nbytes)
                ar["last_off"] = ar["off"]
                v = arena[:, ar["off"]:ar["off"] + nbytes].bitcast(dt)
                ar["off"] += nbytes
            else:
                v = arena[:, at:at + nbytes].bitcast(dt)
            v = v[:, 0:free * nb]
            bufs = []
            for i in range(nb):
                b = Buf(f"{name}{i}")
                b.slot = slots.setdefault(b.name, b)
                for ob in list(ar["old"]) + list(after):
                    toks = ([ob.w] if ob.w is not None else []) + list(ob.rs.values())
                    for t in toks:
                        key = ("e", t[1]) if t[0] == "e" else ("d", t[1])
                        cur = b.rs.get(key)
                        if cur is None or cur[2] < t[2]:
                            b.rs[key] = t
                bufs.append(b)
                ar["bufs"].append(b)
            return v, bufs

        def inherit(dst, srcs):
            for ob in srcs:
                toks = ([ob.w] if ob.w is not None else []) + list(ob.rs.values())
                for t in toks:
                    key = ("e", t[1]) if t[0] == "e" else ("d", t[1])
                    cur = dst.rs.get(key)
                    if cur is None or cur[2] < t[2]:
                        dst.rs[key] = t

        def view(v, pat, **kw):
            return v.rearrange(pat, **kw)

        S.dma("sp", cst[:], cst_d, writes=[cst_b])
        S.dma("pool", cstb[:], cstb_d, writes=[cstb_b])
        A("dve", lambda e: e.memset(ident[:], 0.0), w=[ident_b])
        A("pool", lambda e: e.affine_select(out=ident[:], in_=ident[:], compare_op=ALU.not_equal, fill=1.0,
                                            base=0, pattern=[[-1, 128]], channel_multiplier=1),
          r=[ident_b], w=[ident_b])
        A("act", lambda e: e.activation(out=identb[:], in_=ident[:], func=AF.Copy), r=[ident_b], w=[identb_b])
        A("dve", lambda e: e.memset(ones33[:], 1.0), w=[ones_b])
        A("dve", lambda e: e.memset(mhalf[:], -0.5), w=[mhalf_b])
        A("dve", lambda e: e.memset(bb[:], 0.0), w=[bb_b])
        arena_reset()
        bst, (bst_b,) = take("bst", [33, 2560], F32)
        bhi, (bhi_b,) = take("bhi", [33, 2560], BF16)
        blo, (blo_b,) = take("blo", [33, 2560], F32)
        A("dve", lambda e: e.memset(bst[0:33, :], 0.0), w=[bst_b])
        S.dma("sp", bst[0:1, :], brow_d, writes=[bst_b])
        S.dma("sp", bst[32:33, :], brow_d, writes=[bst_b])
        A("act", lambda e: e.activation(out=bhi[0:33, :], in_=bst[0:33, :], func=AF.Copy), r=[bst_b], w=[bhi_b])
        A("dve", lambda e: e.tensor_tensor(out=blo[0:33, :], in0=bst[0:33, :], in1=bhi[0:33, :], op=ALU.subtract),
          r=[bst_b, bhi_b], w=[blo_b])
        A("dve", lambda e: e.tensor_copy(out=bb[0:1, :], in_=bhi[0:1, :]), r=[bhi_b, bb_b], w=[bb_b])
        A("dve", lambda e: e.tensor_copy(out=bb[32:33, :], in_=blo[32:33, :]), r=[blo_b, bb_b], w=[bb_b])

        bandc = cstb[:, B_BC:B_BC + 1024].rearrange("p (v g t) -> p v g t", v=2, g=4)
        bandp = cstb[:, B_BP:B_BP + 512].rearrange("p (g t) -> p g t", g=4)
        sel = cstb[:, B_SEL:B_SEL + 128].rearrange("p (t g i) -> p t g i", t=2, g=4)
        coefI = cstb[:, B_CI:B_CI + 64].rearrange("p (g i) -> p g i", g=4)
        mk = cstb[:, B_MK:B_MK + 3072].rearrange("p (v h j) -> p v h j", v=6, h=2)
        iab = cstb[:, B_IAB:B_IAB + 256].rearrange("p (h q) -> p h q", h=2)

        cnt = {"sm": 0, "sa": 0, "pp": 0}

        def ln_A1(it):
            i, rows = it["i"], it["rows"]
            Yi = Y[0:rows, i, :]
            s_ = cnt["sm"] % NSL; cnt["sm"] += 1
            it["s"] = s_
            smb = sm_b[s_]
            t = sm[0:rows, s_, :]
            A("dve", lambda e: e.bn_stats(out=t[:, 0:6], in_=Yi[:, 0:512]), r=[yb[i]], w=[smb])
            A("dve", lambda e: e.bn_stats(out=t[:, 6:12], in_=Yi[:, 512:1024]), r=[yb[i], smb], w=[smb])
            A("dve", lambda e: e.bn_aggr(out=t[:, 12:14], in_=t[:, 0:12]), r=[smb], w=[smb])
            A("dve", lambda e: e.tensor_scalar(out=t[:, 14:15], in0=t[:, 13:14], scalar1=EPS, scalar2=None, op0=ALU.add),
              r=[smb], w=[smb])
            A("pool", lambda e: e.tensor_tensor(out=t[:, 15:16], in0=t[:, 14:15], in1=mhalf[0:rows, :], op=ALU.pow),
              r=[smb, mhalf_b], w=[smb])

        def ln_A2(it):
            i, rows, s_ = it["i"], it["rows"], it["s"]
            Yi = Y[0:rows, i, :]
            smb = sm_b[s_]
            t = sm[0:rows, s_, :]
            A("dve", lambda e: e.scalar_tensor_tensor(out=t[:, 16:17], in0=t[:, 12:13], scalar=-1.0, in1=t[:, 15:16],
                                                      op0=ALU.mult, op1=ALU.mult), r=[smb], w=[smb])
            A("act", lambda e: e.activation(out=Yi, in_=Yi, func=AF.Identity, bias=t[:, 16:17], scale=t[:, 15:16]),
              r=[yb[i], smb], w=[yb[i]])

        def ln_A3(it):
            i, rows, ch = it["i"], it["rows"], it["ch"]
            Yi = Y[0:rows, i, :]
            A("dve", lambda e: e.tensor_tensor(out=Yi, in0=Yi, in1=lng[0:rows, :], op=ALU.mult), r=[yb[i], lng_b], w=[yb[i]])
            A("dve", lambda e: e.tensor_tensor(out=Yi, in0=Yi, in1=lnb[0:rows, :], op=ALU.add), r=[yb[i], lnb_b], w=[yb[i]])
            if i < 2:
                c0 = C_TM + ch * 2 + i
                A("dve", lambda e: e.tensor_scalar(out=Yi, in0=Yi, scalar1=cst[0:rows, c0:c0 + 1], scalar2=None, op0=ALU.mult),
                  r=[yb[i], cst_b], w=[yb[i]])

        def ln_B(it):
            i, rows = it["i"], it["rows"]
            Yi = Y[0:rows, i, :]
            if it["need_xt"]:
                rpair = it["rpair"]() if callable(it["rpair"]) else it["rpair"]
                R = pair(rpair)
                for k in range(8):
                    A("pe", lambda e, k=k: e.transpose(out=R[:, k * 128:k * 128 + rows], in_=Yi[:, k * 128:(k + 1) * 128],
                                                       identity=ident[0:rows, 0:rows]),
                      r=[yb[i], ident_b], w=[pair_b[rpair]])
                cx = i * 128
                A("act", lambda e: e.activation(out=XT[:, :, cx:cx + rows],
                                                in_=R.rearrange("p (k t) -> p k t", k=8)[:, :, 0:rows], func=AF.Copy),
                  r=[pair_b[rpair]], w=[xtb[i]])
            if it.get("post") is not None:
                it["post"]()

        lnq = []

        def ln_push(ch, i, rows, need_xt, rpair, post=None):
            it = dict(ch=ch, i=i, rows=rows, need_xt=need_xt, rpair=rpair, post=post, st=1)
            ln_A1(it)
            lnq.append(it)
            if len(lnq) >= 2 and lnq[-2]["st"] == 1:
                ln_A2(lnq[-2]); lnq[-2]["st"] = 2
            if len(lnq) >= 3 and lnq[-3]["st"] == 2:
                ln_A3(lnq[-3]); lnq[-3]["st"] = 3
            if len(lnq) >= 4:
                o = lnq.pop(0)
                ln_B(o)

        def ln_flush():
            while lnq:
                for o in lnq:
                    if o["st"] == 1:
                        ln_A2(o); o["st"] = 2
                    elif o["st"] == 2:
                        ln_A3(o); o["st"] = 3
                    elif o["st"] == 3:
                        ln_B(o); o["st"] = 4
                while lnq and lnq[0]["st"] == 4:
                    lnq.pop(0)

        def ln_core(ch, i, rows, need_xt, rpair, post=None):
            ln_push(ch, i, rows, need_xt, rpair, post)

        def ln_mix(ch, i, rows, mp, need_xt, rpair):
            Yi = Y[0:rows, i, :]
            A("dve", lambda e: e.scalar_tensor_tensor(out=Yi, in0=Yi, scalar=ALPHA, in1=pair(mp)[0:rows, :],
                                                      op0=ALU.mult, op1=ALU.add), r=[yb[i], pair_b[mp]], w=[yb[i]])
            ln_core(ch, i, rows, need_xt, rpair)

        def load_ln(g_d, b_d, l):
            S.dma("sp", lng[:], g_d[l:l + 1, :].partition_broadcast(128), writes=[lng_b])
            S.dma("sp", lnb[:], b_d[l:l + 1, :].partition_broadcast(128), writes=[lnb_b])

        def pool_layer(ch, a):
            has_s = (ch == 1)
            arena_reset()
            psc, (psc_b,) = take("psc", [128, D], F32)
            wpf, (wpf_b,) = take("wpf", [128, 8, 256], F32)
            wp, (wp_b,) = take("wp", [128, 8, 256], BF16)
            ybf, ybf_b = take("ybf", [128, D], BF16, nb=3)
            dT, dT_b = take("dT", [128, 8, 128], BF16, nb=2)
            spb, (spb_b,) = take("spb", [128, 2, D], BF16)
            xnb, (xnb_b,) = take("xnb", [128, D], BF16)
            wpf4 = wpf.rearrange("p (g k e) -> p g k e", g=4, k=2)
            wp4 = wp.rearrange("p (g k e) -> p g k e", g=4, k=2)
            wp3 = wp.rearrange("p (c e) -> p c e", c=8)
            ybf3 = ybf.rearrange("p (s d) -> p s d", s=3)
            dT4 = dT.rearrange("p (s c t) -> p s c t", s=2, c=8)
            spb3 = spb.rearrange("p (t d) -> p t d", t=2)
            S.dma("sp", psc, pool_scale[a:a + 1, :].partition_broadcast(128), writes=[psc_b])
            S.dma("sp", wpf.rearrange("p (c e) -> p c e", c=8), pool_w[a].rearrange("g (k p) e -> p (g k) e", p=128), writes=[wpf_b])
            for kk in range(2):
                A("dve", lambda e, kk=kk: e.tensor_tensor(out=wp4[:, :, kk, :], in0=wpf4[:, :, kk, :],
                                                          in1=psc.rearrange("p (g e) -> p g e", g=4), op=ALU.mult),
                  r=[wpf_b, psc_b], w=[wp_b])
            load_ln(lmg, lmb, a)
            if has_s:
                for t_ in range(2):
                    S.dma("pool", spb3[0:120, t_, :], spool[a, t_ * 120:(t_ + 1) * 120, :], writes=[spb_b])
            def stageA(i):
                sl = i % 3
                A("act", lambda e, i=i, sl=sl: e.activation(out=ybf3[:, sl, :], in_=Y[:, i, :], func=AF.Copy),
                  r=[yb[i]], w=[ybf_b[sl]])
                dp = i % 2
                P = pair(dp)
                var = 0 if (ch == 0 and i == 1) else 1
                for kc in range(8):
                    g = kc // 2
                    A("pe", lambda e, kc=kc, g=g, sl=sl, var=var, P=P, i=i: e.matmul(
                        P[:, kc * 128:(kc + 1) * 128], lhsT=ybf3[:, sl, kc * 128:(kc + 1) * 128], rhs=bandc[:, var, g, :],
                        start=True, stop=(i == 0)), r=[ybf_b[sl], cstb_b], w=[pair_b[dp]])
                    if i > 0:
                        sp_ = (i - 1) % 3
                        A("pe", lambda e, kc=kc, g=g, sp_=sp_, P=P: e.matmul(
                            P[:, kc * 128:(kc + 1) * 128], lhsT=ybf3[:, sp_, kc * 128:(kc + 1) * 128], rhs=bandp[:, g, :],
                            start=False, stop=True), r=[ybf_b[sp_], cstb_b], w=[pair_b[dp]])
                if ch == 1 and i == NT - 1:
                    S.dma("sp", npool_p[a], Y[113:128, i, :], reads=[yb[i]])

            def stageB(i):
                dp = i % 2
                mp = 2 + (i % 2)
                P = pair(dp)
                ds = i % 2
                A("act", lambda e, ds=ds, P=P: e.activation(out=dT4[:, ds, :, :], in_=P.rearrange("p (c t) -> p c t", c=8),
                                                            func=AF.Copy), r=[pair_b[dp]], w=[dT_b[ds]])
                Q = pair(mp)
                for g in range(4):
                    for kk in range(2):
                        A("pe", lambda e, g=g, kk=kk, ds=ds, Q=Q: e.matmul(
                            Q[:, g * 256:(g + 1) * 256], lhsT=dT4[:, ds, 2 * g + kk, :], rhs=wp3[:, 2 * g + kk, :],
                            start=(kk == 0), stop=(kk == 1)), r=[dT_b[ds], wp_b], w=[pair_b[mp]])
                ln_mix(ch, i, 128, mp, True, mp)

            stageA(0)
            for i in range(NT):
                if i + 1 < NT:
                    stageA(i + 1)
                stageB(i)
            if has_s:
                i = NT
                S.dma("sp", npool_s[a, :, 14, :], Y[0:NS, i, :], reads=[yb[i]])
                S.dma("sp", npool_s[a, :, 0:14, :], spool[a].rearrange("(i r) d -> i r d", r=15)[:, 1:15, :], writes=[dram_misc_b])
                A("act", lambda e: e.activation(out=xnb[0:NS, :], in_=Y[0:NS, NT, :], func=AF.Copy), r=[yb[i]], w=[xnb_b])
                dp, mp = 0, 2
                P = pair(dp)
                for kc in range(8):
                    g = kc // 2
                    for t_ in range(2):
                        A("pe", lambda e, kc=kc, g=g, t_=t_: e.matmul(
                            P[:, kc * 128:kc * 128 + NS], lhsT=spb3[0:120, t_, kc * 128:(kc + 1) * 128], rhs=sel[0:120, t_, g, :],
                            start=(t_ == 0), stop=False), r=[spb_b, cstb_b], w=[pair_b[dp]])
                    A("pe", lambda e, kc=kc, g=g: e.matmul(
                        P[:, kc * 128:kc * 128 + NS], lhsT=xnb[0:NS, kc * 128:(kc + 1) * 128], rhs=coefI[0:NS, g, :],
                        start=False, stop=True), r=[xnb_b, cstb_b], w=[pair_b[dp]])
                A("act", lambda e: e.activation(out=dT4[:, 0, :, 0:NS], in_=P.rearrange("p (c t) -> p c t", c=8)[:, :, 0:NS],
                                                func=AF.Copy), r=[pair_b[dp]], w=[dT_b[0]])
                Q = pair(mp)
                for g in range(4):
                    for kk in range(2):
                        A("pe", lambda e, g=g, kk=kk: e.matmul(
                            Q[0:NS, g * 256:(g + 1) * 256], lhsT=dT4[:, 0, 2 * g + kk, 0:NS], rhs=wp3[:, 2 * g + kk, :],
                            start=(kk == 0), stop=(kk == 1)), r=[dT_b[0], wp_b], w=[pair_b[mp]])
                ln_mix(ch, i, NS, mp, True, dp)
            ln_flush()

        def ffn_layer(ch, l):
            has_s = (ch == 1)
            ntile = NT + (1 if has_s else 0)
            arena_reset()
            HT, _ = take("HT", [128, 6, TC], BF16)
            HT3 = HT.rearrange("p (j t) -> p j t", j=6)
            ht_b = [[Buf(f"ht{j}_{t}") for t in range(3)] for j in range(6)]
            for row in ht_b:
                for b in row:
                    for ob in ar["old"]:
                        toks = ([ob.w] if ob.w is not None else []) + list(ob.rs.values())
                        for t in toks:
                            key = ("e", t[1]) if t[0] == "e" else ("d", t[1])
                            cur = b.rs.get(key)
                            if cur is None or cur[2] < t[2]:
                                b.rs[key] = t
                    ar["bufs"].append(b)
            wout, wout_b = take("wout", [128, 6, D], BF16, nb=2)
            wout4 = wout.rearrange("p (s j d) -> p s j d", s=2, j=6)
            win, win_b = take("win", [128, 8, 256], BF16, nb=4)
            win4 = win.rearrange("p (s k c) -> p s k c", s=4, k=8)
            gsb, gsb_b2 = take("gsb", [128, 2 + TC], F32, nb=2)
            gsb3 = gsb.rearrange("p (s t) -> p s t", s=2)
            gsb_b = [[Buf(f"gs{s_}_{t}") for t in range(4)] for s_ in range(2)]
            for s_ in range(2):
                for b in gsb_b[s_]:
                    b.rs = dict(gsb_b2[s_].rs)
                    ar["bufs"].append(b)
            cc, cc_b = take("cc", [128, 512], F32, nb=2)
            cc3 = cc.rearrange("p (s t) -> p s t", s=2)
            ss, ss_b = take("ss", [128, 512], F32, nb=2)
            ss3 = ss.rearrange("p (s t) -> p s t", s=2)
            us, us_b = take("us", [128, 512], F32, nb=2)
            us3 = us.rearrange("p (s t) -> p s t", s=2)
            if has_s:
                scT, (scT_b,) = take("scT", [128, NJ, 32], F32)
                scT3 = scT.rearrange("p (j c) -> p j c", j=NJ)
                cs, (cs_b,) = take("cs", [128, DFF], F32)
            load_ln(lfg, lfb, l)
            for bk_ in range(4, 8):
                inherit(bank_b[bk_], [pair_b[bk_ // 2]])
            i_lo = 1 if l >= 2 else 0
            c_lo = 128 * i_lo
            for s_ in range(2):
                A("dve", lambda e, s_=s_: e.memset(gsb3[:, s_, c_lo:c_lo + 2], 0.0), w=[gsb_b[s_][0]])
            if has_s:
                S.dma("sp", cs[0:32, :], sconv[l], writes=[cs_b])
                S.dma("sp", nconv_s[l, :, 0, :], sconv[l].rearrange("(i r) f -> i r f", r=2)[:, 1, :], writes=[dram_misc_b])
                for j0 in range(0, NJ, 8):
                    nj = min(8, NJ - j0)
                    pp = (j0 // 8) % 2
                    for jj in range(nj):
                        A("pe", lambda e, j0=j0, jj=jj, pp=pp: e.transpose(
                            out=pair(pp)[:, jj * 32:(jj + 1) * 32], in_=cs[0:32, (j0 + jj) * 128:(j0 + jj + 1) * 128],
                            identity=ident[0:32, 0:32]), r=[cs_b, ident_b], w=[pair_b[pp]])
                    A("dve", lambda e, j0=j0, nj=nj, pp=pp: e.tensor_copy(
                        out=scT3[:, j0:j0 + nj, :], in_=pair(pp)[:, 0:nj * 32].rearrange("p (j c) -> p j c", j=nj)),
                      r=[pair_b[pp]], w=[scT_b])
            if i_lo == 0:
                TT = [(0, 512), (512, 512), (1024, 256 + (NS if has_s else 0))]
            else:
                TT = [(128, 512), (640, 512), (1152, 128 + (NS if has_s else 0))]
            winv = w_in[l].rearrange("(k p) c -> p k c", p=128)
            slab = 0
            u1 = 0
            accs = {"n": 0}
            pend = {"f": None}
            exn = {"n": 0}
            pend2 = {"f": None}
            for gi, (j0, j1) in enumerate(GROUPS):
                wo = gi % 2
                ng = j1 - j0
                def load_wout(wo=wo, ng=ng, j0=j0, j1=j1):
                    S.dma("pool", wout4[:, wo, 0:ng, :], w_out[l, j0 * 128:j1 * 128, :].rearrange("(j p) d -> p j d", p=128),
                          writes=[wout_b[wo]])
                if gi > 0:
                    load_wout()
                for j in range(j0, j1):
                    s_ = slab % 4; slab += 1
                    S.dma("pool", win4[:, s_, :, 0:128], winv[:, :, j * 128:(j + 1) * 128], writes=[win_b[s_]])
                    S.dma("pool", win4[:, s_, :, 128:256], winv[:, :, DFF + j * 128:DFF + (j + 1) * 128], writes=[win_b[s_]])
                    if gi == 0 and j == j0 + 2:
                        load_wout()
                    gs = j % 2
                    cw = C_CW + (l * NJ + j) * 3
                    cbc = C_CB + l * NJ + j
                    for tt, (t0, n) in enumerate(TT):
                        set_ = u1 % 2; u1 += 1
                        bg, bu = 4 + 2 * set_, 5 + 2 * set_
                        pg, pu = bank(bg), bank(bu)
                        xr = [xtb[ii] for ii in range(t0 // 128, min(NT, (t0 + n + 127) // 128))]
                        if has_s and tt == 2:
                            xr.append(xtb[NT])
                        for k in range(8):
                            A("pe", lambda e, k=k, s_=s_, pg=pg, t0=t0, n=n: e.matmul(
                                pg[:, 0:n], lhsT=win4[:, s_, k, 0:128], rhs=XT[:, k, t0:t0 + n], start=(k == 0), stop=(k == 7)),
                              r=[win_b[s_]] + xr, w=[bank_b[bg]])
                        for k in range(8):
                            A("pe", lambda e, k=k, s_=s_, pu=pu, t0=t0, n=n: e.matmul(
                                pu[:, 0:n], lhsT=win4[:, s_, k, 128:256], rhs=XT[:, k, t0:t0 + n], start=(k == 0), stop=(k == 7)),
                              r=[win_b[s_]] + xr, w=[bank_b[bu]])
                        npr = min(n, TP - t0)
                        cs_ = u1 % 2
                        A("act", lambda e, gs=gs, t0=t0, n=n, pg=pg: e.activation(
                            out=gsb3[:, gs, 2 + t0:2 + t0 + n], in_=pg[:, 0:n], func=AF.Copy),
                          r=[bank_b[bg]], w=[gsb_b[gs][1 + tt]])
                        A("act", lambda e, cs_=cs_, n=n, pg=pg, cw=cw, cbc=cbc: e.activation(
                            out=cc3[:, cs_, 0:n], in_=pg[:, 0:n], func=AF.Identity, bias=cst[:, cbc:cbc + 1],
                            scale=cst[:, cw + 2:cw + 3]), r=[bank_b[bg]] + [cst_b], w=[cc_b[cs_]])
                        grd = [gsb_b[gs][tt], gsb_b[gs][1 + tt]]
                        A("dve", lambda e, cs_=cs_, gs=gs, t0=t0, npr=npr, cw=cw: e.scalar_tensor_tensor(
                            out=cc3[:, cs_, 0:npr], in0=gsb3[:, gs, 1 + t0:1 + t0 + npr], scalar=cst[:, cw + 1:cw + 2],
                            in1=cc3[:, cs_, 0:npr], op0=ALU.mult, op1=ALU.add), r=grd + [cc_b[cs_], cst_b], w=[cc_b[cs_]])
                        A("dve", lambda e, cs_=cs_, gs=gs, t0=t0, npr=npr, cw=cw: e.scalar_tensor_tensor(
                            out=cc3[:, cs_, 0:npr], in0=gsb3[:, gs, t0:t0 + npr], scalar=cst[:, cw:cw + 1],
                            in1=cc3[:, cs_, 0:npr], op0=ALU.mult, op1=ALU.add), r=grd + [cc_b[cs_], cst_b], w=[cc_b[cs_]])
                        if n > npr:
                            A("dve", lambda e, cs_=cs_, j=j, npr=npr, n=n, cw=cw: e.scalar_tensor_tensor(
                                out=cc3[:, cs_, npr:n], in0=scT3[:, j, 1:32:2], scalar=cst[:, cw + 1:cw + 2],
                                in1=cc3[:, cs_, npr:n], op0=ALU.mult, op1=ALU.add), r=[scT_b, cc_b[cs_], cst_b], w=[cc_b[cs_]])
                            A("dve", lambda e, cs_=cs_, j=j, npr=npr, n=n, cw=cw: e.scalar_tensor_tensor(
                                out=cc3[:, cs_, npr:n], in0=scT3[:, j, 0:32:2], scalar=cst[:, cw:cw + 1],
                                in1=cc3[:, cs_, npr:n], op0=ALU.mult, op1=ALU.add), r=[scT_b, cc_b[cs_], cst_b], w=[cc_b[cs_]])
                        A("act", lambda e, cs_=cs_, n=n, pu=pu: e.activation(out=us3[:, cs_, 0:n], in_=pu[:, 0:n], func=AF.Copy),
                          r=[bank_b[bu]], w=[us_b[cs_]])

                        def second(cs_=cs_, n=n, j=j, j0=j0, t0=t0, tt=tt):
                            A("act", lambda e: e.activation(out=ss3[:, cs_, 0:n], in_=cc3[:, cs_, 0:n], func=AF.Silu),
                              r=[cc_b[cs_]], w=[ss_b[cs_]])
                            A("dve", lambda e: e.tensor_tensor(
                                out=HT3[:, j - j0, t0:t0 + n], in0=ss3[:, cs_, 0:n], in1=us3[:, cs_, 0:n], op=ALU.mult),
                              r=[ss_b[cs_], us_b[cs_]], w=[ht_b[j - j0][tt]])
                        if pend2["f"] is not None:
                            pend2["f"]()
                        pend2["f"] = second
                        if tt == 0 and pend["f"] is not None:
                            pend["f"](); pend["f"] = None
                    if has_s:
                        def export(j=j, gs=gs):
                            eb = exn["n"] % 4; exn["n"] += 1
                            A("pe", lambda e: e.transpose(out=bank(eb)[0:18, 0:128], in_=gsb3[:, gs, TP:TP + 18],
                                                          identity=ident[:, :]),
                              r=[gsb_b[gs][3], ident_b], w=bank_deps(eb))
                            A("act", lambda e: e.activation(out=cs[0:18, j * 128:(j + 1) * 128], in_=bank(eb)[0:18, 0:128],
                                                            func=AF.Copy), r=bank_deps(eb), w=[cs_b])
                        pend["f"] = export
                if pend2["f"] is not None:
                    pend2["f"](); pend2["f"] = None
                if pend["f"] is not None:
                    pend["f"](); pend["f"] = None
                last = (gi == len(GROUPS) - 1)
                for i in range(i_lo, ntile):
                    rows = 128 if i < NT else NS
                    c0 = i * 128
                    tt = min((i - i_lo) // 4, 2)
                    ap_ = accs["n"] % 2; accs["n"] += 1
                    P = pair(ap_)
                    for jj in range(ng):
                        for half in range(2):
                            A("pe", lambda e, jj=jj, half=half, rows=rows, c0=c0, P=P, wo=wo, ng=ng: e.matmul(
                                P[0:rows, half * 512:(half + 1) * 512], lhsT=HT3[:, jj, c0:c0 + rows],
                                rhs=wout4[:, wo, jj, half * 512:(half + 1) * 512], start=(jj == 0), stop=(jj == ng - 1)),
                              r=[ht_b[jj][tt], wout_b[wo]], w=[pair_b[ap_]])
                    Yi = Y[0:rows, i, :]
                    if gi == 0:
                        A("dve", lambda e, Yi=Yi, P=P, rows=rows: e.scalar_tensor_tensor(
                            out=Yi, in0=Yi, scalar=ALPHA, in1=P[0:rows, :], op0=ALU.mult, op1=ALU.add),
                          r=[yb[i], pair_b[ap_]], w=[yb[i]])
                    else:
                        A("dve", lambda e, Yi=Yi, P=P, rows=rows: e.tensor_tensor(out=Yi, in0=Yi, in1=P[0:rows, :], op=ALU.add),
                          r=[yb[i], pair_b[ap_]], w=[yb[i]])
                    if last:
                        def rp_fn():
                            v = accs["n"] % 2; accs["n"] += 1
                            return v
                        post = None
                        if l == NL - 1 and i == NT:
                            post = lambda: S.dma("sp", ys_out, Y[0:NS, NT, :], reads=[yb[NT]])
                        elif l == NL - 1 and i >= 2:
                            def post(i=i):
                                r0 = (ch * 8 + i - 2) * 128
                                S.dma("sp", y_out[r0:r0 + 128, :], Y[:, i, :], reads=[yb[i]])
                        ln_core(ch, i, rows, l in (1, 2), rp_fn, post)
            ln_flush()
            inherit(pair_b[2], [bank_b[4], bank_b[5]])
            inherit(pair_b[3], [bank_b[6], bank_b[7]])
            if has_s:
                S.dma("sp", nconv_p[l], cs[0:2, :], reads=[cs_b])
                S.dma("sp", nconv_s[l, :, 1, :], cs[2:18, :], reads=[cs_b])

        def kv_proj(ch):
            has_s = (ch == 1)
            arena_reset()
            wkt, (wkt_b,) = take("wkt", [128, 8, 512], BF16)
            wkt3 = wkt.rearrange("p (k c) -> p k c", k=8)
            wk2, (wk2_b,) = take("wk2", [128, 8, 4, 128], BF16)
            wk24 = wk2.rearrange("p (k h c) -> p k h c", k=8, h=4)
            kvs, (kvs_b,) = take("kvs", [128, 512], F32)
            wv = w_kv.rearrange("(k p) c -> p k c", p=128)
            for kh in range(4):
                for dup in range(2):
                    S.dma("pool", wk24[:, :, kh, dup * 64:(dup + 1) * 64], wv[:, :, kh * 64:(kh + 1) * 64], writes=[wk2_b])
            S.dma("pool", wkt3, wv, writes=[wkt_b])
            u = 0
            for kh in range(4):
                for (t0, n) in [(0, 512), (512, 512), (1024, 256)]:
                    bk = 4 + (u % 4); u += 1
                    xr = [xtb[ii] for ii in range(t0 // 128, (t0 + n) // 128)]
                    for k in range(8):
                        A("pe", lambda e, k=k, kh=kh, bk=bk, t0=t0, n=n: e.matmul(
                            bank(bk)[:, 0:n], lhsT=wk24[:, k, kh, :], rhs=XT[:, k, t0:t0 + n], start=(k == 0), stop=(k == 7)),
                          r=[wk2_b] + xr, w=bank_deps(bk))
                    A("act", lambda e, kh=kh, bk=bk, t0=t0, n=n: e.activation(
                        out=KT2[:, kh, t0:t0 + n], in_=bank(bk)[:, 0:n], func=AF.Identity,
                        bias=cst[:, C_BKT + kh:C_BKT + kh + 1], scale=1.0), r=bank_deps(bk) + [cst_b], w=[kt_b])
            ntile = NT + (1 if (has_s and not (KVDBG & 2)) else 0)
            for i in range(ntile):
                rows = 128 if i < NT else NS
                c0 = i * 128
                bk = i % 4
                for k in range(8):
                    A("pe", lambda e, k=k, bk=bk, rows=rows, c0=c0: e.matmul(
                        bank(bk)[0:rows, :], lhsT=XT[:, k, c0:c0 + rows], rhs=wkt3[:, k, :], start=(k == 0), stop=False),
                      r=[wkt_b, xtb[i]], w=bank_deps(bk))
                A("pe", lambda e, bk=bk, rows=rows: e.matmul(
                    bank(bk)[0:128, :], lhsT=ones33[:, 0:128], rhs=bb[:, 2048:2560], start=False, stop=True),
                  r=[ones_b, bb_b], w=bank_deps(bk))
                if i < NT:
                    A("act", lambda e, i=i, bk=bk: e.activation(out=V[:, i, :], in_=bank(bk)[:, 256:512], func=AF.Copy),
                      r=bank_deps(bk), w=[v_b[i]])
                if ch == 1 and i == NT - 1 and not (KVDBG & 4):
                    A("act", lambda e, bk=bk: e.activation(out=kvo[:], in_=bank(bk)[:, :], func=AF.Copy), r=bank_deps(bk), w=[kvo_b])
                    S.dma("sp", nk_p, kvo[:, 0:256], reads=[kvo_b])
                    S.dma("sp", nv_p, kvo[:, 256:512], reads=[kvo_b])
                if i == NT:
                    A("dve", lambda e, bk=bk: e.tensor_copy(out=kvs[0:NS, :], in_=bank(bk)[0:NS, :]), r=bank_deps(bk), w=[kvs_b])
                    S.dma("sp", nk_s[:, 127, :], kvs[0:NS, 0:256], reads=[kvs_b], writes=[nks_b])
                    S.dma("sp", nv_s[:, 127, :], kvs[0:NS, 256:512], reads=[kvs_b], writes=[nvs_b])
                    for (a0, a1) in ([] if (KVDBG & 1) else [(1, 33), (33, 65), (65, 97), (97, 128)]):
                        S.dma("sp", nk_s[:, a0 - 1:a1 - 1, :], skw[:, a0:a1, :], writes=[nks_b])
                        S.dma("sp", nv_s[:, a0 - 1:a1 - 1, :], svw[:, a0:a1, :], writes=[nvs_b])

        def attn_layer(ch, l):
            bi = l - 2
            has_s = (ch == 1)
            arena_reset()
            wq, (wq_b,) = take("wq", [128, 8, D], BF16)
            wq_at = ar["last_off"]
            wq3 = wq.rearrange("p (k c) -> p k c", k=8)
            wo_, (wo_b,) = take("wo", [128, 8, D], BF16)
            wo3 = wo_.rearrange("p (k c) -> p k c", k=8)
            qT, (qT_b,) = take("qTa", [128, 8, TP], BF16)
            qT3 = qT.rearrange("p (m t) -> p m t", m=8)
            en, en_b = take("en", [128, 4, 256], BF16, nb=2)
            en4 = en.rearrange("p (d h s) -> p d h s", d=2, h=4)
            ee, ee_b = take("ee", [128, 4, 256], F32, nb=2)
            ee4 = ee.rearrange("p (d h s) -> p d h s", d=2, h=4)
            PT, PT_b = take("PT", [128, 4, 2, 128], BF16, nb=2)
            PT5 = PT.rearrange("p (d h f q) -> p d h f q", d=2, h=4, f=2)
            oT, (oT_b,) = take("oT", [128, 8, 128], BF16)
            oT3 = oT.rearrange("p (m t) -> p m t", m=8)
            S.dma("pool", wq3, w_q[bi].rearrange("(k p) c -> p k c", p=128), writes=[wq_b])
            S.dma("pool", wo3, w_o[bi].rearrange("(k p) c -> p k c", p=128), writes=[wo_b])
            load_ln(lmg, lmb, l)
            sinkb = cst[:, C_SINKB + bi * 16:C_SINKB + bi * 16 + 16]
            nsinkb = cst[:, C_NSINKB + bi * 16:C_NSINKB + bi * 16 + 16]
            uq = 0
            for m in range(8):
                for (t0, n) in [(128, 512), (640, 512), (1152, 128)]:
                    bk = 4 + (uq % 4); uq += 1
                    xr = [xtb[ii] for ii in range(t0 // 128, (t0 + n) // 128)]
                    for k in range(8):
                        A("pe", lambda e, k=k, m=m, bk=bk, t0=t0, n=n: e.matmul(
                            bank(bk)[:, 0:n], lhsT=wq3[:, k, m * 128:(m + 1) * 128], rhs=XT[:, k, t0:t0 + n],
                            start=(k == 0), stop=(k == 7)), r=[wq_b] + xr, w=bank_deps(bk))
                    cq = C_BQT + bi * 8 + m
                    if uq % 2 == 0:
                        A("act", lambda e, m=m, bk=bk, t0=t0, n=n, cq=cq: e.activation(
                            out=qT3[:, m, t0:t0 + n], in_=bank(bk)[:, 0:n], func=AF.Identity, bias=cst[:, cq:cq + 1], scale=1.0),
                          r=bank_deps(bk) + [cst_b], w=[qT_b])
                    else:
                        A("dve", lambda e, m=m, bk=bk, t0=t0, n=n, cq=cq: e.tensor_scalar(
                            out=qT3[:, m, t0:t0 + n], in0=bank(bk)[:, 0:n], scalar1=cst[:, cq:cq + 1], scalar2=None, op0=ALU.add),
                          r=bank_deps(bk) + [cst_b], w=[qT_b])

            def out_proj(rows, osrc, mp):
                MP = pair(mp)
                for half in range(2):
                    for m in range(8):
                        A("pe", lambda e, half=half, m=m: e.matmul(
                            MP[0:rows, half * 512:(half + 1) * 512], lhsT=osrc(m), rhs=wo3[:, m, half * 512:(half + 1) * 512],
                            start=(m == 0), stop=False), r=[oT_b, wo_b], w=[pair_b[mp]])
                    A("pe", lambda e, half=half: e.matmul(
                        MP[0:rows, half * 512:(half + 1) * 512], lhsT=ones33[:, 0:rows],
                        rhs=bb[:, bi * 1024 + half * 512:bi * 1024 + (half + 1) * 512], start=False, stop=True),
                      r=[ones_b, bb_b], w=[pair_b[mp]])

            def make_tile(i):
                OP = pair(1)
                v_ = 0 if i == 1 else (1 if i == 2 else 2)
                mv = ch * 3 + v_
                TPb = pair(0).bitcast(BF16)

                def scores(kh, i=i):
                    spi = 2 + (kh % 2)
                    SPp = pair(spi)
                    for sl4 in range(4):
                        hh = PERM[sl4]
                        h = 4 * kh + hh; m = h // 2; po = 64 * (h % 2)
                        A("pe", lambda e, hh=sl4, m=m, po=po, kh=kh, i=i: e.matmul(
                            SPp[:, hh * 256:(hh + 1) * 256], lhsT=qT3[po:po + 64, m, i * 128:(i + 1) * 128],
                            rhs=KT2[po:po + 64, kh, (i - 1) * 128:(i + 1) * 128], start=True, stop=False),
                          r=[qT_b, kt_b], w=[pair_b[spi]])
                        for hf in range(2):
                            A("pe", lambda e, hh=sl4, po=po, hf=hf: e.matmul(
                                SPp[:, hh * 256:(hh + 1) * 256], lhsT=iab[po:po + 64, hf, :],
                                rhs=mk[po:po + 64, mv, hf, :], start=False, stop=(hf == 1)),
                              r=[cstb_b], w=[pair_b[spi]])

                def c1(kh):
                    d = kh % 2
                    spi = 2 + (kh % 2)
                    SPp = pair(spi)
                    s_ = cnt["sa"] % 4; cnt["sa"] += 1
                    t = sa[:, s_, :]
                    sab = sa_b[s_]
                    sS = SPp.rearrange("p (h s) -> p h s", h=4); eE = ee4[:, d]
                    A("dve", lambda e: e.tensor_reduce(out=t[:, 0:4], in_=sS, axis=AX.X, op=ALU.max), r=[pair_b[spi]], w=[sab])
                    A("dve", lambda e: e.scalar_tensor_tensor(
                        out=t[:, 8:12], in0=t[:, 0:4], scalar=-SCALE, in1=nsinkb[:, 4 * kh:4 * kh + 4], op0=ALU.mult, op1=ALU.min),
                      r=[sab, cst_b], w=[sab])
                    A("dve", lambda e: e.tensor_tensor(out=t[:, 16:20], in0=sinkb[:, 4 * kh:4 * kh + 4], in1=t[:, 8:12],
                                                       op=ALU.add), r=[sab, cst_b], w=[sab])
                    for hh in range(4):
                        A("act", lambda e, hh=hh: e.activation(
                            out=eE[:, hh, :], in_=sS[:, hh, :], func=AF.Exp, bias=t[:, 8 + hh:9 + hh], scale=SCALE,
                            accum_out=t[:, 12 + hh:13 + hh]), r=[pair_b[spi], sab], w=[ee_b[d], sab])
                    A("act", lambda e: e.activation(out=t[:, 20:24], in_=t[:, 16:20], func=AF.Exp), r=[sab], w=[sab])
                    return (t, sab)

                def c2(kh, ts):
                    d = kh % 2
                    t, sab = ts
                    eE = ee4[:, d]; eN = en4[:, d]
                    A("dve", lambda e: e.tensor_tensor(out=t[:, 24:28], in0=t[:, 12:16], in1=t[:, 20:24], op=ALU.add),
                      r=[sab], w=[sab])
                    A("dve", lambda e: e.reciprocal(out=t[:, 28:32], in_=t[:, 24:28]), r=[sab], w=[sab])
                    A("dve", lambda e: e.tensor_tensor(
                        out=eN, in0=eE, in1=t[:, 28:32].unsqueeze(2).broadcast_to([128, 4, 256]), op=ALU.mult),
                      r=[ee_b[d], sab], w=[en_b[d]])

                def tp(kh, i=i):
                    d = kh % 2
                    eN = en4[:, d]
                    for hh in range(4):
                        for half in range(2):
                            A("pe", lambda e, hh=hh, half=half: e.transpose(
                                out=TPb[:, (hh * 2 + half) * 128:(hh * 2 + half + 1) * 128],
                                in_=eN[:, hh, half * 128:(half + 1) * 128], identity=identb[:, :]),
                              r=[en_b[d], identb_b], w=[pair_b[0]])
                    A("act", lambda e: e.activation(out=PT5[:, d].rearrange("p h f q -> p (h f q)"), in_=TPb[:, 0:1024], func=AF.Copy),
                      r=[pair_b[0]], w=[PT_b[d]])
                    for sl4 in range(4):
                        hh = PERM[sl4]
                        h = 4 * kh + hh; m = h // 2; po = 64 * (h % 2)
                        for half in range(2):
                            A("pe", lambda e, hh=sl4, half=half, m=m, po=po, kh=kh, i=i: e.matmul(
                                OP[po:po + 64, m * 128:(m + 1) * 128], lhsT=V[:, i - 1 + half, kh * 64:(kh + 1) * 64],
                                rhs=PT5[:, d, hh, half, :], start=(half == 0), stop=(half == 1)),
                              r=[PT_b[d], v_b[i - 1], v_b[i]], w=[pair_b[1]])

                def otcopy():
                    A("act", lambda e: e.activation(out=oT, in_=OP, func=AF.Copy), r=[pair_b[1]], w=[oT_b])

                def outproj():
                    out_proj(128, lambda m: oT3[:, m, :], 0)

                def zpart():
                    Yi = Y[:, i, :]
                    A("dve", lambda e: e.scalar_tensor_tensor(out=Yi, in0=Yi, scalar=ALPHA, in1=pair(0)[:, :],
                                                              op0=ALU.mult, op1=ALU.add), r=[yb[i], pair_b[0]], w=[yb[i]])

                def lnpush():
                    ln_core(ch, i, 128, True, 0)

                return dict(scores=scores, c1=c1, c2=c2, tp=tp, otcopy=otcopy, outproj=outproj, zpart=zpart, lnpush=lnpush, st={})

            tls = [make_tile(i) for i in range(1, NT)]
            T0 = tls[0]
            T0["scores"](0); T0["st"][0] = T0["c1"](0)
            T0["scores"](1); T0["st"][1] = T0["c1"](1)
            prev = None
            for ti, T_ in enumerate(tls):
                N_ = tls[ti + 1] if ti + 1 < len(tls) else None
                st_ = T_["st"]
                T_["scores"](2)
                if prev is not None:
                    prev["outproj"]()
                T_["c2"](0, st_[0])
                st_[2] = T_["c1"](2)
                if prev is not None:
                    prev["zpart"]()
                T_["tp"](0)
                if prev is not None:
                    prev["lnpush"]()
                T_["scores"](3)
                T_["c2"](1, st_[1]); T_["tp"](1)
                st_[3] = T_["c1"](3)
                if N_ is not None:
                    N_["scores"](0)
                T_["c2"](2, st_[2]); T_["tp"](2)
                if N_ is not None:
                    N_["st"][0] = N_["c1"](0)
                    N_["scores"](1)
                T_["c2"](3, st_[3]); T_["tp"](3)
                T_["otcopy"]()
                if N_ is not None:
                    N_["st"][1] = N_["c1"](1)
                prev = T_
            prev["outproj"]()
            prev["zpart"]()
            prev["lnpush"]()

            if has_s:
                i = NT
                Ksb, (Ksb_b,) = take("Ksb", [128, NS, 256], BF16)
                Ksb3 = Ksb.rearrange("p (i d) -> p i d", i=NS)
                Vsb, (Vsb_b,) = take("Vsb", [128, NS, 256], BF16)
                Vsb3 = Vsb.rearrange("p (i d) -> p i d", i=NS)
                KsT, (KsT_b,) = take("KsT", [128, NS, 4, 128], BF16, at=wq_at, after=[wq_b])
                KsT4 = KsT.rearrange("p (i h s) -> p i h s", i=NS, h=4)
                qsT, (qsT_b,) = take("qsT", [128, 16, NS], BF16)
                qsT3 = qsT.rearrange("p (h i) -> p h i", h=16)
                STs, (STs_b,) = take("STs", [128, 256], F32)
                es, (es_b,) = take("es", [128, 2, 128], F32)
                es3 = es.rearrange("p (f s) -> p f s", f=2)
                PTs, (PTs_b,) = take("PTs", [128, 256], BF16)
                osT, (osT_b,) = take("osT", [128, 8, NS], BF16)
                osT3 = osT.rearrange("p (m i) -> p m i", m=8)
                S.dma("pool", Ksb3, nk_s.rearrange("i s d -> s i d"), reads=[nks_b], writes=[Ksb_b])
                S.dma("pool", Vsb3, nv_s.rearrange("i s d -> s i d"), reads=[nvs_b], writes=[Vsb_b])
                QS = pair(0)
                for h in range(16):
                    for k in range(8):
                        A("pe", lambda e, h=h, k=k: e.matmul(
                            QS[0:64, h * 16:(h + 1) * 16], lhsT=wq3[:, k, h * 64:(h + 1) * 64], rhs=XT[:, k, TP:TP + NS],
                            start=(k == 0), stop=(k == 7)), r=[wq_b, xtb[NT]], w=[pair_b[0]])
                c0 = C_BQH + bi * 16
                A("dve", lambda e, c0=c0: e.tensor_tensor(
                    out=qsT3[0:64, :, :], in0=QS[0:64, 0:256].rearrange("p (h i) -> p h i", h=16),
                    in1=cst[0:64, c0:c0 + 16].unsqueeze(2).broadcast_to([64, 16, NS]), op=ALU.add),
                  r=[pair_b[0], cst_b], w=[qsT_b])
                for i0 in range(0, NS, 4):
                    pp = 2 + (i0 // 4) % 2
                    Pb = pair(pp).bitcast(BF16)
                    for ii in range(4):
                        for kh in range(4):
                            A("pe", lambda e, ii=ii, kh=kh, i0=i0, Pb=Pb: e.transpose(
                                out=Pb[0:64, (ii * 4 + kh) * 128:(ii * 4 + kh + 1) * 128],
                                in_=Ksb3[:, i0 + ii, kh * 64:(kh + 1) * 64], identity=identb[:, :]),
                              r=[Ksb_b, identb_b], w=[pair_b[pp]])
                    A("act", lambda e, i0=i0, Pb=Pb: e.activation(
                        out=KsT4[0:64, i0:i0 + 4, :, :], in_=Pb[0:64, 0:2048].rearrange("p (i h s) -> p i h s", i=4, h=4),
                        func=AF.Copy), r=[pair_b[pp]], w=[KsT_b])
                ST = pair(1)
                for ii in range(NS):
                    for kh in range(4):
                        A("pe", lambda e, ii=ii, kh=kh: e.matmul(
                            ST[:, ii * 16 + 4 * kh:ii * 16 + 4 * kh + 4], lhsT=KsT4[0:64, ii, kh, :],
                            rhs=qsT3[0:64, 4 * kh:4 * kh + 4, ii], start=True, stop=True),
                          r=[KsT_b, qsT_b], w=[pair_b[1]])
                A("dve", lambda e: e.tensor_copy(out=STs, in_=ST[:, 0:256]), r=[pair_b[1]], w=[STs_b])
                S2 = pair(2)
                for hf in range(2):
                    A("pe", lambda e, hf=hf: e.transpose(out=S2[:, hf * 128:(hf + 1) * 128], in_=STs[:, hf * 128:(hf + 1) * 128],
                                                         identity=ident[:, :]), r=[STs_b, ident_b], w=[pair_b[2]])
                s_ = cnt["sa"] % 4; cnt["sa"] += 1
                t = sa[:, s_, :]
                sab = sa_b[s_]
                sk = cst[:, C_SINKS + bi * 2:C_SINKS + bi * 2 + 2]
                A("dve", lambda e: e.tensor_reduce(out=t[:, 0:2], in_=S2[:, 0:256].rearrange("p (f s) -> p f s", f=2),
                                                   axis=AX.X, op=ALU.max), r=[pair_b[2]], w=[sab])
                A("dve", lambda e: e.scalar_tensor_tensor(out=t[:, 4:6], in0=t[:, 0:2], scalar=SCALE, in1=sk,
                                                          op0=ALU.mult, op1=ALU.max), r=[sab, cst_b], w=[sab])
                A("dve", lambda e: e.tensor_scalar(out=t[:, 8:10], in0=t[:, 4:6], scalar1=-1.0, scalar2=None, op0=ALU.mult),
                  r=[sab], w=[sab])
                for hf in range(2):
                    A("act", lambda e, hf=hf: e.activation(
                        out=es3[:, hf, :], in_=S2[:, hf * 128:(hf + 1) * 128], func=AF.Exp, bias=t[:, 8 + hf:9 + hf], scale=SCALE,
                        accum_out=t[:, 12 + hf:13 + hf]), r=[pair_b[2], sab], w=[es_b, sab])
                A("dve", lambda e: e.tensor_tensor(out=t[:, 16:18], in0=sk, in1=t[:, 4:6], op=ALU.subtract), r=[sab, cst_b], w=[sab])
                A("act", lambda e: e.activation(out=t[:, 20:22], in_=t[:, 16:18], func=AF.Exp), r=[sab], w=[sab])
                A("dve", lambda e: e.tensor_tensor(out=t[:, 24:26], in0=t[:, 12:14], in1=t[:, 20:22], op=ALU.add), r=[sab], w=[sab])
                A("dve", lambda e: e.reciprocal(out=t[:, 28:30], in_=t[:, 24:26]), r=[sab], w=[sab])
                A("dve", lambda e: e.tensor_tensor(out=es3, in0=es3, in1=t[:, 28:30].unsqueeze(2).broadcast_to([128, 2, 128]),
                                                   op=ALU.mult), r=[es_b, sab], w=[es_b])
                P2 = pair(3)
                for hf in range(2):
                    A("pe", lambda e, hf=hf: e.transpose(out=P2[:, hf * 128:(hf + 1) * 128], in_=es3[:, hf, :], identity=ident[:, :]),
                      r=[es_b, ident_b], w=[pair_b[3]])
                A("act", lambda e: e.activation(out=PTs, in_=P2[:, 0:256], func=AF.Copy), r=[pair_b[3]], w=[PTs_b])
                OS = pair(1)
                OS3 = OS[:, 0:128].rearrange("p (m i) -> p m i", m=8)
                for ii in range(NS):
                    for kh in range(4):
                        for par in range(2):
                            A("pe", lambda e, ii=ii, kh=kh, par=par: e.matmul(
                                OS3[64 * par:64 * par + 64, 2 * kh:2 * kh + 2, ii], lhsT=Vsb3[:, ii, kh * 64:(kh + 1) * 64],
                                rhs=PTs[:, ii * 16 + 4 * kh + par:ii * 16 + 4 * kh + 4:2], start=True, stop=True),
                              r=[Vsb_b, PTs_b, STs_b], w=[pair_b[1]])
                A("act", lambda e: e.activation(out=osT, in_=OS[:, 0:128], func=AF.Copy), r=[pair_b[1]], w=[osT_b])
                oT_b_save = oT_b
                MP = pair(0)
                for half in range(2):
                    for m in range(8):
                        A("pe", lambda e, half=half, m=m: e.matmul(
                            MP[0:NS, half * 512:(half + 1) * 512], lhsT=osT3[:, m, :], rhs=wo3[:, m, half * 512:(half + 1) * 512],
                            start=(m == 0), stop=False), r=[osT_b, wo_b], w=[pair_b[0]])
                    A("pe", lambda e, half=half: e.matmul(
                        MP[0:128, half * 512:(half + 1) * 512], lhsT=ones33[:, 0:128],
                        rhs=bb[:, bi * 1024 + half * 512:bi * 1024 + (half + 1) * 512], start=False, stop=True),
                      r=[ones_b, bb_b], w=[pair_b[0]])
                ln_mix(ch, i, NS, 0, True, 3)
            ln_flush()

        stage = {"n": 0}

        def go():
            stage["n"] += 1
            return stop is None or stage["n"] <= stop

        for ch in range(2):
            if not go():
                break
            S.dma("sp", Y[:, 0:NT, :], xin[ch * TP:(ch + 1) * TP, :].rearrange("(t p) d -> p t d", p=128), writes=yb[0:NT])
            if ch == 1:
                S.dma("sp", Y[0:NS, NT, :], xs, writes=[yb[NT]])
            for l in range(NL):
                if go():
                    if l < 2:
                        pool_layer(ch, l)
                    else:
                        attn_layer(ch, l)
                if go():
                    ffn_layer(ch, l)
                if l == 1 and go():
                    kv_proj(ch)
        fin = yb + [kvo_b, nks_b, nvs_b, dram_misc_b] + ar["bufs"]
        if dbg:
            dby_b = Buf("dbgyb")
            S.dma("sp", dbgy.rearrange("t p d -> p t d"), Y[:, :, :], reads=yb, writes=[dby_b])
            S.dma("pool", dbgx, XT[:, :, :].rearrange("p k t -> p (k t)"), reads=xtb, writes=[dby_b])
            S.dma("pool", dbgk, KT2[:, :, :].rearrange("p k t -> p (k t)"), reads=[kt_b], writes=[dby_b])
            S.dma("pool", dbgv, V[:, :, :].rearrange("p k t -> p (k t)"), reads=v_b, writes=[dby_b])
            fin = fin + [dby_b]
        S.finalize(st, fin)
    return nc, S


POOL_WINDOWS = (2, 4, 8, 16)


def _consts_for_core(c, inp):
    qd = c % 4
    f32 = np.float32
    cst = np.zeros((128, NCST), f32)
    cw = np.asarray(inp["ffn_conv_w"], f32)
    cbv = np.asarray(inp["ffn_conv_b"], f32)
    cst[:, C_CW:C_CW + 264] = cw.reshape(NL, 3, NJ, 128).transpose(3, 0, 2, 1).reshape(128, 264)
    cst[:, C_CB:C_CB + 88] = cbv.reshape(NL, NJ, 128).transpose(2, 0, 1).reshape(128, 88)
    bq = np.asarray(inp["attn_b_q"], f32)
    cst[:, C_BQT:C_BQT + 16] = bq.reshape(2, 8, 128).transpose(2, 0, 1).reshape(128, 16)
    cst[0:64, C_BQH:C_BQH + 32] = bq.reshape(2, 16, 64).transpose(2, 0, 1).reshape(64, 32)
    bkv = np.asarray(inp["b_kv"], f32)
    bk = bkv[:256].reshape(4, 64)
    cst[:, C_BKT:C_BKT + 4] = np.concatenate([bk.T, bk.T], axis=0)
    sinks = np.asarray(inp["attn_sinks"], f32)
    sperm = sinks.reshape(2, 4, 4)[:, :, PERM].reshape(1, 32)
    cst[:, C_SINKB:C_SINKB + 32] = np.broadcast_to(sperm, (128, 32))
    cst[:, C_NSINKB:C_NSINKB + 32] = np.broadcast_to(-sperm, (128, 32))
    pidx = np.arange(128)
    for bi in range(2):
        for hf in range(2):
            cst[:, C_SINKS + bi * 2 + hf] = sinks[bi, pidx % 16]

    def real(ch, t, r):
        blk = 16 * qd + 8 * ch - 1 + t
        return (blk * 128 + r) >= 112

    r = np.arange(128)
    for ch in range(2):
        for i in range(2):
            cst[:, C_TM + ch * 2 + i] = real(ch, i, r).astype(f32)
    q = np.arange(128)[:, None]
    j = np.arange(256)[None, :]
    band = (q < j) & (j <= q + 128)
    amask = np.zeros((6, 128, 256), f32)
    for ch in range(2):
        for v in range(3):
            if v == 2:
                ok = band
            else:
                ti = 1 + v
                kr = np.where(j < 128, real(ch, ti - 1, j % 128), real(ch, ti, j % 128))
                ok = band & kr
            amask[ch * 3 + v] = np.where(ok, 0.0, NEG).astype(f32)

    cstb = np.zeros((128, NCSTB), f32)
    s = np.arange(128)[:, None]
    t = np.arange(128)[None, :]
    bc = np.zeros((128, 2, 4, 128), f32)
    bp = np.zeros((128, 4, 128), f32)
    for g, w in enumerate(POOL_WINDOWS):
        inwin = (s > t - w) & (s <= t)
        gen = inwin.astype(f32) / w - (s == t).astype(f32)
        bc[:, 1, g, :] = gen
        if qd == 0:
            tseq = t - 112
            cnt = np.where(tseq >= 0, np.minimum(w, tseq + 1), w).astype(f32)
            bc[:, 0, g, :] = inwin.astype(f32) / cnt - (s == t).astype(f32)
        else:
            bc[:, 0, g, :] = gen
        bp[:, g, :] = ((s > 128 + t - w).astype(f32)) / w
    cstb[:, B_BC:B_BC + 1024] = bc.reshape(128, 1024)
    cstb[:, B_BP:B_BP + 512] = bp.reshape(128, 512)
    sel = np.zeros((128, 2, 4, 16), f32)
    ci = np.zeros((128, 4, 16), f32)
    for g, w in enumerate(POOL_WINDOWS):
        for p in range(120):
            rr = p % 15
            if rr >= 16 - w:
                for t_ in range(2):
                    sel[p, t_, g, t_ * 8 + p // 15] = 1.0 / w
        for p in range(16):
            ci[p, g, p] = 1.0 / w - 1.0
    cstb[:, B_SEL:B_SEL + 128] = sel.reshape(128, 128)
    cstb[:, B_CI:B_CI + 64] = ci.reshape(128, 64)
    mkh = np.zeros((128, 6, 2, 256), f32)
    iab = np.zeros((128, 2, 128), f32)
    for p in range(128):
        for h in range(2):
            mkh[p, :, h, :] = amask[:, h * 64 + p % 64, :]
            iab[p, h, h * 64 + p % 64] = 1.0
    cstb[:, B_MK:B_MK + 3072] = mkh.reshape(128, 3072)
    cstb[:, B_IAB:B_IAB + 256] = iab.reshape(128, 256)
    return cst, cstb


_NC_CACHE = {}


def kernel(**inputs):
    f32 = np.float32
    inp = {k: np.asarray(v) for k, v in inputs.items()}
    xp = inp["x_prompt"].astype(f32, copy=False)
    meta = inp["meta_tokens"].astype(f32, copy=False)
    n = 8
    if "nc" not in _NC_CACHE:
        _NC_CACHE["nc"] = build_nc()[0]
    nc = _NC_CACHE["nc"]
    brow = np.concatenate([inp["attn_b_o"][0], inp["attn_b_o"][1], inp["b_kv"]]).astype(f32).reshape(1, 2560)
    shared = {
        "pool_w": np.ascontiguousarray(inp["pool_w"], f32), "pool_scale": np.ascontiguousarray(inp["pool_scale"], f32),
        "w_kv": np.ascontiguousarray(inp["w_kv"], f32), "w_q": np.ascontiguousarray(inp["attn_w_q"], f32),
        "w_o": np.ascontiguousarray(inp["attn_w_o"], f32), "w_in": np.ascontiguousarray(inp["ffn_w_in"], f32),
        "w_out": np.ascontiguousarray(inp["ffn_w_out"], f32),
        "lmg": np.ascontiguousarray(inp["ln_mix_g"], f32), "lmb": np.ascontiguousarray(inp["ln_mix_b"], f32),
        "lfg": np.ascontiguousarray(inp["ln_ffn_g"], f32), "lfb": np.ascontiguousarray(inp["ln_ffn_b"], f32),
        "brow": brow,
    }
    in_maps = []
    for c in range(n):
        b, qd = c // 4, c % 4
        xin = np.zeros((2, NT, 128, D), f32)
        for ch in range(2):
            for i in range(NT):
                blk = 16 * qd + 8 * ch - 1 + i
                if blk >= 1:
                    xin[ch, i] = xp[b, (blk - 1) * 128:blk * 128]
                elif blk == 0:
                    xin[ch, i, 112:128] = meta
        cst, cstb = _consts_for_core(c, inp)
        sl = slice(NS * c, NS * (c + 1))
        m = dict(shared)
        m.update({
            "xin": xin.reshape(2 * NT * 128, D),
            "xs": np.ascontiguousarray(inp["x_sample"][sl, 0, :], f32),
            "spool": np.ascontiguousarray(inp["state_pool"][:, sl], f32).reshape(2, 240, D),
            "sconv": np.ascontiguousarray(inp["state_conv"][:, sl], f32).reshape(NL, 32, DFF),
            "skw": np.ascontiguousarray(inp["state_k_win"][sl], f32).reshape(NS, 128, 256),
            "svw": np.ascontiguousarray(inp["state_v_win"][sl], f32).reshape(NS, 128, 256),
            "cst": cst, "cstb": cstb,
        })
        in_maps.append(m)
    res = run_bass_kernel_spmd(nc, in_maps, core_ids=list(range(n)))
    R = res.results
    y_prompt = np.zeros((2, 8192, D), f32)
    y_sample = np.zeros((128, 1, D), f32)
    npp = np.zeros((2, 2, 15, D), f32); nps = np.zeros((2, 128, 15, D), f32)
    ncp = np.zeros((NL, 2, 2, DFF), f32); ncs = np.zeros((NL, 128, 2, DFF), f32)
    nkp = np.zeros((2, 128, 4, 64), f32); nvp = np.zeros((2, 128, 4, 64), f32)
    nks = np.zeros((128, 128, 4, 64), f32); nvs = np.zeros((128, 128, 4, 64), f32)
    for c in range(n):
        b, qd = c // 4, c % 4
        r = R[c]
        sl = slice(NS * c, NS * (c + 1))
        y_prompt[b, 2048 * qd:2048 * (qd + 1)] = r["y_out"]
        y_sample[sl, 0] = r["ys_out"]
        nps[:, sl] = r["npool_s"]
        ncs[:, sl] = r["nconv_s"]
        nks[sl] = r["nk_s"].reshape(NS, 128, 4, 64)
        nvs[sl] = r["nv_s"].reshape(NS, 128, 4, 64)
        if qd == 3:
            npp[:, b] = r["npool_p"]
            ncp[:, b] = r["nconv_p"]
            nkp[b] = r["nk_p"].reshape(128, 4, 64)
            nvp[b] = r["nv_p"].reshape(128, 4, 64)
    return (y_prompt, y_sample, npp, nps, ncp, ncs, nkp, nvp, nks, nvs)
```

```python
import contextlib
import numpy as np
import concourse.bass as bass
import concourse.mybir as mybir
from concourse.bass_utils import run_bass_kernel_spmd

F32 = mybir.dt.float32
BF16 = mybir.dt.bfloat16
AF = mybir.ActivationFunctionType
ALU = mybir.AluOpType
AX = mybir.AxisListType


class Buf:
    __slots__ = ("name", "w", "rs", "dsem", "dcnt", "slot")

    def __init__(self, name):
        self.name = name
        self.w = None
        self.rs = {}
        self.dsem = None
        self.dcnt = 0
        self.slot = self


class Sched:
    ENG = ["pe", "act", "dve", "pool", "sp"]

    def __init__(self, nc):
        self.nc = nc
        self.ops = {e: [] for e in self.ENG}
        self.waited = {e: {} for e in self.ENG}
        self.dbufs = []

    def _deps(self, eng, reads, writes):
        best = {}
        idx = len(self.ops[eng])

        def add(tok):
            if tok is None:
                return
            if tok[0] == "e":
                _, pe, pidx = tok
                if pe == eng and eng == "pe":
                    return
                key = ("e", pe)
                v = pidx
            else:
                _, b, v = tok
                key = ("d", b)
            if best.get(key, -1) < v:
                best[key] = v

        for b in reads:
            add(b.w)
        for b in writes:
            add(b.w)
            for t in b.rs.values():
                add(t)
        waits = []
        for key, v in best.items():
            if self.waited[eng].get(key, -1) >= v:
                continue
            self.waited[eng][key] = v
            waits.append((key, v))
        return waits

    def _commit(self, tok, reads, writes):
        for b in writes:
            b.w = tok
            b.rs = {}
        for b in reads:
            if b in writes:
                continue
            if tok[0] == "e":
                b.rs[("e", tok[1])] = tok
            else:
                b.rs[("d", tok[1])] = tok

    def op(self, eng, fn, reads=(), writes=()):
        waits = self._deps(eng, reads, writes)
        idx = len(self.ops[eng])
        self.ops[eng].append(dict(fn=fn, waits=waits, sig=False, dma=None))
        tok = ("e", eng, idx)
        self._commit(tok, reads, writes)
        return tok

    def dma(self, eng, out, in_, reads=(), writes=(), **kw):
        waits = self._deps(eng, reads, writes)
        pb = (writes[0] if writes else reads[0]).slot
        if pb.dsem is None:
            pb.dsem = True
            self.dbufs.append(pb)
        pb.dcnt += 16
        self.ops[eng].append(dict(
            fn=lambda e: e.dma_start(out=out, in_=in_, **kw), waits=waits, sig=False, dma=pb))
        tok = ("d", pb, pb.dcnt)
        self._commit(tok, reads, writes)
        return tok

    def finalize(self, stack, final_bufs=()):
        nc = self.nc
        waits = self._deps("sp", list(final_bufs), list(final_bufs))
        self.ops["sp"].append(dict(fn=None, waits=waits, sig=False, dma=None))
        for e in self.ENG:
            for rec in self.ops[e]:
                for key, v in rec["waits"]:
                    if key[0] == "e":
                        self.ops[key[1]][v]["sig"] = True
        cum = {}
        for e in self.ENG:
            c = 0
            arr = []
            for rec in self.ops[e]:
                if rec["sig"]:
                    c += 1
                arr.append(c)
            cum[e] = arr
        esem = {e: stack.enter_context(nc.semaphore("s_" + e)) for e in self.ENG}
        for b in self.dbufs:
            b.dsem = stack.enter_context(nc.semaphore("d_" + b.name))
        engobj = {"pe": "tensor", "act": "scalar", "dve": "vector", "pool": "gpsimd", "sp": "sync"}

        def emit(name, e):
            for rec in self.ops[name]:
                for key, v in rec["waits"]:
                    if key[0] == "e":
                        e.wait_ge(esem[key[1]], cum[key[1]][v])
                    else:
                        e.wait_ge(key[1].dsem, v)
                if rec["fn"] is None:
                    continue
                ins = rec["fn"](e)
                if rec["dma"] is not None:
                    ins.then_inc(rec["dma"].dsem, 16)
                elif rec["sig"]:
                    ins.then_inc(esem[name], 1)

        block = stack.enter_context(nc.Block())
        for name in self.ENG:
            getattr(block, engobj[name])(lambda e, name=name: emit(name, e))
        self.stats = {e: len(self.ops[e]) for e in self.ENG}
        self.nsem = 5 + len(self.dbufs)

D = 1024; DFF = 2816; NJ = 22; NL = 4; NT = 10; TP = NT * 128; NS = 16; TC = TP + NS
ALPHA = (2.0 * 4) ** 0.25; EPS = 1e-5; SCALE = 0.125; NEG = -1e30
GROUPS = [(0, 6), (6, 12), (12, 17), (17, 22)]
C_CW = 0; C_CB = 264; C_BQT = 352; C_BQH = 368; C_BKT = 400; C_SINKB = 404; C_SINKS = 436; C_TM = 440; C_AM = 444; C_NSINKB = 1980; NCST = 2012
B_BC = 0; B_BP = 1024; B_SEL = 1536; B_CI = 1664; NCSTB = 1728
U8 = mybir.dt.uint8
PERM = [0, 2, 1, 3]
import os
KVDBG = int(os.environ.get('KVDBG', '0'))
ARENA_BYTES = 100 * 1024


def build_nc(stop=None, dbg=False):
    nc = bass.Bass("TRN2", target_bir_lowering=False)

    def din(name, shape):
        return nc.dram_tensor(name, list(shape), F32, kind="ExternalInput").ap()

    def dout(name, shape):
        return nc.dram_tensor(name, list(shape), F32, kind="ExternalOutput").ap()

    xin = din("xin", [2 * NT * 128, D]); xs = din("xs", [NS, D])
    spool = din("spool", [2, 240, D]); sconv = din("sconv", [NL, 32, DFF])
    skw = din("skw", [NS, 128, 256]); svw = din("svw", [NS, 128, 256])
    pool_w = din("pool_w", [2, 4, 256, 256]); pool_scale = din("pool_scale", [2, D])
    w_kv = din("w_kv", [D, 512])
    w_q = din("w_q", [2, D, D]); w_o = din("w_o", [2, D, D])
    w_in = din("w_in", [NL, D, 2 * DFF]); w_out = din("w_out", [NL, DFF, D])
    lmg = din("lmg", [NL, D]); lmb = din("lmb", [NL, D]); lfg = din("lfg", [NL, D]); lfb = din("lfb", [NL, D])
    cst_d = din("cst", [128, NCST]); cstb_d = din("cstb", [128, NCSTB]); brow_d = din("brow", [1, 2560])

    y_out = dout("y_out", [2 * 8 * 128, D]); ys_out = dout("ys_out", [NS, D])
    npool_p = dout("npool_p", [2, 15, D]); npool_s = dout("npool_s", [2, NS, 15, D])
    nconv_p = dout("nconv_p", [NL, 2, DFF]); nconv_s = dout("nconv_s", [NL, NS, 2, DFF])
    nk_p = dout("nk_p", [128, 256]); nv_p = dout("nv_p", [128, 256])
    nk_s = dout("nk_s", [NS, 128, 256]); nv_s = dout("nv_s", [NS, 128, 256])

    if dbg:
        dbgy = dout("dbgy", [NT + 1, 128, D]); dbgx = dout("dbgx", [128, 8 * TC])
        dbgk = dout("dbgk", [128, 4 * TP]); dbgv = dout("dbgv", [128, NT * 256])
    S = Sched(nc)
    with contextlib.ExitStack() as st:
        def sbt(name, shape, dt):
            return st.enter_context(nc.sbuf_tensor("sb_" + name, list(shape), dt))

        def A(eng, fn, r=(), w=()):
            return S.op(eng, fn, reads=list(r), writes=list(w))

        cst = sbt("cst", [128, NCST], F32); cst_b = Buf("cst")
        cstb = sbt("cstb", [128, NCSTB], BF16); cstb_b = Buf("cstb")
        Y = sbt("Y", [128, NT + 1, D], F32); yb = [Buf(f"y{i}") for i in range(NT + 1)]
        for i_ in range(1, NT):
            yb[i_].slot = yb[0]
        XT = sbt("XT", [128, 8, TC], BF16); xtb = [Buf(f"xt{i}") for i in range(NT + 1)]
        lng = sbt("lng", [128, D], F32); lng_b = Buf("lng")
        lnb = sbt("lnb", [128, D], F32); lnb_b = Buf("lnb")
        KT2 = sbt("KT2", [128, 4, TP], BF16); kt_b = Buf("kt2")
        V = sbt("V", [128, NT, 256], BF16); v_b = [Buf(f"v{i}") for i in range(NT)]
        ident = sbt("ident", [128, 128], F32); ident_b = Buf("ident")
        identb = sbt("identb", [128, 128], BF16); identb_b = Buf("identb")
        ones33 = sbt("ones33", [33, 128], BF16); ones_b = Buf("ones33")
        bb = sbt("bb", [33, 2560], BF16); bb_b = Buf("bb")
        mhalf = sbt("mhalf", [128, 1], F32); mhalf_b = Buf("mhalf")
        NSL = 4
        sm = sbt("sm", [128, NSL, 24], F32); sm_b = [Buf(f"sm{i}") for i in range(NSL)]
        sa = sbt("sa", [128, 4, 48], F32); sa_b = [Buf(f"sa{i}") for i in range(4)]
        kvo = sbt("kvo", [128, 512], F32); kvo_b = Buf("kvo")
        arena = sbt("arena", [128, ARENA_BYTES], U8)
        PS = st.enter_context(nc.psum_tensor("PS", [128, 8, 512], F32))
        pair_b = [Buf(f"pp{i}") for i in range(4)]
        bank_b = [Buf(f"pb{i}") for i in range(8)]

        def pair(p):
            return PS[:, 2 * p:2 * p + 2, :].rearrange("p a b -> p (a b)")

        def bank(bk):
            return PS[:, bk, :]

        def bank_deps(bk):
            return [pair_b[bk // 2], bank_b[bk]]

        nks_b = Buf("nks"); nvs_b = Buf("nvs"); dram_misc_b = Buf("dmisc")

        ar = {"off": 0, "bufs": [], "old": []}
        slots = {}

        def arena_reset():
            ar["old"] = ar["old"][-200:] + ar["bufs"] if False else ar["bufs"]
            ar["bufs"] = []
            ar["off"] = 0

        def take(name, shape, dt, nb=1, at=None, after=()):
            esz = 4 if dt == F32 else 2
            free = 1
            for s_ in shape[1:]:
                free *= s_
            nbytes = free * esz * nb
            nbytes = (nbytes + 63) // 64 * 64
            if at is None:
                assert ar["off"] + nbytes <= ARENA_BYTES, (name, ar["off"], nbytes)
                ar["last_off"] = ar["off"]
                v = arena[:, ar["off"]:ar["off"] + nbytes].bitcast(dt)
                ar["off"] += nbytes
            else:
                v = arena[:, at:at + nbytes].bitcast(dt)
            v = v[:, 0:free * nb]
            bufs = []
            for i in range(nb):
                b = Buf(f"{name}{i}")
                b.slot = slots.setdefault(b.name, b)
                for ob in list(ar["old"]) + list(after):
                    toks = ([ob.w] if ob.w is not None else []) + list(ob.rs.values())
                    for t in toks:
                        key = ("e", t[1]) if t[0] == "e" else ("d", t[1])
                        cur = b.rs.get(key)
                        if cur is None or cur[2] < t[2]:
                            b.rs[key] = t
                bufs.append(b)
                ar["bufs"].append(b)
            return v, bufs

        def inherit(dst, srcs):
            for ob in srcs:
                toks = ([ob.w] if ob.w is not None else []) + list(ob.rs.values())
                for t in toks:
                    key = ("e", t[1]) if t[0] == "e" else ("d", t[1])
                    cur = dst.rs.get(key)
                    if cur is None or cur[2] < t[2]:
                        dst.rs[key] = t

        def view(v, pat, **kw):
            return v.rearrange(pat, **kw)

        S.dma("sp", cst[:], cst_d, writes=[cst_b])
        S.dma("pool", cstb[:], cstb_d, writes=[cstb_b])
        A("dve", lambda e: e.memset(ident[:], 0.0), w=[ident_b])
        A("pool", lambda e: e.affine_select(out=ident[:], in_=ident[:], compare_op=ALU.not_equal, fill=1.0,
                                            base=0, pattern=[[-1, 128]], channel_multiplier=1),
          r=[ident_b], w=[ident_b])
        A("act", lambda e: e.activation(out=identb[:], in_=ident[:], func=AF.Copy), r=[ident_b], w=[identb_b])
        A("dve", lambda e: e.memset(ones33[:], 1.0), w=[ones_b])
        A("dve", lambda e: e.memset(mhalf[:], -0.5), w=[mhalf_b])
        A("dve", lambda e: e.memset(bb[:], 0.0), w=[bb_b])
        arena_reset()
        bst, (bst_b,) = take("bst", [33, 2560], F32)
        bhi, (bhi_b,) = take("bhi", [33, 2560], BF16)
        blo, (blo_b,) = take("blo", [33, 2560], F32)
        A("dve", lambda e: e.memset(bst[0:33, :], 0.0), w=[bst_b])
        S.dma("sp", bst[0:1, :], brow_d, writes=[bst_b])
        S.dma("sp", bst[32:33, :], brow_d, writes=[bst_b])
        A("act", lambda e: e.activation(out=bhi[0:33, :], in_=bst[0:33, :], func=AF.Copy), r=[bst_b], w=[bhi_b])
        A("dve", lambda e: e.tensor_tensor(out=blo[0:33, :], in0=bst[0:33, :], in1=bhi[0:33, :], op=ALU.subtract),
          r=[bst_b, bhi_b], w=[blo_b])
        A("dve", lambda e: e.tensor_copy(out=bb[0:1, :], in_=bhi[0:1, :]), r=[bhi_b, bb_b], w=[bb_b])
        A("dve", lambda e: e.tensor_copy(out=bb[32:33, :], in_=blo[32:33, :]), r=[blo_b, bb_b], w=[bb_b])

        bandc = cstb[:, B_BC:B_BC + 1024].rearrange("p (v g t) -> p v g t", v=2, g=4)
        bandp = cstb[:, B_BP:B_BP + 512].rearrange("p (g t) -> p g t", g=4)
        sel = cstb[:, B_SEL:B_SEL + 128].rearrange("p (t g i) -> p t g i", t=2, g=4)
        coefI = cstb[:, B_CI:B_CI + 64].rearrange("p (g i) -> p g i", g=4)

        cnt = {"sm": 0, "sa": 0, "pp": 0}

        def ln_A1(it):
            i, rows = it["i"], it["rows"]
            Yi = Y[0:rows, i, :]
            s_ = cnt["sm"] % NSL; cnt["sm"] += 1
            it["s"] = s_
            smb = sm_b[s_]
            t = sm[0:rows, s_, :]
            A("dve", lambda e: e.bn_stats(out=t[:, 0:6], in_=Yi[:, 0:512]), r=[yb[i]], w=[smb])
            A("dve", lambda e: e.bn_stats(out=t[:, 6:12], in_=Yi[:, 512:1024]), r=[yb[i], smb], w=[smb])
            A("dve", lambda e: e.bn_aggr(out=t[:, 12:14], in_=t[:, 0:12]), r=[smb], w=[smb])
            A("dve", lambda e: e.tensor_scalar(out=t[:, 14:15], in0=t[:, 13:14], scalar1=EPS, scalar2=None, op0=ALU.add),
              r=[smb], w=[smb])
            A("pool", lambda e: e.tensor_tensor(out=t[:, 15:16], in0=t[:, 14:15], in1=mhalf[0:rows, :], op=ALU.pow),
              r=[smb, mhalf_b], w=[smb])

        def ln_A2(it):
            i, rows, s_ = it["i"], it["rows"], it["s"]
            Yi = Y[0:rows, i, :]
            smb = sm_b[s_]
            t = sm[0:rows, s_, :]
            A("dve", lambda e: e.scalar_tensor_tensor(out=t[:, 16:17], in0=t[:, 12:13], scalar=-1.0, in1=t[:, 15:16],
                                                      op0=ALU.mult, op1=ALU.mult), r=[smb], w=[smb])
            A("act", lambda e: e.activation(out=Yi, in_=Yi, func=AF.Identity, bias=t[:, 16:17], scale=t[:, 15:16]),
              r=[yb[i], smb], w=[yb[i]])

        def ln_A3(it):
            i, rows, ch = it["i"], it["rows"], it["ch"]
            Yi = Y[0:rows, i, :]
            A("dve", lambda e: e.tensor_tensor(out=Yi, in0=Yi, in1=lng[0:rows, :], op=ALU.mult), r=[yb[i], lng_b], w=[yb[i]])
            A("dve", lambda e: e.tensor_tensor(out=Yi, in0=Yi, in1=lnb[0:rows, :], op=ALU.add), r=[yb[i], lnb_b], w=[yb[i]])
            if i < 2:
                c0 = C_TM + ch * 2 + i
                A("dve", lambda e: e.tensor_scalar(out=Yi, in0=Yi, scalar1=cst[0:rows, c0:c0 + 1], scalar2=None, op0=ALU.mult),
                  r=[yb[i], cst_b], w=[yb[i]])

        def ln_B(it):
            i, rows = it["i"], it["rows"]
            Yi = Y[0:rows, i, :]
            if it["need_xt"]:
                rpair = it["rpair"]() if callable(it["rpair"]) else it["rpair"]
                R = pair(rpair)
                for k in range(8):
                    A("pe", lambda e, k=k: e.transpose(out=R[:, k * 128:k * 128 + rows], in_=Yi[:, k * 128:(k + 1) * 128],
                                                       identity=ident[0:rows, 0:rows]),
                      r=[yb[i], ident_b], w=[pair_b[rpair]])
                cx = i * 128
                A("act", lambda e: e.activation(out=XT[:, :, cx:cx + rows],
                                                in_=R.rearrange("p (k t) -> p k t", k=8)[:, :, 0:rows], func=AF.Copy),
                  r=[pair_b[rpair]], w=[xtb[i]])
            if it.get("post") is not None:
                it["post"]()

        lnq = []

        def ln_push(ch, i, rows, need_xt, rpair, post=None):
            it = dict(ch=ch, i=i, rows=rows, need_xt=need_xt, rpair=rpair, post=post, st=1)
            ln_A1(it)
            lnq.append(it)
            if len(lnq) >= 2 and lnq[-2]["st"] == 1:
                ln_A2(lnq[-2]); lnq[-2]["st"] = 2
            if len(lnq) >= 3 and lnq[-3]["st"] == 2:
                ln_A3(lnq[-3]); lnq[-3]["st"] = 3
            if len(lnq) >= 4:
                o = lnq.pop(0)
                ln_B(o)

        def ln_flush():
            while lnq:
                for o in lnq:
                    if o["st"] == 1:
                        ln_A2(o); o["st"] = 2
                    elif o["st"] == 2:
                        ln_A3(o); o["st"] = 3
                    elif o["st"] == 3:
                        ln_B(o); o["st"] = 4
                while lnq and lnq[0]["st"] == 4:
                    lnq.pop(0)

        def ln_core(ch, i, rows, need_xt, rpair, post=None):
            ln_push(ch, i, rows, need_xt, rpair, post)

        def ln_mix(ch, i, rows, mp, need_xt, rpair):
            Yi = Y[0:rows, i, :]
            A("dve", lambda e: e.scalar_tensor_tensor(out=Yi, in0=Yi, scalar=ALPHA, in1=pair(mp)[0:rows, :],
                                                      op0=ALU.mult, op1=ALU.add), r=[yb[i], pair_b[mp]], w=[yb[i]])
            ln_core(ch, i, rows, need_xt, rpair)

        def load_ln(g_d, b_d, l):
            S.dma("sp", lng[:], g_d[l:l + 1, :].partition_broadcast(128), writes=[lng_b])
            S.dma("sp", lnb[:], b_d[l:l + 1, :].partition_broadcast(128), writes=[lnb_b])

        def pool_layer(ch, a):
            has_s = (ch == 1)
            arena_reset()
            psc, (psc_b,) = take("psc", [128, D], F32)
            wpf, (wpf_b,) = take("wpf", [128, 8, 256], F32)
            wp, (wp_b,) = take("wp", [128, 8, 256], BF16)
            ybf, ybf_b = take("ybf", [128, D], BF16, nb=3)
            dT, dT_b = take("dT", [128, 8, 128], BF16, nb=3)
            spb, (spb_b,) = take("spb", [128, 2, D], BF16)
            xnb, (xnb_b,) = take("xnb", [128, D], BF16)
            wpf4 = wpf.rearrange("p (g k e) -> p g k e", g=4, k=2)
            wp4 = wp.rearrange("p (g k e) -> p g k e", g=4, k=2)
            wp3 = wp.rearrange("p (c e) -> p c e", c=8)
            ybf3 = ybf.rearrange("p (s d) -> p s d", s=3)
            dT4 = dT.rearrange("p (s c t) -> p s c t", s=3, c=8)
            spb3 = spb.rearrange("p (t d) -> p t d", t=2)
            S.dma("sp", psc, pool_scale[a:a + 1, :].partition_broadcast(128), writes=[psc_b])
            S.dma("sp", wpf.rearrange("p (c e) -> p c e", c=8), pool_w[a].rearrange("g (k p) e -> p (g k) e", p=128), writes=[wpf_b])
            for kk in range(2):
                A("dve", lambda e, kk=kk: e.tensor_tensor(out=wp4[:, :, kk, :], in0=wpf4[:, :, kk, :],
                                                          in1=psc.rearrange("p (g e) -> p g e", g=4), op=ALU.mult),
                  r=[wpf_b, psc_b], w=[wp_b])
            load_ln(lmg, lmb, a)
            if has_s:
                for t_ in range(2):
                    S.dma("pool", spb3[0:120, t_, :], spool[a, t_ * 120:(t_ + 1) * 120, :], writes=[spb_b])
            def stageA(i):
                sl = i % 3
                A("act", lambda e, i=i, sl=sl: e.activation(out=ybf3[:, sl, :], in_=Y[:, i, :], func=AF.Copy),
                  r=[yb[i]], w=[ybf_b[sl]])
                dp = i % 2
                P = pair(dp)
                var = 0 if (ch == 0 and i == 1) else 1
                for kc in range(8):
                    g = kc // 2
                    A("pe", lambda e, kc=kc, g=g, sl=sl, var=var, P=P, i=i: e.matmul(
                        P[:, kc * 128:(kc + 1) * 128], lhsT=ybf3[:, sl, kc * 128:(kc + 1) * 128], rhs=bandc[:, var, g, :],
                        start=True, stop=(i == 0)), r=[ybf_b[sl], cstb_b], w=[pair_b[dp]])
                    if i > 0:
                        sp_ = (i - 1) % 3
                        A("pe", lambda e, kc=kc, g=g, sp_=sp_, P=P: e.matmul(
                            P[:, kc * 128:(kc + 1) * 128], lhsT=ybf3[:, sp_, kc * 128:(kc + 1) * 128], rhs=bandp[:, g, :],
                            start=False, stop=True), r=[ybf_b[sp_], cstb_b], w=[pair_b[dp]])
                if ch == 1 and i == NT - 1:
                    S.dma("sp", npool_p[a], Y[113:128, i, :], reads=[yb[i]])
                ds = i % 3
                A("act", lambda e, ds=ds, P=P: e.activation(out=dT4[:, ds, :, :], in_=P.rearrange("p (c t) -> p c t", c=8),
                                                            func=AF.Copy), r=[pair_b[dp]], w=[dT_b[ds]])

            def stageB(i):
                mp = 2 + (i % 2)
                ds = i % 3
                Q = pair(mp)
                for g in range(4):
                    for kk in range(2):
                        A("pe", lambda e, g=g, kk=kk, ds=ds, Q=Q: e.matmul(
                            Q[:, g * 256:(g + 1) * 256], lhsT=dT4[:, ds, 2 * g + kk, :], rhs=wp3[:, 2 * g + kk, :],
                            start=(kk == 0), stop=(kk == 1)), r=[dT_b[ds], wp_b], w=[pair_b[mp]])
                ln_mix(ch, i, 128, mp, True, mp)

            stageA(0)
            stageA(1)
            for i in range(NT):
                if i + 2 < NT:
                    stageA(i + 2)
                stageB(i)
            if has_s:
                i = NT
                S.dma("sp", npool_s[a, :, 14, :], Y[0:NS, i, :], reads=[yb[i]])
                S.dma("sp", npool_s[a, :, 0:14, :], spool[a].rearrange("(i r) d -> i r d", r=15)[:, 1:15, :], writes=[dram_misc_b])
                A("act", lambda e: e.activation(out=xnb[0:NS, :], in_=Y[0:NS, NT, :], func=AF.Copy), r=[yb[i]], w=[xnb_b])
                dp, mp = 0, 2
                P = pair(dp)
                for kc in range(8):
                    g = kc // 2
                    for t_ in range(2):
                        A("pe", lambda e, kc=kc, g=g, t_=t_: e.matmul(
                            P[:, kc * 128:kc * 128 + NS], lhsT=spb3[0:120, t_, kc * 128:(kc + 1) * 128], rhs=sel[0:120, t_, g, :],
                            start=(t_ == 0), stop=False), r=[spb_b, cstb_b], w=[pair_b[dp]])
                    A("pe", lambda e, kc=kc, g=g: e.matmul(
                        P[:, kc * 128:kc * 128 + NS], lhsT=xnb[0:NS, kc * 128:(kc + 1) * 128], rhs=coefI[0:NS, g, :],
                        start=False, stop=True), r=[xnb_b, cstb_b], w=[pair_b[dp]])
                A("act", lambda e: e.activation(out=dT4[:, 0, :, 0:NS], in_=P.rearrange("p (c t) -> p c t", c=8)[:, :, 0:NS],
                                                func=AF.Copy), r=[pair_b[dp]], w=[dT_b[0]])
                Q = pair(mp)
                for g in range(4):
                    for kk in range(2):
                        A("pe", lambda e, g=g, kk=kk: e.matmul(
                            Q[0:NS, g * 256:(g + 1) * 256], lhsT=dT4[:, 0, 2 * g + kk, 0:NS], rhs=wp3[:, 2 * g + kk, :],
                            start=(kk == 0), stop=(kk == 1)), r=[dT_b[0], wp_b], w=[pair_b[mp]])
                ln_mix(ch, i, NS, mp, True, dp)
            ln_flush()

        def ffn_layer(ch, l):
            has_s = (ch == 1)
            ntile = NT + (1 if has_s else 0)
            arena_reset()
            HT, _ = take("HT", [128, 6, TC], BF16)
            HT3 = HT.rearrange("p (j t) -> p j t", j=6)
            ht_b = [[Buf(f"ht{j}_{t}") for t in range(3)] for j in range(6)]
            for row in ht_b:
                for b in row:
                    for ob in ar["old"]:
                        toks = ([ob.w] if ob.w is not None else []) + list(ob.rs.values())
                        for t in toks:
                            key = ("e", t[1]) if t[0] == "e" else ("d", t[1])
                            cur = b.rs.get(key)
                            if cur is None or cur[2] < t[2]:
                                b.rs[key] = t
                    ar["bufs"].append(b)
            wout, wout_b = take("wout", [128, 6, D], BF16, nb=2)
            wout4 = wout.rearrange("p (s j d) -> p s j d", s=2, j=6)
            win, win_b = take("win", [128, 8, 256], BF16, nb=4)
            win4 = win.rearrange("p (s k c) -> p s k c", s=4, k=8)
            gsb, gsb_b2 = take("gsb", [128, 2 + TC], F32, nb=2)
            gsb3 = gsb.rearrange("p (s t) -> p s t", s=2)
            gsb_b = [[Buf(f"gs{s_}_{t}") for t in range(4)] for s_ in range(2)]
            for s_ in range(2):
                for b in gsb_b[s_]:
                    b.rs = dict(gsb_b2[s_].rs)
                    ar["bufs"].append(b)
            cc, cc_b = take("cc", [128, 512], F32, nb=2)
            cc3 = cc.rearrange("p (s t) -> p s t", s=2)
            ss, ss_b = take("ss", [128, 512], F32, nb=2)
            ss3 = ss.rearrange("p (s t) -> p s t", s=2)
            us, us_b = take("us", [128, 512], F32, nb=2)
            us3 = us.rearrange("p (s t) -> p s t", s=2)
            if has_s:
                scT, (scT_b,) = take("scT", [128, NJ, 32], F32)
                scT3 = scT.rearrange("p (j c) -> p j c", j=NJ)
                cs, (cs_b,) = take("cs", [128, DFF], F32)
            load_ln(lfg, lfb, l)
            for bk_ in range(4, 8):
                inherit(bank_b[bk_], [pair_b[bk_ // 2]])
            i_lo = 1 if l >= 2 else 0
            c_lo = 128 * i_lo
            for s_ in range(2):
                A("dve", lambda e, s_=s_: e.memset(gsb3[:, s_, c_lo:c_lo + 2], 0.0), w=[gsb_b[s_][0]])
            if has_s:
                S.dma("sp", cs[0:32, :], sconv[l], writes=[cs_b])
                S.dma("sp", nconv_s[l, :, 0, :], sconv[l].rearrange("(i r) f -> i r f", r=2)[:, 1, :], writes=[dram_misc_b])
                for j0 in range(0, NJ, 8):
                    nj = min(8, NJ - j0)
                    pp = (j0 // 8) % 2
                    for jj in range(nj):
                        A("pe", lambda e, j0=j0, jj=jj, pp=pp: e.transpose(
                            out=pair(pp)[:, jj * 32:(jj + 1) * 32], in_=cs[0:32, (j0 + jj) * 128:(j0 + jj + 1) * 128],
                            identity=ident[0:32, 0:32]), r=[cs_b, ident_b], w=[pair_b[pp]])
                    A("dve", lambda e, j0=j0, nj=nj, pp=pp: e.tensor_copy(
                        out=scT3[:, j0:j0 + nj, :], in_=pair(pp)[:, 0:nj * 32].rearrange("p (j c) -> p j c", j=nj)),
                      r=[pair_b[pp]], w=[scT_b])
            if i_lo == 0:
                TT = [(0, 512), (512, 512), (1024, 256 + (NS if has_s else 0))]
            else:
                TT = [(128, 512), (640, 512), (1152, 128 + (NS if has_s else 0))]
            winv = w_in[l].rearrange("(k p) c -> p k c", p=128)
            slab = 0
            u1 = 0
            accs = {"n": 0}
            pend = {"f": None}
            exn = {"n": 0}
            pend2 = {"f": None}
            for gi, (j0, j1) in enumerate(GROUPS):
                wo = gi % 2
                ng = j1 - j0
                def load_wout(wo=wo, ng=ng, j0=j0, j1=j1):
                    S.dma("pool", wout4[:, wo, 0:ng, :], w_out[l, j0 * 128:j1 * 128, :].rearrange("(j p) d -> p j d", p=128),
                          writes=[wout_b[wo]])
                if gi > 0:
                    load_wout()
                for j in range(j0, j1):
                    s_ = slab % 4; slab += 1
                    S.dma("pool", win4[:, s_, :, 0:128], winv[:, :, j * 128:(j + 1) * 128], writes=[win_b[s_]])
                    S.dma("pool", win4[:, s_, :, 128:256], winv[:, :, DFF + j * 128:DFF + (j + 1) * 128], writes=[win_b[s_]])
                    if gi == 0 and j == j0 + 2:
                        load_wout()
                    gs = j % 2
                    cw = C_CW + (l * NJ + j) * 3
                    cbc = C_CB + l * NJ + j
                    for tt, (t0, n) in enumerate(TT):
                        set_ = u1 % 2; u1 += 1
                        bg, bu = 4 + 2 * set_, 5 + 2 * set_
                        pg, pu = bank(bg), bank(bu)
                        xr = [xtb[ii] for ii in range(t0 // 128, min(NT, (t0 + n + 127) // 128))]
                        if has_s and tt == 2:
                            xr.append(xtb[NT])
                        for k in range(8):
                            A("pe", lambda e, k=k, s_=s_, pg=pg, t0=t0, n=n: e.matmul(
                                pg[:, 0:n], lhsT=win4[:, s_, k, 0:128], rhs=XT[:, k, t0:t0 + n], start=(k == 0), stop=(k == 7)),
                              r=[win_b[s_]] + xr, w=[bank_b[bg]])
                        for k in range(8):
                            A("pe", lambda e, k=k, s_=s_, pu=pu, t0=t0, n=n: e.matmul(
                                pu[:, 0:n], lhsT=win4[:, s_, k, 128:256], rhs=XT[:, k, t0:t0 + n], start=(k == 0), stop=(k == 7)),
                              r=[win_b[s_]] + xr, w=[bank_b[bu]])
                        npr = min(n, TP - t0)
                        cs_ = u1 % 2
                        A("act", lambda e, gs=gs, t0=t0, n=n, pg=pg: e.activation(
                            out=gsb3[:, gs, 2 + t0:2 + t0 + n], in_=pg[:, 0:n], func=AF.Copy),
                          r=[bank_b[bg]], w=[gsb_b[gs][1 + tt]])
                        A("act", lambda e, cs_=cs_, n=n, pg=pg, cw=cw, cbc=cbc: e.activation(
                            out=cc3[:, cs_, 0:n], in_=pg[:, 0:n], func=AF.Identity, bias=cst[:, cbc:cbc + 1],
                            scale=cst[:, cw + 2:cw + 3]), r=[bank_b[bg]] + [cst_b], w=[cc_b[cs_]])
                        grd = [gsb_b[gs][tt], gsb_b[gs][1 + tt]]
                        A("dve", lambda e, cs_=cs_, gs=gs, t0=t0, npr=npr, cw=cw: e.scalar_tensor_tensor(
                            out=cc3[:, cs_, 0:npr], in0=gsb3[:, gs, 1 + t0:1 + t0 + npr], scalar=cst[:, cw + 1:cw + 2],
                            in1=cc3[:, cs_, 0:npr], op0=ALU.mult, op1=ALU.add), r=grd + [cc_b[cs_], cst_b], w=[cc_b[cs_]])
                        A("dve", lambda e, cs_=cs_, gs=gs, t0=t0, npr=npr, cw=cw: e.scalar_tensor_tensor(
                            out=cc3[:, cs_, 0:npr], in0=gsb3[:, gs, t0:t0 + npr], scalar=cst[:, cw:cw + 1],
                            in1=cc3[:, cs_, 0:npr], op0=ALU.mult, op1=ALU.add), r=grd + [cc_b[cs_], cst_b], w=[cc_b[cs_]])
                        if n > npr:
                            A("dve", lambda e, cs_=cs_, j=j, npr=npr, n=n, cw=cw: e.scalar_tensor_tensor(
                                out=cc3[:, cs_, npr:n], in0=scT3[:, j, 1:32:2], scalar=cst[:, cw + 1:cw + 2],
                                in1=cc3[:, cs_, npr:n], op0=ALU.mult, op1=ALU.add), r=[scT_b, cc_b[cs_], cst_b], w=[cc_b[cs_]])
                            A("dve", lambda e, cs_=cs_, j=j, npr=npr, n=n, cw=cw: e.scalar_tensor_tensor(
                                out=cc3[:, cs_, npr:n], in0=scT3[:, j, 0:32:2], scalar=cst[:, cw:cw + 1],
                                in1=cc3[:, cs_, npr:n], op0=ALU.mult, op1=ALU.add), r=[scT_b, cc_b[cs_], cst_b], w=[cc_b[cs_]])
                        A("act", lambda e, cs_=cs_, n=n, pu=pu: e.activation(out=us3[:, cs_, 0:n], in_=pu[:, 0:n], func=AF.Copy),
                          r=[bank_b[bu]], w=[us_b[cs_]])

                        def second(cs_=cs_, n=n, j=j, j0=j0, t0=t0, tt=tt):
                            A("act", lambda e: e.activation(out=ss3[:, cs_, 0:n], in_=cc3[:, cs_, 0:n], func=AF.Silu),
                              r=[cc_b[cs_]], w=[ss_b[cs_]])
                            A("dve", lambda e: e.tensor_tensor(
                                out=HT3[:, j - j0, t0:t0 + n], in0=ss3[:, cs_, 0:n], in1=us3[:, cs_, 0:n], op=ALU.mult),
                              r=[ss_b[cs_], us_b[cs_]], w=[ht_b[j - j0][tt]])
                        if pend2["f"] is not None:
                            pend2["f"]()
                        pend2["f"] = second
                        if tt == 0 and pend["f"] is not None:
                            pend["f"](); pend["f"] = None
                    if has_s:
                        def export(j=j, gs=gs):
                            eb = exn["n"] % 4; exn["n"] += 1
                            A("pe", lambda e: e.transpose(out=bank(eb)[0:18, 0:128], in_=gsb3[:, gs, TP:TP + 18],
                                                          identity=ident[:, :]),
                              r=[gsb_b[gs][3], ident_b], w=bank_deps(eb))
                            A("act", lambda e: e.activation(out=cs[0:18, j * 128:(j + 1) * 128], in_=bank(eb)[0:18, 0:128],
                                                            func=AF.Copy), r=bank_deps(eb), w=[cs_b])
                        pend["f"] = export
                if pend2["f"] is not None:
                    pend2["f"](); pend2["f"] = None
                if pend["f"] is not None:
                    pend["f"](); pend["f"] = None
                last = (gi == len(GROUPS) - 1)
                for i in range(i_lo, ntile):
                    rows = 128 if i < NT else NS
                    c0 = i * 128
                    tt = min((i - i_lo) // 4, 2)
                    ap_ = accs["n"] % 2; accs["n"] += 1
                    P = pair(ap_)
                    for jj in range(ng):
                        for half in range(2):
                            A("pe", lambda e, jj=jj, half=half, rows=rows, c0=c0, P=P, wo=wo, ng=ng: e.matmul(
                                P[0:rows, half * 512:(half + 1) * 512], lhsT=HT3[:, jj, c0:c0 + rows],
                                rhs=wout4[:, wo, jj, half * 512:(half + 1) * 512], start=(jj == 0), stop=(jj == ng - 1)),
                              r=[ht_b[jj][tt], wout_b[wo]], w=[pair_b[ap_]])
                    Yi = Y[0:rows, i, :]
                    if gi == 0:
                        A("dve", lambda e, Yi=Yi, P=P, rows=rows: e.scalar_tensor_tensor(
                            out=Yi, in0=Yi, scalar=ALPHA, in1=P[0:rows, :], op0=ALU.mult, op1=ALU.add),
                          r=[yb[i], pair_b[ap_]], w=[yb[i]])
                    else:
                        A("dve", lambda e, Yi=Yi, P=P, rows=rows: e.tensor_tensor(out=Yi, in0=Yi, in1=P[0:rows, :], op=ALU.add),
                          r=[yb[i], pair_b[ap_]], w=[yb[i]])
                    if last:
                        def rp_fn():
                            v = accs["n"] % 2; accs["n"] += 1
                            return v
                        post = None
                        if l == NL - 1 and i == NT:
                            post = lambda: S.dma("sp", ys_out, Y[0:NS, NT, :], reads=[yb[NT]])
                        elif l == NL - 1 and i >= 2:
                            def post(i=i):
                                r0 = (ch * 8 + i - 2) * 128
                                S.dma("sp", y_out[r0:r0 + 128, :], Y[:, i, :], reads=[yb[i]])
                        ln_core(ch, i, rows, l in (1, 2), rp_fn, post)
            ln_flush()
            inherit(pair_b[2], [bank_b[4], bank_b[5]])
            inherit(pair_b[3], [bank_b[6], bank_b[7]])
            if has_s:
                S.dma("sp", nconv_p[l], cs[0:2, :], reads=[cs_b])
                S.dma("sp", nconv_s[l, :, 1, :], cs[2:18, :], reads=[cs_b])

        def kv_proj(ch):
            has_s = (ch == 1)
            arena_reset()
            wkt, (wkt_b,) = take("wkt", [128, 8, 512], BF16)
            wkt3 = wkt.rearrange("p (k c) -> p k c", k=8)
            wk2, (wk2_b,) = take("wk2", [128, 8, 4, 128], BF16)
            wk24 = wk2.rearrange("p (k h c) -> p k h c", k=8, h=4)
            kvs, (kvs_b,) = take("kvs", [128, 512], F32)
            wv = w_kv.rearrange("(k p) c -> p k c", p=128)
            for kh in range(4):
                for dup in range(2):
                    S.dma("pool", wk24[:, :, kh, dup * 64:(dup + 1) * 64], wv[:, :, kh * 64:(kh + 1) * 64], writes=[wk2_b])
            S.dma("pool", wkt3, wv, writes=[wkt_b])
            u = 0
            for kh in range(4):
                for (t0, n) in [(0, 512), (512, 512), (1024, 256)]:
                    bk = 4 + (u % 4); u += 1
                    xr = [xtb[ii] for ii in range(t0 // 128, (t0 + n) // 128)]
                    for k in range(8):
                        A("pe", lambda e, k=k, kh=kh, bk=bk, t0=t0, n=n: e.matmul(
                            bank(bk)[:, 0:n], lhsT=wk24[:, k, kh, :], rhs=XT[:, k, t0:t0 + n], start=(k == 0), stop=(k == 7)),
                          r=[wk2_b] + xr, w=bank_deps(bk))
                    A("act", lambda e, kh=kh, bk=bk, t0=t0, n=n: e.activation(
                        out=KT2[:, kh, t0:t0 + n], in_=bank(bk)[:, 0:n], func=AF.Identity,
                        bias=cst[:, C_BKT + kh:C_BKT + kh + 1], scale=1.0), r=bank_deps(bk) + [cst_b], w=[kt_b])
            ntile = NT + (1 if (has_s and not (KVDBG & 2)) else 0)
            for i in range(ntile):
                rows = 128 if i < NT else NS
                c0 = i * 128
                bk = i % 4
                for k in range(8):
                    A("pe", lambda e, k=k, bk=bk, rows=rows, c0=c0: e.matmul(
                        bank(bk)[0:rows, :], lhsT=XT[:, k, c0:c0 + rows], rhs=wkt3[:, k, :], start=(k == 0), stop=False),
                      r=[wkt_b, xtb[i]], w=bank_deps(bk))
                A("pe", lambda e, bk=bk, rows=rows: e.matmul(
                    bank(bk)[0:128, :], lhsT=ones33[:, 0:128], rhs=bb[:, 2048:2560], start=False, stop=True),
                  r=[ones_b, bb_b], w=bank_deps(bk))
                if i < NT:
                    A("act", lambda e, i=i, bk=bk: e.activation(out=V[:, i, :], in_=bank(bk)[:, 256:512], func=AF.Copy),
                      r=bank_deps(bk), w=[v_b[i]])
                if ch == 1 and i == NT - 1 and not (KVDBG & 4):
                    A("act", lambda e, bk=bk: e.activation(out=kvo[:], in_=bank(bk)[:, :], func=AF.Copy), r=bank_deps(bk), w=[kvo_b])
                    S.dma("sp", nk_p, kvo[:, 0:256], reads=[kvo_b])
                    S.dma("sp", nv_p, kvo[:, 256:512], reads=[kvo_b])
                if i == NT:
                    A("dve", lambda e, bk=bk: e.tensor_copy(out=kvs[0:NS, :], in_=bank(bk)[0:NS, :]), r=bank_deps(bk), w=[kvs_b])
                    S.dma("sp", nk_s[:, 127, :], kvs[0:NS, 0:256], reads=[kvs_b], writes=[nks_b])
                    S.dma("sp", nv_s[:, 127, :], kvs[0:NS, 256:512], reads=[kvs_b], writes=[nvs_b])
                    for (a0, a1) in ([] if (KVDBG & 1) else [(1, 33), (33, 65), (65, 97), (97, 128)]):
                        S.dma("sp", nk_s[:, a0 - 1:a1 - 1, :], skw[:, a0:a1, :], writes=[nks_b])
                        S.dma("sp", nv_s[:, a0 - 1:a1 - 1, :], svw[:, a0:a1, :], writes=[nvs_b])

        def attn_layer(ch, l):
            bi = l - 2
            has_s = (ch == 1)
            arena_reset()
            wq, (wq_b,) = take("wq", [128, 8, D], BF16)
            wq_at = ar["last_off"]
            wq3 = wq.rearrange("p (k c) -> p k c", k=8)
            wo_, (wo_b,) = take("wo", [128, 8, D], BF16)
            wo3 = wo_.rearrange("p (k c) -> p k c", k=8)
            qT, (qT_b,) = take("qTa", [128, 8, TP], BF16)
            qT3 = qT.rearrange("p (m t) -> p m t", m=8)
            en, en_b = take("en", [128, 4, 256], BF16, nb=2)
            en4 = en.rearrange("p (d h s) -> p d h s", d=2, h=4)
            ssb, ssb_b = take("ssb", [128, 4, 256], F32, nb=2)
            ssb4 = ssb.rearrange("p (d h s) -> p d h s", d=2, h=4)
            ee, ee_b = take("ee", [128, 4, 256], F32, nb=2)
            ee4 = ee.rearrange("p (d h s) -> p d h s", d=2, h=4)
            PT, PT_b = take("PT", [128, 4, 2, 128], BF16, nb=2)
            PT5 = PT.rearrange("p (d h f q) -> p d h f q", d=2, h=4, f=2)
            oT, (oT_b,) = take("oT", [128, 8, 128], BF16)
            oT3 = oT.rearrange("p (m t) -> p m t", m=8)
            S.dma("pool", wq3, w_q[bi].rearrange("(k p) c -> p k c", p=128), writes=[wq_b])
            S.dma("pool", wo3, w_o[bi].rearrange("(k p) c -> p k c", p=128), writes=[wo_b])
            load_ln(lmg, lmb, l)
            sinkb = cst[:, C_SINKB + bi * 16:C_SINKB + bi * 16 + 16]
            nsinkb = cst[:, C_NSINKB + bi * 16:C_NSINKB + bi * 16 + 16]
            uq = 0
            for m in range(8):
                for (t0, n) in [(128, 512), (640, 512), (1152, 128)]:
                    bk = 4 + (uq % 4); uq += 1
                    xr = [xtb[ii] for ii in range(t0 // 128, (t0 + n) // 128)]
                    for k in range(8):
                        A("pe", lambda e, k=k, m=m, bk=bk, t0=t0, n=n: e.matmul(
                            bank(bk)[:, 0:n], lhsT=wq3[:, k, m * 128:(m + 1) * 128], rhs=XT[:, k, t0:t0 + n],
                            start=(k == 0), stop=(k == 7)), r=[wq_b] + xr, w=bank_deps(bk))
                    cq = C_BQT + bi * 8 + m
                    if uq % 2 == 0:
                        A("act", lambda e, m=m, bk=bk, t0=t0, n=n, cq=cq: e.activation(
                            out=qT3[:, m, t0:t0 + n], in_=bank(bk)[:, 0:n], func=AF.Identity, bias=cst[:, cq:cq + 1], scale=1.0),
                          r=bank_deps(bk) + [cst_b], w=[qT_b])
                    else:
                        A("dve", lambda e, m=m, bk=bk, t0=t0, n=n, cq=cq: e.tensor_scalar(
                            out=qT3[:, m, t0:t0 + n], in0=bank(bk)[:, 0:n], scalar1=cst[:, cq:cq + 1], scalar2=None, op0=ALU.add),
                          r=bank_deps(bk) + [cst_b], w=[qT_b])

            def out_proj(rows, osrc, mp):
                MP = pair(mp)
                for half in range(2):
                    for m in range(8):
                        A("pe", lambda e, half=half, m=m: e.matmul(
                            MP[0:rows, half * 512:(half + 1) * 512], lhsT=osrc(m), rhs=wo3[:, m, half * 512:(half + 1) * 512],
                            start=(m == 0), stop=False), r=[oT_b, wo_b], w=[pair_b[mp]])
                    A("pe", lambda e, half=half: e.matmul(
                        MP[0:rows, half * 512:(half + 1) * 512], lhsT=ones33[:, 0:rows],
                        rhs=bb[:, bi * 1024 + half * 512:bi * 1024 + (half + 1) * 512], start=False, stop=True),
                      r=[ones_b, bb_b], w=[pair_b[mp]])

            def make_tile(i):
                OP = pair(1)
                v_ = 0 if i == 1 else (1 if i == 2 else 2)
                am0 = C_AM + (ch * 3 + v_) * 256
                SPp = pair(2)
                TPb = pair(3).bitcast(BF16)

                def scores(kh, i=i):
                    for sl4 in range(4):
                        hh = PERM[sl4]
                        h = 4 * kh + hh; m = h // 2; po = 64 * (h % 2)
                        A("pe", lambda e, hh=sl4, m=m, po=po, kh=kh, i=i: e.matmul(
                            SPp[:, hh * 256:(hh + 1) * 256], lhsT=qT3[po:po + 64, m, i * 128:(i + 1) * 128],
                            rhs=KT2[po:po + 64, kh, (i - 1) * 128:(i + 1) * 128], start=True, stop=True),
                          r=[qT_b, kt_b], w=[pair_b[2]])

                def c1(kh, am0=am0):
                    d = kh % 2
                    s_ = cnt["sa"] % 4; cnt["sa"] += 1
                    t = sa[:, s_, :]
                    sab = sa_b[s_]
                    sS = ssb4[:, d]; eE = ee4[:, d]
                    A("dve", lambda e: e.tensor_tensor(
                        out=sS, in0=SPp.rearrange("p (h s) -> p h s", h=4),
                        in1=cst[:, am0:am0 + 256].unsqueeze(1).broadcast_to([128, 4, 256]), op=ALU.add),
                      r=[pair_b[2], cst_b], w=[ssb_b[d]])
                    A("dve", lambda e: e.tensor_reduce(out=t[:, 0:4], in_=sS, axis=AX.X, op=ALU.max), r=[ssb_b[d]], w=[sab])
                    A("dve", lambda e: e.scalar_tensor_tensor(
                        out=t[:, 8:12], in0=t[:, 0:4], scalar=-SCALE, in1=nsinkb[:, 4 * kh:4 * kh + 4], op0=ALU.mult, op1=ALU.min),
                      r=[sab, cst_b], w=[sab])
                    A("dve", lambda e: e.tensor_tensor(out=t[:, 16:20], in0=sinkb[:, 4 * kh:4 * kh + 4], in1=t[:, 8:12],
                                                       op=ALU.add), r=[sab, cst_b], w=[sab])
                    for hh in range(4):
                        A("act", lambda e, hh=hh: e.activation(
                            out=eE[:, hh, :], in_=sS[:, hh, :], func=AF.Exp, bias=t[:, 8 + hh:9 + hh], scale=SCALE,
                            accum_out=t[:, 12 + hh:13 + hh]), r=[ssb_b[d], sab], w=[ee_b[d], sab])
                    A("act", lambda e: e.activation(out=t[:, 20:24], in_=t[:, 16:20], func=AF.Exp), r=[sab], w=[sab])
                    return (t, sab)

                def c2(kh, ts):
                    d = kh % 2
                    t, sab = ts
                    eE = ee4[:, d]; eN = en4[:, d]
                    A("dve", lambda e: e.tensor_tensor(out=t[:, 24:28], in0=t[:, 12:16], in1=t[:, 20:24], op=ALU.add),
                      r=[sab], w=[sab])
                    A("dve", lambda e: e.reciprocal(out=t[:, 28:32], in_=t[:, 24:28]), r=[sab], w=[sab])
                    A("dve", lambda e: e.tensor_tensor(
                        out=eN, in0=eE, in1=t[:, 28:32].unsqueeze(2).broadcast_to([128, 4, 256]), op=ALU.mult),
                      r=[ee_b[d], sab], w=[en_b[d]])

                def tp(kh, i=i):
                    d = kh % 2
                    eN = en4[:, d]
                    for hh in range(4):
                        for half in range(2):
                            A("pe", lambda e, hh=hh, half=half: e.transpose(
                                out=TPb[:, (hh * 2 + half) * 128:(hh * 2 + half + 1) * 128],
                                in_=eN[:, hh, half * 128:(half + 1) * 128], identity=identb[:, :]),
                              r=[en_b[d], identb_b], w=[pair_b[3]])
                    A("act", lambda e: e.activation(out=PT5[:, d].rearrange("p h f q -> p (h f q)"), in_=TPb[:, 0:1024], func=AF.Copy),
                      r=[pair_b[3]], w=[PT_b[d]])
                    for sl4 in range(4):
                        hh = PERM[sl4]
                        h = 4 * kh + hh; m = h // 2; po = 64 * (h % 2)
                        for half in range(2):
                            A("pe", lambda e, hh=sl4, half=half, m=m, po=po, kh=kh, i=i: e.matmul(
                                OP[po:po + 64, m * 128:(m + 1) * 128], lhsT=V[:, i - 1 + half, kh * 64:(kh + 1) * 64],
                                rhs=PT5[:, d, hh, half, :], start=(half == 0), stop=(half == 1)),
                              r=[PT_b[d], v_b[i - 1], v_b[i]], w=[pair_b[1]])

                def otcopy():
                    A("act", lambda e: e.activation(out=oT, in_=OP, func=AF.Copy), r=[pair_b[1]], w=[oT_b])

                def outproj():
                    out_proj(128, lambda m: oT3[:, m, :], 0)

                def lnpart():
                    ln_mix(ch, i, 128, 0, True, 3)

                return dict(scores=scores, c1=c1, c2=c2, tp=tp, otcopy=otcopy, outproj=outproj, lnpart=lnpart, st={})

            tls = [make_tile(i) for i in range(1, NT)]
            T0 = tls[0]
            T0["scores"](0); T0["st"][0] = T0["c1"](0)
            T0["scores"](1); T0["st"][1] = T0["c1"](1)
            prev = None
            for ti, T_ in enumerate(tls):
                N_ = tls[ti + 1] if ti + 1 < len(tls) else None
                st_ = T_["st"]
                T_["scores"](2)
                if prev is not None:
                    prev["outproj"]()
                T_["c2"](0, st_[0]); T_["tp"](0)
                st_[2] = T_["c1"](2)
                if prev is not None:
                    prev["lnpart"]()
                T_["scores"](3)
                T_["c2"](1, st_[1]); T_["tp"](1)
                st_[3] = T_["c1"](3)
                if N_ is not None:
                    N_["scores"](0)
                T_["c2"](2, st_[2]); T_["tp"](2)
                if N_ is not None:
                    N_["st"][0] = N_["c1"](0)
                    N_["scores"](1)
                T_["c2"](3, st_[3]); T_["tp"](3)
                T_["otcopy"]()
                if N_ is not None:
                    N_["st"][1] = N_["c1"](1)
                prev = T_
            prev["outproj"]()
            prev["lnpart"]()

            if has_s:
                i = NT
                Ksb, (Ksb_b,) = take("Ksb", [128, NS, 256], BF16)
                Ksb3 = Ksb.rearrange("p (i d) -> p i d", i=NS)
                Vsb, (Vsb_b,) = take("Vsb", [128, NS, 256], BF16)
                Vsb3 = Vsb.rearrange("p (i d) -> p i d", i=NS)
                KsT, (KsT_b,) = take("KsT", [128, NS, 4, 128], BF16, at=wq_at, after=[wq_b])
                KsT4 = KsT.rearrange("p (i h s) -> p i h s", i=NS, h=4)
                qsT, (qsT_b,) = take("qsT", [128, 16, NS], BF16)
                qsT3 = qsT.rearrange("p (h i) -> p h i", h=16)
                STs, (STs_b,) = take("STs", [128, 256], F32)
                es, (es_b,) = take("es", [128, 2, 128], F32)
                es3 = es.rearrange("p (f s) -> p f s", f=2)
                PTs, (PTs_b,) = take("PTs", [128, 256], BF16)
                osT, (osT_b,) = take("osT", [128, 8, NS], BF16)
                osT3 = osT.rearrange("p (m i) -> p m i", m=8)
                S.dma("pool", Ksb3, nk_s.rearrange("i s d -> s i d"), reads=[nks_b], writes=[Ksb_b])
                S.dma("pool", Vsb3, nv_s.rearrange("i s d -> s i d"), reads=[nvs_b], writes=[Vsb_b])
                QS = pair(0)
                for h in range(16):
                    for k in range(8):
                        A("pe", lambda e, h=h, k=k: e.matmul(
                            QS[0:64, h * 16:(h + 1) * 16], lhsT=wq3[:, k, h * 64:(h + 1) * 64], rhs=XT[:, k, TP:TP + NS],
                            start=(k == 0), stop=(k == 7)), r=[wq_b, xtb[NT]], w=[pair_b[0]])
                c0 = C_BQH + bi * 16
                A("dve", lambda e, c0=c0: e.tensor_tensor(
                    out=qsT3[0:64, :, :], in0=QS[0:64, 0:256].rearrange("p (h i) -> p h i", h=16),
                    in1=cst[0:64, c0:c0 + 16].unsqueeze(2).broadcast_to([64, 16, NS]), op=ALU.add),
                  r=[pair_b[0], cst_b], w=[qsT_b])
                for i0 in range(0, NS, 4):
                    pp = 2 + (i0 // 4) % 2
                    Pb = pair(pp).bitcast(BF16)
                    for ii in range(4):
                        for kh in range(4):
                            A("pe", lambda e, ii=ii, kh=kh, i0=i0, Pb=Pb: e.transpose(
                                out=Pb[0:64, (ii * 4 + kh) * 128:(ii * 4 + kh + 1) * 128],
                                in_=Ksb3[:, i0 + ii, kh * 64:(kh + 1) * 64], identity=identb[:, :]),
                              r=[Ksb_b, identb_b], w=[pair_b[pp]])
                    A("act", lambda e, i0=i0, Pb=Pb: e.activation(
                        out=KsT4[0:64, i0:i0 + 4, :, :], in_=Pb[0:64, 0:2048].rearrange("p (i h s) -> p i h s", i=4, h=4),
                        func=AF.Copy), r=[pair_b[pp]], w=[KsT_b])
                ST = pair(1)
                for ii in range(NS):
                    for kh in range(4):
                        A("pe", lambda e, ii=ii, kh=kh: e.matmul(
                            ST[:, ii * 16 + 4 * kh:ii * 16 + 4 * kh + 4], lhsT=KsT4[0:64, ii, kh, :],
                            rhs=qsT3[0:64, 4 * kh:4 * kh + 4, ii], start=True, stop=True),
                          r=[KsT_b, qsT_b], w=[pair_b[1]])
                A("dve", lambda e: e.tensor_copy(out=STs, in_=ST[:, 0:256]), r=[pair_b[1]], w=[STs_b])
                S2 = pair(2)
                for hf in range(2):
                    A("pe", lambda e, hf=hf: e.transpose(out=S2[:, hf * 128:(hf + 1) * 128], in_=STs[:, hf * 128:(hf + 1) * 128],
                                                         identity=ident[:, :]), r=[STs_b, ident_b], w=[pair_b[2]])
                s_ = cnt["sa"] % 4; cnt["sa"] += 1
                t = sa[:, s_, :]
                sab = sa_b[s_]
                sk = cst[:, C_SINKS + bi * 2:C_SINKS + bi * 2 + 2]
                A("dve", lambda e: e.tensor_reduce(out=t[:, 0:2], in_=S2[:, 0:256].rearrange("p (f s) -> p f s", f=2),
                                                   axis=AX.X, op=ALU.max), r=[pair_b[2]], w=[sab])
                A("dve", lambda e: e.scalar_tensor_tensor(out=t[:, 4:6], in0=t[:, 0:2], scalar=SCALE, in1=sk,
                                                          op0=ALU.mult, op1=ALU.max), r=[sab, cst_b], w=[sab])
                A("dve", lambda e: e.tensor_scalar(out=t[:, 8:10], in0=t[:, 4:6], scalar1=-1.0, scalar2=None, op0=ALU.mult),
                  r=[sab], w=[sab])
                for hf in range(2):
                    A("act", lambda e, hf=hf: e.activation(
                        out=es3[:, hf, :], in_=S2[:, hf * 128:(hf + 1) * 128], func=AF.Exp, bias=t[:, 8 + hf:9 + hf], scale=SCALE,
                        accum_out=t[:, 12 + hf:13 + hf]), r=[pair_b[2], sab], w=[es_b, sab])
                A("dve", lambda e: e.tensor_tensor(out=t[:, 16:18], in0=sk, in1=t[:, 4:6], op=ALU.subtract), r=[sab, cst_b], w=[sab])
                A("act", lambda e: e.activation(out=t[:, 20:22], in_=t[:, 16:18], func=AF.Exp), r=[sab], w=[sab])
                A("dve", lambda e: e.tensor_tensor(out=t[:, 24:26], in0=t[:, 12:14], in1=t[:, 20:22], op=ALU.add), r=[sab], w=[sab])
                A("dve", lambda e: e.reciprocal(out=t[:, 28:30], in_=t[:, 24:26]), r=[sab], w=[sab])
                A("dve", lambda e: e.tensor_tensor(out=es3, in0=es3, in1=t[:, 28:30].unsqueeze(2).broadcast_to([128, 2, 128]),
                                                   op=ALU.mult), r=[es_b, sab], w=[es_b])
                P2 = pair(3)
                for hf in range(2):
                    A("pe", lambda e, hf=hf: e.transpose(out=P2[:, hf * 128:(hf + 1) * 128], in_=es3[:, hf, :], identity=ident[:, :]),
                      r=[es_b, ident_b], w=[pair_b[3]])
                A("act", lambda e: e.activation(out=PTs, in_=P2[:, 0:256], func=AF.Copy), r=[pair_b[3]], w=[PTs_b])
                OS = pair(1)
                OS3 = OS[:, 0:128].rearrange("p (m i) -> p m i", m=8)
                for ii in range(NS):
                    for kh in range(4):
                        for par in range(2):
                            A("pe", lambda e, ii=ii, kh=kh, par=par: e.matmul(
                                OS3[64 * par:64 * par + 64, 2 * kh:2 * kh + 2, ii], lhsT=Vsb3[:, ii, kh * 64:(kh + 1) * 64],
                                rhs=PTs[:, ii * 16 + 4 * kh + par:ii * 16 + 4 * kh + 4:2], start=True, stop=True),
                              r=[Vsb_b, PTs_b, STs_b], w=[pair_b[1]])
                A("act", lambda e: e.activation(out=osT, in_=OS[:, 0:128], func=AF.Copy), r=[pair_b[1]], w=[osT_b])
                oT_b_save = oT_b
                MP = pair(0)
                for half in range(2):
                    for m in range(8):
                        A("pe", lambda e, half=half, m=m: e.matmul(
                            MP[0:NS, half * 512:(half + 1) * 512], lhsT=osT3[:, m, :], rhs=wo3[:, m, half * 512:(half + 1) * 512],
                            start=(m == 0), stop=False), r=[osT_b, wo_b], w=[pair_b[0]])
                    A("pe", lambda e, half=half: e.matmul(
                        MP[0:128, half * 512:(half + 1) * 512], lhsT=ones33[:, 0:128],
                        rhs=bb[:, bi * 1024 + half * 512:bi * 1024 + (half + 1) * 512], start=False, stop=True),
                      r=[ones_b, bb_b], w=[pair_b[0]])
                ln_mix(ch, i, NS, 0, True, 3)
            ln_flush()

        stage = {"n": 0}

        def go():
            stage["n"] += 1
            return stop is None or stage["n"] <= stop

        for ch in range(2):
            if not go():
                break
            S.dma("sp", Y[:, 0:NT, :], xin[ch * TP:(ch + 1) * TP, :].rearrange("(t p) d -> p t d", p=128), writes=yb[0:NT])
            if ch == 1:
                S.dma("sp", Y[0:NS, NT, :], xs, writes=[yb[NT]])
            for l in range(NL):
                if go():
                    if l < 2:
                        pool_layer(ch, l)
                    else:
                        attn_layer(ch, l)
                if go():
                    ffn_layer(ch, l)
                if l == 1 and go():
                    kv_proj(ch)
        fin = yb + [kvo_b, nks_b, nvs_b, dram_misc_b] + ar["bufs"]
        if dbg:
            dby_b = Buf("dbgyb")
            S.dma("sp", dbgy.rearrange("t p d -> p t d"), Y[:, :, :], reads=yb, writes=[dby_b])
            S.dma("pool", dbgx, XT[:, :, :].rearrange("p k t -> p (k t)"), reads=xtb, writes=[dby_b])
            S.dma("pool", dbgk, KT2[:, :, :].rearrange("p k t -> p (k t)"), reads=[kt_b], writes=[dby_b])
            S.dma("pool", dbgv, V[:, :, :].rearrange("p k t -> p (k t)"), reads=v_b, writes=[dby_b])
            fin = fin + [dby_b]
        S.finalize(st, fin)
    return nc, S


POOL_WINDOWS = (2, 4, 8, 16)


def _consts_for_core(c, inp):
    qd = c % 4
    f32 = np.float32
    cst = np.zeros((128, NCST), f32)
    cw = np.asarray(inp["ffn_conv_w"], f32)
    cbv = np.asarray(inp["ffn_conv_b"], f32)
    cst[:, C_CW:C_CW + 264] = cw.reshape(NL, 3, NJ, 128).transpose(3, 0, 2, 1).reshape(128, 264)
    cst[:, C_CB:C_CB + 88] = cbv.reshape(NL, NJ, 128).transpose(2, 0, 1).reshape(128, 88)
    bq = np.asarray(inp["attn_b_q"], f32)
    cst[:, C_BQT:C_BQT + 16] = bq.reshape(2, 8, 128).transpose(2, 0, 1).reshape(128, 16)
    cst[0:64, C_BQH:C_BQH + 32] = bq.reshape(2, 16, 64).transpose(2, 0, 1).reshape(64, 32)
    bkv = np.asarray(inp["b_kv"], f32)
    bk = bkv[:256].reshape(4, 64)
    cst[:, C_BKT:C_BKT + 4] = np.concatenate([bk.T, bk.T], axis=0)
    sinks = np.asarray(inp["attn_sinks"], f32)
    sperm = sinks.reshape(2, 4, 4)[:, :, PERM].reshape(1, 32)
    cst[:, C_SINKB:C_SINKB + 32] = np.broadcast_to(sperm, (128, 32))
    cst[:, C_NSINKB:C_NSINKB + 32] = np.broadcast_to(-sperm, (128, 32))
    pidx = np.arange(128)
    for bi in range(2):
        for hf in range(2):
            cst[:, C_SINKS + bi * 2 + hf] = sinks[bi, pidx % 16]

    def real(ch, t, r):
        blk = 16 * qd + 8 * ch - 1 + t
        return (blk * 128 + r) >= 112

    r = np.arange(128)
    for ch in range(2):
        for i in range(2):
            cst[:, C_TM + ch * 2 + i] = real(ch, i, r).astype(f32)
    q = np.arange(128)[:, None]
    j = np.arange(256)[None, :]
    band = (q < j) & (j <= q + 128)
    for ch in range(2):
        for v in range(3):
            if v == 2:
                ok = band
            else:
                ti = 1 + v
                kr = np.where(j < 128, real(ch, ti - 1, j % 128), real(ch, ti, j % 128))
                ok = band & kr
            cst[:, C_AM + (ch * 3 + v) * 256:C_AM + (ch * 3 + v + 1) * 256] = np.where(ok, 0.0, NEG).astype(f32)

    cstb = np.zeros((128, NCSTB), f32)
    s = np.arange(128)[:, None]
    t = np.arange(128)[None, :]
    bc = np.zeros((128, 2, 4, 128), f32)
    bp = np.zeros((128, 4, 128), f32)
    for g, w in enumerate(POOL_WINDOWS):
        inwin = (s > t - w) & (s <= t)
        gen = inwin.astype(f32) / w - (s == t).astype(f32)
        bc[:, 1, g, :] = gen
        if qd == 0:
            tseq = t - 112
            cnt = np.where(tseq >= 0, np.minimum(w, tseq + 1), w).astype(f32)
            bc[:, 0, g, :] = inwin.astype(f32) / cnt - (s == t).astype(f32)
        else:
            bc[:, 0, g, :] = gen
        bp[:, g, :] = ((s > 128 + t - w).astype(f32)) / w
    cstb[:, B_BC:B_BC + 1024] = bc.reshape(128, 1024)
    cstb[:, B_BP:B_BP + 512] = bp.reshape(128, 512)
    sel = np.zeros((128, 2, 4, 16), f32)
    ci = np.zeros((128, 4, 16), f32)
    for g, w in enumerate(POOL_WINDOWS):
        for p in range(120):
            rr = p % 15
            if rr >= 16 - w:
                for t_ in range(2):
                    sel[p, t_, g, t_ * 8 + p // 15] = 1.0 / w
        for p in range(16):
            ci[p, g, p] = 1.0 / w - 1.0
    cstb[:, B_SEL:B_SEL + 128] = sel.reshape(128, 128)
    cstb[:, B_CI:B_CI + 64] = ci.reshape(128, 64)
    return cst, cstb


_NC_CACHE = {}


def kernel(**inputs):
    f32 = np.float32
    inp = {k: np.asarray(v) for k, v in inputs.items()}
    xp = inp["x_prompt"].astype(f32, copy=False)
    meta = inp["meta_tokens"].astype(f32, copy=False)
    n = 8
    if "nc" not in _NC_CACHE:
        _NC_CACHE["nc"] = build_nc()[0]
    nc = _NC_CACHE["nc"]
    brow = np.concatenate([inp["attn_b_o"][0], inp["attn_b_o"][1], inp["b_kv"]]).astype(f32).reshape(1, 2560)
    shared = {
        "pool_w": np.ascontiguousarray(inp["pool_w"], f32), "pool_scale": np.ascontiguousarray(inp["pool_scale"], f32),
        "w_kv": np.ascontiguousarray(inp["w_kv"], f32), "w_q": np.ascontiguousarray(inp["attn_w_q"], f32),
        "w_o": np.ascontiguousarray(inp["attn_w_o"], f32), "w_in": np.ascontiguousarray(inp["ffn_w_in"], f32),
        "w_out": np.ascontiguousarray(inp["ffn_w_out"], f32),
        "lmg": np.ascontiguousarray(inp["ln_mix_g"], f32), "lmb": np.ascontiguousarray(inp["ln_mix_b"], f32),
        "lfg": np.ascontiguousarray(inp["ln_ffn_g"], f32), "lfb": np.ascontiguousarray(inp["ln_ffn_b"], f32),
        "brow": brow,
    }
    in_maps = []
    for c in range(n):
        b, qd = c // 4, c % 4
        xin = np.zeros((2, NT, 128, D), f32)
        for ch in range(2):
            for i in range(NT):
                blk = 16 * qd + 8 * ch - 1 + i
                if blk >= 1:
                    xin[ch, i] = xp[b, (blk - 1) * 128:blk * 128]
                elif blk == 0:
                    xin[ch, i, 112:128] = meta
        cst, cstb = _consts_for_core(c, inp)
        sl = slice(NS * c, NS * (c + 1))
        m = dict(shared)
        m.update({
            "xin": xin.reshape(2 * NT * 128, D),
            "xs": np.ascontiguousarray(inp["x_sample"][sl, 0, :], f32),
            "spool": np.ascontiguousarray(inp["state_pool"][:, sl], f32).reshape(2, 240, D),
            "sconv": np.ascontiguousarray(inp["state_conv"][:, sl], f32).reshape(NL, 32, DFF),
            "skw": np.ascontiguousarray(inp["state_k_win"][sl], f32).reshape(NS, 128, 256),
            "svw": np.ascontiguousarray(inp["state_v_win"][sl], f32).reshape(NS, 128, 256),
            "cst": cst, "cstb": cstb,
        })
        in_maps.append(m)
    res = run_bass_kernel_spmd(nc, in_maps, core_ids=list(range(n)))
    R = res.results
    y_prompt = np.zeros((2, 8192, D), f32)
    y_sample = np.zeros((128, 1, D), f32)
    npp = np.zeros((2, 2, 15, D), f32); nps = np.zeros((2, 128, 15, D), f32)
    ncp = np.zeros((NL, 2, 2, DFF), f32); ncs = np.zeros((NL, 128, 2, DFF), f32)
    nkp = np.zeros((2, 128, 4, 64), f32); nvp = np.zeros((2, 128, 4, 64), f32)
    nks = np.zeros((128, 128, 4, 64), f32); nvs = np.zeros((128, 128, 4, 64), f32)
    for c in range(n):
        b, qd = c // 4, c % 4
        r = R[c]
        sl = slice(NS * c, NS * (c + 1))
        y_prompt[b, 2048 * qd:2048 * (qd + 1)] = r["y_out"]
        y_sample[sl, 0] = r["ys_out"]
        nps[:, sl] = r["npool_s"]
        ncs[:, sl] = r["nconv_s"]
        nks[sl] = r["nk_s"].reshape(NS, 128, 4, 64)
        nvs[sl] = r["nv_s"].reshape(NS, 128, 4, 64)
        if qd == 3:
            npp[:, b] = r["npool_p"]
            ncp[:, b] = r["nconv_p"]
            nkp[b] = r["nk_p"].reshape(128, 4, 64)
            nvp[b] = r["nv_p"].reshape(128, 4, 64)
    return (y_prompt, y_sample, npp, nps, ncp, ncs, nkp, nvp, nks, nvs)
```

```python
import contextlib
import numpy as np
import concourse.bass as bass
import concourse.mybir as mybir
from concourse.bass_utils import run_bass_kernel_spmd

F32 = mybir.dt.float32
BF16 = mybir.dt.bfloat16
AF = mybir.ActivationFunctionType
ALU = mybir.AluOpType
AX = mybir.AxisListType


class Buf:
    __slots__ = ("name", "w", "rs", "dsem", "dcnt", "slot")

    def __init__(self, name):
        self.name = name
        self.w = None
        self.rs = {}
        self.dsem = None
        self.dcnt = 0
        self.slot = self


class Sched:
    ENG = ["pe", "act", "dve", "pool", "sp"]

    def __init__(self, nc):
        self.nc = nc
        self.ops = {e: [] for e in self.ENG}
        self.waited = {e: {} for e in self.ENG}
        self.dbufs = []

    def _deps(self, eng, reads, writes):
        best = {}
        idx = len(self.ops[eng])

        def add(tok):
            if tok is None:
                return
            if tok[0] == "e":
                _, pe, pidx = tok
                if pe == eng and eng == "pe":
                    return
                key = ("e", pe)
                v = pidx
            else:
                _, b, v = tok
                key = ("d", b)
            if best.get(key, -1) < v:
                best[key] = v

        for b in reads:
            add(b.w)
        for b in writes:
            add(b.w)
            for t in b.rs.values():
                add(t)
        waits = []
        for key, v in best.items():
            if self.waited[eng].get(key, -1) >= v:
                continue
            self.waited[eng][key] = v
            waits.append((key, v))
        return waits

    def _commit(self, tok, reads, writes):
        for b in writes:
            b.w = tok
            b.rs = {}
        for b in reads:
            if b in writes:
                continue
            if tok[0] == "e":
                b.rs[("e", tok[1])] = tok
            else:
                b.rs[("d", tok[1])] = tok

    def op(self, eng, fn, reads=(), writes=()):
        waits = self._deps(eng, reads, writes)
        idx = len(self.ops[eng])
        self.ops[eng].append(dict(fn=fn, waits=waits, sig=False, dma=None))
        tok = ("e", eng, idx)
        self._commit(tok, reads, writes)
        return tok

    def dma(self, eng, out, in_, reads=(), writes=(), **kw):
        waits = self._deps(eng, reads, writes)
        pb = (writes[0] if writes else reads[0]).slot
        if pb.dsem is None:
            pb.dsem = True
            self.dbufs.append(pb)
        pb.dcnt += 16
        self.ops[eng].append(dict(
            fn=lambda e: e.dma_start(out=out, in_=in_, **kw), waits=waits, sig=False, dma=pb))
        tok = ("d", pb, pb.dcnt)
        self._commit(tok, reads, writes)
        return tok

    def finalize(self, stack, final_bufs=()):
        nc = self.nc
        waits = self._deps("sp", list(final_bufs), list(final_bufs))
        self.ops["sp"].append(dict(fn=None, waits=waits, sig=False, dma=None))
        for e in self.ENG:
            for rec in self.ops[e]:
                for key, v in rec["waits"]:
                    if key[0] == "e":
                        self.ops[key[1]][v]["sig"] = True
        cum = {}
        for e in self.ENG:
            c = 0
            arr = []
            for rec in self.ops[e]:
                if rec["sig"]:
                    c += 1
                arr.append(c)
            cum[e] = arr
        esem = {e: stack.enter_context(nc.semaphore("s_" + e)) for e in self.ENG}
        for b in self.dbufs:
            b.dsem = stack.enter_context(nc.semaphore("d_" + b.name))
        engobj = {"pe": "tensor", "act": "scalar", "dve": "vector", "pool": "gpsimd", "sp": "sync"}

        def emit(name, e):
            for rec in self.ops[name]:
                for key, v in rec["waits"]:
                    if key[0] == "e":
                        e.wait_ge(esem[key[1]], cum[key[1]][v])
                    else:
                        e.wait_ge(key[1].dsem, v)
                if rec["fn"] is None:
                    continue
                ins = rec["fn"](e)
                if rec["dma"] is not None:
                    ins.then_inc(rec["dma"].dsem, 16)
                elif rec["sig"]:
                    ins.then_inc(esem[name], 1)

        block = stack.enter_context(nc.Block())
        for name in self.ENG:
            getattr(block, engobj[name])(lambda e, name=name: emit(name, e))
        self.stats = {e: len(self.ops[e]) for e in self.ENG}
        self.nsem = 5 + len(self.dbufs)

D = 1024; DFF = 2816; NJ = 22; NL = 4; NT = 10; TP = NT * 128; NS = 16; TC = TP + NS
ALPHA = (2.0 * 4) ** 0.25; EPS = 1e-5; SCALE = 0.125; NEG = -1e30
GROUPS = [(0, 6), (6, 12), (12, 17), (17, 22)]
C_CW = 0; C_CB = 264; C_BQT = 352; C_BQH = 368; C_BKT = 400; C_SINKB = 404; C_SINKS = 436; C_TM = 440; C_AM = 444; C_NSINKB = 1980; NCST = 2012
B_BC = 0; B_BP = 1024; B_SEL = 1536; B_CI = 1664; NCSTB = 1728
U8 = mybir.dt.uint8
PERM = [0, 2, 1, 3]
import os
KVDBG = int(os.environ.get('KVDBG', '0'))
ARENA_BYTES = 100 * 1024


def build_nc(stop=None, dbg=False):
    nc = bass.Bass("TRN2", target_bir_lowering=False)

    def din(name, shape):
        return nc.dram_tensor(name, list(shape), F32, kind="ExternalInput").ap()

    def dout(name, shape):
        return nc.dram_tensor(name, list(shape), F32, kind="ExternalOutput").ap()

    xin = din("xin", [2 * NT * 128, D]); xs = din("xs", [NS, D])
    spool = din("spool", [2, 240, D]); sconv = din("sconv", [NL, 32, DFF])
    skw = din("skw", [NS, 128, 256]); svw = din("svw", [NS, 128, 256])
    pool_w = din("pool_w", [2, 4, 256, 256]); pool_scale = din("pool_scale", [2, D])
    w_kv = din("w_kv", [D, 512])
    w_q = din("w_q", [2, D, D]); w_o = din("w_o", [2, D, D])
    w_in = din("w_in", [NL, D, 2 * DFF]); w_out = din("w_out", [NL, DFF, D])
    lmg = din("lmg", [NL, D]); lmb = din("lmb", [NL, D]); lfg = din("lfg", [NL, D]); lfb = din("lfb", [NL, D])
    cst_d = din("cst", [128, NCST]); cstb_d = din("cstb", [128, NCSTB]); brow_d = din("brow", [1, 2560])

    y_out = dout("y_out", [2 * 8 * 128, D]); ys_out = dout("ys_out", [NS, D])
    npool_p = dout("npool_p", [2, 15, D]); npool_s = dout("npool_s", [2, NS, 15, D])
    nconv_p = dout("nconv_p", [NL, 2, DFF]); nconv_s = dout("nconv_s", [NL, NS, 2, DFF])
    nk_p = dout("nk_p", [128, 256]); nv_p = dout("nv_p", [128, 256])
    nk_s = dout("nk_s", [NS, 128, 256]); nv_s = dout("nv_s", [NS, 128, 256])

    if dbg:
        dbgy = dout("dbgy", [NT + 1, 128, D]); dbgx = dout("dbgx", [128, 8 * TC])
        dbgk = dout("dbgk", [128, 4 * TP]); dbgv = dout("dbgv", [128, NT * 256])
    S = Sched(nc)
    with contextlib.ExitStack() as st:
        def sbt(name, shape, dt):
            return st.enter_context(nc.sbuf_tensor("sb_" + name, list(shape), dt))

        def A(eng, fn, r=(), w=()):
            return S.op(eng, fn, reads=list(r), writes=list(w))

        cst = sbt("cst", [128, NCST], F32); cst_b = Buf("cst")
        cstb = sbt("cstb", [128, NCSTB], BF16); cstb_b = Buf("cstb")
        Y = sbt("Y", [128, NT + 1, D], F32); yb = [Buf(f"y{i}") for i in range(NT + 1)]
        for i_ in range(1, NT):
            yb[i_].slot = yb[0]
        XT = sbt("XT", [128, 8, TC], BF16); xtb = [Buf(f"xt{i}") for i in range(NT + 1)]
        lng = sbt("lng", [128, D], F32); lng_b = Buf("lng")
        lnb = sbt("lnb", [128, D], F32); lnb_b = Buf("lnb")
        KT2 = sbt("KT2", [128, 4, TP], BF16); kt_b = Buf("kt2")
        V = sbt("V", [128, NT, 256], BF16); v_b = [Buf(f"v{i}") for i in range(NT)]
        ident = sbt("ident", [128, 128], F32); ident_b = Buf("ident")
        identb = sbt("identb", [128, 128], BF16); identb_b = Buf("identb")
        ones33 = sbt("ones33", [33, 128], BF16); ones_b = Buf("ones33")
        bb = sbt("bb", [33, 2560], BF16); bb_b = Buf("bb")
        mhalf = sbt("mhalf", [128, 1], F32); mhalf_b = Buf("mhalf")
        NSL = 4
        sm = sbt("sm", [128, NSL, 24], F32); sm_b = [Buf(f"sm{i}") for i in range(NSL)]
        sa = sbt("sa", [128, 4, 48], F32); sa_b = [Buf(f"sa{i}") for i in range(4)]
        kvo = sbt("kvo", [128, 512], F32); kvo_b = Buf("kvo")
        arena = sbt("arena", [128, ARENA_BYTES], U8)
        PS = st.enter_context(nc.psum_tensor("PS", [128, 8, 512], F32))
        pair_b = [Buf(f"pp{i}") for i in range(4)]
        bank_b = [Buf(f"pb{i}") for i in range(8)]

        def pair(p):
            return PS[:, 2 * p:2 * p + 2, :].rearrange("p a b -> p (a b)")

        def bank(bk):
            return PS[:, bk, :]

        def bank_deps(bk):
            return [pair_b[bk // 2], bank_b[bk]]

        nks_b = Buf("nks"); nvs_b = Buf("nvs"); dram_misc_b = Buf("dmisc")

        ar = {"off": 0, "bufs": [], "old": []}
        slots = {}

        def arena_reset():
            ar["old"] = ar["old"][-200:] + ar["bufs"] if False else ar["bufs"]
            ar["bufs"] = []
            ar["off"] = 0

        def take(name, shape, dt, nb=1, at=None, after=()):
            esz = 4 if dt == F32 else 2
            free = 1
            for s_ in shape[1:]:
                free *= s_
            nbytes = free * esz * nb
            nbytes = (nbytes + 63) // 64 * 64
            if at is None:
                assert ar["off"] + nbytes <= ARENA_BYTES, (name, ar["off"], nbytes)
                ar["last_off"] = ar["off"]
                v = arena[:, ar["off"]:ar["off"] + nbytes].bitcast(dt)
                ar["off"] += nbytes
            else:
                v = arena[:, at:at + nbytes].bitcast(dt)
            v = v[:, 0:free * nb]
            bufs = []
            for i in range(nb):
                b = Buf(f"{name}{i}")
                b.slot = slots.setdefault(b.name, b)
                for ob in list(ar["old"]) + list(after):
                    toks = ([ob.w] if ob.w is not None else []) + list(ob.rs.values())
                    for t in toks:
                        key = ("e", t[1]) if t[0] == "e" else ("d", t[1])
                        cur = b.rs.get(key)
                        if cur is None or cur[2] < t[2]:
                            b.rs[key] = t
                bufs.append(b)
                ar["bufs"].append(b)
            return v, bufs

        def inherit(dst, srcs):
            for ob in srcs:
                toks = ([ob.w] if ob.w is not None else []) + list(ob.rs.values())
                for t in toks:
                    key = ("e", t[1]) if t[0] == "e" else ("d", t[1])
                    cur = dst.rs.get(key)
                    if cur is None or cur[2] < t[2]:
                        dst.rs[key] = t

        def view(v, pat, **kw):
            return v.rearrange(pat, **kw)

        S.dma("sp", cst[:], cst_d, writes=[cst_b])
        S.dma("pool", cstb[:], cstb_d, writes=[cstb_b])
        A("dve", lambda e: e.memset(ident[:], 0.0), w=[ident_b])
        A("pool", lambda e: e.affine_select(out=ident[:], in_=ident[:], compare_op=ALU.not_equal, fill=1.0,
                                            base=0, pattern=[[-1, 128]], channel_multiplier=1),
          r=[ident_b], w=[ident_b])
        A("act", lambda e: e.activation(out=identb[:], in_=ident[:], func=AF.Copy), r=[ident_b], w=[identb_b])
        A("dve", lambda e: e.memset(ones33[:], 1.0), w=[ones_b])
        A("dve", lambda e: e.memset(mhalf[:], -0.5), w=[mhalf_b])
        A("dve", lambda e: e.memset(bb[:], 0.0), w=[bb_b])
        arena_reset()
        bst, (bst_b,) = take("bst", [33, 2560], F32)
        bhi, (bhi_b,) = take("bhi", [33, 2560], BF16)
        blo, (blo_b,) = take("blo", [33, 2560], F32)
        A("dve", lambda e: e.memset(bst[0:33, :], 0.0), w=[bst_b])
        S.dma("sp", bst[0:1, :], brow_d, writes=[bst_b])
        S.dma("sp", bst[32:33, :], brow_d, writes=[bst_b])
        A("act", lambda e: e.activation(out=bhi[0:33, :], in_=bst[0:33, :], func=AF.Copy), r=[bst_b], w=[bhi_b])
        A("dve", lambda e: e.tensor_tensor(out=blo[0:33, :], in0=bst[0:33, :], in1=bhi[0:33, :], op=ALU.subtract),
          r=[bst_b, bhi_b], w=[blo_b])
        A("dve", lambda e: e.tensor_copy(out=bb[0:1, :], in_=bhi[0:1, :]), r=[bhi_b, bb_b], w=[bb_b])
        A("dve", lambda e: e.tensor_copy(out=bb[32:33, :], in_=blo[32:33, :]), r=[blo_b, bb_b], w=[bb_b])

        bandc = cstb[:, B_BC:B_BC + 1024].rearrange("p (v g t) -> p v g t", v=2, g=4)
        bandp = cstb[:, B_BP:B_BP + 512].rearrange("p (g t) -> p g t", g=4)
        sel = cstb[:, B_SEL:B_SEL + 128].rearrange("p (t g i) -> p t g i", t=2, g=4)
        coefI = cstb[:, B_CI:B_CI + 64].rearrange("p (g i) -> p g i", g=4)

        cnt = {"sm": 0, "sa": 0, "pp": 0}

        def ln_A1(it):
            i, rows = it["i"], it["rows"]
            Yi = Y[0:rows, i, :]
            s_ = cnt["sm"] % NSL; cnt["sm"] += 1
            it["s"] = s_
            smb = sm_b[s_]
            t = sm[0:rows, s_, :]
            A("dve", lambda e: e.bn_stats(out=t[:, 0:6], in_=Yi[:, 0:512]), r=[yb[i]], w=[smb])
            A("dve", lambda e: e.bn_stats(out=t[:, 6:12], in_=Yi[:, 512:1024]), r=[yb[i], smb], w=[smb])
            A("dve", lambda e: e.bn_aggr(out=t[:, 12:14], in_=t[:, 0:12]), r=[smb], w=[smb])
            A("dve", lambda e: e.tensor_scalar(out=t[:, 14:15], in0=t[:, 13:14], scalar1=EPS, scalar2=None, op0=ALU.add),
              r=[smb], w=[smb])
            A("pool", lambda e: e.tensor_tensor(out=t[:, 15:16], in0=t[:, 14:15], in1=mhalf[0:rows, :], op=ALU.pow),
              r=[smb, mhalf_b], w=[smb])

        def ln_A2(it):
            i, rows, s_ = it["i"], it["rows"], it["s"]
            Yi = Y[0:rows, i, :]
            smb = sm_b[s_]
            t = sm[0:rows, s_, :]
            A("dve", lambda e: e.scalar_tensor_tensor(out=t[:, 16:17], in0=t[:, 12:13], scalar=-1.0, in1=t[:, 15:16],
                                                      op0=ALU.mult, op1=ALU.mult), r=[smb], w=[smb])
            A("act", lambda e: e.activation(out=Yi, in_=Yi, func=AF.Identity, bias=t[:, 16:17], scale=t[:, 15:16]),
              r=[yb[i], smb], w=[yb[i]])

        def ln_A3(it):
            i, rows, ch = it["i"], it["rows"], it["ch"]
            Yi = Y[0:rows, i, :]
            A("dve", lambda e: e.tensor_tensor(out=Yi, in0=Yi, in1=lng[0:rows, :], op=ALU.mult), r=[yb[i], lng_b], w=[yb[i]])
            A("dve", lambda e: e.tensor_tensor(out=Yi, in0=Yi, in1=lnb[0:rows, :], op=ALU.add), r=[yb[i], lnb_b], w=[yb[i]])
            if i < 2:
                c0 = C_TM + ch * 2 + i
                A("dve", lambda e: e.tensor_scalar(out=Yi, in0=Yi, scalar1=cst[0:rows, c0:c0 + 1], scalar2=None, op0=ALU.mult),
                  r=[yb[i], cst_b], w=[yb[i]])

        def ln_B(it):
            i, rows = it["i"], it["rows"]
            Yi = Y[0:rows, i, :]
            if it["need_xt"]:
                rpair = it["rpair"]() if callable(it["rpair"]) else it["rpair"]
                R = pair(rpair)
                for k in range(8):
                    A("pe", lambda e, k=k: e.transpose(out=R[:, k * 128:k * 128 + rows], in_=Yi[:, k * 128:(k + 1) * 128],
                                                       identity=ident[0:rows, 0:rows]),
                      r=[yb[i], ident_b], w=[pair_b[rpair]])
                cx = i * 128
                A("act", lambda e: e.activation(out=XT[:, :, cx:cx + rows],
                                                in_=R.rearrange("p (k t) -> p k t", k=8)[:, :, 0:rows], func=AF.Copy),
                  r=[pair_b[rpair]], w=[xtb[i]])
            if it.get("post") is not None:
                it["post"]()

        lnq = []

        def ln_push(ch, i, rows, need_xt, rpair, post=None):
            it = dict(ch=ch, i=i, rows=rows, need_xt=need_xt, rpair=rpair, post=post, st=1)
            ln_A1(it)
            lnq.append(it)
            if len(lnq) >= 2 and lnq[-2]["st"] == 1:
                ln_A2(lnq[-2]); lnq[-2]["st"] = 2
            if len(lnq) >= 3 and lnq[-3]["st"] == 2:
                ln_A3(lnq[-3]); lnq[-3]["st"] = 3
            if len(lnq) >= 4:
                o = lnq.pop(0)
                ln_B(o)

        def ln_flush():
            while lnq:
                for o in lnq:
                    if o["st"] == 1:
                        ln_A2(o); o["st"] = 2
                    elif o["st"] == 2:
                        ln_A3(o); o["st"] = 3
                    elif o["st"] == 3:
                        ln_B(o); o["st"] = 4
                while lnq and lnq[0]["st"] == 4:
                    lnq.pop(0)

        def ln_core(ch, i, rows, need_xt, rpair, post=None):
            ln_push(ch, i, rows, need_xt, rpair, post)

        def ln_mix(ch, i, rows, mp, need_xt, rpair):
            Yi = Y[0:rows, i, :]
            A("dve", lambda e: e.scalar_tensor_tensor(out=Yi, in0=Yi, scalar=ALPHA, in1=pair(mp)[0:rows, :],
                                                      op0=ALU.mult, op1=ALU.add), r=[yb[i], pair_b[mp]], w=[yb[i]])
            ln_core(ch, i, rows, need_xt, rpair)

        def load_ln(g_d, b_d, l):
            S.dma("sp", lng[:], g_d[l:l + 1, :].partition_broadcast(128), writes=[lng_b])
            S.dma("sp", lnb[:], b_d[l:l + 1, :].partition_broadcast(128), writes=[lnb_b])

        def pool_layer(ch, a):
            has_s = (ch == 1)
            arena_reset()
            psc, (psc_b,) = take("psc", [128, D], F32)
            wpf, (wpf_b,) = take("wpf", [128, 8, 256], F32)
            wp, (wp_b,) = take("wp", [128, 8, 256], BF16)
            ybf, ybf_b = take("ybf", [128, D], BF16, nb=3)
            dT, dT_b = take("dT", [128, 8, 128], BF16, nb=3)
            spb, (spb_b,) = take("spb", [128, 2, D], BF16)
            xnb, (xnb_b,) = take("xnb", [128, D], BF16)
            wpf4 = wpf.rearrange("p (g k e) -> p g k e", g=4, k=2)
            wp4 = wp.rearrange("p (g k e) -> p g k e", g=4, k=2)
            wp3 = wp.rearrange("p (c e) -> p c e", c=8)
            ybf3 = ybf.rearrange("p (s d) -> p s d", s=3)
            dT4 = dT.rearrange("p (s c t) -> p s c t", s=3, c=8)
            spb3 = spb.rearrange("p (t d) -> p t d", t=2)
            S.dma("sp", psc, pool_scale[a:a + 1, :].partition_broadcast(128), writes=[psc_b])
            S.dma("sp", wpf.rearrange("p (c e) -> p c e", c=8), pool_w[a].rearrange("g (k p) e -> p (g k) e", p=128), writes=[wpf_b])
            for kk in range(2):
                A("dve", lambda e, kk=kk: e.tensor_tensor(out=wp4[:, :, kk, :], in0=wpf4[:, :, kk, :],
                                                          in1=psc.rearrange("p (g e) -> p g e", g=4), op=ALU.mult),
                  r=[wpf_b, psc_b], w=[wp_b])
            load_ln(lmg, lmb, a)
            if has_s:
                for t_ in range(2):
                    S.dma("pool", spb3[0:120, t_, :], spool[a, t_ * 120:(t_ + 1) * 120, :], writes=[spb_b])
            def stageA0(i):
                sl = i % 3
                A("act", lambda e, i=i, sl=sl: e.activation(out=ybf3[:, sl, :], in_=Y[:, i, :], func=AF.Copy),
                  r=[yb[i]], w=[ybf_b[sl]])
                if ch == 1 and i == NT - 1:
                    S.dma("sp", npool_p[a], Y[113:128, i, :], reads=[yb[i]])

            def stageA(i):
                sl = i % 3
                dp = i % 2
                P = pair(dp)
                var = 0 if (ch == 0 and i == 1) else 1
                for kc in range(8):
                    g = kc // 2
                    A("pe", lambda e, kc=kc, g=g, sl=sl, var=var, P=P, i=i: e.matmul(
                        P[:, kc * 128:(kc + 1) * 128], lhsT=ybf3[:, sl, kc * 128:(kc + 1) * 128], rhs=bandc[:, var, g, :],
                        start=True, stop=(i == 0)), r=[ybf_b[sl], cstb_b], w=[pair_b[dp]])
                    if i > 0:
                        sp_ = (i - 1) % 3
                        A("pe", lambda e, kc=kc, g=g, sp_=sp_, P=P: e.matmul(
                            P[:, kc * 128:(kc + 1) * 128], lhsT=ybf3[:, sp_, kc * 128:(kc + 1) * 128], rhs=bandp[:, g, :],
                            start=False, stop=True), r=[ybf_b[sp_], cstb_b], w=[pair_b[dp]])
                ds = i % 3
                A("act", lambda e, ds=ds, P=P: e.activation(out=dT4[:, ds, :, :], in_=P.rearrange("p (c t) -> p c t", c=8),
                                                            func=AF.Copy), r=[pair_b[dp]], w=[dT_b[ds]])

            def stageB(i):
                mp = 2 + (i % 2)
                ds = i % 3
                Q = pair(mp)
                for g in range(4):
                    for kk in range(2):
                        A("pe", lambda e, g=g, kk=kk, ds=ds, Q=Q: e.matmul(
                            Q[:, g * 256:(g + 1) * 256], lhsT=dT4[:, ds, 2 * g + kk, :], rhs=wp3[:, 2 * g + kk, :],
                            start=(kk == 0), stop=(kk == 1)), r=[dT_b[ds], wp_b], w=[pair_b[mp]])
                Yi = Y[:, i, :]
                A("dve", lambda e: e.scalar_tensor_tensor(out=Yi, in0=Yi, scalar=ALPHA, in1=Q[:, :],
                                                          op0=ALU.mult, op1=ALU.add), r=[yb[i], pair_b[mp]], w=[yb[i]])
                if i + 3 < NT:
                    stageA0(i + 3)
                ln_core(ch, i, 128, True, mp)

            for i_ in range(min(3, NT)):
                stageA0(i_)
            stageA(0)
            stageA(1)
            for i in range(NT):
                if i + 2 < NT:
                    stageA(i + 2)
                stageB(i)
            if has_s:
                i = NT
                S.dma("sp", npool_s[a, :, 14, :], Y[0:NS, i, :], reads=[yb[i]])
                S.dma("sp", npool_s[a, :, 0:14, :], spool[a].rearrange("(i r) d -> i r d", r=15)[:, 1:15, :], writes=[dram_misc_b])
                A("act", lambda e: e.activation(out=xnb[0:NS, :], in_=Y[0:NS, NT, :], func=AF.Copy), r=[yb[i]], w=[xnb_b])
                dp, mp = 0, 2
                P = pair(dp)
                for kc in range(8):
                    g = kc // 2
                    for t_ in range(2):
                        A("pe", lambda e, kc=kc, g=g, t_=t_: e.matmul(
                            P[:, kc * 128:kc * 128 + NS], lhsT=spb3[0:120, t_, kc * 128:(kc + 1) * 128], rhs=sel[0:120, t_, g, :],
                            start=(t_ == 0), stop=False), r=[spb_b, cstb_b], w=[pair_b[dp]])
                    A("pe", lambda e, kc=kc, g=g: e.matmul(
                        P[:, kc * 128:kc * 128 + NS], lhsT=xnb[0:NS, kc * 128:(kc + 1) * 128], rhs=coefI[0:NS, g, :],
                        start=False, stop=True), r=[xnb_b, cstb_b], w=[pair_b[dp]])
                A("act", lambda e: e.activation(out=dT4[:, 0, :, 0:NS], in_=P.rearrange("p (c t) -> p c t", c=8)[:, :, 0:NS],
                                                func=AF.Copy), r=[pair_b[dp]], w=[dT_b[0]])
                Q = pair(mp)
                for g in range(4):
                    for kk in range(2):
                        A("pe", lambda e, g=g, kk=kk: e.matmul(
                            Q[0:NS, g * 256:(g + 1) * 256], lhsT=dT4[:, 0, 2 * g + kk, 0:NS], rhs=wp3[:, 2 * g + kk, :],
                            start=(kk == 0), stop=(kk == 1)), r=[dT_b[0], wp_b], w=[pair_b[mp]])
                ln_mix(ch, i, NS, mp, True, dp)
            ln_flush()

        def ffn_layer(ch, l):
            has_s = (ch == 1)
            ntile = NT + (1 if has_s else 0)
            arena_reset()
            HT, _ = take("HT", [128, 6, TC], BF16)
            HT3 = HT.rearrange("p (j t) -> p j t", j=6)
            ht_b = [[Buf(f"ht{j}_{t}") for t in range(3)] for j in range(6)]
            for row in ht_b:
                for b in row:
                    for ob in ar["old"]:
                        toks = ([ob.w] if ob.w is not None else []) + list(ob.rs.values())
                        for t in toks:
                            key = ("e", t[1]) if t[0] == "e" else ("d", t[1])
                            cur = b.rs.get(key)
                            if cur is None or cur[2] < t[2]:
                                b.rs[key] = t
                    ar["bufs"].append(b)
            wout, wout_b = take("wout", [128, 6, D], BF16, nb=2)
            wout4 = wout.rearrange("p (s j d) -> p s j d", s=2, j=6)
            win, win_b = take("win", [128, 8, 256], BF16, nb=4)
            win4 = win.rearrange("p (s k c) -> p s k c", s=4, k=8)
            gsb, gsb_b2 = take("gsb", [128, 2 + TC], F32, nb=2)
            gsb3 = gsb.rearrange("p (s t) -> p s t", s=2)
            gsb_b = [[Buf(f"gs{s_}_{t}") for t in range(4)] for s_ in range(2)]
            for s_ in range(2):
                for b in gsb_b[s_]:
                    b.rs = dict(gsb_b2[s_].rs)
                    ar["bufs"].append(b)
            cc, cc_b = take("cc", [128, 512], F32, nb=2)
            cc3 = cc.rearrange("p (s t) -> p s t", s=2)
            ss, ss_b = take("ss", [128, 512], F32, nb=2)
            ss3 = ss.rearrange("p (s t) -> p s t", s=2)
            us, us_b = take("us", [128, 512], F32, nb=2)
            us3 = us.rearrange("p (s t) -> p s t", s=2)
            if has_s:
                scT, (scT_b,) = take("scT", [128, NJ, 32], F32)
                scT3 = scT.rearrange("p (j c) -> p j c", j=NJ)
                cs, (cs_b,) = take("cs", [128, DFF], F32)
            load_ln(lfg, lfb, l)
            for bk_ in range(4, 8):
                inherit(bank_b[bk_], [pair_b[bk_ // 2]])
            i_lo = 1 if l >= 2 else 0
            c_lo = 128 * i_lo
            for s_ in range(2):
                A("dve", lambda e, s_=s_: e.memset(gsb3[:, s_, c_lo:c_lo + 2], 0.0), w=[gsb_b[s_][0]])
            if has_s:
                S.dma("sp", cs[0:32, :], sconv[l], writes=[cs_b])
                S.dma("sp", nconv_s[l, :, 0, :], sconv[l].rearrange("(i r) f -> i r f", r=2)[:, 1, :], writes=[dram_misc_b])
                for j0 in range(0, NJ, 8):
                    nj = min(8, NJ - j0)
                    pp = (j0 // 8) % 2
                    for jj in range(nj):
                        A("pe", lambda e, j0=j0, jj=jj, pp=pp: e.transpose(
                            out=pair(pp)[:, jj * 32:(jj + 1) * 32], in_=cs[0:32, (j0 + jj) * 128:(j0 + jj + 1) * 128],
                            identity=ident[0:32, 0:32]), r=[cs_b, ident_b], w=[pair_b[pp]])
                    A("dve", lambda e, j0=j0, nj=nj, pp=pp: e.tensor_copy(
                        out=scT3[:, j0:j0 + nj, :], in_=pair(pp)[:, 0:nj * 32].rearrange("p (j c) -> p j c", j=nj)),
                      r=[pair_b[pp]], w=[scT_b])
            if i_lo == 0:
                TT = [(0, 512), (512, 512), (1024, 256 + (NS if has_s else 0))]
            else:
                TT = [(128, 512), (640, 512), (1152, 128 + (NS if has_s else 0))]
            winv = w_in[l].rearrange("(k p) c -> p k c", p=128)
            slab = 0
            u1 = 0
            accs = {"n": 0}
            pend = {"f": None}
            exn = {"n": 0}
            pend2 = {"f": None}
            for gi, (j0, j1) in enumerate(GROUPS):
                wo = gi % 2
                ng = j1 - j0
                def load_wout(wo=wo, ng=ng, j0=j0, j1=j1):
                    S.dma("pool", wout4[:, wo, 0:ng, :], w_out[l, j0 * 128:j1 * 128, :].rearrange("(j p) d -> p j d", p=128),
                          writes=[wout_b[wo]])
                if gi > 0:
                    load_wout()
                for j in range(j0, j1):
                    s_ = slab % 4; slab += 1
                    S.dma("pool", win4[:, s_, :, 0:128], winv[:, :, j * 128:(j + 1) * 128], writes=[win_b[s_]])
                    S.dma("pool", win4[:, s_, :, 128:256], winv[:, :, DFF + j * 128:DFF + (j + 1) * 128], writes=[win_b[s_]])
                    if gi == 0 and j == j0 + 2:
                        load_wout()
                    gs = j % 2
                    cw = C_CW + (l * NJ + j) * 3
                    cbc = C_CB + l * NJ + j
                    for tt, (t0, n) in enumerate(TT):
                        set_ = u1 % 2; u1 += 1
                        bg, bu = 4 + 2 * set_, 5 + 2 * set_
                        pg, pu = bank(bg), bank(bu)
                        xr = [xtb[ii] for ii in range(t0 // 128, min(NT, (t0 + n + 127) // 128))]
                        if has_s and tt == 2:
                            xr.append(xtb[NT])
                        for k in range(8):
                            A("pe", lambda e, k=k, s_=s_, pg=pg, t0=t0, n=n: e.matmul(
                                pg[:, 0:n], lhsT=win4[:, s_, k, 0:128], rhs=XT[:, k, t0:t0 + n], start=(k == 0), stop=(k == 7)),
                              r=[win_b[s_]] + xr, w=[bank_b[bg]])
                        for k in range(8):
                            A("pe", lambda e, k=k, s_=s_, pu=pu, t0=t0, n=n: e.matmul(
                                pu[:, 0:n], lhsT=win4[:, s_, k, 128:256], rhs=XT[:, k, t0:t0 + n], start=(k == 0), stop=(k == 7)),
                              r=[win_b[s_]] + xr, w=[bank_b[bu]])
                        npr = min(n, TP - t0)
                        cs_ = u1 % 2
                        A("act", lambda e, gs=gs, t0=t0, n=n, pg=pg: e.activation(
                            out=gsb3[:, gs, 2 + t0:2 + t0 + n], in_=pg[:, 0:n], func=AF.Copy),
                          r=[bank_b[bg]], w=[gsb_b[gs][1 + tt]])
                        A("act", lambda e, cs_=cs_, n=n, pg=pg, cw=cw, cbc=cbc: e.activation(
                            out=cc3[:, cs_, 0:n], in_=pg[:, 0:n], func=AF.Identity, bias=cst[:, cbc:cbc + 1],
                            scale=cst[:, cw + 2:cw + 3]), r=[bank_b[bg]] + [cst_b], w=[cc_b[cs_]])
                        grd = [gsb_b[gs][tt], gsb_b[gs][1 + tt]]
                        A("dve", lambda e, cs_=cs_, gs=gs, t0=t0, npr=npr, cw=cw: e.scalar_tensor_tensor(
                            out=cc3[:, cs_, 0:npr], in0=gsb3[:, gs, 1 + t0:1 + t0 + npr], scalar=cst[:, cw + 1:cw + 2],
                            in1=cc3[:, cs_, 0:npr], op0=ALU.mult, op1=ALU.add), r=grd + [cc_b[cs_], cst_b], w=[cc_b[cs_]])
                        A("dve", lambda e, cs_=cs_, gs=gs, t0=t0, npr=npr, cw=cw: e.scalar_tensor_tensor(
                            out=cc3[:, cs_, 0:npr], in0=gsb3[:, gs, t0:t0 + npr], scalar=cst[:, cw:cw + 1],
                            in1=cc3[:, cs_, 0:npr], op0=ALU.mult, op1=ALU.add), r=grd + [cc_b[cs_], cst_b], w=[cc_b[cs_]])
                        if n > npr:
                            A("dve", lambda e, cs_=cs_, j=j, npr=npr, n=n, cw=cw: e.scalar_tensor_tensor(
                                out=cc3[:, cs_, npr:n], in0=scT3[:, j, 1:32:2], scalar=cst[:, cw + 1:cw + 2],
                                in1=cc3[:, cs_, npr:n], op0=ALU.mult, op1=ALU.add), r=[scT_b, cc_b[cs_], cst_b], w=[cc_b[cs_]])
                            A("dve", lambda e, cs_=cs_, j=j, npr=npr, n=n, cw=cw: e.scalar_tensor_tensor(
                                out=cc3[:, cs_, npr:n], in0=scT3[:, j, 0:32:2], scalar=cst[:, cw:cw + 1],
                                in1=cc3[:, cs_, npr:n], op0=ALU.mult, op1=ALU.add), r=[scT_b, cc_b[cs_], cst_b], w=[cc_b[cs_]])
                        A("act", lambda e, cs_=cs_, n=n, pu=pu: e.activation(out=us3[:, cs_, 0:n], in_=pu[:, 0:n], func=AF.Copy),
                          r=[bank_b[bu]], w=[us_b[cs_]])

                        def second(cs_=cs_, n=n, j=j, j0=j0, t0=t0, tt=tt):
                            A("act", lambda e: e.activation(out=ss3[:, cs_, 0:n], in_=cc3[:, cs_, 0:n], func=AF.Silu),
                              r=[cc_b[cs_]], w=[ss_b[cs_]])
                            A("dve", lambda e: e.tensor_tensor(
                                out=HT3[:, j - j0, t0:t0 + n], in0=ss3[:, cs_, 0:n], in1=us3[:, cs_, 0:n], op=ALU.mult),
                              r=[ss_b[cs_], us_b[cs_]], w=[ht_b[j - j0][tt]])
                        if pend2["f"] is not None:
                            pend2["f"]()
                        pend2["f"] = second
                        if tt == 0 and pend["f"] is not None:
                            pend["f"](); pend["f"] = None
                    if has_s:
                        def export(j=j, gs=gs):
                            eb = exn["n"] % 4; exn["n"] += 1
                            A("pe", lambda e: e.transpose(out=bank(eb)[0:18, 0:128], in_=gsb3[:, gs, TP:TP + 18],
                                                          identity=ident[:, :]),
                              r=[gsb_b[gs][3], ident_b], w=bank_deps(eb))
                            A("act", lambda e: e.activation(out=cs[0:18, j * 128:(j + 1) * 128], in_=bank(eb)[0:18, 0:128],
                                                            func=AF.Copy), r=bank_deps(eb), w=[cs_b])
                        pend["f"] = export
                if pend2["f"] is not None:
                    pend2["f"](); pend2["f"] = None
                if pend["f"] is not None:
                    pend["f"](); pend["f"] = None
                last = (gi == len(GROUPS) - 1)
                for i in range(i_lo, ntile):
                    rows = 128 if i < NT else NS
                    c0 = i * 128
                    tt = min((i - i_lo) // 4, 2)
                    ap_ = accs["n"] % 2; accs["n"] += 1
                    P = pair(ap_)
                    for jj in range(ng):
                        for half in range(2):
                            A("pe", lambda e, jj=jj, half=half, rows=rows, c0=c0, P=P, wo=wo, ng=ng: e.matmul(
                                P[0:rows, half * 512:(half + 1) * 512], lhsT=HT3[:, jj, c0:c0 + rows],
                                rhs=wout4[:, wo, jj, half * 512:(half + 1) * 512], start=(jj == 0), stop=(jj == ng - 1)),
                              r=[ht_b[jj][tt], wout_b[wo]], w=[pair_b[ap_]])
                    Yi = Y[0:rows, i, :]
                    if gi == 0:
                        A("dve", lambda e, Yi=Yi, P=P, rows=rows: e.scalar_tensor_tensor(
                            out=Yi, in0=Yi, scalar=ALPHA, in1=P[0:rows, :], op0=ALU.mult, op1=ALU.add),
                          r=[yb[i], pair_b[ap_]], w=[yb[i]])
                    else:
                        A("dve", lambda e, Yi=Yi, P=P, rows=rows: e.tensor_tensor(out=Yi, in0=Yi, in1=P[0:rows, :], op=ALU.add),
                          r=[yb[i], pair_b[ap_]], w=[yb[i]])
                    if last:
                        def rp_fn():
                            v = accs["n"] % 2; accs["n"] += 1
                            return v
                        post = None
                        if l == NL - 1 and i == NT:
                            post = lambda: S.dma("sp", ys_out, Y[0:NS, NT, :], reads=[yb[NT]])
                        elif l == NL - 1 and i >= 2:
                            def post(i=i):
                                r0 = (ch * 8 + i - 2) * 128
                                S.dma("sp", y_out[r0:r0 + 128, :], Y[:, i, :], reads=[yb[i]])
                        ln_core(ch, i, rows, l in (1, 2), rp_fn, post)
            ln_flush()
            inherit(pair_b[2], [bank_b[4], bank_b[5]])
            inherit(pair_b[3], [bank_b[6], bank_b[7]])
            if has_s:
                S.dma("sp", nconv_p[l], cs[0:2, :], reads=[cs_b])
                S.dma("sp", nconv_s[l, :, 1, :], cs[2:18, :], reads=[cs_b])

        def kv_proj(ch):
            has_s = (ch == 1)
            arena_reset()
            wkt, (wkt_b,) = take("wkt", [128, 8, 512], BF16)
            wkt3 = wkt.rearrange("p (k c) -> p k c", k=8)
            wk2, (wk2_b,) = take("wk2", [128, 8, 4, 128], BF16)
            wk24 = wk2.rearrange("p (k h c) -> p k h c", k=8, h=4)
            kvs, (kvs_b,) = take("kvs", [128, 512], F32)
            wv = w_kv.rearrange("(k p) c -> p k c", p=128)
            for kh in range(4):
                for dup in range(2):
                    S.dma("pool", wk24[:, :, kh, dup * 64:(dup + 1) * 64], wv[:, :, kh * 64:(kh + 1) * 64], writes=[wk2_b])
            S.dma("pool", wkt3, wv, writes=[wkt_b])
            u = 0
            for kh in range(4):
                for (t0, n) in [(0, 512), (512, 512), (1024, 256)]:
                    bk = 4 + (u % 4); u += 1
                    xr = [xtb[ii] for ii in range(t0 // 128, (t0 + n) // 128)]
                    for k in range(8):
                        A("pe", lambda e, k=k, kh=kh, bk=bk, t0=t0, n=n: e.matmul(
                            bank(bk)[:, 0:n], lhsT=wk24[:, k, kh, :], rhs=XT[:, k, t0:t0 + n], start=(k == 0), stop=(k == 7)),
                          r=[wk2_b] + xr, w=bank_deps(bk))
                    A("act", lambda e, kh=kh, bk=bk, t0=t0, n=n: e.activation(
                        out=KT2[:, kh, t0:t0 + n], in_=bank(bk)[:, 0:n], func=AF.Identity,
                        bias=cst[:, C_BKT + kh:C_BKT + kh + 1], scale=1.0), r=bank_deps(bk) + [cst_b], w=[kt_b])
            ntile = NT + (1 if (has_s and not (KVDBG & 2)) else 0)
            for i in range(ntile):
                rows = 128 if i < NT else NS
                c0 = i * 128
                bk = i % 4
                for k in range(8):
                    A("pe", lambda e, k=k, bk=bk, rows=rows, c0=c0: e.matmul(
                        bank(bk)[0:rows, :], lhsT=XT[:, k, c0:c0 + rows], rhs=wkt3[:, k, :], start=(k == 0), stop=False),
                      r=[wkt_b, xtb[i]], w=bank_deps(bk))
                A("pe", lambda e, bk=bk, rows=rows: e.matmul(
                    bank(bk)[0:128, :], lhsT=ones33[:, 0:128], rhs=bb[:, 2048:2560], start=False, stop=True),
                  r=[ones_b, bb_b], w=bank_deps(bk))
                if i < NT:
                    A("act", lambda e, i=i, bk=bk: e.activation(out=V[:, i, :], in_=bank(bk)[:, 256:512], func=AF.Copy),
                      r=bank_deps(bk), w=[v_b[i]])
                if ch == 1 and i == NT - 1 and not (KVDBG & 4):
                    A("act", lambda e, bk=bk: e.activation(out=kvo[:], in_=bank(bk)[:, :], func=AF.Copy), r=bank_deps(bk), w=[kvo_b])
                    S.dma("sp", nk_p, kvo[:, 0:256], reads=[kvo_b])
                    S.dma("sp", nv_p, kvo[:, 256:512], reads=[kvo_b])
                if i == NT:
                    A("dve", lambda e, bk=bk: e.tensor_copy(out=kvs[0:NS, :], in_=bank(bk)[0:NS, :]), r=bank_deps(bk), w=[kvs_b])
                    S.dma("sp", nk_s[:, 127, :], kvs[0:NS, 0:256], reads=[kvs_b], writes=[nks_b])
                    S.dma("sp", nv_s[:, 127, :], kvs[0:NS, 256:512], reads=[kvs_b], writes=[nvs_b])
                    for (a0, a1) in ([] if (KVDBG & 1) else [(1, 33), (33, 65), (65, 97), (97, 128)]):
                        S.dma("sp", nk_s[:, a0 - 1:a1 - 1, :], skw[:, a0:a1, :], writes=[nks_b])
                        S.dma("sp", nv_s[:, a0 - 1:a1 - 1, :], svw[:, a0:a1, :], writes=[nvs_b])

        def attn_layer(ch, l):
            bi = l - 2
            has_s = (ch == 1)
            arena_reset()
            wq, (wq_b,) = take("wq", [128, 8, D], BF16)
            wq_at = ar["last_off"]
            wq3 = wq.rearrange("p (k c) -> p k c", k=8)
            wo_, (wo_b,) = take("wo", [128, 8, D], BF16)
            wo3 = wo_.rearrange("p (k c) -> p k c", k=8)
            qT, (qT_b,) = take("qTa", [128, 8, TP], BF16)
            qT3 = qT.rearrange("p (m t) -> p m t", m=8)
            en, en_b = take("en", [128, 4, 256], BF16, nb=2)
            en4 = en.rearrange("p (d h s) -> p d h s", d=2, h=4)
            ssb, ssb_b = take("ssb", [128, 4, 256], F32, nb=2)
            ssb4 = ssb.rearrange("p (d h s) -> p d h s", d=2, h=4)
            ee, ee_b = take("ee", [128, 4, 256], F32, nb=2)
            ee4 = ee.rearrange("p (d h s) -> p d h s", d=2, h=4)
            PT, PT_b = take("PT", [128, 4, 2, 128], BF16, nb=2)
            PT5 = PT.rearrange("p (d h f q) -> p d h f q", d=2, h=4, f=2)
            oT, (oT_b,) = take("oT", [128, 8, 128], BF16)
            oT3 = oT.rearrange("p (m t) -> p m t", m=8)
            S.dma("pool", wq3, w_q[bi].rearrange("(k p) c -> p k c", p=128), writes=[wq_b])
            S.dma("pool", wo3, w_o[bi].rearrange("(k p) c -> p k c", p=128), writes=[wo_b])
            load_ln(lmg, lmb, l)
            sinkb = cst[:, C_SINKB + bi * 16:C_SINKB + bi * 16 + 16]
            nsinkb = cst[:, C_NSINKB + bi * 16:C_NSINKB + bi * 16 + 16]
            uq = 0
            for m in range(8):
                for (t0, n) in [(128, 512), (640, 512), (1152, 128)]:
                    bk = 4 + (uq % 4); uq += 1
                    xr = [xtb[ii] for ii in range(t0 // 128, (t0 + n) // 128)]
                    for k in range(8):
                        A("pe", lambda e, k=k, m=m, bk=bk, t0=t0, n=n: e.matmul(
                            bank(bk)[:, 0:n], lhsT=wq3[:, k, m * 128:(m + 1) * 128], rhs=XT[:, k, t0:t0 + n],
                            start=(k == 0), stop=(k == 7)), r=[wq_b] + xr, w=bank_deps(bk))
                    cq = C_BQT + bi * 8 + m
                    if uq % 2 == 0:
                        A("act", lambda e, m=m, bk=bk, t0=t0, n=n, cq=cq: e.activation(
                            out=qT3[:, m, t0:t0 + n], in_=bank(bk)[:, 0:n], func=AF.Identity, bias=cst[:, cq:cq + 1], scale=1.0),
                          r=bank_deps(bk) + [cst_b], w=[qT_b])
                    else:
                        A("dve", lambda e, m=m, bk=bk, t0=t0, n=n, cq=cq: e.tensor_scalar(
                            out=qT3[:, m, t0:t0 + n], in0=bank(bk)[:, 0:n], scalar1=cst[:, cq:cq + 1], scalar2=None, op0=ALU.add),
                          r=bank_deps(bk) + [cst_b], w=[qT_b])

            def out_proj(rows, osrc, mp):
                MP = pair(mp)
                for half in range(2):
                    for m in range(8):
                        A("pe", lambda e, half=half, m=m: e.matmul(
                            MP[0:rows, half * 512:(half + 1) * 512], lhsT=osrc(m), rhs=wo3[:, m, half * 512:(half + 1) * 512],
                            start=(m == 0), stop=False), r=[oT_b, wo_b], w=[pair_b[mp]])
                    A("pe", lambda e, half=half: e.matmul(
                        MP[0:rows, half * 512:(half + 1) * 512], lhsT=ones33[:, 0:rows],
                        rhs=bb[:, bi * 1024 + half * 512:bi * 1024 + (half + 1) * 512], start=False, stop=True),
                      r=[ones_b, bb_b], w=[pair_b[mp]])

            def make_tile(i):
                OP = pair(1)
                v_ = 0 if i == 1 else (1 if i == 2 else 2)
                am0 = C_AM + (ch * 3 + v_) * 256
                SPp = pair(2)
                TPb = pair(3).bitcast(BF16)

                def scores(kh, i=i):
                    for sl4 in range(4):
                        hh = PERM[sl4]
                        h = 4 * kh + hh; m = h // 2; po = 64 * (h % 2)
                        A("pe", lambda e, hh=sl4, m=m, po=po, kh=kh, i=i: e.matmul(
                            SPp[:, hh * 256:(hh + 1) * 256], lhsT=qT3[po:po + 64, m, i * 128:(i + 1) * 128],
                            rhs=KT2[po:po + 64, kh, (i - 1) * 128:(i + 1) * 128], start=True, stop=True),
                          r=[qT_b, kt_b], w=[pair_b[2]])

                def c1(kh, am0=am0):
                    d = kh % 2
                    s_ = cnt["sa"] % 4; cnt["sa"] += 1
                    t = sa[:, s_, :]
                    sab = sa_b[s_]
                    sS = ssb4[:, d]; eE = ee4[:, d]
                    A("dve", lambda e: e.tensor_tensor(
                        out=sS, in0=SPp.rearrange("p (h s) -> p h s", h=4),
                        in1=cst[:, am0:am0 + 256].unsqueeze(1).broadcast_to([128, 4, 256]), op=ALU.add),
                      r=[pair_b[2], cst_b], w=[ssb_b[d]])
                    A("dve", lambda e: e.tensor_reduce(out=t[:, 0:4], in_=sS, axis=AX.X, op=ALU.max), r=[ssb_b[d]], w=[sab])
                    A("dve", lambda e: e.scalar_tensor_tensor(
                        out=t[:, 8:12], in0=t[:, 0:4], scalar=-SCALE, in1=nsinkb[:, 4 * kh:4 * kh + 4], op0=ALU.mult, op1=ALU.min),
                      r=[sab, cst_b], w=[sab])
                    A("dve", lambda e: e.tensor_tensor(out=t[:, 16:20], in0=sinkb[:, 4 * kh:4 * kh + 4], in1=t[:, 8:12],
                                                       op=ALU.add), r=[sab, cst_b], w=[sab])
                    for hh in range(4):
                        A("act", lambda e, hh=hh: e.activation(
                            out=eE[:, hh, :], in_=sS[:, hh, :], func=AF.Exp, bias=t[:, 8 + hh:9 + hh], scale=SCALE,
                            accum_out=t[:, 12 + hh:13 + hh]), r=[ssb_b[d], sab], w=[ee_b[d], sab])
                    A("act", lambda e: e.activation(out=t[:, 20:24], in_=t[:, 16:20], func=AF.Exp), r=[sab], w=[sab])
                    return (t, sab)

                def c2(kh, ts):
                    d = kh % 2
                    t, sab = ts
                    eE = ee4[:, d]; eN = en4[:, d]
                    A("dve", lambda e: e.tensor_tensor(out=t[:, 24:28], in0=t[:, 12:16], in1=t[:, 20:24], op=ALU.add),
                      r=[sab], w=[sab])
                    A("dve", lambda e: e.reciprocal(out=t[:, 28:32], in_=t[:, 24:28]), r=[sab], w=[sab])
                    A("dve", lambda e: e.tensor_tensor(
                        out=eN, in0=eE, in1=t[:, 28:32].unsqueeze(2).broadcast_to([128, 4, 256]), op=ALU.mult),
                      r=[ee_b[d], sab], w=[en_b[d]])

                def tp(kh, i=i):
                    d = kh % 2
                    eN = en4[:, d]
                    for hh in range(4):
                        for half in range(2):
                            A("pe", lambda e, hh=hh, half=half: e.transpose(
                                out=TPb[:, (hh * 2 + half) * 128:(hh * 2 + half + 1) * 128],
                                in_=eN[:, hh, half * 128:(half + 1) * 128], identity=identb[:, :]),
                              r=[en_b[d], identb_b], w=[pair_b[3]])
                    A("act", lambda e: e.activation(out=PT5[:, d].rearrange("p h f q -> p (h f q)"), in_=TPb[:, 0:1024], func=AF.Copy),
                      r=[pair_b[3]], w=[PT_b[d]])
                    for sl4 in range(4):
                        hh = PERM[sl4]
                        h = 4 * kh + hh; m = h // 2; po = 64 * (h % 2)
                        for half in range(2):
                            A("pe", lambda e, hh=sl4, half=half, m=m, po=po, kh=kh, i=i: e.matmul(
                                OP[po:po + 64, m * 128:(m + 1) * 128], lhsT=V[:, i - 1 + half, kh * 64:(kh + 1) * 64],
                                rhs=PT5[:, d, hh, half, :], start=(half == 0), stop=(half == 1)),
                              r=[PT_b[d], v_b[i - 1], v_b[i]], w=[pair_b[1]])

                def otcopy():
                    A("act", lambda e: e.activation(out=oT, in_=OP, func=AF.Copy), r=[pair_b[1]], w=[oT_b])

                def outproj():
                    out_proj(128, lambda m: oT3[:, m, :], 0)

                def lnpart():
                    ln_mix(ch, i, 128, 0, True, 3)

                return dict(scores=scores, c1=c1, c2=c2, tp=tp, otcopy=otcopy, outproj=outproj, lnpart=lnpart, st={})

            tls = [make_tile(i) for i in range(1, NT)]
            T0 = tls[0]
            T0["scores"](0); T0["st"][0] = T0["c1"](0)
            T0["scores"](1); T0["st"][1] = T0["c1"](1)
            prev = None
            for ti, T_ in enumerate(tls):
                N_ = tls[ti + 1] if ti + 1 < len(tls) else None
                st_ = T_["st"]
                T_["scores"](2)
                if prev is not None:
                    prev["outproj"]()
                T_["c2"](0, st_[0]); T_["tp"](0)
                st_[2] = T_["c1"](2)
                if prev is not None:
                    prev["lnpart"]()
                T_["scores"](3)
                T_["c2"](1, st_[1]); T_["tp"](1)
                st_[3] = T_["c1"](3)
                if N_ is not None:
                    N_["scores"](0)
                T_["c2"](2, st_[2]); T_["tp"](2)
                if N_ is not None:
                    N_["st"][0] = N_["c1"](0)
                    N_["scores"](1)
                T_["c2"](3, st_[3]); T_["tp"](3)
                T_["otcopy"]()
                if N_ is not None:
                    N_["st"][1] = N_["c1"](1)
                prev = T_
            prev["outproj"]()
            prev["lnpart"]()

            if has_s:
                i = NT
                Ksb, (Ksb_b,) = take("Ksb", [128, NS, 256], BF16)
                Ksb3 = Ksb.rearrange("p (i d) -> p i d", i=NS)
                Vsb, (Vsb_b,) = take("Vsb", [128, NS, 256], BF16)
                Vsb3 = Vsb.rearrange("p (i d) -> p i d", i=NS)
                KsT, (KsT_b,) = take("KsT", [128, NS, 4, 128], BF16, at=wq_at, after=[wq_b])
                KsT4 = KsT.rearrange("p (i h s) -> p i h s", i=NS, h=4)
                qsT, (qsT_b,) = take("qsT", [128, 16, NS], BF16)
                qsT3 = qsT.rearrange("p (h i) -> p h i", h=16)
                STs, (STs_b,) = take("STs", [128, 256], F32)
                es, (es_b,) = take("es", [128, 2, 128], F32)
                es3 = es.rearrange("p (f s) -> p f s", f=2)
                PTs, (PTs_b,) = take("PTs", [128, 256], BF16)
                osT, (osT_b,) = take("osT", [128, 8, NS], BF16)
                osT3 = osT.rearrange("p (m i) -> p m i", m=8)
                S.dma("pool", Ksb3, nk_s.rearrange("i s d -> s i d"), reads=[nks_b], writes=[Ksb_b])
                S.dma("pool", Vsb3, nv_s.rearrange("i s d -> s i d"), reads=[nvs_b], writes=[Vsb_b])
                QS = pair(0)
                for h in range(16):
                    for k in range(8):
                        A("pe", lambda e, h=h, k=k: e.matmul(
                            QS[0:64, h * 16:(h + 1) * 16], lhsT=wq3[:, k, h * 64:(h + 1) * 64], rhs=XT[:, k, TP:TP + NS],
                            start=(k == 0), stop=(k == 7)), r=[wq_b, xtb[NT]], w=[pair_b[0]])
                c0 = C_BQH + bi * 16
                A("dve", lambda e, c0=c0: e.tensor_tensor(
                    out=qsT3[0:64, :, :], in0=QS[0:64, 0:256].rearrange("p (h i) -> p h i", h=16),
                    in1=cst[0:64, c0:c0 + 16].unsqueeze(2).broadcast_to([64, 16, NS]), op=ALU.add),
                  r=[pair_b[0], cst_b], w=[qsT_b])
                for i0 in range(0, NS, 4):
                    pp = 2 + (i0 // 4) % 2
                    Pb = pair(pp).bitcast(BF16)
                    for ii in range(4):
                        for kh in range(4):
                            A("pe", lambda e, ii=ii, kh=kh, i0=i0, Pb=Pb: e.transpose(
                                out=Pb[0:64, (ii * 4 + kh) * 128:(ii * 4 + kh + 1) * 128],
                                in_=Ksb3[:, i0 + ii, kh * 64:(kh + 1) * 64], identity=identb[:, :]),
                              r=[Ksb_b, identb_b], w=[pair_b[pp]])
                    A("act", lambda e, i0=i0, Pb=Pb: e.activation(
                        out=KsT4[0:64, i0:i0 + 4, :, :], in_=Pb[0:64, 0:2048].rearrange("p (i h s) -> p i h s", i=4, h=4),
                        func=AF.Copy), r=[pair_b[pp]], w=[KsT_b])
                ST = pair(1)
                for ii in range(NS):
                    for kh in range(4):
                        A("pe", lambda e, ii=ii, kh=kh: e.matmul(
                            ST[:, ii * 16 + 4 * kh:ii * 16 + 4 * kh + 4], lhsT=KsT4[0:64, ii, kh, :],
                            rhs=qsT3[0:64, 4 * kh:4 * kh + 4, ii], start=True, stop=True),
                          r=[KsT_b, qsT_b], w=[pair_b[1]])
                A("dve", lambda e: e.tensor_copy(out=STs, in_=ST[:, 0:256]), r=[pair_b[1]], w=[STs_b])
                S2 = pair(2)
                for hf in range(2):
                    A("pe", lambda e, hf=hf: e.transpose(out=S2[:, hf * 128:(hf + 1) * 128], in_=STs[:, hf * 128:(hf + 1) * 128],
                                                         identity=ident[:, :]), r=[STs_b, ident_b], w=[pair_b[2]])
                s_ = cnt["sa"] % 4; cnt["sa"] += 1
                t = sa[:, s_, :]
                sab = sa_b[s_]
                sk = cst[:, C_SINKS + bi * 2:C_SINKS + bi * 2 + 2]
                A("dve", lambda e: e.tensor_reduce(out=t[:, 0:2], in_=S2[:, 0:256].rearrange("p (f s) -> p f s", f=2),
                                                   axis=AX.X, op=ALU.max), r=[pair_b[2]], w=[sab])
                A("dve", lambda e: e.scalar_tensor_tensor(out=t[:, 4:6], in0=t[:, 0:2], scalar=SCALE, in1=sk,
                                                          op0=ALU.mult, op1=ALU.max), r=[sab, cst_b], w=[sab])
                A("dve", lambda e: e.tensor_scalar(out=t[:, 8:10], in0=t[:, 4:6], scalar1=-1.0, scalar2=None, op0=ALU.mult),
                  r=[sab], w=[sab])
                for hf in range(2):
                    A("act", lambda e, hf=hf: e.activation(
                        out=es3[:, hf, :], in_=S2[:, hf * 128:(hf + 1) * 128], func=AF.Exp, bias=t[:, 8 + hf:9 + hf], scale=SCALE,
                        accum_out=t[:, 12 + hf:13 + hf]), r=[pair_b[2], sab], w=[es_b, sab])
                A("dve", lambda e: e.tensor_tensor(out=t[:, 16:18], in0=sk, in1=t[:, 4:6], op=ALU.subtract), r=[sab, cst_b], w=[sab])
                A("act", lambda e: e.activation(out=t[:, 20:22], in_=t[:, 16:18], func=AF.Exp), r=[sab], w=[sab])
                A("dve", lambda e: e.tensor_tensor(out=t[:, 24:26], in0=t[:, 12:14], in1=t[:, 20:22], op=ALU.add), r=[sab], w=[sab])
                A("dve", lambda e: e.reciprocal(out=t[:, 28:30], in_=t[:, 24:26]), r=[sab], w=[sab])
                A("dve", lambda e: e.tensor_tensor(out=es3, in0=es3, in1=t[:, 28:30].unsqueeze(2).broadcast_to([128, 2, 128]),
                                                   op=ALU.mult), r=[es_b, sab], w=[es_b])
                P2 = pair(3)
                for hf in range(2):
                    A("pe", lambda e, hf=hf: e.transpose(out=P2[:, hf * 128:(hf + 1) * 128], in_=es3[:, hf, :], identity=ident[:, :]),
                      r=[es_b, ident_b], w=[pair_b[3]])
                A("act", lambda e: e.activation(out=PTs, in_=P2[:, 0:256], func=AF.Copy), r=[pair_b[3]], w=[PTs_b])
                OS = pair(1)
                OS3 = OS[:, 0:128].rearrange("p (m i) -> p m i", m=8)
                for ii in range(NS):
                    for kh in range(4):
                        for par in range(2):
                            A("pe", lambda e, ii=ii, kh=kh, par=par: e.matmul(
                                OS3[64 * par:64 * par + 64, 2 * kh:2 * kh + 2, ii], lhsT=Vsb3[:, ii, kh * 64:(kh + 1) * 64],
                                rhs=PTs[:, ii * 16 + 4 * kh + par:ii * 16 + 4 * kh + 4:2], start=True, stop=True),
                              r=[Vsb_b, PTs_b, STs_b], w=[pair_b[1]])
                A("act", lambda e: e.activation(out=osT, in_=OS[:, 0:128], func=AF.Copy), r=[pair_b[1]], w=[osT_b])
                oT_b_save = oT_b
                MP = pair(0)
                for half in range(2):
                    for m in range(8):
                        A("pe", lambda e, half=half, m=m: e.matmul(
                            MP[0:NS, half * 512:(half + 1) * 512], lhsT=osT3[:, m, :], rhs=wo3[:, m, half * 512:(half + 1) * 512],
                            start=(m == 0), stop=False), r=[osT_b, wo_b], w=[pair_b[0]])
                    A("pe", lambda e, half=half: e.matmul(
                        MP[0:128, half * 512:(half + 1) * 512], lhsT=ones33[:, 0:128],
                        rhs=bb[:, bi * 1024 + half * 512:bi * 1024 + (half + 1) * 512], start=False, stop=True),
                      r=[ones_b, bb_b], w=[pair_b[0]])
                ln_mix(ch, i, NS, 0, True, 3)
            ln_flush()

        stage = {"n": 0}

        def go():
            stage["n"] += 1
            return stop is None or stage["n"] <= stop

        for ch in range(2):
            if not go():
                break
            S.dma("sp", Y[:, 0:NT, :], xin[ch * TP:(ch + 1) * TP, :].rearrange("(t p) d -> p t d", p=128), writes=yb[0:NT])
            if ch == 1:
                S.dma("sp", Y[0:NS, NT, :], xs, writes=[yb[NT]])
            for l in range(NL):
                if go():
                    if l < 2:
                        pool_layer(ch, l)
                    else:
                        attn_layer(ch, l)
                if go():
                    ffn_layer(ch, l)
                if l == 1 and go():
                    kv_proj(ch)
        fin = yb + [kvo_b, nks_b, nvs_b, dram_misc_b] + ar["bufs"]
        if dbg:
            dby_b = Buf("dbgyb")
            S.dma("sp", dbgy.rearrange("t p d -> p t d"), Y[:, :, :], reads=yb, writes=[dby_b])
            S.dma("pool", dbgx, XT[:, :, :].rearrange("p k t -> p (k t)"), reads=xtb, writes=[dby_b])
            S.dma("pool", dbgk, KT2[:, :, :].rearrange("p k t -> p (k t)"), reads=[kt_b], writes=[dby_b])
            S.dma("pool", dbgv, V[:, :, :].rearrange("p k t -> p (k t)"), reads=v_b, writes=[dby_b])
            fin = fin + [dby_b]
        S.finalize(st, fin)
    return nc, S


POOL_WINDOWS = (2, 4, 8, 16)


def _consts_for_core(c, inp):
    qd = c % 4
    f32 = np.float32
    cst = np.zeros((128, NCST), f32)
    cw = np.asarray(inp["ffn_conv_w"], f32)
    cbv = np.asarray(inp["ffn_conv_b"], f32)
    cst[:, C_CW:C_CW + 264] = cw.reshape(NL, 3, NJ, 128).transpose(3, 0, 2, 1).reshape(128, 264)
    cst[:, C_CB:C_CB + 88] = cbv.reshape(NL, NJ, 128).transpose(2, 0, 1).reshape(128, 88)
    bq = np.asarray(inp["attn_b_q"], f32)
    cst[:, C_BQT:C_BQT + 16] = bq.reshape(2, 8, 128).transpose(2, 0, 1).reshape(128, 16)
    cst[0:64, C_BQH:C_BQH + 32] = bq.reshape(2, 16, 64).transpose(2, 0, 1).reshape(64, 32)
    bkv = np.asarray(inp["b_kv"], f32)
    bk = bkv[:256].reshape(4, 64)
    cst[:, C_BKT:C_BKT + 4] = np.concatenate([bk.T, bk.T], axis=0)
    sinks = np.asarray(inp["attn_sinks"], f32)
    sperm = sinks.reshape(2, 4, 4)[:, :, PERM].reshape(1, 32)
    cst[:, C_SINKB:C_SINKB + 32] = np.broadcast_to(sperm, (128, 32))
    cst[:, C_NSINKB:C_NSINKB + 32] = np.broadcast_to(-sperm, (128, 32))
    pidx = np.arange(128)
    for bi in range(2):
        for hf in range(2):
            cst[:, C_SINKS + bi * 2 + hf] = sinks[bi, pidx % 16]

    def real(ch, t, r):
        blk = 16 * qd + 8 * ch - 1 + t
        return (blk * 128 + r) >= 112

    r = np.arange(128)
    for ch in range(2):
        for i in range(2):
            cst[:, C_TM + ch * 2 + i] = real(ch, i, r).astype(f32)
    q = np.arange(128)[:, None]
    j = np.arange(256)[None, :]
    band = (q < j) & (j <= q + 128)
    for ch in range(2):
        for v in range(3):
            if v == 2:
                ok = band
            else:
                ti = 1 + v
                kr = np.where(j < 128, real(ch, ti - 1, j % 128), real(ch, ti, j % 128))
                ok = band & kr
            cst[:, C_AM + (ch * 3 + v) * 256:C_AM + (ch * 3 + v + 1) * 256] = np.where(ok, 0.0, NEG).astype(f32)

    cstb = np.zeros((128, NCSTB), f32)
    s = np.arange(128)[:, None]
    t = np.arange(128)[None, :]
    bc = np.zeros((128, 2, 4, 128), f32)
    bp = np.zeros((128, 4, 128), f32)
    for g, w in enumerate(POOL_WINDOWS):
        inwin = (s > t - w) & (s <= t)
        gen = inwin.astype(f32) / w - (s == t).astype(f32)
        bc[:, 1, g, :] = gen
        if qd == 0:
            tseq = t - 112
            cnt = np.where(tseq >= 0, np.minimum(w, tseq + 1), w).astype(f32)
            bc[:, 0, g, :] = inwin.astype(f32) / cnt - (s == t).astype(f32)
        else:
            bc[:, 0, g, :] = gen
        bp[:, g, :] = ((s > 128 + t - w).astype(f32)) / w
    cstb[:, B_BC:B_BC + 1024] = bc.reshape(128, 1024)
    cstb[:, B_BP:B_BP + 512] = bp.reshape(128, 512)
    sel = np.zeros((128, 2, 4, 16), f32)
    ci = np.zeros((128, 4, 16), f32)
    for g, w in enumerate(POOL_WINDOWS):
        for p in range(120):
            rr = p % 15
            if rr >= 16 - w:
                for t_ in range(2):
                    sel[p, t_, g, t_ * 8 + p // 15] = 1.0 / w
        for p in range(16):
            ci[p, g, p] = 1.0 / w - 1.0
    cstb[:, B_SEL:B_SEL + 128] = sel.reshape(128, 128)
    cstb[:, B_CI:B_CI + 64] = ci.reshape(128, 64)
    return cst, cstb


_NC_CACHE = {}


def kernel(**inputs):
    f32 = np.float32
    inp = {k: np.asarray(v) for k, v in inputs.items()}
    xp = inp["x_prompt"].astype(f32, copy=False)
    meta = inp["meta_tokens"].astype(f32, copy=False)
    n = 8
    if "nc" not in _NC_CACHE:
        _NC_CACHE["nc"] = build_nc()[0]
    nc = _NC_CACHE["nc"]
    brow = np.concatenate([inp["attn_b_o"][0], inp["attn_b_o"][1], inp["b_kv"]]).astype(f32).reshape(1, 2560)
    shared = {
        "pool_w": np.ascontiguousarray(inp["pool_w"], f32), "pool_scale": np.ascontiguousarray(inp["pool_scale"], f32),
        "w_kv": np.ascontiguousarray(inp["w_kv"], f32), "w_q": np.ascontiguousarray(inp["attn_w_q"], f32),
        "w_o": np.ascontiguousarray(inp["attn_w_o"], f32), "w_in": np.ascontiguousarray(inp["ffn_w_in"], f32),
        "w_out": np.ascontiguousarray(inp["ffn_w_out"], f32),
        "lmg": np.ascontiguousarray(inp["ln_mix_g"], f32), "lmb": np.ascontiguousarray(inp["ln_mix_b"], f32),
        "lfg": np.ascontiguousarray(inp["ln_ffn_g"], f32), "lfb": np.ascontiguousarray(inp["ln_ffn_b"], f32),
        "brow": brow,
    }
    in_maps = []
    for c in range(n):
        b, qd = c // 4, c % 4
        xin = np.zeros((2, NT, 128, D), f32)
        for ch in range(2):
            for i in range(NT):
                blk = 16 * qd + 8 * ch - 1 + i
                if blk >= 1:
                    xin[ch, i] = xp[b, (blk - 1) * 128:blk * 128]
                elif blk == 0:
                    xin[ch, i, 112:128] = meta
        cst, cstb = _consts_for_core(c, inp)
        sl = slice(NS * c, NS * (c + 1))
        m = dict(shared)
        m.update({
            "xin": xin.reshape(2 * NT * 128, D),
            "xs": np.ascontiguousarray(inp["x_sample"][sl, 0, :], f32),
            "spool": np.ascontiguousarray(inp["state_pool"][:, sl], f32).reshape(2, 240, D),
            "sconv": np.ascontiguousarray(inp["state_conv"][:, sl], f32).reshape(NL, 32, DFF),
            "skw": np.ascontiguousarray(inp["state_k_win"][sl], f32).reshape(NS, 128, 256),
            "svw": np.ascontiguousarray(inp["state_v_win"][sl], f32).reshape(NS, 128, 256),
            "cst": cst, "cstb": cstb,
        })
        in_maps.append(m)
    res = run_bass_kernel_spmd(nc, in_maps, core_ids=list(range(n)))
    R = res.results
    y_prompt = np.zeros((2, 8192, D), f32)
    y_sample = np.zeros((128, 1, D), f32)
    npp = np.zeros((2, 2, 15, D), f32); nps = np.zeros((2, 128, 15, D), f32)
    ncp = np.zeros((NL, 2, 2, DFF), f32); ncs = np.zeros((NL, 128, 2, DFF), f32)
    nkp = np.zeros((2, 128, 4, 64), f32); nvp = np.zeros((2, 128, 4, 64), f32)
    nks = np.zeros((128, 128, 4, 64), f32); nvs = np.zeros((128, 128, 4, 64), f32)
    for c in range(n):
        b, qd = c // 4, c % 4
        r = R[c]
        sl = slice(NS * c, NS * (c + 1))
        y_prompt[b, 2048 * qd:2048 * (qd + 1)] = r["y_out"]
        y_sample[sl, 0] = r["ys_out"]
        nps[:, sl] = r["npool_s"]
        ncs[:, sl] = r["nconv_s"]
        nks[sl] = r["nk_s"].reshape(NS, 128, 4, 64)
        nvs[sl] = r["nv_s"].reshape(NS, 128, 4, 64)
        if qd == 3:
            npp[:, b] = r["npool_p"]
            ncp[:, b] = r["nconv_p"]
            nkp[b] = r["nk_p"].reshape(128, 4, 64)
            nvp[b] = r["nv_p"].reshape(128, 4, 64)
    return (y_prompt, y_sample, npp, nps, ncp, ncs, nkp, nvp, nks, nvs)
```

```python
import contextlib
import numpy as np
import concourse.bass as bass
import concourse.mybir as mybir
from concourse.bass_utils import run_bass_kernel_spmd

F32 = mybir.dt.float32
BF16 = mybir.dt.bfloat16
AF = mybir.ActivationFunctionType
ALU = mybir.AluOpType
AX = mybir.AxisListType


class Buf:
    __slots__ = ("name", "w", "rs", "dsem", "dcnt", "slot")

    def __init__(self, name):
        self.name = name
        self.w = None
        self.rs = {}
        self.dsem = None
        self.dcnt = 0
        self.slot = self


class Sched:
    ENG = ["pe", "act", "dve", "pool", "sp"]

    def __init__(self, nc):
        self.nc = nc
        self.ops = {e: [] for e in self.ENG}
        self.waited = {e: {} for e in self.ENG}
        self.dbufs = []

    def _deps(self, eng, reads, writes):
        best = {}
        idx = len(self.ops[eng])

        def add(tok):
            if tok is None:
                return
            if tok[0] == "e":
                _, pe, pidx = tok
                if pe == eng and eng == "pe":
                    return
                key = ("e", pe)
                v = pidx
            else:
                _, b, v = tok
                key = ("d", b)
            if best.get(key, -1) < v:
                best[key] = v

        for b in reads:
            add(b.w)
        for b in writes:
            add(b.w)
            for t in b.rs.values():
                add(t)
        waits = []
        for key, v in best.items():
            if self.waited[eng].get(key, -1) >= v:
                continue
            self.waited[eng][key] = v
            waits.append((key, v))
        return waits

    def _commit(self, tok, reads, writes):
        for b in writes:
            b.w = tok
            b.rs = {}
        for b in reads:
            if b in writes:
                continue
            if tok[0] == "e":
                b.rs[("e", tok[1])] = tok
            else:
                b.rs[("d", tok[1])] = tok

    def op(self, eng, fn, reads=(), writes=()):
        waits = self._deps(eng, reads, writes)
        idx = len(self.ops[eng])
        self.ops[eng].append(dict(fn=fn, waits=waits, sig=False, dma=None))
        tok = ("e", eng, idx)
        self._commit(tok, reads, writes)
        return tok

    def dma(self, eng, out, in_, reads=(), writes=(), **kw):
        waits = self._deps(eng, reads, writes)
        pb = (writes[0] if writes else reads[0]).slot
        if pb.dsem is None:
            pb.dsem = True
            self.dbufs.append(pb)
        pb.dcnt += 16
        self.ops[eng].append(dict(
            fn=lambda e: e.dma_start(out=out, in_=in_, **kw), waits=waits, sig=False, dma=pb))
        tok = ("d", pb, pb.dcnt)
        self._commit(tok, reads, writes)
        return tok

    def finalize(self, stack, final_bufs=()):
        nc = self.nc
        waits = self._deps("sp", list(final_bufs), list(final_bufs))
        self.ops["sp"].append(dict(fn=None, waits=waits, sig=False, dma=None))
        for e in self.ENG:
            for rec in self.ops[e]:
                for key, v in rec["waits"]:
                    if key[0] == "e":
                        self.ops[key[1]][v]["sig"] = True
        cum = {}
        for e in self.ENG:
            c = 0
            arr = []
            for rec in self.ops[e]:
                if rec["sig"]:
                    c += 1
                arr.append(c)
            cum[e] = arr
        esem = {e: stack.enter_context(nc.semaphore("s_" + e)) for e in self.ENG}
        for b in self.dbufs:
            b.dsem = stack.enter_context(nc.semaphore("d_" + b.name))
        engobj = {"pe": "tensor", "act": "scalar", "dve": "vector", "pool": "gpsimd", "sp": "sync"}

        def emit(name, e):
            for rec in self.ops[name]:
                for key, v in rec["waits"]:
                    if key[0] == "e":
                        e.wait_ge(esem[key[1]], cum[key[1]][v])
                    else:
                        e.wait_ge(key[1].dsem, v)
                if rec["fn"] is None:
                    continue
                ins = rec["fn"](e)
                if rec["dma"] is not None:
                    ins.then_inc(rec["dma"].dsem, 16)
                elif rec["sig"]:
                    ins.then_inc(esem[name], 1)

        block = stack.enter_context(nc.Block())
        for name in self.ENG:
            getattr(block, engobj[name])(lambda e, name=name: emit(name, e))
        self.stats = {e: len(self.ops[e]) for e in self.ENG}
        self.nsem = 5 + len(self.dbufs)

D = 1024; DFF = 2816; NJ = 22; NL = 4; NT = 10; TP = NT * 128; NS = 16; TC = TP + NS
ALPHA = (2.0 * 4) ** 0.25; EPS = 1e-5; SCALE = 0.125; NEG = -1e30
GROUPS = [(0, 6), (6, 12), (12, 17), (17, 22)]
C_CW = 0; C_CB = 264; C_BQT = 352; C_BQH = 368; C_BKT = 400; C_SINKB = 404; C_SINKS = 436; C_TM = 440; C_AM = 444; C_NSINKB = 1980; NCST = 2012
B_BC = 0; B_BP = 1024; B_SEL = 1536; B_CI = 1664; NCSTB = 1728
U8 = mybir.dt.uint8
PERM = [0, 2, 1, 3]
import os
KVDBG = int(os.environ.get('KVDBG', '0'))
ARENA_BYTES = 100 * 1024


def build_nc(stop=None, dbg=False):
    nc = bass.Bass("TRN2", target_bir_lowering=False)

    def din(name, shape):
        return nc.dram_tensor(name, list(shape), F32, kind="ExternalInput").ap()

    def dout(name, shape):
        return nc.dram_tensor(name, list(shape), F32, kind="ExternalOutput").ap()

    xin = din("xin", [2 * NT * 128, D]); xs = din("xs", [NS, D])
    spool = din("spool", [2, 240, D]); sconv = din("sconv", [NL, 32, DFF])
    skw = din("skw", [NS, 128, 256]); svw = din("svw", [NS, 128, 256])
    pool_w = din("pool_w", [2, 4, 256, 256]); pool_scale = din("pool_scale", [2, D])
    w_kv = din("w_kv", [D, 512])
    w_q = din("w_q", [2, D, D]); w_o = din("w_o", [2, D, D])
    w_in = din("w_in", [NL, D, 2 * DFF]); w_out = din("w_out", [NL, DFF, D])
    lmg = din("lmg", [NL, D]); lmb = din("lmb", [NL, D]); lfg = din("lfg", [NL, D]); lfb = din("lfb", [NL, D])
    cst_d = din("cst", [128, NCST]); cstb_d = din("cstb", [128, NCSTB]); brow_d = din("brow", [1, 2560])

    y_out = dout("y_out", [2 * 8 * 128, D]); ys_out = dout("ys_out", [NS, D])
    npool_p = dout("npool_p", [2, 15, D]); npool_s = dout("npool_s", [2, NS, 15, D])
    nconv_p = dout("nconv_p", [NL, 2, DFF]); nconv_s = dout("nconv_s", [NL, NS, 2, DFF])
    nk_p = dout("nk_p", [128, 256]); nv_p = dout("nv_p", [128, 256])
    nk_s = dout("nk_s", [NS, 128, 256]); nv_s = dout("nv_s", [NS, 128, 256])

    if dbg:
        dbgy = dout("dbgy", [NT + 1, 128, D]); dbgx = dout("dbgx", [128, 8 * TC])
        dbgk = dout("dbgk", [128, 4 * TP]); dbgv = dout("dbgv", [128, NT * 256])
    S = Sched(nc)
    with contextlib.ExitStack() as st:
        def sbt(name, shape, dt):
            return st.enter_context(nc.sbuf_tensor("sb_" + name, list(shape), dt))

        def A(eng, fn, r=(), w=()):
            return S.op(eng, fn, reads=list(r), writes=list(w))

        cst = sbt("cst", [128, NCST], F32); cst_b = Buf("cst")
        cstb = sbt("cstb", [128, NCSTB], BF16); cstb_b = Buf("cstb")
        Y = sbt("Y", [128, NT + 1, D], F32); yb = [Buf(f"y{i}") for i in range(NT + 1)]
        for i_ in range(1, NT):
            yb[i_].slot = yb[0]
        XT = sbt("XT", [128, 8, TC], BF16); xtb = [Buf(f"xt{i}") for i in range(NT + 1)]
        lng = sbt("lng", [128, D], F32); lng_b = Buf("lng")
        lnb = sbt("lnb", [128, D], F32); lnb_b = Buf("lnb")
        KT2 = sbt("KT2", [128, 4, TP], BF16); kt_b = Buf("kt2")
        V = sbt("V", [128, NT, 256], BF16); v_b = [Buf(f"v{i}") for i in range(NT)]
        ident = sbt("ident", [128, 128], F32); ident_b = Buf("ident")
        identb = sbt("identb", [128, 128], BF16); identb_b = Buf("identb")
        ones33 = sbt("ones33", [33, 128], BF16); ones_b = Buf("ones33")
        bb = sbt("bb", [33, 2560], BF16); bb_b = Buf("bb")
        mhalf = sbt("mhalf", [128, 1], F32); mhalf_b = Buf("mhalf")
        NSL = 4
        sm = sbt("sm", [128, NSL, 24], F32); sm_b = [Buf(f"sm{i}") for i in range(NSL)]
        sa = sbt("sa", [128, 4, 48], F32); sa_b = [Buf(f"sa{i}") for i in range(4)]
        kvo = sbt("kvo", [128, 512], F32); kvo_b = Buf("kvo")
        arena = sbt("arena", [128, ARENA_BYTES], U8)
        PS = st.enter_context(nc.psum_tensor("PS", [128, 8, 512], F32))
        pair_b = [Buf(f"pp{i}") for i in range(4)]
        bank_b = [Buf(f"pb{i}") for i in range(8)]

        def pair(p):
            return PS[:, 2 * p:2 * p + 2, :].rearrange("p a b -> p (a b)")

        def bank(bk):
            return PS[:, bk, :]

        def bank_deps(bk):
            return [pair_b[bk // 2], bank_b[bk]]

        nks_b = Buf("nks"); nvs_b = Buf("nvs"); dram_misc_b = Buf("dmisc")

        ar = {"off": 0, "bufs": [], "old": []}
        slots = {}

        def arena_reset():
            ar["old"] = ar["old"][-200:] + ar["bufs"] if False else ar["bufs"]
            ar["bufs"] = []
            ar["off"] = 0

        def take(name, shape, dt, nb=1, at=None, after=()):
            esz = 4 if dt == F32 else 2
            free = 1
            for s_ in shape[1:]:
                free *= s_
            nbytes = free * esz * nb
            nbytes = (nbytes + 63) // 64 * 64
            if at is None:
                assert ar["off"] + nbytes <= ARENA_BYTES, (name, ar["off"], nbytes)
                ar["last_off"] = ar["off"]
                v = arena[:, ar["off"]:ar["off"] + nbytes].bitcast(dt)
                ar["off"] += nbytes
            else:
                v = arena[:, at:at + nbytes].bitcast(dt)
            v = v[:, 0:free * nb]
            bufs = []
            for i in range(nb):
                b = Buf(f"{name}{i}")
                b.slot = slots.setdefault(b.name, b)
                for ob in list(ar["old"]) + list(after):
                    toks = ([ob.w] if ob.w is not None else []) + list(ob.rs.values())
                    for t in toks:
                        key = ("e", t[1]) if t[0] == "e" else ("d", t[1])
                        cur = b.rs.get(key)
                        if cur is None or cur[2] < t[2]:
                            b.rs[key] = t
                bufs.append(b)
                ar["bufs"].append(b)
            return v, bufs

        def inherit(dst, srcs):
            for ob in srcs:
                toks = ([ob.w] if ob.w is not None else []) + list(ob.rs.values())
                for t in toks:
                    key = ("e", t[1]) if t[0] == "e" else ("d", t[1])
                    cur = dst.rs.get(key)
                    if cur is None or cur[2] < t[2]:
                        dst.rs[key] = t

        def view(v, pat, **kw):
            return v.rearrange(pat, **kw)

        S.dma("sp", cst[:], cst_d, writes=[cst_b])
        S.dma("pool", cstb[:], cstb_d, writes=[cstb_b])
        A("dve", lambda e: e.memset(ident[:], 0.0), w=[ident_b])
        A("pool", lambda e: e.affine_select(out=ident[:], in_=ident[:], compare_op=ALU.not_equal, fill=1.0,
                                            base=0, pattern=[[-1, 128]], channel_multiplier=1),
          r=[ident_b], w=[ident_b])
        A("act", lambda e: e.activation(out=identb[:], in_=ident[:], func=AF.Copy), r=[ident_b], w=[identb_b])
        A("dve", lambda e: e.memset(ones33[:], 1.0), w=[ones_b])
        A("dve", lambda e: e.memset(mhalf[:], -0.5), w=[mhalf_b])
        A("dve", lambda e: e.memset(bb[:], 0.0), w=[bb_b])
        arena_reset()
        bst, (bst_b,) = take("bst", [33, 2560], F32)
        bhi, (bhi_b,) = take("bhi", [33, 2560], BF16)
        blo, (blo_b,) = take("blo", [33, 2560], F32)
        A("dve", lambda e: e.memset(bst[0:33, :], 0.0), w=[bst_b])
        S.dma("sp", bst[0:1, :], brow_d, writes=[bst_b])
        S.dma("sp", bst[32:33, :], brow_d, writes=[bst_b])
        A("act", lambda e: e.activation(out=bhi[0:33, :], in_=bst[0:33, :], func=AF.Copy), r=[bst_b], w=[bhi_b])
        A("dve", lambda e: e.tensor_tensor(out=blo[0:33, :], in0=bst[0:33, :], in1=bhi[0:33, :], op=ALU.subtract),
          r=[bst_b, bhi_b], w=[blo_b])
        A("dve", lambda e: e.tensor_copy(out=bb[0:1, :], in_=bhi[0:1, :]), r=[bhi_b, bb_b], w=[bb_b])
        A("dve", lambda e: e.tensor_copy(out=bb[32:33, :], in_=blo[32:33, :]), r=[blo_b, bb_b], w=[bb_b])

        bandc = cstb[:, B_BC:B_BC + 1024].rearrange("p (v g t) -> p v g t", v=2, g=4)
        bandp = cstb[:, B_BP:B_BP + 512].rearrange("p (g t) -> p g t", g=4)
        sel = cstb[:, B_SEL:B_SEL + 128].rearrange("p (t g i) -> p t g i", t=2, g=4)
        coefI = cstb[:, B_CI:B_CI + 64].rearrange("p (g i) -> p g i", g=4)

        cnt = {"sm": 0, "sa": 0, "pp": 0}

        def ln_A1(it):
            i, rows = it["i"], it["rows"]
            Yi = Y[0:rows, i, :]
            s_ = cnt["sm"] % NSL; cnt["sm"] += 1
            it["s"] = s_
            smb = sm_b[s_]
            t = sm[0:rows, s_, :]
            A("dve", lambda e: e.bn_stats(out=t[:, 0:6], in_=Yi[:, 0:512]), r=[yb[i]], w=[smb])
            A("dve", lambda e: e.bn_stats(out=t[:, 6:12], in_=Yi[:, 512:1024]), r=[yb[i], smb], w=[smb])
            A("dve", lambda e: e.bn_aggr(out=t[:, 12:14], in_=t[:, 0:12]), r=[smb], w=[smb])
            A("dve", lambda e: e.tensor_scalar(out=t[:, 14:15], in0=t[:, 13:14], scalar1=EPS, scalar2=None, op0=ALU.add),
              r=[smb], w=[smb])
            A("pool", lambda e: e.tensor_tensor(out=t[:, 15:16], in0=t[:, 14:15], in1=mhalf[0:rows, :], op=ALU.pow),
              r=[smb, mhalf_b], w=[smb])

        def ln_A2(it):
            i, rows, s_ = it["i"], it["rows"], it["s"]
            Yi = Y[0:rows, i, :]
            smb = sm_b[s_]
            t = sm[0:rows, s_, :]
            A("dve", lambda e: e.scalar_tensor_tensor(out=t[:, 16:17], in0=t[:, 12:13], scalar=-1.0, in1=t[:, 15:16],
                                                      op0=ALU.mult, op1=ALU.mult), r=[smb], w=[smb])
            A("act", lambda e: e.activation(out=Yi, in_=Yi, func=AF.Identity, bias=t[:, 16:17], scale=t[:, 15:16]),
              r=[yb[i], smb], w=[yb[i]])

        def ln_A3(it):
            i, rows, ch = it["i"], it["rows"], it["ch"]
            Yi = Y[0:rows, i, :]
            A("dve", lambda e: e.tensor_tensor(out=Yi, in0=Yi, in1=lng[0:rows, :], op=ALU.mult), r=[yb[i], lng_b], w=[yb[i]])
            A("dve", lambda e: e.tensor_tensor(out=Yi, in0=Yi, in1=lnb[0:rows, :], op=ALU.add), r=[yb[i], lnb_b], w=[yb[i]])
            if i < 2:
                c0 = C_TM + ch * 2 + i
                A("dve", lambda e: e.tensor_scalar(out=Yi, in0=Yi, scalar1=cst[0:rows, c0:c0 + 1], scalar2=None, op0=ALU.mult),
                  r=[yb[i], cst_b], w=[yb[i]])

        def ln_B(it):
            i, rows = it["i"], it["rows"]
            Yi = Y[0:rows, i, :]
            if it["need_xt"]:
                rpair = it["rpair"]() if callable(it["rpair"]) else it["rpair"]
                R = pair(rpair)
                for k in range(8):
                    A("pe", lambda e, k=k: e.transpose(out=R[:, k * 128:k * 128 + rows], in_=Yi[:, k * 128:(k + 1) * 128],
                                                       identity=ident[0:rows, 0:rows]),
                      r=[yb[i], ident_b], w=[pair_b[rpair]])
                cx = i * 128
                A("act", lambda e: e.activation(out=XT[:, :, cx:cx + rows],
                                                in_=R.rearrange("p (k t) -> p k t", k=8)[:, :, 0:rows], func=AF.Copy),
                  r=[pair_b[rpair]], w=[xtb[i]])
            if it.get("post") is not None:
                it["post"]()

        lnq = []

        def ln_push(ch, i, rows, need_xt, rpair, post=None):
            it = dict(ch=ch, i=i, rows=rows, need_xt=need_xt, rpair=rpair, post=post, st=1)
            ln_A1(it)
            lnq.append(it)
            if len(lnq) >= 2 and lnq[-2]["st"] == 1:
                ln_A2(lnq[-2]); lnq[-2]["st"] = 2
            if len(lnq) >= 3 and lnq[-3]["st"] == 2:
                ln_A3(lnq[-3]); lnq[-3]["st"] = 3
            if len(lnq) >= 4:
                o = lnq.pop(0)
                ln_B(o)

        def ln_flush():
            while lnq:
                for o in lnq:
                    if o["st"] == 1:
                        ln_A2(o); o["st"] = 2
                    elif o["st"] == 2:
                        ln_A3(o); o["st"] = 3
                    elif o["st"] == 3:
                        ln_B(o); o["st"] = 4
                while lnq and lnq[0]["st"] == 4:
                    lnq.pop(0)

        def ln_core(ch, i, rows, need_xt, rpair, post=None):
            ln_push(ch, i, rows, need_xt, rpair, post)

        def ln_mix(ch, i, rows, mp, need_xt, rpair):
            Yi = Y[0:rows, i, :]
            A("dve", lambda e: e.scalar_tensor_tensor(out=Yi, in0=Yi, scalar=ALPHA, in1=pair(mp)[0:rows, :],
                                                      op0=ALU.mult, op1=ALU.add), r=[yb[i], pair_b[mp]], w=[yb[i]])
            ln_core(ch, i, rows, need_xt, rpair)

        def load_ln(g_d, b_d, l):
            S.dma("sp", lng[:], g_d[l:l + 1, :].partition_broadcast(128), writes=[lng_b])
            S.dma("sp", lnb[:], b_d[l:l + 1, :].partition_broadcast(128), writes=[lnb_b])

        def pool_layer(ch, a):
            has_s = (ch == 1)
            arena_reset()
            psc, (psc_b,) = take("psc", [128, D], F32)
            wpf, (wpf_b,) = take("wpf", [128, 8, 256], F32)
            wp, (wp_b,) = take("wp", [128, 8, 256], BF16)
            ybf, ybf_b = take("ybf", [128, D], BF16, nb=3)
            dT, dT_b = take("dT", [128, 8, 128], BF16, nb=3)
            spb, (spb_b,) = take("spb", [128, 2, D], BF16)
            xnb, (xnb_b,) = take("xnb", [128, D], BF16)
            wpf4 = wpf.rearrange("p (g k e) -> p g k e", g=4, k=2)
            wp4 = wp.rearrange("p (g k e) -> p g k e", g=4, k=2)
            wp3 = wp.rearrange("p (c e) -> p c e", c=8)
            ybf3 = ybf.rearrange("p (s d) -> p s d", s=3)
            dT4 = dT.rearrange("p (s c t) -> p s c t", s=3, c=8)
            spb3 = spb.rearrange("p (t d) -> p t d", t=2)
            S.dma("sp", psc, pool_scale[a:a + 1, :].partition_broadcast(128), writes=[psc_b])
            S.dma("sp", wpf.rearrange("p (c e) -> p c e", c=8), pool_w[a].rearrange("g (k p) e -> p (g k) e", p=128), writes=[wpf_b])
            for kk in range(2):
                A("dve", lambda e, kk=kk: e.tensor_tensor(out=wp4[:, :, kk, :], in0=wpf4[:, :, kk, :],
                                                          in1=psc.rearrange("p (g e) -> p g e", g=4), op=ALU.mult),
                  r=[wpf_b, psc_b], w=[wp_b])
            load_ln(lmg, lmb, a)
            if has_s:
                for t_ in range(2):
                    S.dma("pool", spb3[0:120, t_, :], spool[a, t_ * 120:(t_ + 1) * 120, :], writes=[spb_b])
            def stageA0(i):
                sl = i % 3
                A("act", lambda e, i=i, sl=sl: e.activation(out=ybf3[:, sl, :], in_=Y[:, i, :], func=AF.Copy),
                  r=[yb[i]], w=[ybf_b[sl]])
                if ch == 1 and i == NT - 1:
                    S.dma("sp", npool_p[a], Y[113:128, i, :], reads=[yb[i]])

            def stageA(i):
                sl = i % 3
                dp = i % 2
                P = pair(dp)
                var = 0 if (ch == 0 and i == 1) else 1
                for kc in range(8):
                    g = kc // 2
                    A("pe", lambda e, kc=kc, g=g, sl=sl, var=var, P=P, i=i: e.matmul(
                        P[:, kc * 128:(kc + 1) * 128], lhsT=ybf3[:, sl, kc * 128:(kc + 1) * 128], rhs=bandc[:, var, g, :],
                        start=True, stop=(i == 0)), r=[ybf_b[sl], cstb_b], w=[pair_b[dp]])
                    if i > 0:
                        sp_ = (i - 1) % 3
                        A("pe", lambda e, kc=kc, g=g, sp_=sp_, P=P: e.matmul(
                            P[:, kc * 128:(kc + 1) * 128], lhsT=ybf3[:, sp_, kc * 128:(kc + 1) * 128], rhs=bandp[:, g, :],
                            start=False, stop=True), r=[ybf_b[sp_], cstb_b], w=[pair_b[dp]])
                ds = i % 3
                A("act", lambda e, ds=ds, P=P: e.activation(out=dT4[:, ds, :, :], in_=P.rearrange("p (c t) -> p c t", c=8),
                                                            func=AF.Copy), r=[pair_b[dp]], w=[dT_b[ds]])

            def stageB(i):
                mp = 2 + (i % 2)
                ds = i % 3
                Q = pair(mp)
                for g in range(4):
                    for kk in range(2):
                        A("pe", lambda e, g=g, kk=kk, ds=ds, Q=Q: e.matmul(
                            Q[:, g * 256:(g + 1) * 256], lhsT=dT4[:, ds, 2 * g + kk, :], rhs=wp3[:, 2 * g + kk, :],
                            start=(kk == 0), stop=(kk == 1)), r=[dT_b[ds], wp_b], w=[pair_b[mp]])
                Yi = Y[:, i, :]
                A("dve", lambda e: e.scalar_tensor_tensor(out=Yi, in0=Yi, scalar=ALPHA, in1=Q[:, :],
                                                          op0=ALU.mult, op1=ALU.add), r=[yb[i], pair_b[mp]], w=[yb[i]])
                if i + 3 < NT:
                    stageA0(i + 3)
                ln_core(ch, i, 128, True, mp)

            for i_ in range(min(3, NT)):
                stageA0(i_)
            stageA(0)
            stageA(1)
            for i in range(NT):
                if i + 2 < NT:
                    stageA(i + 2)
                stageB(i)
            if has_s:
                i = NT
                S.dma("sp", npool_s[a, :, 14, :], Y[0:NS, i, :], reads=[yb[i]])
                S.dma("sp", npool_s[a, :, 0:14, :], spool[a].rearrange("(i r) d -> i r d", r=15)[:, 1:15, :], writes=[dram_misc_b])
                A("act", lambda e: e.activation(out=xnb[0:NS, :], in_=Y[0:NS, NT, :], func=AF.Copy), r=[yb[i]], w=[xnb_b])
                dp, mp = 0, 2
                P = pair(dp)
                for kc in range(8):
                    g = kc // 2
                    for t_ in range(2):
                        A("pe", lambda e, kc=kc, g=g, t_=t_: e.matmul(
                            P[:, kc * 128:kc * 128 + NS], lhsT=spb3[0:120, t_, kc * 128:(kc + 1) * 128], rhs=sel[0:120, t_, g, :],
                            start=(t_ == 0), stop=False), r=[spb_b, cstb_b], w=[pair_b[dp]])
                    A("pe", lambda e, kc=kc, g=g: e.matmul(
                        P[:, kc * 128:kc * 128 + NS], lhsT=xnb[0:NS, kc * 128:(kc + 1) * 128], rhs=coefI[0:NS, g, :],
                        start=False, stop=True), r=[xnb_b, cstb_b], w=[pair_b[dp]])
                A("act", lambda e: e.activation(out=dT4[:, 0, :, 0:NS], in_=P.rearrange("p (c t) -> p c t", c=8)[:, :, 0:NS],
                                                func=AF.Copy), r=[pair_b[dp]], w=[dT_b[0]])
                Q = pair(mp)
                for g in range(4):
                    for kk in range(2):
                        A("pe", lambda e, g=g, kk=kk: e.matmul(
                            Q[0:NS, g * 256:(g + 1) * 256], lhsT=dT4[:, 0, 2 * g + kk, 0:NS], rhs=wp3[:, 2 * g + kk, :],
                            start=(kk == 0), stop=(kk == 1)), r=[dT_b[0], wp_b], w=[pair_b[mp]])
                ln_mix(ch, i, NS, mp, True, dp)
            ln_flush()

        def ffn_layer(ch, l):
            has_s = (ch == 1)
            ntile = NT + (1 if has_s else 0)
            arena_reset()
            HT, _ = take("HT", [128, 6, TC], BF16)
            HT3 = HT.rearrange("p (j t) -> p j t", j=6)
            ht_b = [[Buf(f"ht{j}_{t}") for t in range(3)] for j in range(6)]
            for row in ht_b:
                for b in row:
                    for ob in ar["old"]:
                        toks = ([ob.w] if ob.w is not None else []) + list(ob.rs.values())
                        for t in toks:
                            key = ("e", t[1]) if t[0] == "e" else ("d", t[1])
                            cur = b.rs.get(key)
                            if cur is None or cur[2] < t[2]:
                                b.rs[key] = t
                    ar["bufs"].append(b)
            wout, wout_b = take("wout", [128, 6, D], BF16, nb=2)
            wout4 = wout.rearrange("p (s j d) -> p s j d", s=2, j=6)
            win, win_b = take("win", [128, 8, 256], BF16, nb=4)
            win4 = win.rearrange("p (s k c) -> p s k c", s=4, k=8)
            gsb, gsb_b2 = take("gsb", [128, 2 + TC], F32, nb=2)
            gsb3 = gsb.rearrange("p (s t) -> p s t", s=2)
            gsb_b = [[Buf(f"gs{s_}_{t}") for t in range(4)] for s_ in range(2)]
            for s_ in range(2):
                for b in gsb_b[s_]:
                    b.rs = dict(gsb_b2[s_].rs)
                    ar["bufs"].append(b)
            cc, cc_b = take("cc", [128, 512], F32, nb=2)
            cc3 = cc.rearrange("p (s t) -> p s t", s=2)
            ss, ss_b = take("ss", [128, 512], F32, nb=2)
            ss3 = ss.rearrange("p (s t) -> p s t", s=2)
            us, us_b = take("us", [128, 512], F32, nb=2)
            us3 = us.rearrange("p (s t) -> p s t", s=2)
            if has_s:
                scT, (scT_b,) = take("scT", [128, NJ, 32], F32)
                scT3 = scT.rearrange("p (j c) -> p j c", j=NJ)
                cs, (cs_b,) = take("cs", [128, DFF], F32)
            load_ln(lfg, lfb, l)
            for bk_ in range(4, 8):
                inherit(bank_b[bk_], [pair_b[bk_ // 2]])
            i_lo = 1 if l >= 2 else 0
            c_lo = 128 * i_lo
            for s_ in range(2):
                A("dve", lambda e, s_=s_: e.memset(gsb3[:, s_, c_lo:c_lo + 2], 0.0), w=[gsb_b[s_][0]])
            if has_s:
                S.dma("sp", cs[0:32, :], sconv[l], writes=[cs_b])
                S.dma("sp", nconv_s[l, :, 0, :], sconv[l].rearrange("(i r) f -> i r f", r=2)[:, 1, :], writes=[dram_misc_b])
                for j0 in range(0, NJ, 8):
                    nj = min(8, NJ - j0)
                    pp = (j0 // 8) % 2
                    for jj in range(nj):
                        A("pe", lambda e, j0=j0, jj=jj, pp=pp: e.transpose(
                            out=pair(pp)[:, jj * 32:(jj + 1) * 32], in_=cs[0:32, (j0 + jj) * 128:(j0 + jj + 1) * 128],
                            identity=ident[0:32, 0:32]), r=[cs_b, ident_b], w=[pair_b[pp]])
                    A("dve", lambda e, j0=j0, nj=nj, pp=pp: e.tensor_copy(
                        out=scT3[:, j0:j0 + nj, :], in_=pair(pp)[:, 0:nj * 32].rearrange("p (j c) -> p j c", j=nj)),
                      r=[pair_b[pp]], w=[scT_b])
            if i_lo == 0:
                TT = [(0, 512), (512, 512), (1024, 256 + (NS if has_s else 0))]
            else:
                TT = [(128, 512), (640, 512), (1152, 128 + (NS if has_s else 0))]
            winv = w_in[l].rearrange("(k p) c -> p k c", p=128)
            slab = 0
            u1 = 0
            accs = {"n": 0}
            pend = {"f": None}
            exn = {"n": 0}
            pend2 = {"f": None}
            for gi, (j0, j1) in enumerate(GROUPS):
                wo = gi % 2
                ng = j1 - j0
                def load_wout(wo=wo, ng=ng, j0=j0, j1=j1):
                    S.dma("pool", wout4[:, wo, 0:ng, :], w_out[l, j0 * 128:j1 * 128, :].rearrange("(j p) d -> p j d", p=128),
                          writes=[wout_b[wo]])
                if gi > 0:
                    load_wout()
                for j in range(j0, j1):
                    s_ = slab % 4; slab += 1
                    S.dma("pool", win4[:, s_, :, 0:128], winv[:, :, j * 128:(j + 1) * 128], writes=[win_b[s_]])
                    S.dma("pool", win4[:, s_, :, 128:256], winv[:, :, DFF + j * 128:DFF + (j + 1) * 128], writes=[win_b[s_]])
                    if gi == 0 and j == j0 + 2:
                        load_wout()
                    gs = j % 2
                    cw = C_CW + (l * NJ + j) * 3
                    cbc = C_CB + l * NJ + j
                    for tt, (t0, n) in enumerate(TT):
                        set_ = u1 % 2; u1 += 1
                        bg, bu = 4 + 2 * set_, 5 + 2 * set_
                        pg, pu = bank(bg), bank(bu)
                        xr = [xtb[ii] for ii in range(t0 // 128, min(NT, (t0 + n + 127) // 128))]
                        if has_s and tt == 2:
                            xr.append(xtb[NT])
                        for k in range(8):
                            A("pe", lambda e, k=k, s_=s_, pg=pg, t0=t0, n=n: e.matmul(
                                pg[:, 0:n], lhsT=win4[:, s_, k, 0:128], rhs=XT[:, k, t0:t0 + n], start=(k == 0), stop=(k == 7)),
                              r=[win_b[s_]] + xr, w=[bank_b[bg]])
                        for k in range(8):
                            A("pe", lambda e, k=k, s_=s_, pu=pu, t0=t0, n=n: e.matmul(
                                pu[:, 0:n], lhsT=win4[:, s_, k, 128:256], rhs=XT[:, k, t0:t0 + n], start=(k == 0), stop=(k == 7)),
                              r=[win_b[s_]] + xr, w=[bank_b[bu]])
                        npr = min(n, TP - t0)
                        cs_ = u1 % 2
                        A("act", lambda e, gs=gs, t0=t0, n=n, pg=pg: e.activation(
                            out=gsb3[:, gs, 2 + t0:2 + t0 + n], in_=pg[:, 0:n], func=AF.Copy),
                          r=[bank_b[bg]], w=[gsb_b[gs][1 + tt]])
                        A("act", lambda e, cs_=cs_, n=n, pg=pg, cw=cw, cbc=cbc: e.activation(
                            out=cc3[:, cs_, 0:n], in_=pg[:, 0:n], func=AF.Identity, bias=cst[:, cbc:cbc + 1],
                            scale=cst[:, cw + 2:cw + 3]), r=[bank_b[bg]] + [cst_b], w=[cc_b[cs_]])
                        grd = [gsb_b[gs][tt], gsb_b[gs][1 + tt]]
                        A("dve", lambda e, cs_=cs_, gs=gs, t0=t0, npr=npr, cw=cw: e.scalar_tensor_tensor(
                            out=cc3[:, cs_, 0:npr], in0=gsb3[:, gs, 1 + t0:1 + t0 + npr], scalar=cst[:, cw + 1:cw + 2],
                            in1=cc3[:, cs_, 0:npr], op0=ALU.mult, op1=ALU.add), r=grd + [cc_b[cs_], cst_b], w=[cc_b[cs_]])
                        A("dve", lambda e, cs_=cs_, gs=gs, t0=t0, npr=npr, cw=cw: e.scalar_tensor_tensor(
                            out=cc3[:, cs_, 0:npr], in0=gsb3[:, gs, t0:t0 + npr], scalar=cst[:, cw:cw + 1],
                            in1=cc3[:, cs_, 0:npr], op0=ALU.mult, op1=ALU.add), r=grd + [cc_b[cs_], cst_b], w=[cc_b[cs_]])
                        if n > npr:
                            A("dve", lambda e, cs_=cs_, j=j, npr=npr, n=n, cw=cw: e.scalar_tensor_tensor(
                                out=cc3[:, cs_, npr:n], in0=scT3[:, j, 1:32:2], scalar=cst[:, cw + 1:cw + 2],
                                in1=cc3[:, cs_, npr:n], op0=ALU.mult, op1=ALU.add), r=[scT_b, cc_b[cs_], cst_b], w=[cc_b[cs_]])
                            A("dve", lambda e, cs_=cs_, j=j, npr=npr, n=n, cw=cw: e.scalar_tensor_tensor(
                                out=cc3[:, cs_, npr:n], in0=scT3[:, j, 0:32:2], scalar=cst[:, cw:cw + 1],
                                in1=cc3[:, cs_, npr:n], op0=ALU.mult, op1=ALU.add), r=[scT_b, cc_b[cs_], cst_b], w=[cc_b[cs_]])
                        A("act", lambda e, cs_=cs_, n=n, pu=pu: e.activation(out=us3[:, cs_, 0:n], in_=pu[:, 0:n], func=AF.Copy),
                          r=[bank_b[bu]], w=[us_b[cs_]])

                        def second(cs_=cs_, n=n, j=j, j0=j0, t0=t0, tt=tt):
                            A("act", lambda e: e.activation(out=ss3[:, cs_, 0:n], in_=cc3[:, cs_, 0:n], func=AF.Silu),
                              r=[cc_b[cs_]], w=[ss_b[cs_]])
                            A("dve", lambda e: e.tensor_tensor(
                                out=HT3[:, j - j0, t0:t0 + n], in0=ss3[:, cs_, 0:n], in1=us3[:, cs_, 0:n], op=ALU.mult),
                              r=[ss_b[cs_], us_b[cs_]], w=[ht_b[j - j0][tt]])
                        if pend2["f"] is not None:
                            pend2["f"]()
                        pend2["f"] = second
                        if tt == 0 and pend["f"] is not None:
                            pend["f"](); pend["f"] = None
                    if has_s:
                        def export(j=j, gs=gs):
                            eb = exn["n"] % 4; exn["n"] += 1
                            A("pe", lambda e: e.transpose(out=bank(eb)[0:18, 0:128], in_=gsb3[:, gs, TP:TP + 18],
                                                          identity=ident[:, :]),
                              r=[gsb_b[gs][3], ident_b], w=bank_deps(eb))
                            A("act", lambda e: e.activation(out=cs[0:18, j * 128:(j + 1) * 128], in_=bank(eb)[0:18, 0:128],
                                                            func=AF.Copy), r=bank_deps(eb), w=[cs_b])
                        pend["f"] = export
                if pend2["f"] is not None:
                    pend2["f"](); pend2["f"] = None
                if pend["f"] is not None:
                    pend["f"](); pend["f"] = None
                last = (gi == len(GROUPS) - 1)
                for i in range(i_lo, ntile):
                    rows = 128 if i < NT else NS
                    c0 = i * 128
                    tt = min((i - i_lo) // 4, 2)
                    ap_ = accs["n"] % 2; accs["n"] += 1
                    P = pair(ap_)
                    for jj in range(ng):
                        for half in range(2):
                            A("pe", lambda e, jj=jj, half=half, rows=rows, c0=c0, P=P, wo=wo, ng=ng: e.matmul(
                                P[0:rows, half * 512:(half + 1) * 512], lhsT=HT3[:, jj, c0:c0 + rows],
                                rhs=wout4[:, wo, jj, half * 512:(half + 1) * 512], start=(jj == 0), stop=(jj == ng - 1)),
                              r=[ht_b[jj][tt], wout_b[wo]], w=[pair_b[ap_]])
                    Yi = Y[0:rows, i, :]
                    if gi == 0:
                        A("dve", lambda e, Yi=Yi, P=P, rows=rows: e.scalar_tensor_tensor(
                            out=Yi, in0=Yi, scalar=ALPHA, in1=P[0:rows, :], op0=ALU.mult, op1=ALU.add),
                          r=[yb[i], pair_b[ap_]], w=[yb[i]])
                    else:
                        A("dve", lambda e, Yi=Yi, P=P, rows=rows: e.tensor_tensor(out=Yi, in0=Yi, in1=P[0:rows, :], op=ALU.add),
                          r=[yb[i], pair_b[ap_]], w=[yb[i]])
                    if last:
                        def rp_fn():
                            v = accs["n"] % 2; accs["n"] += 1
                            return v
                        post = None
                        if l == NL - 1 and i == NT:
                            post = lambda: S.dma("sp", ys_out, Y[0:NS, NT, :], reads=[yb[NT]])
                        elif l == NL - 1 and i >= 2:
                            def post(i=i):
                                r0 = (ch * 8 + i - 2) * 128
                                S.dma("sp", y_out[r0:r0 + 128, :], Y[:, i, :], reads=[yb[i]])
                        ln_core(ch, i, rows, l in (1, 2), rp_fn, post)
            ln_flush()
            inherit(pair_b[2], [bank_b[4], bank_b[5]])
            inherit(pair_b[3], [bank_b[6], bank_b[7]])
            if has_s:
                S.dma("sp", nconv_p[l], cs[0:2, :], reads=[cs_b])
                S.dma("sp", nconv_s[l, :, 1, :], cs[2:18, :], reads=[cs_b])

        def kv_proj(ch):
            has_s = (ch == 1)
            arena_reset()
            wkt, (wkt_b,) = take("wkt", [128, 8, 512], BF16)
            wkt3 = wkt.rearrange("p (k c) -> p k c", k=8)
            wk2, (wk2_b,) = take("wk2", [128, 8, 4, 128], BF16)
            wk24 = wk2.rearrange("p (k h c) -> p k h c", k=8, h=4)
            kvs, (kvs_b,) = take("kvs", [128, 512], F32)
            wv = w_kv.rearrange("(k p) c -> p k c", p=128)
            for kh in range(4):
                for dup in range(2):
                    S.dma("pool", wk24[:, :, kh, dup * 64:(dup + 1) * 64], wv[:, :, kh * 64:(kh + 1) * 64], writes=[wk2_b])
            S.dma("pool", wkt3, wv, writes=[wkt_b])
            u = 0
            for kh in range(4):
                for (t0, n) in [(0, 512), (512, 512), (1024, 256)]:
                    bk = 4 + (u % 4); u += 1
                    xr = [xtb[ii] for ii in range(t0 // 128, (t0 + n) // 128)]
                    for k in range(8):
                        A("pe", lambda e, k=k, kh=kh, bk=bk, t0=t0, n=n: e.matmul(
                            bank(bk)[:, 0:n], lhsT=wk24[:, k, kh, :], rhs=XT[:, k, t0:t0 + n], start=(k == 0), stop=(k == 7)),
                          r=[wk2_b] + xr, w=bank_deps(bk))
                    A("act", lambda e, kh=kh, bk=bk, t0=t0, n=n: e.activation(
                        out=KT2[:, kh, t0:t0 + n], in_=bank(bk)[:, 0:n], func=AF.Identity,
                        bias=cst[:, C_BKT + kh:C_BKT + kh + 1], scale=1.0), r=bank_deps(bk) + [cst_b], w=[kt_b])
            ntile = NT + (1 if (has_s and not (KVDBG & 2)) else 0)
            for i in range(ntile):
                rows = 128 if i < NT else NS
                c0 = i * 128
                bk = i % 4
                for k in range(8):
                    A("pe", lambda e, k=k, bk=bk, rows=rows, c0=c0: e.matmul(
                        bank(bk)[0:rows, :], lhsT=XT[:, k, c0:c0 + rows], rhs=wkt3[:, k, :], start=(k == 0), stop=False),
                      r=[wkt_b, xtb[i]], w=bank_deps(bk))
                A("pe", lambda e, bk=bk, rows=rows: e.matmul(
                    bank(bk)[0:128, :], lhsT=ones33[:, 0:128], rhs=bb[:, 2048:2560], start=False, stop=True),
                  r=[ones_b, bb_b], w=bank_deps(bk))
                if i < NT:
                    A("act", lambda e, i=i, bk=bk: e.activation(out=V[:, i, :], in_=bank(bk)[:, 256:512], func=AF.Copy),
                      r=bank_deps(bk), w=[v_b[i]])
                if ch == 1 and i == NT - 1 and not (KVDBG & 4):
                    A("act", lambda e, bk=bk: e.activation(out=kvo[:], in_=bank(bk)[:, :], func=AF.Copy), r=bank_deps(bk), w=[kvo_b])
                    S.dma("sp", nk_p, kvo[:, 0:256], reads=[kvo_b])
                    S.dma("sp", nv_p, kvo[:, 256:512], reads=[kvo_b])
                if i == NT:
                    A("dve", lambda e, bk=bk: e.tensor_copy(out=kvs[0:NS, :], in_=bank(bk)[0:NS, :]), r=bank_deps(bk), w=[kvs_b])
                    S.dma("sp", nk_s[:, 127, :], kvs[0:NS, 0:256], reads=[kvs_b], writes=[nks_b])
                    S.dma("sp", nv_s[:, 127, :], kvs[0:NS, 256:512], reads=[kvs_b], writes=[nvs_b])
                    for (a0, a1) in ([] if (KVDBG & 1) else [(1, 33), (33, 65), (65, 97), (97, 128)]):
                        S.dma("sp", nk_s[:, a0 - 1:a1 - 1, :], skw[:, a0:a1, :], writes=[nks_b])
                        S.dma("sp", nv_s[:, a0 - 1:a1 - 1, :], svw[:, a0:a1, :], writes=[nvs_b])

        def attn_layer(ch, l):
            bi = l - 2
            has_s = (ch == 1)
            arena_reset()
            wq, (wq_b,) = take("wq", [128, 8, D], BF16)
            wq_at = ar["last_off"]
            wq3 = wq.rearrange("p (k c) -> p k c", k=8)
            wo_, (wo_b,) = take("wo", [128, 8, D], BF16)
            wo3 = wo_.rearrange("p (k c) -> p k c", k=8)
            qT, (qT_b,) = take("qTa", [128, 8, TP], BF16)
            qT3 = qT.rearrange("p (m t) -> p m t", m=8)
            en, en_b = take("en", [128, 4, 256], BF16, nb=2)
            en4 = en.rearrange("p (d h s) -> p d h s", d=2, h=4)
            ssb, ssb_b = take("ssb", [128, 4, 256], F32, nb=2)
            ssb4 = ssb.rearrange("p (d h s) -> p d h s", d=2, h=4)
            ee, ee_b = take("ee", [128, 4, 256], F32, nb=2)
            ee4 = ee.rearrange("p (d h s) -> p d h s", d=2, h=4)
            PT, PT_b = take("PT", [128, 4, 2, 128], BF16, nb=2)
            PT5 = PT.rearrange("p (d h f q) -> p d h f q", d=2, h=4, f=2)
            oT, (oT_b,) = take("oT", [128, 8, 128], BF16)
            oT3 = oT.rearrange("p (m t) -> p m t", m=8)
            S.dma("pool", wq3, w_q[bi].rearrange("(k p) c -> p k c", p=128), writes=[wq_b])
            S.dma("pool", wo3, w_o[bi].rearrange("(k p) c -> p k c", p=128), writes=[wo_b])
            load_ln(lmg, lmb, l)
            sinkb = cst[:, C_SINKB + bi * 16:C_SINKB + bi * 16 + 16]
            nsinkb = cst[:, C_NSINKB + bi * 16:C_NSINKB + bi * 16 + 16]
            uq = 0
            for m in range(8):
                for (t0, n) in [(128, 512), (640, 512), (1152, 128)]:
                    bk = 4 + (uq % 4); uq += 1
                    xr = [xtb[ii] for ii in range(t0 // 128, (t0 + n) // 128)]
                    for k in range(8):
                        A("pe", lambda e, k=k, m=m, bk=bk, t0=t0, n=n: e.matmul(
                            bank(bk)[:, 0:n], lhsT=wq3[:, k, m * 128:(m + 1) * 128], rhs=XT[:, k, t0:t0 + n],
                            start=(k == 0), stop=(k == 7)), r=[wq_b] + xr, w=bank_deps(bk))
                    cq = C_BQT + bi * 8 + m
                    if uq % 2 == 0:
                        A("act", lambda e, m=m, bk=bk, t0=t0, n=n, cq=cq: e.activation(
                            out=qT3[:, m, t0:t0 + n], in_=bank(bk)[:, 0:n], func=AF.Identity, bias=cst[:, cq:cq + 1], scale=1.0),
                          r=bank_deps(bk) + [cst_b], w=[qT_b])
                    else:
                        A("dve", lambda e, m=m, bk=bk, t0=t0, n=n, cq=cq: e.tensor_scalar(
                            out=qT3[:, m, t0:t0 + n], in0=bank(bk)[:, 0:n], scalar1=cst[:, cq:cq + 1], scalar2=None, op0=ALU.add),
                          r=bank_deps(bk) + [cst_b], w=[qT_b])

            def out_proj(rows, osrc, mp):
                MP = pair(mp)
                for half in range(2):
                    for m in range(8):
                        A("pe", lambda e, half=half, m=m: e.matmul(
                            MP[0:rows, half * 512:(half + 1) * 512], lhsT=osrc(m), rhs=wo3[:, m, half * 512:(half + 1) * 512],
                            start=(m == 0), stop=False), r=[oT_b, wo_b], w=[pair_b[mp]])
                    A("pe", lambda e, half=half: e.matmul(
                        MP[0:rows, half * 512:(half + 1) * 512], lhsT=ones33[:, 0:rows],
                        rhs=bb[:, bi * 1024 + half * 512:bi * 1024 + (half + 1) * 512], start=False, stop=True),
                      r=[ones_b, bb_b], w=[pair_b[mp]])

            def make_tile(i):
                OP = pair(1)
                v_ = 0 if i == 1 else (1 if i == 2 else 2)
                am0 = C_AM + (ch * 3 + v_) * 256
                SPp = pair(2)
                TPb = pair(3).bitcast(BF16)

                def scores(kh, i=i):
                    for sl4 in range(4):
                        hh = PERM[sl4]
                        h = 4 * kh + hh; m = h // 2; po = 64 * (h % 2)
                        A("pe", lambda e, hh=sl4, m=m, po=po, kh=kh, i=i: e.matmul(
                            SPp[:, hh * 256:(hh + 1) * 256], lhsT=qT3[po:po + 64, m, i * 128:(i + 1) * 128],
                            rhs=KT2[po:po + 64, kh, (i - 1) * 128:(i + 1) * 128], start=True, stop=True),
                          r=[qT_b, kt_b], w=[pair_b[2]])

                def c1(kh, am0=am0):
                    d = kh % 2
                    s_ = cnt["sa"] % 4; cnt["sa"] += 1
                    t = sa[:, s_, :]
                    sab = sa_b[s_]
                    sS = ssb4[:, d]; eE = ee4[:, d]
                    A("dve", lambda e: e.tensor_tensor(
                        out=sS, in0=SPp.rearrange("p (h s) -> p h s", h=4),
                        in1=cst[:, am0:am0 + 256].unsqueeze(1).broadcast_to([128, 4, 256]), op=ALU.add),
                      r=[pair_b[2], cst_b], w=[ssb_b[d]])
                    A("dve", lambda e: e.tensor_reduce(out=t[:, 0:4], in_=sS, axis=AX.X, op=ALU.max), r=[ssb_b[d]], w=[sab])
                    A("dve", lambda e: e.scalar_tensor_tensor(
                        out=t[:, 8:12], in0=t[:, 0:4], scalar=-SCALE, in1=nsinkb[:, 4 * kh:4 * kh + 4], op0=ALU.mult, op1=ALU.min),
                      r=[sab, cst_b], w=[sab])
                    A("dve", lambda e: e.tensor_tensor(out=t[:, 16:20], in0=sinkb[:, 4 * kh:4 * kh + 4], in1=t[:, 8:12],
                                                       op=ALU.add), r=[sab, cst_b], w=[sab])
                    for hh in range(4):
                        A("act", lambda e, hh=hh: e.activation(
                            out=eE[:, hh, :], in_=sS[:, hh, :], func=AF.Exp, bias=t[:, 8 + hh:9 + hh], scale=SCALE,
                            accum_out=t[:, 12 + hh:13 + hh]), r=[ssb_b[d], sab], w=[ee_b[d], sab])
                    A("act", lambda e: e.activation(out=t[:, 20:24], in_=t[:, 16:20], func=AF.Exp), r=[sab], w=[sab])
                    return (t, sab)

                def c2(kh, ts):
                    d = kh % 2
                    t, sab = ts
                    eE = ee4[:, d]; eN = en4[:, d]
                    A("dve", lambda e: e.tensor_tensor(out=t[:, 24:28], in0=t[:, 12:16], in1=t[:, 20:24], op=ALU.add),
                      r=[sab], w=[sab])
                    A("dve", lambda e: e.reciprocal(out=t[:, 28:32], in_=t[:, 24:28]), r=[sab], w=[sab])
                    A("dve", lambda e: e.tensor_tensor(
                        out=eN, in0=eE, in1=t[:, 28:32].unsqueeze(2).broadcast_to([128, 4, 256]), op=ALU.mult),
                      r=[ee_b[d], sab], w=[en_b[d]])

                def tp(kh, i=i):
                    d = kh % 2
                    eN = en4[:, d]
                    for hh in range(4):
                        for half in range(2):
                            A("pe", lambda e, hh=hh, half=half: e.transpose(
                                out=TPb[:, (hh * 2 + half) * 128:(hh * 2 + half + 1) * 128],
                                in_=eN[:, hh, half * 128:(half + 1) * 128], identity=identb[:, :]),
                              r=[en_b[d], identb_b], w=[pair_b[3]])
                    A("act", lambda e: e.activation(out=PT5[:, d].rearrange("p h f q -> p (h f q)"), in_=TPb[:, 0:1024], func=AF.Copy),
                      r=[pair_b[3]], w=[PT_b[d]])
                    for sl4 in range(4):
                        hh = PERM[sl4]
                        h = 4 * kh + hh; m = h // 2; po = 64 * (h % 2)
                        for half in range(2):
                            A("pe", lambda e, hh=sl4, half=half, m=m, po=po, kh=kh, i=i: e.matmul(
                                OP[po:po + 64, m * 128:(m + 1) * 128], lhsT=V[:, i - 1 + half, kh * 64:(kh + 1) * 64],
                                rhs=PT5[:, d, hh, half, :], start=(half == 0), stop=(half == 1)),
                              r=[PT_b[d], v_b[i - 1], v_b[i]], w=[pair_b[1]])

                def otcopy():
                    A("act", lambda e: e.activation(out=oT, in_=OP, func=AF.Copy), r=[pair_b[1]], w=[oT_b])

                def outproj():
                    out_proj(128, lambda m: oT3[:, m, :], 0)

                def lnpart():
                    ln_mix(ch, i, 128, 0, True, 3)

                return dict(scores=scores, c1=c1, c2=c2, tp=tp, otcopy=otcopy, outproj=outproj, lnpart=lnpart, st={})

            tls = [make_tile(i) for i in range(1, NT)]
            T0 = tls[0]
            T0["scores"](0); T0["st"][0] = T0["c1"](0)
            T0["scores"](1); T0["st"][1] = T0["c1"](1)
            prev = None
            for ti, T_ in enumerate(tls):
                N_ = tls[ti + 1] if ti + 1 < len(tls) else None
                st_ = T_["st"]
                T_["scores"](2)
                if prev is not None:
                    prev["outproj"]()
                T_["c2"](0, st_[0]); T_["tp"](0)
                st_[2] = T_["c1"](2)
                T_["scores"](3)
                T_["c2"](1, st_[1])
                if prev is not None:
                    prev["lnpart"]()
                T_["tp"](1)
                st_[3] = T_["c1"](3)
                if N_ is not None:
                    N_["scores"](0)
                T_["c2"](2, st_[2]); T_["tp"](2)
                if N_ is not None:
                    N_["st"][0] = N_["c1"](0)
                    N_["scores"](1)
                T_["c2"](3, st_[3]); T_["tp"](3)
                T_["otcopy"]()
                if N_ is not None:
                    N_["st"][1] = N_["c1"](1)
                prev = T_
            prev["outproj"]()
            prev["lnpart"]()

            if has_s:
                i = NT
                Ksb, (Ksb_b,) = take("Ksb", [128, NS, 256], BF16)
                Ksb3 = Ksb.rearrange("p (i d) -> p i d", i=NS)
                Vsb, (Vsb_b,) = take("Vsb", [128, NS, 256], BF16)
                Vsb3 = Vsb.rearrange("p (i d) -> p i d", i=NS)
                KsT, (KsT_b,) = take("KsT", [128, NS, 4, 128], BF16, at=wq_at, after=[wq_b])
                KsT4 = KsT.rearrange("p (i h s) -> p i h s", i=NS, h=4)
                qsT, (qsT_b,) = take("qsT", [128, 16, NS], BF16)
                qsT3 = qsT.rearrange("p (h i) -> p h i", h=16)
                STs, (STs_b,) = take("STs", [128, 256], F32)
                es, (es_b,) = take("es", [128, 2, 128], F32)
                es3 = es.rearrange("p (f s) -> p f s", f=2)
                PTs, (PTs_b,) = take("PTs", [128, 256], BF16)
                osT, (osT_b,) = take("osT", [128, 8, NS], BF16)
                osT3 = osT.rearrange("p (m i) -> p m i", m=8)
                S.dma("pool", Ksb3, nk_s.rearrange("i s d -> s i d"), reads=[nks_b], writes=[Ksb_b])
                S.dma("pool", Vsb3, nv_s.rearrange("i s d -> s i d"), reads=[nvs_b], writes=[Vsb_b])
                QS = pair(0)
                for h in range(16):
                    for k in range(8):
                        A("pe", lambda e, h=h, k=k: e.matmul(
                            QS[0:64, h * 16:(h + 1) * 16], lhsT=wq3[:, k, h * 64:(h + 1) * 64], rhs=XT[:, k, TP:TP + NS],
                            start=(k == 0), stop=(k == 7)), r=[wq_b, xtb[NT]], w=[pair_b[0]])
                c0 = C_BQH + bi * 16
                A("dve", lambda e, c0=c0: e.tensor_tensor(
                    out=qsT3[0:64, :, :], in0=QS[0:64, 0:256].rearrange("p (h i) -> p h i", h=16),
                    in1=cst[0:64, c0:c0 + 16].unsqueeze(2).broadcast_to([64, 16, NS]), op=ALU.add),
                  r=[pair_b[0], cst_b], w=[qsT_b])
                for i0 in range(0, NS, 4):
                    pp = 2 + (i0 // 4) % 2
                    Pb = pair(pp).bitcast(BF16)
                    for ii in range(4):
                        for kh in range(4):
                            A("pe", lambda e, ii=ii, kh=kh, i0=i0, Pb=Pb: e.transpose(
                                out=Pb[0:64, (ii * 4 + kh) * 128:(ii * 4 + kh + 1) * 128],
                                in_=Ksb3[:, i0 + ii, kh * 64:(kh + 1) * 64], identity=identb[:, :]),
                              r=[Ksb_b, identb_b], w=[pair_b[pp]])
                    A("act", lambda e, i0=i0, Pb=Pb: e.activation(
                        out=KsT4[0:64, i0:i0 + 4, :, :], in_=Pb[0:64, 0:2048].rearrange("p (i h s) -> p i h s", i=4, h=4),
                        func=AF.Copy), r=[pair_b[pp]], w=[KsT_b])
                ST = pair(1)
                for ii in range(NS):
                    for kh in range(4):
                        A("pe", lambda e, ii=ii, kh=kh: e.matmul(
                            ST[:, ii * 16 + 4 * kh:ii * 16 + 4 * kh + 4], lhsT=KsT4[0:64, ii, kh, :],
                            rhs=qsT3[0:64, 4 * kh:4 * kh + 4, ii], start=True, stop=True),
                          r=[KsT_b, qsT_b], w=[pair_b[1]])
                A("dve", lambda e: e.tensor_copy(out=STs, in_=ST[:, 0:256]), r=[pair_b[1]], w=[STs_b])
                S2 = pair(2)
                for hf in range(2):
                    A("pe", lambda e, hf=hf: e.transpose(out=S2[:, hf * 128:(hf + 1) * 128], in_=STs[:, hf * 128:(hf + 1) * 128],
                                                         identity=ident[:, :]), r=[STs_b, ident_b], w=[pair_b[2]])
                s_ = cnt["sa"] % 4; cnt["sa"] += 1
                t = sa[:, s_, :]
                sab = sa_b[s_]
                sk = cst[:, C_SINKS + bi * 2:C_SINKS + bi * 2 + 2]
                A("dve", lambda e: e.tensor_reduce(out=t[:, 0:2], in_=S2[:, 0:256].rearrange("p (f s) -> p f s", f=2),
                                                   axis=AX.X, op=ALU.max), r=[pair_b[2]], w=[sab])
                A("dve", lambda e: e.scalar_tensor_tensor(out=t[:, 4:6], in0=t[:, 0:2], scalar=SCALE, in1=sk,
                                                          op0=ALU.mult, op1=ALU.max), r=[sab, cst_b], w=[sab])
                A("dve", lambda e: e.tensor_scalar(out=t[:, 8:10], in0=t[:, 4:6], scalar1=-1.0, scalar2=None, op0=ALU.mult),
                  r=[sab], w=[sab])
                for hf in range(2):
                    A("act", lambda e, hf=hf: e.activation(
                        out=es3[:, hf, :], in_=S2[:, hf * 128:(hf + 1) * 128], func=AF.Exp, bias=t[:, 8 + hf:9 + hf], scale=SCALE,
                        accum_out=t[:, 12 + hf:13 + hf]), r=[pair_b[2], sab], w=[es_b, sab])
                A("dve", lambda e: e.tensor_tensor(out=t[:, 16:18], in0=sk, in1=t[:, 4:6], op=ALU.subtract), r=[sab, cst_b], w=[sab])
                A("act", lambda e: e.activation(out=t[:, 20:22], in_=t[:, 16:18], func=AF.Exp), r=[sab], w=[sab])
                A("dve", lambda e: e.tensor_tensor(out=t[:, 24:26], in0=t[:, 12:14], in1=t[:, 20:22], op=ALU.add), r=[sab], w=[sab])
                A("dve", lambda e: e.reciprocal(out=t[:, 28:30], in_=t[:, 24:26]), r=[sab], w=[sab])
                A("dve", lambda e: e.tensor_tensor(out=es3, in0=es3, in1=t[:, 28:30].unsqueeze(2).broadcast_to([128, 2, 128]),
                                                   op=ALU.mult), r=[es_b, sab], w=[es_b])
                P2 = pair(3)
                for hf in range(2):
                    A("pe", lambda e, hf=hf: e.transpose(out=P2[:, hf * 128:(hf + 1) * 128], in_=es3[:, hf, :], identity=ident[:, :]),
                      r=[es_b, ident_b], w=[pair_b[3]])
                A("act", lambda e: e.activation(out=PTs, in_=P2[:, 0:256], func=AF.Copy), r=[pair_b[3]], w=[PTs_b])
                OS = pair(1)
                OS3 = OS[:, 0:128].rearrange("p (m i) -> p m i", m=8)
                for ii in range(NS):
                    for kh in range(4):
                        for par in range(2):
                            A("pe", lambda e, ii=ii, kh=kh, par=par: e.matmul(
                                OS3[64 * par:64 * par + 64, 2 * kh:2 * kh + 2, ii], lhsT=Vsb3[:, ii, kh * 64:(kh + 1) * 64],
                                rhs=PTs[:, ii * 16 + 4 * kh + par:ii * 16 + 4 * kh + 4:2], start=True, stop=True),
                              r=[Vsb_b, PTs_b, STs_b], w=[pair_b[1]])
                A("act", lambda e: e.activation(out=osT, in_=OS[:, 0:128], func=AF.Copy), r=[pair_b[1]], w=[osT_b])
                oT_b_save = oT_b
                MP = pair(0)
                for half in range(2):
                    for m in range(8):
                        A("pe", lambda e, half=half, m=m: e.matmul(
                            MP[0:NS, half * 512:(half + 1) * 512], lhsT=osT3[:, m, :], rhs=wo3[:, m, half * 512:(half + 1) * 512],
                            start=(m == 0), stop=False), r=[osT_b, wo_b], w=[pair_b[0]])
                    A("pe", lambda e, half=half: e.matmul(
                        MP[0:128, half * 512:(half + 1) * 512], lhsT=ones33[:, 0:128],
                        rhs=bb[:, bi * 1024 + half * 512:bi * 1024 + (half + 1) * 512], start=False, stop=True),
                      r=[ones_b, bb_b], w=[pair_b[0]])
                ln_mix(ch, i, NS, 0, True, 3)
            ln_flush()

        stage = {"n": 0}

        def go():
            stage["n"] += 1
            return stop is None or stage["n"] <= stop

        for ch in range(2):
            if not go():
                break
            S.dma("sp", Y[:, 0:NT, :], xin[ch * TP:(ch + 1) * TP, :].rearrange("(t p) d -> p t d", p=128), writes=yb[0:NT])
            if ch == 1:
                S.dma("sp", Y[0:NS, NT, :], xs, writes=[yb[NT]])
            for l in range(NL):
                if go():
                    if l < 2:
                        pool_layer(ch, l)
                    else:
                        attn_layer(ch, l)
                if go():
                    ffn_layer(ch, l)
                if l == 1 and go():
                    kv_proj(ch)
        fin = yb + [kvo_b, nks_b, nvs_b, dram_misc_b] + ar["bufs"]
        if dbg:
            dby_b = Buf("dbgyb")
            S.dma("sp", dbgy.rearrange("t p d -> p t d"), Y[:, :, :], reads=yb, writes=[dby_b])
            S.dma("pool", dbgx, XT[:, :, :].rearrange("p k t -> p (k t)"), reads=xtb, writes=[dby_b])
            S.dma("pool", dbgk, KT2[:, :, :].rearrange("p k t -> p (k t)"), reads=[kt_b], writes=[dby_b])
            S.dma("pool", dbgv, V[:, :, :].rearrange("p k t -> p (k t)"), reads=v_b, writes=[dby_b])
            fin = fin + [dby_b]
        S.finalize(st, fin)
    return nc, S


POOL_WINDOWS = (2, 4, 8, 16)


def _consts_for_core(c, inp):
    qd = c % 4
    f32 = np.float32
    cst = np.zeros((128, NCST), f32)
    cw = np.asarray(inp["ffn_conv_w"], f32)
    cbv = np.asarray(inp["ffn_conv_b"], f32)
    cst[:, C_CW:C_CW + 264] = cw.reshape(NL, 3, NJ, 128).transpose(3, 0, 2, 1).reshape(128, 264)
    cst[:, C_CB:C_CB + 88] = cbv.reshape(NL, NJ, 128).transpose(2, 0, 1).reshape(128, 88)
    bq = np.asarray(inp["attn_b_q"], f32)
    cst[:, C_BQT:C_BQT + 16] = bq.reshape(2, 8, 128).transpose(2, 0, 1).reshape(128, 16)
    cst[0:64, C_BQH:C_BQH + 32] = bq.reshape(2, 16, 64).transpose(2, 0, 1).reshape(64, 32)
    bkv = np.asarray(inp["b_kv"], f32)
    bk = bkv[:256].reshape(4, 64)
    cst[:, C_BKT:C_BKT + 4] = np.concatenate([bk.T, bk.T], axis=0)
    sinks = np.asarray(inp["attn_sinks"], f32)
    sperm = sinks.reshape(2, 4, 4)[:, :, PERM].reshape(1, 32)
    cst[:, C_SINKB:C_SINKB + 32] = np.broadcast_to(sperm, (128, 32))
    cst[:, C_NSINKB:C_NSINKB + 32] = np.broadcast_to(-sperm, (128, 32))
    pidx = np.arange(128)
    for bi in range(2):
        for hf in range(2):
            cst[:, C_SINKS + bi * 2 + hf] = sinks[bi, pidx % 16]

    def real(ch, t, r):
        blk = 16 * qd + 8 * ch - 1 + t
        return (blk * 128 + r) >= 112

    r = np.arange(128)
    for ch in range(2):
        for i in range(2):
            cst[:, C_TM + ch * 2 + i] = real(ch, i, r).astype(f32)
    q = np.arange(128)[:, None]
    j = np.arange(256)[None, :]
    band = (q < j) & (j <= q + 128)
    for ch in range(2):
        for v in range(3):
            if v == 2:
                ok = band
            else:
                ti = 1 + v
                kr = np.where(j < 128, real(ch, ti - 1, j % 128), real(ch, ti, j % 128))
                ok = band & kr
            cst[:, C_AM + (ch * 3 + v) * 256:C_AM + (ch * 3 + v + 1) * 256] = np.where(ok, 0.0, NEG).astype(f32)

    cstb = np.zeros((128, NCSTB), f32)
    s = np.arange(128)[:, None]
    t = np.arange(128)[None, :]
    bc = np.zeros((128, 2, 4, 128), f32)
    bp = np.zeros((128, 4, 128), f32)
    for g, w in enumerate(POOL_WINDOWS):
        inwin = (s > t - w) & (s <= t)
        gen = inwin.astype(f32) / w - (s == t).astype(f32)
        bc[:, 1, g, :] = gen
        if qd == 0:
            tseq = t - 112
            cnt = np.where(tseq >= 0, np.minimum(w, tseq + 1), w).astype(f32)
            bc[:, 0, g, :] = inwin.astype(f32) / cnt - (s == t).astype(f32)
        else:
            bc[:, 0, g, :] = gen
        bp[:, g, :] = ((s > 128 + t - w).astype(f32)) / w
    cstb[:, B_BC:B_BC + 1024] = bc.reshape(128, 1024)
    cstb[:, B_BP:B_BP + 512] = bp.reshape(128, 512)
    sel = np.zeros((128, 2, 4, 16), f32)
    ci = np.zeros((128, 4, 16), f32)
    for g, w in enumerate(POOL_WINDOWS):
        for p in range(120):
            rr = p % 15
            if rr >= 16 - w:
                for t_ in range(2):
                    sel[p, t_, g, t_ * 8 + p // 15] = 1.0 / w
        for p in range(16):
            ci[p, g, p] = 1.0 / w - 1.0
    cstb[:, B_SEL:B_SEL + 128] = sel.reshape(128, 128)
    cstb[:, B_CI:B_CI + 64] = ci.reshape(128, 64)
    return cst, cstb


_NC_CACHE = {}


def kernel(**inputs):
    f32 = np.float32
    inp = {k: np.asarray(v) for k, v in inputs.items()}
    xp = inp["x_prompt"].astype(f32, copy=False)
    meta = inp["meta_tokens"].astype(f32, copy=False)
    n = 8
    if "nc" not in _NC_CACHE:
        _NC_CACHE["nc"] = build_nc()[0]
    nc = _NC_CACHE["nc"]
    brow = np.concatenate([inp["attn_b_o"][0], inp["attn_b_o"][1], inp["b_kv"]]).astype(f32).reshape(1, 2560)
    shared = {
        "pool_w": np.ascontiguousarray(inp["pool_w"], f32), "pool_scale": np.ascontiguousarray(inp["pool_scale"], f32),
        "w_kv": np.ascontiguousarray(inp["w_kv"], f32), "w_q": np.ascontiguousarray(inp["attn_w_q"], f32),
        "w_o": np.ascontiguousarray(inp["attn_w_o"], f32), "w_in": np.ascontiguousarray(inp["ffn_w_in"], f32),
        "w_out": np.ascontiguousarray(inp["ffn_w_out"], f32),
        "lmg": np.ascontiguousarray(inp["ln_mix_g"], f32), "lmb": np.ascontiguousarray(inp["ln_mix_b"], f32),
        "lfg": np.ascontiguousarray(inp["ln_ffn_g"], f32), "lfb": np.ascontiguousarray(inp["ln_ffn_b"], f32),
        "brow": brow,
    }
    in_maps = []
    for c in range(n):
        b, qd = c // 4, c % 4
        xin = np.zeros((2, NT, 128, D), f32)
        for ch in range(2):
            for i in range(NT):
                blk = 16 * qd + 8 * ch - 1 + i
                if blk >= 1:
                    xin[ch, i] = xp[b, (blk - 1) * 128:blk * 128]
                elif blk == 0:
                    xin[ch, i, 112:128] = meta
        cst, cstb = _consts_for_core(c, inp)
        sl = slice(NS * c, NS * (c + 1))
        m = dict(shared)
        m.update({
            "xin": xin.reshape(2 * NT * 128, D),
            "xs": np.ascontiguousarray(inp["x_sample"][sl, 0, :], f32),
            "spool": np.ascontiguousarray(inp["state_pool"][:, sl], f32).reshape(2, 240, D),
            "sconv": np.ascontiguousarray(inp["state_conv"][:, sl], f32).reshape(NL, 32, DFF),
            "skw": np.ascontiguousarray(inp["state_k_win"][sl], f32).reshape(NS, 128, 256),
            "svw": np.ascontiguousarray(inp["state_v_win"][sl], f32).reshape(NS, 128, 256),
            "cst": cst, "cstb": cstb,
        })
        in_maps.append(m)
    res = run_bass_kernel_spmd(nc, in_maps, core_ids=list(range(n)))
    R = res.results
    y_prompt = np.zeros((2, 8192, D), f32)
    y_sample = np.zeros((128, 1, D), f32)
    npp = np.zeros((2, 2, 15, D), f32); nps = np.zeros((2, 128, 15, D), f32)
    ncp = np.zeros((NL, 2, 2, DFF), f32); ncs = np.zeros((NL, 128, 2, DFF), f32)
    nkp = np.zeros((2, 128, 4, 64), f32); nvp = np.zeros((2, 128, 4, 64), f32)
    nks = np.zeros((128, 128, 4, 64), f32); nvs = np.zeros((128, 128, 4, 64), f32)
    for c in range(n):
        b, qd = c // 4, c % 4
        r = R[c]
        sl = slice(NS * c, NS * (c + 1))
        y_prompt[b, 2048 * qd:2048 * (qd + 1)] = r["y_out"]
        y_sample[sl, 0] = r["ys_out"]
        nps[:, sl] = r["npool_s"]
        ncs[:, sl] = r["nconv_s"]
        nks[sl] = r["nk_s"].reshape(NS, 128, 4, 64)
        nvs[sl] = r["nv_s"].reshape(NS, 128, 4, 64)
        if qd == 3:
            npp[:, b] = r["npool_p"]
            ncp[:, b] = r["nconv_p"]
            nkp[b] = r["nk_p"].reshape(128, 4, 64)
            nvp[b] = r["nv_p"].reshape(128, 4, 64)
    return (y_prompt, y_sample, npp, nps, ncp, ncs, nkp, nvp, nks, nvs)
```

```python
import contextlib
import numpy as np
import concourse.bass as bass
import concourse.mybir as mybir
from concourse.bass_utils import run_bass_kernel_spmd

F32 = mybir.dt.float32
BF16 = mybir.dt.bfloat16
AF = mybir.ActivationFunctionType
ALU = mybir.AluOpType
AX = mybir.AxisListType


class Buf:
    __slots__ = ("name", "w", "rs", "dsem", "dcnt", "slot")

    def __init__(self, name):
        self.name = name
        self.w = None
        self.rs = {}
        self.dsem = None
        self.dcnt = 0
        self.slot = self


class Sched:
    ENG = ["pe", "act", "dve", "pool", "sp"]

    def __init__(self, nc):
        self.nc = nc
        self.ops = {e: [] for e in self.ENG}
        self.waited = {e: {} for e in self.ENG}
        self.dbufs = []

    def _deps(self, eng, reads, writes):
        best = {}
        idx = len(self.ops[eng])

        def add(tok):
            if tok is None:
                return
            if tok[0] == "e":
                _, pe, pidx = tok
                if pe == eng and eng == "pe":
                    return
                key = ("e", pe)
                v = pidx
            else:
                _, b, v = tok
                key = ("d", b)
            if best.get(key, -1) < v:
                best[key] = v

        for b in reads:
            add(b.w)
        for b in writes:
            add(b.w)
            for t in b.rs.values():
                add(t)
        waits = []
        for key, v in best.items():
            if self.waited[eng].get(key, -1) >= v:
                continue
            self.waited[eng][key] = v
            waits.append((key, v))
        return waits

    def _commit(self, tok, reads, writes):
        for b in writes:
            b.w = tok
            b.rs = {}
        for b in reads:
            if b in writes:
                continue
            if tok[0] == "e":
                b.rs[("e", tok[1])] = tok
            else:
                b.rs[("d", tok[1])] = tok

    def op(self, eng, fn, reads=(), writes=()):
        waits = self._deps(eng, reads, writes)
        idx = len(self.ops[eng])
        self.ops[eng].append(dict(fn=fn, waits=waits, sig=False, dma=None))
        tok = ("e", eng, idx)
        self._commit(tok, reads, writes)
        return tok

    def dma(self, eng, out, in_, reads=(), writes=(), **kw):
        waits = self._deps(eng, reads, writes)
        pb = (writes[0] if writes else reads[0]).slot
        if pb.dsem is None:
            pb.dsem = True
            self.dbufs.append(pb)
        pb.dcnt += 16
        self.ops[eng].append(dict(
            fn=lambda e: e.dma_start(out=out, in_=in_, **kw), waits=waits, sig=False, dma=pb))
        tok = ("d", pb, pb.dcnt)
        self._commit(tok, reads, writes)
        return tok

    def finalize(self, stack, final_bufs=()):
        nc = self.nc
        waits = self._deps("sp", list(final_bufs), list(final_bufs))
        self.ops["sp"].append(dict(fn=None, waits=waits, sig=False, dma=None))
        for e in self.ENG:
            for rec in self.ops[e]:
                for key, v in rec["waits"]:
                    if key[0] == "e":
                        self.ops[key[1]][v]["sig"] = True
        cum = {}
        for e in self.ENG:
            c = 0
            arr = []
            for rec in self.ops[e]:
                if rec["sig"]:
                    c += 1
                arr.append(c)
            cum[e] = arr
        esem = {e: stack.enter_context(nc.semaphore("s_" + e)) for e in self.ENG}
        for b in self.dbufs:
            b.dsem = stack.enter_context(nc.semaphore("d_" + b.name))
        engobj = {"pe": "tensor", "act": "scalar", "dve": "vector", "pool": "gpsimd", "sp": "sync"}

        def emit(name, e):
            for rec in self.ops[name]:
                for key, v in rec["waits"]:
                    if key[0] == "e":
                        e.wait_ge(esem[key[1]], cum[key[1]][v])
                    else:
                        e.wait_ge(key[1].dsem, v)
                if rec["fn"] is None:
                    continue
                ins = rec["fn"](e)
                if rec["dma"] is not None:
                    ins.then_inc(rec["dma"].dsem, 16)
                elif rec["sig"]:
                    ins.then_inc(esem[name], 1)

        block = stack.enter_context(nc.Block())
        for name in self.ENG:
            getattr(block, engobj[name])(lambda e, name=name: emit(name, e))
        self.stats = {e: len(self.ops[e]) for e in self.ENG}
        self.nsem = 5 + len(self.dbufs)

D = 1024; DFF = 2816; NJ = 22; NL = 4; NT = 10; TP = NT * 128; NS = 16; TC = TP + NS
ALPHA = (2.0 * 4) ** 0.25; EPS = 1e-5; SCALE = 0.125; NEG = -1e30
GROUPS = [(0, 6), (6, 12), (12, 17), (17, 22)]
C_CW = 0; C_CB = 264; C_BQT = 352; C_BQH = 368; C_BKT = 400; C_SINKB = 404; C_SINKS = 436; C_TM = 440; C_AM = 444; C_NSINKB = 1980; NCST = 2012
B_BC = 0; B_BP = 1024; B_SEL = 1536; B_CI = 1664; NCSTB = 1728
U8 = mybir.dt.uint8
PERM = [0, 2, 1, 3]
import os
KVDBG = int(os.environ.get('KVDBG', '0'))
ARENA_BYTES = 100 * 1024


def build_nc(stop=None, dbg=False):
    nc = bass.Bass("TRN2", target_bir_lowering=False)

    def din(name, shape):
        return nc.dram_tensor(name, list(shape), F32, kind="ExternalInput").ap()

    def dout(name, shape):
        return nc.dram_tensor(name, list(shape), F32, kind="ExternalOutput").ap()

    xin = din("xin", [2 * NT * 128, D]); xs = din("xs", [NS, D])
    spool = din("spool", [2, 240, D]); sconv = din("sconv", [NL, 32, DFF])
    skw = din("skw", [NS, 128, 256]); svw = din("svw", [NS, 128, 256])
    pool_w = din("pool_w", [2, 4, 256, 256]); pool_scale = din("pool_scale", [2, D])
    w_kv = din("w_kv", [D, 512])
    w_q = din("w_q", [2, D, D]); w_o = din("w_o", [2, D, D])
    w_in = din("w_in", [NL, D, 2 * DFF]); w_out = din("w_out", [NL, DFF, D])
    lmg = din("lmg", [NL, D]); lmb = din("lmb", [NL, D]); lfg = din("lfg", [NL, D]); lfb = din("lfb", [NL, D])
    cst_d = din("cst", [128, NCST]); cstb_d = din("cstb", [128, NCSTB]); brow_d = din("brow", [1, 2560])

    y_out = dout("y_out", [2 * 8 * 128, D]); ys_out = dout("ys_out", [NS, D])
    npool_p = dout("npool_p", [2, 15, D]); npool_s = dout("npool_s", [2, NS, 15, D])
    nconv_p = dout("nconv_p", [NL, 2, DFF]); nconv_s = dout("nconv_s", [NL, NS, 2, DFF])
    nk_p = dout("nk_p", [128, 256]); nv_p = dout("nv_p", [128, 256])
    nk_s = dout("nk_s", [NS, 128, 256]); nv_s = dout("nv_s", [NS, 128, 256])

    if dbg:
        dbgy = dout("dbgy", [NT + 1, 128, D]); dbgx = dout("dbgx", [128, 8 * TC])
        dbgk = dout("dbgk", [128, 4 * TP]); dbgv = dout("dbgv", [128, NT * 256])
    S = Sched(nc)
    with contextlib.ExitStack() as st:
        def sbt(name, shape, dt):
            return st.enter_context(nc.sbuf_tensor("sb_" + name, list(shape), dt))

        def A(eng, fn, r=(), w=()):
            return S.op(eng, fn, reads=list(r), writes=list(w))

        cst = sbt("cst", [128, NCST], F32); cst_b = Buf("cst")
        cstb = sbt("cstb", [128, NCSTB], BF16); cstb_b = Buf("cstb")
        Y = sbt("Y", [128, NT + 1, D], F32); yb = [Buf(f"y{i}") for i in range(NT + 1)]
        for i_ in range(1, NT):
            yb[i_].slot = yb[0]
        XT = sbt("XT", [128, 8, TC], BF16); xtb = [Buf(f"xt{i}") for i in range(NT + 1)]
        lng = sbt("lng", [128, D], F32); lng_b = Buf("lng")
        lnb = sbt("lnb", [128, D], F32); lnb_b = Buf("lnb")
        KT2 = sbt("KT2", [128, 4, TP], BF16); kt_b = Buf("kt2")
        V = sbt("V", [128, NT, 256], BF16); v_b = [Buf(f"v{i}") for i in range(NT)]
        ident = sbt("ident", [128, 128], F32); ident_b = Buf("ident")
        identb = sbt("identb", [128, 128], BF16); identb_b = Buf("identb")
        ones33 = sbt("ones33", [33, 128], BF16); ones_b = Buf("ones33")
        bb = sbt("bb", [33, 2560], BF16); bb_b = Buf("bb")
        mhalf = sbt("mhalf", [128, 1], F32); mhalf_b = Buf("mhalf")
        NSL = 4
        sm = sbt("sm", [128, NSL, 24], F32); sm_b = [Buf(f"sm{i}") for i in range(NSL)]
        sa = sbt("sa", [128, 4, 48], F32); sa_b = [Buf(f"sa{i}") for i in range(4)]
        kvo = sbt("kvo", [128, 512], F32); kvo_b = Buf("kvo")
        arena = sbt("arena", [128, ARENA_BYTES], U8)
        PS = st.enter_context(nc.psum_tensor("PS", [128, 8, 512], F32))
        pair_b = [Buf(f"pp{i}") for i in range(4)]
        bank_b = [Buf(f"pb{i}") for i in range(8)]

        def pair(p):
            return PS[:, 2 * p:2 * p + 2, :].rearrange("p a b -> p (a b)")

        def bank(bk):
            return PS[:, bk, :]

        def bank_deps(bk):
            return [pair_b[bk // 2], bank_b[bk]]

        nks_b = Buf("nks"); nvs_b = Buf("nvs"); dram_misc_b = Buf("dmisc")

        ar = {"off": 0, "bufs": [], "old": []}
        slots = {}

        def arena_reset():
            ar["old"] = ar["old"][-200:] + ar["bufs"] if False else ar["bufs"]
            ar["bufs"] = []
            ar["off"] = 0

        def take(name, shape, dt, nb=1, at=None, after=()):
            esz = 4 if dt == F32 else 2
            free = 1
            for s_ in shape[1:]:
                free *= s_
            nbytes = free * esz * nb
            nbytes = (nbytes + 63) // 64 * 64
            if at is None:
                assert ar["off"] + nbytes <= ARENA_BYTES, (name, ar["off"], nbytes)
                ar["last_off"] = ar["off"]
                v = arena[:, ar["off"]:ar["off"] + nbytes].bitcast(dt)
                ar["off"] += nbytes
            else:
                v = arena[:, at:at + nbytes].bitcast(dt)
            v = v[:, 0:free * nb]
            bufs = []
            for i in range(nb):
                b = Buf(f"{name}{i}")
                b.slot = slots.setdefault(b.name, b)
                for ob in list(ar["old"]) + list(after):
                    toks = ([ob.w] if ob.w is not None else []) + list(ob.rs.values())
                    for t in toks:
                        key = ("e", t[1]) if t[0] == "e" else ("d", t[1])
                        cur = b.rs.get(key)
                        if cur is None or cur[2] < t[2]:
                            b.rs[key] = t
                bufs.append(b)
                ar["bufs"].append(b)
            return v, bufs

        def inherit(dst, srcs):
            for ob in srcs:
                toks = ([ob.w] if ob.w is not None else []) + list(ob.rs.values())
                for t in toks:
                    key = ("e", t[1]) if t[0] == "e" else ("d", t[1])
                    cur = dst.rs.get(key)
                    if cur is None or cur[2] < t[2]:
                        dst.rs[key] = t

        def view(v, pat, **kw):
            return v.rearrange(pat, **kw)

        S.dma("sp", cst[:], cst_d, writes=[cst_b])
        S.dma("pool", cstb[:], cstb_d, writes=[cstb_b])
        A("dve", lambda e: e.memset(ident[:], 0.0), w=[ident_b])
        A("pool", lambda e: e.affine_select(out=ident[:], in_=ident[:], compare_op=ALU.not_equal, fill=1.0,
                                            base=0, pattern=[[-1, 128]], channel_multiplier=1),
          r=[ident_b], w=[ident_b])
        A("act", lambda e: e.activation(out=identb[:], in_=ident[:], func=AF.Copy), r=[ident_b], w=[identb_b])
        A("dve", lambda e: e.memset(ones33[:], 1.0), w=[ones_b])
        A("dve", lambda e: e.memset(mhalf[:], -0.5), w=[mhalf_b])
        A("dve", lambda e: e.memset(bb[:], 0.0), w=[bb_b])
        arena_reset()
        bst, (bst_b,) = take("bst", [33, 2560], F32)
        bhi, (bhi_b,) = take("bhi", [33, 2560], BF16)
        blo, (blo_b,) = take("blo", [33, 2560], F32)
        A("dve", lambda e: e.memset(bst[0:33, :], 0.0), w=[bst_b])
        S.dma("sp", bst[0:1, :], brow_d, writes=[bst_b])
        S.dma("sp", bst[32:33, :], brow_d, writes=[bst_b])
        A("act", lambda e: e.activation(out=bhi[0:33, :], in_=bst[0:33, :], func=AF.Copy), r=[bst_b], w=[bhi_b])
        A("dve", lambda e: e.tensor_tensor(out=blo[0:33, :], in0=bst[0:33, :], in1=bhi[0:33, :], op=ALU.subtract),
          r=[bst_b, bhi_b], w=[blo_b])
        A("dve", lambda e: e.tensor_copy(out=bb[0:1, :], in_=bhi[0:1, :]), r=[bhi_b, bb_b], w=[bb_b])
        A("dve", lambda e: e.tensor_copy(out=bb[32:33, :], in_=blo[32:33, :]), r=[blo_b, bb_b], w=[bb_b])

        bandc = cstb[:, B_BC:B_BC + 1024].rearrange("p (v g t) -> p v g t", v=2, g=4)
        bandp = cstb[:, B_BP:B_BP + 512].rearrange("p (g t) -> p g t", g=4)
        sel = cstb[:, B_SEL:B_SEL + 128].rearrange("p (t g i) -> p t g i", t=2, g=4)
        coefI = cstb[:, B_CI:B_CI + 64].rearrange("p (g i) -> p g i", g=4)

        cnt = {"sm": 0, "sa": 0, "pp": 0}

        def ln_A1(it):
            i, rows = it["i"], it["rows"]
            Yi = Y[0:rows, i, :]
            s_ = cnt["sm"] % NSL; cnt["sm"] += 1
            it["s"] = s_
            smb = sm_b[s_]
            t = sm[0:rows, s_, :]
            A("dve", lambda e: e.bn_stats(out=t[:, 0:6], in_=Yi[:, 0:512]), r=[yb[i]], w=[smb])
            A("dve", lambda e: e.bn_stats(out=t[:, 6:12], in_=Yi[:, 512:1024]), r=[yb[i], smb], w=[smb])
            A("dve", lambda e: e.bn_aggr(out=t[:, 12:14], in_=t[:, 0:12]), r=[smb], w=[smb])
            A("dve", lambda e: e.tensor_scalar(out=t[:, 14:15], in0=t[:, 13:14], scalar1=EPS, scalar2=None, op0=ALU.add),
              r=[smb], w=[smb])
            A("pool", lambda e: e.tensor_tensor(out=t[:, 15:16], in0=t[:, 14:15], in1=mhalf[0:rows, :], op=ALU.pow),
              r=[smb, mhalf_b], w=[smb])

        def ln_A2(it):
            i, rows, s_ = it["i"], it["rows"], it["s"]
            Yi = Y[0:rows, i, :]
            smb = sm_b[s_]
            t = sm[0:rows, s_, :]
            A("dve", lambda e: e.scalar_tensor_tensor(out=t[:, 16:17], in0=t[:, 12:13], scalar=-1.0, in1=t[:, 15:16],
                                                      op0=ALU.mult, op1=ALU.mult), r=[smb], w=[smb])
            A("act", lambda e: e.activation(out=Yi, in_=Yi, func=AF.Identity, bias=t[:, 16:17], scale=t[:, 15:16]),
              r=[yb[i], smb], w=[yb[i]])

        def ln_A3(it):
            i, rows, ch = it["i"], it["rows"], it["ch"]
            Yi = Y[0:rows, i, :]
            A("dve", lambda e: e.tensor_tensor(out=Yi, in0=Yi, in1=lng[0:rows, :], op=ALU.mult), r=[yb[i], lng_b], w=[yb[i]])
            A("dve", lambda e: e.tensor_tensor(out=Yi, in0=Yi, in1=lnb[0:rows, :], op=ALU.add), r=[yb[i], lnb_b], w=[yb[i]])
            if i < 2:
                c0 = C_TM + ch * 2 + i
                A("dve", lambda e: e.tensor_scalar(out=Yi, in0=Yi, scalar1=cst[0:rows, c0:c0 + 1], scalar2=None, op0=ALU.mult),
                  r=[yb[i], cst_b], w=[yb[i]])

        def ln_B(it):
            i, rows = it["i"], it["rows"]
            Yi = Y[0:rows, i, :]
            if it["need_xt"]:
                rpair = it["rpair"]() if callable(it["rpair"]) else it["rpair"]
                R = pair(rpair)
                for k in range(8):
                    A("pe", lambda e, k=k: e.transpose(out=R[:, k * 128:k * 128 + rows], in_=Yi[:, k * 128:(k + 1) * 128],
                                                       identity=ident[0:rows, 0:rows]),
                      r=[yb[i], ident_b], w=[pair_b[rpair]])
                cx = i * 128
                A("act", lambda e: e.activation(out=XT[:, :, cx:cx + rows],
                                                in_=R.rearrange("p (k t) -> p k t", k=8)[:, :, 0:rows], func=AF.Copy),
                  r=[pair_b[rpair]], w=[xtb[i]])
            if it.get("post") is not None:
                it["post"]()

        lnq = []

        def ln_push(ch, i, rows, need_xt, rpair, post=None):
            it = dict(ch=ch, i=i, rows=rows, need_xt=need_xt, rpair=rpair, post=post, st=1)
            ln_A1(it)
            lnq.append(it)
            if len(lnq) >= 2 and lnq[-2]["st"] == 1:
                ln_A2(lnq[-2]); lnq[-2]["st"] = 2
            if len(lnq) >= 3 and lnq[-3]["st"] == 2:
                ln_A3(lnq[-3]); lnq[-3]["st"] = 3
            if len(lnq) >= 4:
                o = lnq.pop(0)
                ln_B(o)

        def ln_flush():
            while lnq:
                for o in lnq:
                    if o["st"] == 1:
                        ln_A2(o); o["st"] = 2
                    elif o["st"] == 2:
                        ln_A3(o); o["st"] = 3
                    elif o["st"] == 3:
                        ln_B(o); o["st"] = 4
                while lnq and lnq[0]["st"] == 4:
                    lnq.pop(0)

        def ln_core(ch, i, rows, need_xt, rpair, post=None):
            ln_push(ch, i, rows, need_xt, rpair, post)

        def ln_mix(ch, i, rows, mp, need_xt, rpair):
            Yi = Y[0:rows, i, :]
            A("dve", lambda e: e.scalar_tensor_tensor(out=Yi, in0=Yi, scalar=ALPHA, in1=pair(mp)[0:rows, :],
                                                      op0=ALU.mult, op1=ALU.add), r=[yb[i], pair_b[mp]], w=[yb[i]])
            ln_core(ch, i, rows, need_xt, rpair)

        def load_ln(g_d, b_d, l):
            S.dma("sp", lng[:], g_d[l:l + 1, :].partition_broadcast(128), writes=[lng_b])
            S.dma("sp", lnb[:], b_d[l:l + 1, :].partition_broadcast(128), writes=[lnb_b])

        def pool_layer(ch, a):
            has_s = (ch == 1)
            arena_reset()
            psc, (psc_b,) = take("psc", [128, D], F32)
            wpf, (wpf_b,) = take("wpf", [128, 8, 256], F32)
            wp, (wp_b,) = take("wp", [128, 8, 256], BF16)
            ybf, ybf_b = take("ybf", [128, D], BF16, nb=3)
            dT, dT_b = take("dT", [128, 8, 128], BF16, nb=3)
            spb, (spb_b,) = take("spb", [128, 2, D], BF16)
            xnb, (xnb_b,) = take("xnb", [128, D], BF16)
            wpf4 = wpf.rearrange("p (g k e) -> p g k e", g=4, k=2)
            wp4 = wp.rearrange("p (g k e) -> p g k e", g=4, k=2)
            wp3 = wp.rearrange("p (c e) -> p c e", c=8)
            ybf3 = ybf.rearrange("p (s d) -> p s d", s=3)
            dT4 = dT.rearrange("p (s c t) -> p s c t", s=3, c=8)
            spb3 = spb.rearrange("p (t d) -> p t d", t=2)
            S.dma("sp", psc, pool_scale[a:a + 1, :].partition_broadcast(128), writes=[psc_b])
            S.dma("sp", wpf.rearrange("p (c e) -> p c e", c=8), pool_w[a].rearrange("g (k p) e -> p (g k) e", p=128), writes=[wpf_b])
            for kk in range(2):
                A("dve", lambda e, kk=kk: e.tensor_tensor(out=wp4[:, :, kk, :], in0=wpf4[:, :, kk, :],
                                                          in1=psc.rearrange("p (g e) -> p g e", g=4), op=ALU.mult),
                  r=[wpf_b, psc_b], w=[wp_b])
            load_ln(lmg, lmb, a)
            if has_s:
                for t_ in range(2):
                    S.dma("pool", spb3[0:120, t_, :], spool[a, t_ * 120:(t_ + 1) * 120, :], writes=[spb_b])
            def stageA0(i):
                sl = i % 3
                A("act", lambda e, i=i, sl=sl: e.activation(out=ybf3[:, sl, :], in_=Y[:, i, :], func=AF.Copy),
                  r=[yb[i]], w=[ybf_b[sl]])
                if ch == 1 and i == NT - 1:
                    S.dma("sp", npool_p[a], Y[113:128, i, :], reads=[yb[i]])

            def stageA(i):
                sl = i % 3
                dp = i % 2
                P = pair(dp)
                var = 0 if (ch == 0 and i == 1) else 1
                for kc in range(8):
                    g = kc // 2
                    A("pe", lambda e, kc=kc, g=g, sl=sl, var=var, P=P, i=i: e.matmul(
                        P[:, kc * 128:(kc + 1) * 128], lhsT=ybf3[:, sl, kc * 128:(kc + 1) * 128], rhs=bandc[:, var, g, :],
                        start=True, stop=(i == 0)), r=[ybf_b[sl], cstb_b], w=[pair_b[dp]])
                    if i > 0:
                        sp_ = (i - 1) % 3
                        A("pe", lambda e, kc=kc, g=g, sp_=sp_, P=P: e.matmul(
                            P[:, kc * 128:(kc + 1) * 128], lhsT=ybf3[:, sp_, kc * 128:(kc + 1) * 128], rhs=bandp[:, g, :],
                            start=False, stop=True), r=[ybf_b[sp_], cstb_b], w=[pair_b[dp]])
                ds = i % 3
                A("act", lambda e, ds=ds, P=P: e.activation(out=dT4[:, ds, :, :], in_=P.rearrange("p (c t) -> p c t", c=8),
                                                            func=AF.Copy), r=[pair_b[dp]], w=[dT_b[ds]])

            def stageB(i):
                mp = 2 + (i % 2)
                ds = i % 3
                Q = pair(mp)
                for g in range(4):
                    for kk in range(2):
                        A("pe", lambda e, g=g, kk=kk, ds=ds, Q=Q: e.matmul(
                            Q[:, g * 256:(g + 1) * 256], lhsT=dT4[:, ds, 2 * g + kk, :], rhs=wp3[:, 2 * g + kk, :],
                            start=(kk == 0), stop=(kk == 1)), r=[dT_b[ds], wp_b], w=[pair_b[mp]])
                Yi = Y[:, i, :]
                A("dve", lambda e: e.scalar_tensor_tensor(out=Yi, in0=Yi, scalar=ALPHA, in1=Q[:, :],
                                                          op0=ALU.mult, op1=ALU.add), r=[yb[i], pair_b[mp]], w=[yb[i]])
                if i + 3 < NT:
                    stageA0(i + 3)
                ln_core(ch, i, 128, True, mp)

            for i_ in range(min(3, NT)):
                stageA0(i_)
            stageA(0)
            stageA(1)
            for i in range(NT):
                if i + 2 < NT:
                    stageA(i + 2)
                stageB(i)
            if has_s:
                i = NT
                S.dma("sp", npool_s[a, :, 14, :], Y[0:NS, i, :], reads=[yb[i]])
                S.dma("sp", npool_s[a, :, 0:14, :], spool[a].rearrange("(i r) d -> i r d", r=15)[:, 1:15, :], writes=[dram_misc_b])
                A("act", lambda e: e.activation(out=xnb[0:NS, :], in_=Y[0:NS, NT, :], func=AF.Copy), r=[yb[i]], w=[xnb_b])
                dp, mp = 0, 2
                P = pair(dp)
                for kc in range(8):
                    g = kc // 2
                    for t_ in range(2):
                        A("pe", lambda e, kc=kc, g=g, t_=t_: e.matmul(
                            P[:, kc * 128:kc * 128 + NS], lhsT=spb3[0:120, t_, kc * 128:(kc + 1) * 128], rhs=sel[0:120, t_, g, :],
                            start=(t_ == 0), stop=False), r=[spb_b, cstb_b], w=[pair_b[dp]])
                    A("pe", lambda e, kc=kc, g=g: e.matmul(
                        P[:, kc * 128:kc * 128 + NS], lhsT=xnb[0:NS, kc * 128:(kc + 1) * 128], rhs=coefI[0:NS, g, :],
                        start=False, stop=True), r=[xnb_b, cstb_b], w=[pair_b[dp]])
                A("act", lambda e: e.activation(out=dT4[:, 0, :, 0:NS], in_=P.rearrange("p (c t) -> p c t", c=8)[:, :, 0:NS],
                                                func=AF.Copy), r=[pair_b[dp]], w=[dT_b[0]])
                Q = pair(mp)
                for g in range(4):
                    for kk in range(2):
                        A("pe", lambda e, g=g, kk=kk: e.matmul(
                            Q[0:NS, g * 256:(g + 1) * 256], lhsT=dT4[:, 0, 2 * g + kk, 0:NS], rhs=wp3[:, 2 * g + kk, :],
                            start=(kk == 0), stop=(kk == 1)), r=[dT_b[0], wp_b], w=[pair_b[mp]])
                ln_mix(ch, i, NS, mp, True, dp)
            ln_flush()

        def ffn_layer(ch, l):
            has_s = (ch == 1)
            ntile = NT + (1 if has_s else 0)
            arena_reset()
            HT, _ = take("HT", [128, 6, TC], BF16)
            HT3 = HT.rearrange("p (j t) -> p j t", j=6)
            ht_b = [[Buf(f"ht{j}_{t}") for t in range(3)] for j in range(6)]
            for row in ht_b:
                for b in row:
                    for ob in ar["old"]:
                        toks = ([ob.w] if ob.w is not None else []) + list(ob.rs.values())
                        for t in toks:
                            key = ("e", t[1]) if t[0] == "e" else ("d", t[1])
                            cur = b.rs.get(key)
                            if cur is None or cur[2] < t[2]:
                                b.rs[key] = t
                    ar["bufs"].append(b)
            wout, wout_b = take("wout", [128, 6, D], BF16, nb=2)
            wout4 = wout.rearrange("p (s j d) -> p s j d", s=2, j=6)
            win, win_b = take("win", [128, 8, 256], BF16, nb=4)
            win4 = win.rearrange("p (s k c) -> p s k c", s=4, k=8)
            gsb, gsb_b2 = take("gsb", [128, 2 + TC], F32, nb=2)
            gsb3 = gsb.rearrange("p (s t) -> p s t", s=2)
            gsb_b = [[Buf(f"gs{s_}_{t}") for t in range(4)] for s_ in range(2)]
            for s_ in range(2):
                for b in gsb_b[s_]:
                    b.rs = dict(gsb_b2[s_].rs)
                    ar["bufs"].append(b)
            cc, cc_b = take("cc", [128, 512], F32, nb=2)
            cc3 = cc.rearrange("p (s t) -> p s t", s=2)
            ss, ss_b = take("ss", [128, 512], F32, nb=2)
            ss3 = ss.rearrange("p (s t) -> p s t", s=2)
            us, us_b = take("us", [128, 512], F32, nb=2)
            us3 = us.rearrange("p (s t) -> p s t", s=2)
            if has_s:
                scT, (scT_b,) = take("scT", [128, NJ, 32], F32)
                scT3 = scT.rearrange("p (j c) -> p j c", j=NJ)
                cs, (cs_b,) = take("cs", [128, DFF], F32)
            load_ln(lfg, lfb, l)
            for bk_ in range(4, 8):
                inherit(bank_b[bk_], [pair_b[bk_ // 2]])
            i_lo = 1 if l >= 2 else 0
            c_lo = 128 * i_lo
            for s_ in range(2):
                A("dve", lambda e, s_=s_: e.memset(gsb3[:, s_, c_lo:c_lo + 2], 0.0), w=[gsb_b[s_][0]])
            if has_s:
                S.dma("sp", cs[0:32, :], sconv[l], writes=[cs_b])
                S.dma("sp", nconv_s[l, :, 0, :], sconv[l].rearrange("(i r) f -> i r f", r=2)[:, 1, :], writes=[dram_misc_b])
                for j0 in range(0, NJ, 8):
                    nj = min(8, NJ - j0)
                    pp = (j0 // 8) % 2
                    for jj in range(nj):
                        A("pe", lambda e, j0=j0, jj=jj, pp=pp: e.transpose(
                            out=pair(pp)[:, jj * 32:(jj + 1) * 32], in_=cs[0:32, (j0 + jj) * 128:(j0 + jj + 1) * 128],
                            identity=ident[0:32, 0:32]), r=[cs_b, ident_b], w=[pair_b[pp]])
                    A("dve", lambda e, j0=j0, nj=nj, pp=pp: e.tensor_copy(
                        out=scT3[:, j0:j0 + nj, :], in_=pair(pp)[:, 0:nj * 32].rearrange("p (j c) -> p j c", j=nj)),
                      r=[pair_b[pp]], w=[scT_b])
            if i_lo == 0:
                TT = [(0, 512), (512, 512), (1024, 256 + (NS if has_s else 0))]
            else:
                TT = [(128, 512), (640, 512), (1152, 128 + (NS if has_s else 0))]
            winv = w_in[l].rearrange("(k p) c -> p k c", p=128)
            slab = 0
            u1 = 0
            accs = {"n": 0}
            pend = {"f": None}
            exn = {"n": 0}
            pend2 = {"f": None}
            for gi, (j0, j1) in enumerate(GROUPS):
                wo = gi % 2
                ng = j1 - j0
                def load_wout(wo=wo, ng=ng, j0=j0, j1=j1):
                    S.dma("pool", wout4[:, wo, 0:ng, :], w_out[l, j0 * 128:j1 * 128, :].rearrange("(j p) d -> p j d", p=128),
                          writes=[wout_b[wo]])
                if gi > 0:
                    load_wout()
                for j in range(j0, j1):
                    s_ = slab % 4; slab += 1
                    S.dma("pool", win4[:, s_, :, 0:128], winv[:, :, j * 128:(j + 1) * 128], writes=[win_b[s_]])
                    S.dma("pool", win4[:, s_, :, 128:256], winv[:, :, DFF + j * 128:DFF + (j + 1) * 128], writes=[win_b[s_]])
                    if gi == 0 and j == j0 + 2:
                        load_wout()
                    gs = j % 2
                    cw = C_CW + (l * NJ + j) * 3
                    cbc = C_CB + l * NJ + j
                    for tt, (t0, n) in enumerate(TT):
                        set_ = u1 % 2; u1 += 1
                        bg, bu = 4 + 2 * set_, 5 + 2 * set_
                        pg, pu = bank(bg), bank(bu)
                        xr = [xtb[ii] for ii in range(t0 // 128, min(NT, (t0 + n + 127) // 128))]
                        if has_s and tt == 2:
                            xr.append(xtb[NT])
                        for k in range(8):
                            A("pe", lambda e, k=k, s_=s_, pg=pg, t0=t0, n=n: e.matmul(
                                pg[:, 0:n], lhsT=win4[:, s_, k, 0:128], rhs=XT[:, k, t0:t0 + n], start=(k == 0), stop=(k == 7)),
                              r=[win_b[s_]] + xr, w=[bank_b[bg]])
                        for k in range(8):
                            A("pe", lambda e, k=k, s_=s_, pu=pu, t0=t0, n=n: e.matmul(
                                pu[:, 0:n], lhsT=win4[:, s_, k, 128:256], rhs=XT[:, k, t0:t0 + n], start=(k == 0), stop=(k == 7)),
                              r=[win_b[s_]] + xr, w=[bank_b[bu]])
                        npr = min(n, TP - t0)
                        cs_ = u1 % 2
                        A("act", lambda e, gs=gs, t0=t0, n=n, pg=pg: e.activation(
                            out=gsb3[:, gs, 2 + t0:2 + t0 + n], in_=pg[:, 0:n], func=AF.Copy),
                          r=[bank_b[bg]], w=[gsb_b[gs][1 + tt]])
                        A("act", lambda e, cs_=cs_, n=n, pg=pg, cw=cw, cbc=cbc: e.activation(
                            out=cc3[:, cs_, 0:n], in_=pg[:, 0:n], func=AF.Identity, bias=cst[:, cbc:cbc + 1],
                            scale=cst[:, cw + 2:cw + 3]), r=[bank_b[bg]] + [cst_b], w=[cc_b[cs_]])
                        grd = [gsb_b[gs][tt], gsb_b[gs][1 + tt]]
                        A("dve", lambda e, cs_=cs_, gs=gs, t0=t0, npr=npr, cw=cw: e.scalar_tensor_tensor(
                            out=cc3[:, cs_, 0:npr], in0=gsb3[:, gs, 1 + t0:1 + t0 + npr], scalar=cst[:, cw + 1:cw + 2],
                            in1=cc3[:, cs_, 0:npr], op0=ALU.mult, op1=ALU.add), r=grd + [cc_b[cs_], cst_b], w=[cc_b[cs_]])
                        A("dve", lambda e, cs_=cs_, gs=gs, t0=t0, npr=npr, cw=cw: e.scalar_tensor_tensor(
                            out=cc3[:, cs_, 0:npr], in0=gsb3[:, gs, t0:t0 + npr], scalar=cst[:, cw:cw + 1],
                            in1=cc3[:, cs_, 0:npr], op0=ALU.mult, op1=ALU.add), r=grd + [cc_b[cs_], cst_b], w=[cc_b[cs_]])
                        if n > npr:
                            A("dve", lambda e, cs_=cs_, j=j, npr=npr, n=n, cw=cw: e.scalar_tensor_tensor(
                                out=cc3[:, cs_, npr:n], in0=scT3[:, j, 1:32:2], scalar=cst[:, cw + 1:cw + 2],
                                in1=cc3[:, cs_, npr:n], op0=ALU.mult, op1=ALU.add), r=[scT_b, cc_b[cs_], cst_b], w=[cc_b[cs_]])
                            A("dve", lambda e, cs_=cs_, j=j, npr=npr, n=n, cw=cw: e.scalar_tensor_tensor(
                                out=cc3[:, cs_, npr:n], in0=scT3[:, j, 0:32:2], scalar=cst[:, cw:cw + 1],
                                in1=cc3[:, cs_, npr:n], op0=ALU.mult, op1=ALU.add), r=[scT_b, cc_b[cs_], cst_b], w=[cc_b[cs_]])
                        A("act", lambda e, cs_=cs_, n=n, pu=pu: e.activation(out=us3[:, cs_, 0:n], in_=pu[:, 0:n], func=AF.Copy),
                          r=[bank_b[bu]], w=[us_b[cs_]])

                        def second(cs_=cs_, n=n, j=j, j0=j0, t0=t0, tt=tt):
                            A("act", lambda e: e.activation(out=ss3[:, cs_, 0:n], in_=cc3[:, cs_, 0:n], func=AF.Silu),
                              r=[cc_b[cs_]], w=[ss_b[cs_]])
                            A("dve", lambda e: e.tensor_tensor(
                                out=HT3[:, j - j0, t0:t0 + n], in0=ss3[:, cs_, 0:n], in1=us3[:, cs_, 0:n], op=ALU.mult),
                              r=[ss_b[cs_], us_b[cs_]], w=[ht_b[j - j0][tt]])
                        if pend2["f"] is not None:
                            pend2["f"]()
                        pend2["f"] = second
                        if tt == 0 and pend["f"] is not None:
                            pend["f"](); pend["f"] = None
                    if has_s:
                        def export(j=j, gs=gs):
                            eb = exn["n"] % 4; exn["n"] += 1
                            A("pe", lambda e: e.transpose(out=bank(eb)[0:18, 0:128], in_=gsb3[:, gs, TP:TP + 18],
                                                          identity=ident[:, :]),
                              r=[gsb_b[gs][3], ident_b], w=bank_deps(eb))
                            A("act", lambda e: e.activation(out=cs[0:18, j * 128:(j + 1) * 128], in_=bank(eb)[0:18, 0:128],
                                                            func=AF.Copy), r=bank_deps(eb), w=[cs_b])
                        pend["f"] = export
                if pend2["f"] is not None:
                    pend2["f"](); pend2["f"] = None
                if pend["f"] is not None:
                    pend["f"](); pend["f"] = None
                last = (gi == len(GROUPS) - 1)
                for i in range(i_lo, ntile):
                    rows = 128 if i < NT else NS
                    c0 = i * 128
                    tt = min((i - i_lo) // 4, 2)
                    ap_ = accs["n"] % 2; accs["n"] += 1
                    P = pair(ap_)
                    for jj in range(ng):
                        for half in range(2):
                            A("pe", lambda e, jj=jj, half=half, rows=rows, c0=c0, P=P, wo=wo, ng=ng: e.matmul(
                                P[0:rows, half * 512:(half + 1) * 512], lhsT=HT3[:, jj, c0:c0 + rows],
                                rhs=wout4[:, wo, jj, half * 512:(half + 1) * 512], start=(jj == 0), stop=(jj == ng - 1)),
                              r=[ht_b[jj][tt], wout_b[wo]], w=[pair_b[ap_]])
                    Yi = Y[0:rows, i, :]
                    if gi == 0:
                        A("dve", lambda e, Yi=Yi, P=P, rows=rows: e.scalar_tensor_tensor(
                            out=Yi, in0=Yi, scalar=ALPHA, in1=P[0:rows, :], op0=ALU.mult, op1=ALU.add),
                          r=[yb[i], pair_b[ap_]], w=[yb[i]])
                    else:
                        A("dve", lambda e, Yi=Yi, P=P, rows=rows: e.tensor_tensor(out=Yi, in0=Yi, in1=P[0:rows, :], op=ALU.add),
                          r=[yb[i], pair_b[ap_]], w=[yb[i]])
                    if last:
                        def rp_fn():
                            v = accs["n"] % 2; accs["n"] += 1
                            return v
                        post = None
                        if l == NL - 1 and i == NT:
                            post = lambda: S.dma("sp", ys_out, Y[0:NS, NT, :], reads=[yb[NT]])
                        elif l == NL - 1 and i >= 2:
                            def post(i=i):
                                r0 = (ch * 8 + i - 2) * 128
                                S.dma("sp", y_out[r0:r0 + 128, :], Y[:, i, :], reads=[yb[i]])
                        ln_core(ch, i, rows, l in (1, 2), rp_fn, post)
            ln_flush()
            inherit(pair_b[2], [bank_b[4], bank_b[5]])
            inherit(pair_b[3], [bank_b[6], bank_b[7]])
            if has_s:
                S.dma("sp", nconv_p[l], cs[0:2, :], reads=[cs_b])
                S.dma("sp", nconv_s[l, :, 1, :], cs[2:18, :], reads=[cs_b])

        def kv_proj(ch):
            has_s = (ch == 1)
            arena_reset()
            wkt, (wkt_b,) = take("wkt", [128, 8, 512], BF16)
            wkt3 = wkt.rearrange("p (k c) -> p k c", k=8)
            wk2, (wk2_b,) = take("wk2", [128, 8, 4, 128], BF16)
            wk24 = wk2.rearrange("p (k h c) -> p k h c", k=8, h=4)
            kvs, (kvs_b,) = take("kvs", [128, 512], F32)
            wv = w_kv.rearrange("(k p) c -> p k c", p=128)
            for kh in range(4):
                for dup in range(2):
                    S.dma("pool", wk24[:, :, kh, dup * 64:(dup + 1) * 64], wv[:, :, kh * 64:(kh + 1) * 64], writes=[wk2_b])
            S.dma("pool", wkt3, wv, writes=[wkt_b])
            u = 0
            for kh in range(4):
                for (t0, n) in [(0, 512), (512, 512), (1024, 256)]:
                    bk = 4 + (u % 4); u += 1
                    xr = [xtb[ii] for ii in range(t0 // 128, (t0 + n) // 128)]
                    for k in range(8):
                        A("pe", lambda e, k=k, kh=kh, bk=bk, t0=t0, n=n: e.matmul(
                            bank(bk)[:, 0:n], lhsT=wk24[:, k, kh, :], rhs=XT[:, k, t0:t0 + n], start=(k == 0), stop=(k == 7)),
                          r=[wk2_b] + xr, w=bank_deps(bk))
                    A("act", lambda e, kh=kh, bk=bk, t0=t0, n=n: e.activation(
                        out=KT2[:, kh, t0:t0 + n], in_=bank(bk)[:, 0:n], func=AF.Identity,
                        bias=cst[:, C_BKT + kh:C_BKT + kh + 1], scale=1.0), r=bank_deps(bk) + [cst_b], w=[kt_b])
            ntile = NT + (1 if (has_s and not (KVDBG & 2)) else 0)
            for i in range(ntile):
                rows = 128 if i < NT else NS
                c0 = i * 128
                bk = i % 4
                for k in range(8):
                    A("pe", lambda e, k=k, bk=bk, rows=rows, c0=c0: e.matmul(
                        bank(bk)[0:rows, :], lhsT=XT[:, k, c0:c0 + rows], rhs=wkt3[:, k, :], start=(k == 0), stop=False),
                      r=[wkt_b, xtb[i]], w=bank_deps(bk))
                A("pe", lambda e, bk=bk, rows=rows: e.matmul(
                    bank(bk)[0:128, :], lhsT=ones33[:, 0:128], rhs=bb[:, 2048:2560], start=False, stop=True),
                  r=[ones_b, bb_b], w=bank_deps(bk))
                if i < NT:
                    A("act", lambda e, i=i, bk=bk: e.activation(out=V[:, i, :], in_=bank(bk)[:, 256:512], func=AF.Copy),
                      r=bank_deps(bk), w=[v_b[i]])
                if ch == 1 and i == NT - 1 and not (KVDBG & 4):
                    A("act", lambda e, bk=bk: e.activation(out=kvo[:], in_=bank(bk)[:, :], func=AF.Copy), r=bank_deps(bk), w=[kvo_b])
                    S.dma("sp", nk_p, kvo[:, 0:256], reads=[kvo_b])
                    S.dma("sp", nv_p, kvo[:, 256:512], reads=[kvo_b])
                if i == NT:
                    A("dve", lambda e, bk=bk: e.tensor_copy(out=kvs[0:NS, :], in_=bank(bk)[0:NS, :]), r=bank_deps(bk), w=[kvs_b])
                    S.dma("sp", nk_s[:, 127, :], kvs[0:NS, 0:256], reads=[kvs_b], writes=[nks_b])
                    S.dma("sp", nv_s[:, 127, :], kvs[0:NS, 256:512], reads=[kvs_b], writes=[nvs_b])
                    for (a0, a1) in ([] if (KVDBG & 1) else [(1, 33), (33, 65), (65, 97), (97, 128)]):
                        S.dma("sp", nk_s[:, a0 - 1:a1 - 1, :], skw[:, a0:a1, :], writes=[nks_b])
                        S.dma("sp", nv_s[:, a0 - 1:a1 - 1, :], svw[:, a0:a1, :], writes=[nvs_b])

        def attn_layer(ch, l):
            bi = l - 2
            has_s = (ch == 1)
            arena_reset()
            wq, (wq_b,) = take("wq", [128, 8, D], BF16)
            wq_at = ar["last_off"]
            wq3 = wq.rearrange("p (k c) -> p k c", k=8)
            wo_, (wo_b,) = take("wo", [128, 8, D], BF16)
            wo3 = wo_.rearrange("p (k c) -> p k c", k=8)
            qT, (qT_b,) = take("qTa", [128, 8, TP], BF16)
            qT3 = qT.rearrange("p (m t) -> p m t", m=8)
            en, en_b = take("en", [128, 4, 256], BF16, nb=2)
            en4 = en.rearrange("p (d h s) -> p d h s", d=2, h=4)
            ssb, ssb_b = take("ssb", [128, 4, 256], F32, nb=2)
            ssb4 = ssb.rearrange("p (d h s) -> p d h s", d=2, h=4)
            ee, ee_b = take("ee", [128, 4, 256], F32, nb=2)
            ee4 = ee.rearrange("p (d h s) -> p d h s", d=2, h=4)
            PT, PT_b = take("PT", [128, 4, 2, 128], BF16, nb=2)
            PT5 = PT.rearrange("p (d h f q) -> p d h f q", d=2, h=4, f=2)
            oT, (oT_b,) = take("oT", [128, 8, 128], BF16)
            oT3 = oT.rearrange("p (m t) -> p m t", m=8)
            S.dma("pool", wq3, w_q[bi].rearrange("(k p) c -> p k c", p=128), writes=[wq_b])
            S.dma("pool", wo3, w_o[bi].rearrange("(k p) c -> p k c", p=128), writes=[wo_b])
            load_ln(lmg, lmb, l)
            if has_s:
                Ksb, (Ksb_b,) = take("Ksb", [128, NS, 256], BF16)
                Ksb3 = Ksb.rearrange("p (i d) -> p i d", i=NS)
                Vsb, (Vsb_b,) = take("Vsb", [128, NS, 256], BF16)
                Vsb3 = Vsb.rearrange("p (i d) -> p i d", i=NS)
                S.dma("pool", Ksb3, nk_s.rearrange("i s d -> s i d"), reads=[nks_b], writes=[Ksb_b])
                S.dma("pool", Vsb3, nv_s.rearrange("i s d -> s i d"), reads=[nvs_b], writes=[Vsb_b])
            sinkb = cst[:, C_SINKB + bi * 16:C_SINKB + bi * 16 + 16]
            nsinkb = cst[:, C_NSINKB + bi * 16:C_NSINKB + bi * 16 + 16]
            uq = 0
            for m in range(8):
                for (t0, n) in [(128, 512), (640, 512), (1152, 128)]:
                    bk = 4 + (uq % 4); uq += 1
                    xr = [xtb[ii] for ii in range(t0 // 128, (t0 + n) // 128)]
                    for k in range(8):
                        A("pe", lambda e, k=k, m=m, bk=bk, t0=t0, n=n: e.matmul(
                            bank(bk)[:, 0:n], lhsT=wq3[:, k, m * 128:(m + 1) * 128], rhs=XT[:, k, t0:t0 + n],
                            start=(k == 0), stop=(k == 7)), r=[wq_b] + xr, w=bank_deps(bk))
                    cq = C_BQT + bi * 8 + m
                    if uq % 2 == 0:
                        A("act", lambda e, m=m, bk=bk, t0=t0, n=n, cq=cq: e.activation(
                            out=qT3[:, m, t0:t0 + n], in_=bank(bk)[:, 0:n], func=AF.Identity, bias=cst[:, cq:cq + 1], scale=1.0),
                          r=bank_deps(bk) + [cst_b], w=[qT_b])
                    else:
                        A("dve", lambda e, m=m, bk=bk, t0=t0, n=n, cq=cq: e.tensor_scalar(
                            out=qT3[:, m, t0:t0 + n], in0=bank(bk)[:, 0:n], scalar1=cst[:, cq:cq + 1], scalar2=None, op0=ALU.add),
                          r=bank_deps(bk) + [cst_b], w=[qT_b])

            def out_proj(rows, osrc, mp):
                MP = pair(mp)
                for half in range(2):
                    for m in range(8):
                        A("pe", lambda e, half=half, m=m: e.matmul(
                            MP[0:rows, half * 512:(half + 1) * 512], lhsT=osrc(m), rhs=wo3[:, m, half * 512:(half + 1) * 512],
                            start=(m == 0), stop=False), r=[oT_b, wo_b], w=[pair_b[mp]])
                    A("pe", lambda e, half=half: e.matmul(
                        MP[0:rows, half * 512:(half + 1) * 512], lhsT=ones33[:, 0:rows],
                        rhs=bb[:, bi * 1024 + half * 512:bi * 1024 + (half + 1) * 512], start=False, stop=True),
                      r=[ones_b, bb_b], w=[pair_b[mp]])

            def make_tile(i):
                OP = pair(1)
                v_ = 0 if i == 1 else (1 if i == 2 else 2)
                am0 = C_AM + (ch * 3 + v_) * 256
                SPp = pair(2)
                TPb = pair(3).bitcast(BF16)

                def scores(kh, i=i):
                    for sl4 in range(4):
                        hh = PERM[sl4]
                        h = 4 * kh + hh; m = h // 2; po = 64 * (h % 2)
                        A("pe", lambda e, hh=sl4, m=m, po=po, kh=kh, i=i: e.matmul(
                            SPp[:, hh * 256:(hh + 1) * 256], lhsT=qT3[po:po + 64, m, i * 128:(i + 1) * 128],
                            rhs=KT2[po:po + 64, kh, (i - 1) * 128:(i + 1) * 128], start=True, stop=True),
                          r=[qT_b, kt_b], w=[pair_b[2]])

                def c1(kh, am0=am0):
                    d = kh % 2
                    s_ = cnt["sa"] % 4; cnt["sa"] += 1
                    t = sa[:, s_, :]
                    sab = sa_b[s_]
                    sS = ssb4[:, d]; eE = ee4[:, d]
                    A("dve", lambda e: e.tensor_tensor(
                        out=sS, in0=SPp.rearrange("p (h s) -> p h s", h=4),
                        in1=cst[:, am0:am0 + 256].unsqueeze(1).broadcast_to([128, 4, 256]), op=ALU.add),
                      r=[pair_b[2], cst_b], w=[ssb_b[d]])
                    A("dve", lambda e: e.tensor_reduce(out=t[:, 0:4], in_=sS, axis=AX.X, op=ALU.max), r=[ssb_b[d]], w=[sab])
                    A("dve", lambda e: e.scalar_tensor_tensor(
                        out=t[:, 8:12], in0=t[:, 0:4], scalar=-SCALE, in1=nsinkb[:, 4 * kh:4 * kh + 4], op0=ALU.mult, op1=ALU.min),
                      r=[sab, cst_b], w=[sab])
                    A("dve", lambda e: e.tensor_tensor(out=t[:, 16:20], in0=sinkb[:, 4 * kh:4 * kh + 4], in1=t[:, 8:12],
                                                       op=ALU.add), r=[sab, cst_b], w=[sab])
                    for hh in range(4):
                        A("act", lambda e, hh=hh: e.activation(
                            out=eE[:, hh, :], in_=sS[:, hh, :], func=AF.Exp, bias=t[:, 8 + hh:9 + hh], scale=SCALE,
                            accum_out=t[:, 12 + hh:13 + hh]), r=[ssb_b[d], sab], w=[ee_b[d], sab])
                    A("act", lambda e: e.activation(out=t[:, 20:24], in_=t[:, 16:20], func=AF.Exp), r=[sab], w=[sab])
                    return (t, sab)

                def c2(kh, ts):
                    d = kh % 2
                    t, sab = ts
                    eE = ee4[:, d]; eN = en4[:, d]
                    A("dve", lambda e: e.tensor_tensor(out=t[:, 24:28], in0=t[:, 12:16], in1=t[:, 20:24], op=ALU.add),
                      r=[sab], w=[sab])
                    A("dve", lambda e: e.reciprocal(out=t[:, 28:32], in_=t[:, 24:28]), r=[sab], w=[sab])
                    A("dve", lambda e: e.tensor_tensor(
                        out=eN, in0=eE, in1=t[:, 28:32].unsqueeze(2).broadcast_to([128, 4, 256]), op=ALU.mult),
                      r=[ee_b[d], sab], w=[en_b[d]])

                def tp(kh, i=i):
                    d = kh % 2
                    eN = en4[:, d]
                    for hh in range(4):
                        for half in range(2):
                            A("pe", lambda e, hh=hh, half=half: e.transpose(
                                out=TPb[:, (hh * 2 + half) * 128:(hh * 2 + half + 1) * 128],
                                in_=eN[:, hh, half * 128:(half + 1) * 128], identity=identb[:, :]),
                              r=[en_b[d], identb_b], w=[pair_b[3]])
                    A("act", lambda e: e.activation(out=PT5[:, d].rearrange("p h f q -> p (h f q)"), in_=TPb[:, 0:1024], func=AF.Copy),
                      r=[pair_b[3]], w=[PT_b[d]])
                    for sl4 in range(4):
                        hh = PERM[sl4]
                        h = 4 * kh + hh; m = h // 2; po = 64 * (h % 2)
                        for half in range(2):
                            A("pe", lambda e, hh=sl4, half=half, m=m, po=po, kh=kh, i=i: e.matmul(
                                OP[po:po + 64, m * 128:(m + 1) * 128], lhsT=V[:, i - 1 + half, kh * 64:(kh + 1) * 64],
                                rhs=PT5[:, d, hh, half, :], start=(half == 0), stop=(half == 1)),
                              r=[PT_b[d], v_b[i - 1], v_b[i]], w=[pair_b[1]])

                def otcopy():
                    A("act", lambda e: e.activation(out=oT, in_=OP, func=AF.Copy), r=[pair_b[1]], w=[oT_b])

                def outproj():
                    out_proj(128, lambda m: oT3[:, m, :], 0)

                def lnpart():
                    ln_mix(ch, i, 128, 0, True, 3)

                return dict(scores=scores, c1=c1, c2=c2, tp=tp, otcopy=otcopy, outproj=outproj, lnpart=lnpart, st={})

            tls = [make_tile(i) for i in range(1, NT)]
            T0 = tls[0]
            T0["scores"](0); T0["st"][0] = T0["c1"](0)
            T0["scores"](1); T0["st"][1] = T0["c1"](1)
            prev = None
            for ti, T_ in enumerate(tls):
                N_ = tls[ti + 1] if ti + 1 < len(tls) else None
                st_ = T_["st"]
                T_["scores"](2)
                if prev is not None:
                    prev["outproj"]()
                T_["c2"](0, st_[0]); T_["tp"](0)
                st_[2] = T_["c1"](2)
                T_["scores"](3)
                T_["c2"](1, st_[1])
                if prev is not None:
                    prev["lnpart"]()
                T_["tp"](1)
                st_[3] = T_["c1"](3)
                if N_ is not None:
                    N_["scores"](0)
                T_["c2"](2, st_[2]); T_["tp"](2)
                if N_ is not None:
                    N_["st"][0] = N_["c1"](0)
                    N_["scores"](1)
                T_["c2"](3, st_[3]); T_["tp"](3)
                T_["otcopy"]()
                if N_ is not None:
                    N_["st"][1] = N_["c1"](1)
                prev = T_
            prev["outproj"]()
            prev["lnpart"]()

            if has_s:
                i = NT
                KsT, (KsT_b,) = take("KsT", [128, NS, 4, 128], BF16, at=wq_at, after=[wq_b])
                KsT4 = KsT.rearrange("p (i h s) -> p i h s", i=NS, h=4)
                qsT, (qsT_b,) = take("qsT", [128, 16, NS], BF16)
                qsT3 = qsT.rearrange("p (h i) -> p h i", h=16)
                STs, (STs_b,) = take("STs", [128, 256], F32)
                es, (es_b,) = take("es", [128, 2, 128], F32)
                es3 = es.rearrange("p (f s) -> p f s", f=2)
                PTs, (PTs_b,) = take("PTs", [128, 256], BF16)
                osT, (osT_b,) = take("osT", [128, 8, NS], BF16)
                osT3 = osT.rearrange("p (m i) -> p m i", m=8)
                QS = pair(0)
                for h in range(16):
                    for k in range(8):
                        A("pe", lambda e, h=h, k=k: e.matmul(
                            QS[0:64, h * 16:(h + 1) * 16], lhsT=wq3[:, k, h * 64:(h + 1) * 64], rhs=XT[:, k, TP:TP + NS],
                            start=(k == 0), stop=(k == 7)), r=[wq_b, xtb[NT]], w=[pair_b[0]])
                c0 = C_BQH + bi * 16
                A("dve", lambda e, c0=c0: e.tensor_tensor(
                    out=qsT3[0:64, :, :], in0=QS[0:64, 0:256].rearrange("p (h i) -> p h i", h=16),
                    in1=cst[0:64, c0:c0 + 16].unsqueeze(2).broadcast_to([64, 16, NS]), op=ALU.add),
                  r=[pair_b[0], cst_b], w=[qsT_b])
                for i0 in range(0, NS, 4):
                    pp = 2 + (i0 // 4) % 2
                    Pb = pair(pp).bitcast(BF16)
                    for ii in range(4):
                        for kh in range(4):
                            A("pe", lambda e, ii=ii, kh=kh, i0=i0, Pb=Pb: e.transpose(
                                out=Pb[0:64, (ii * 4 + kh) * 128:(ii * 4 + kh + 1) * 128],
                                in_=Ksb3[:, i0 + ii, kh * 64:(kh + 1) * 64], identity=identb[:, :]),
                              r=[Ksb_b, identb_b], w=[pair_b[pp]])
                    A("act", lambda e, i0=i0, Pb=Pb: e.activation(
                        out=KsT4[0:64, i0:i0 + 4, :, :], in_=Pb[0:64, 0:2048].rearrange("p (i h s) -> p i h s", i=4, h=4),
                        func=AF.Copy), r=[pair_b[pp]], w=[KsT_b])
                ST = pair(1)
                for ii in range(NS):
                    for kh in range(4):
                        A("pe", lambda e, ii=ii, kh=kh: e.matmul(
                            ST[:, ii * 16 + 4 * kh:ii * 16 + 4 * kh + 4], lhsT=KsT4[0:64, ii, kh, :],
                            rhs=qsT3[0:64, 4 * kh:4 * kh + 4, ii], start=True, stop=True),
                          r=[KsT_b, qsT_b], w=[pair_b[1]])
                A("dve", lambda e: e.tensor_copy(out=STs, in_=ST[:, 0:256]), r=[pair_b[1]], w=[STs_b])
                S2 = pair(2)
                for hf in range(2):
                    A("pe", lambda e, hf=hf: e.transpose(out=S2[:, hf * 128:(hf + 1) * 128], in_=STs[:, hf * 128:(hf + 1) * 128],
                                                         identity=ident[:, :]), r=[STs_b, ident_b], w=[pair_b[2]])
                s_ = cnt["sa"] % 4; cnt["sa"] += 1
                t = sa[:, s_, :]
                sab = sa_b[s_]
                sk = cst[:, C_SINKS + bi * 2:C_SINKS + bi * 2 + 2]
                A("dve", lambda e: e.tensor_reduce(out=t[:, 0:2], in_=S2[:, 0:256].rearrange("p (f s) -> p f s", f=2),
                                                   axis=AX.X, op=ALU.max), r=[pair_b[2]], w=[sab])
                A("dve", lambda e: e.scalar_tensor_tensor(out=t[:, 4:6], in0=t[:, 0:2], scalar=SCALE, in1=sk,
                                                          op0=ALU.mult, op1=ALU.max), r=[sab, cst_b], w=[sab])
                A("dve", lambda e: e.tensor_scalar(out=t[:, 8:10], in0=t[:, 4:6], scalar1=-1.0, scalar2=None, op0=ALU.mult),
                  r=[sab], w=[sab])
                for hf in range(2):
                    A("act", lambda e, hf=hf: e.activation(
                        out=es3[:, hf, :], in_=S2[:, hf * 128:(hf + 1) * 128], func=AF.Exp, bias=t[:, 8 + hf:9 + hf], scale=SCALE,
                        accum_out=t[:, 12 + hf:13 + hf]), r=[pair_b[2], sab], w=[es_b, sab])
                A("dve", lambda e: e.tensor_tensor(out=t[:, 16:18], in0=sk, in1=t[:, 4:6], op=ALU.subtract), r=[sab, cst_b], w=[sab])
                A("act", lambda e: e.activation(out=t[:, 20:22], in_=t[:, 16:18], func=AF.Exp), r=[sab], w=[sab])
                A("dve", lambda e: e.tensor_tensor(out=t[:, 24:26], in0=t[:, 12:14], in1=t[:, 20:22], op=ALU.add), r=[sab], w=[sab])
                A("dve", lambda e: e.reciprocal(out=t[:, 28:30], in_=t[:, 24:26]), r=[sab], w=[sab])
                A("dve", lambda e: e.tensor_tensor(out=es3, in0=es3, in1=t[:, 28:30].unsqueeze(2).broadcast_to([128, 2, 128]),
                                                   op=ALU.mult), r=[es_b, sab], w=[es_b])
                P2 = pair(3)
                for hf in range(2):
                    A("pe", lambda e, hf=hf: e.transpose(out=P2[:, hf * 128:(hf + 1) * 128], in_=es3[:, hf, :], identity=ident[:, :]),
                      r=[es_b, ident_b], w=[pair_b[3]])
                A("act", lambda e: e.activation(out=PTs, in_=P2[:, 0:256], func=AF.Copy), r=[pair_b[3]], w=[PTs_b])
                OS = pair(1)
                OS3 = OS[:, 0:128].rearrange("p (m i) -> p m i", m=8)
                for ii in range(NS):
                    for kh in range(4):
                        for par in range(2):
                            A("pe", lambda e, ii=ii, kh=kh, par=par: e.matmul(
                                OS3[64 * par:64 * par + 64, 2 * kh:2 * kh + 2, ii], lhsT=Vsb3[:, ii, kh * 64:(kh + 1) * 64],
                                rhs=PTs[:, ii * 16 + 4 * kh + par:ii * 16 + 4 * kh + 4:2], start=True, stop=True),
                              r=[Vsb_b, PTs_b, STs_b], w=[pair_b[1]])
                A("act", lambda e: e.activation(out=osT, in_=OS[:, 0:128], func=AF.Copy), r=[pair_b[1]], w=[osT_b])
                oT_b_save = oT_b
                MP = pair(0)
                for half in range(2):
                    for m in range(8):
                        A("pe", lambda e, half=half, m=m: e.matmul(
                            MP[0:NS, half * 512:(half + 1) * 512], lhsT=osT3[:, m, :], rhs=wo3[:, m, half * 512:(half + 1) * 512],
                            start=(m == 0), stop=False), r=[osT_b, wo_b], w=[pair_b[0]])
                    A("pe", lambda e, half=half: e.matmul(
                        MP[0:128, half * 512:(half + 1) * 512], lhsT=ones33[:, 0:128],
                        rhs=bb[:, bi * 1024 + half * 512:bi * 1024 + (half + 1) * 512], start=False, stop=True),
                      r=[ones_b, bb_b], w=[pair_b[0]])
                ln_mix(ch, i, NS, 0, True, 3)
            ln_flush()

        stage = {"n": 0}

        def go():
            stage["n"] += 1
            return stop is None or stage["n"] <= stop

        for ch in range(2):
            if not go():
                break
            S.dma("sp", Y[:, 0:NT, :], xin[ch * TP:(ch + 1) * TP, :].rearrange("(t p) d -> p t d", p=128), writes=yb[0:NT])
            if ch == 1:
                S.dma("sp", Y[0:NS, NT, :], xs, writes=[yb[NT]])
            for l in range(NL):
                if go():
                    if l < 2:
                        pool_layer(ch, l)
                    else:
                        attn_layer(ch, l)
                if go():
                    ffn_layer(ch, l)
                if l == 1 and go():
                    kv_proj(ch)
        fin = yb + [kvo_b, nks_b, nvs_b, dram_misc_b] + ar["bufs"]
        if dbg:
            dby_b = Buf("dbgyb")
            S.dma("sp", dbgy.rearrange("t p d -> p t d"), Y[:, :, :], reads=yb, writes=[dby_b])
            S.dma("pool", dbgx, XT[:, :, :].rearrange("p k t -> p (k t)"), reads=xtb, writes=[dby_b])
            S.dma("pool", dbgk, KT2[:, :, :].rearrange("p k t -> p (k t)"), reads=[kt_b], writes=[dby_b])
            S.dma("pool", dbgv, V[:, :, :].rearrange("p k t -> p (k t)"), reads=v_b, writes=[dby_b])
            fin = fin + [dby_b]
        S.finalize(st, fin)
    return nc, S


POOL_WINDOWS = (2, 4, 8, 16)


def _consts_for_core(c, inp):
    qd = c % 4
    f32 = np.float32
    cst = np.zeros((128, NCST), f32)
    cw = np.asarray(inp["ffn_conv_w"], f32)
    cbv = np.asarray(inp["ffn_conv_b"], f32)
    cst[:, C_CW:C_CW + 264] = cw.reshape(NL, 3, NJ, 128).transpose(3, 0, 2, 1).reshape(128, 264)
    cst[:, C_CB:C_CB + 88] = cbv.reshape(NL, NJ, 128).transpose(2, 0, 1).reshape(128, 88)
    bq = np.asarray(inp["attn_b_q"], f32)
    cst[:, C_BQT:C_BQT + 16] = bq.reshape(2, 8, 128).transpose(2, 0, 1).reshape(128, 16)
    cst[0:64, C_BQH:C_BQH + 32] = bq.reshape(2, 16, 64).transpose(2, 0, 1).reshape(64, 32)
    bkv = np.asarray(inp["b_kv"], f32)
    bk = bkv[:256].reshape(4, 64)
    cst[:, C_BKT:C_BKT + 4] = np.concatenate([bk.T, bk.T], axis=0)
    sinks = np.asarray(inp["attn_sinks"], f32)
    sperm = sinks.reshape(2, 4, 4)[:, :, PERM].reshape(1, 32)
    cst[:, C_SINKB:C_SINKB + 32] = np.broadcast_to(sperm, (128, 32))
    cst[:, C_NSINKB:C_NSINKB + 32] = np.broadcast_to(-sperm, (128, 32))
    pidx = np.arange(128)
    for bi in range(2):
        for hf in range(2):
            cst[:, C_SINKS + bi * 2 + hf] = sinks[bi, pidx % 16]

    def real(ch, t, r):
        blk = 16 * qd + 8 * ch - 1 + t
        return (blk * 128 + r) >= 112

    r = np.arange(128)
    for ch in range(2):
        for i in range(2):
            cst[:, C_TM + ch * 2 + i] = real(ch, i, r).astype(f32)
    q = np.arange(128)[:, None]
    j = np.arange(256)[None, :]
    band = (q < j) & (j <= q + 128)
    for ch in range(2):
        for v in range(3):
            if v == 2:
                ok = band
            else:
                ti = 1 + v
                kr = np.where(j < 128, real(ch, ti - 1, j % 128), real(ch, ti, j % 128))
                ok = band & kr
            cst[:, C_AM + (ch * 3 + v) * 256:C_AM + (ch * 3 + v + 1) * 256] = np.where(ok, 0.0, NEG).astype(f32)

    cstb = np.zeros((128, NCSTB), f32)
    s = np.arange(128)[:, None]
    t = np.arange(128)[None, :]
    bc = np.zeros((128, 2, 4, 128), f32)
    bp = np.zeros((128, 4, 128), f32)
    for g, w in enumerate(POOL_WINDOWS):
        inwin = (s > t - w) & (s <= t)
        gen = inwin.astype(f32) / w - (s == t).astype(f32)
        bc[:, 1, g, :] = gen
        if qd == 0:
            tseq = t - 112
            cnt = np.where(tseq >= 0, np.minimum(w, tseq + 1), w).astype(f32)
            bc[:, 0, g, :] = inwin.astype(f32) / cnt - (s == t).astype(f32)
        else:
            bc[:, 0, g, :] = gen
        bp[:, g, :] = ((s > 128 + t - w).astype(f32)) / w
    cstb[:, B_BC:B_BC + 1024] = bc.reshape(128, 1024)
    cstb[:, B_BP:B_BP + 512] = bp.reshape(128, 512)
    sel = np.zeros((128, 2, 4, 16), f32)
    ci = np.zeros((128, 4, 16), f32)
    for g, w in enumerate(POOL_WINDOWS):
        for p in range(120):
            rr = p % 15
            if rr >= 16 - w:
                for t_ in range(2):
                    sel[p, t_, g, t_ * 8 + p // 15] = 1.0 / w
        for p in range(16):
            ci[p, g, p] = 1.0 / w - 1.0
    cstb[:, B_SEL:B_SEL + 128] = sel.reshape(128, 128)
    cstb[:, B_CI:B_CI + 64] = ci.reshape(128, 64)
    return cst, cstb


_NC_CACHE = {}


def kernel(**inputs):
    f32 = np.float32
    inp = {k: np.asarray(v) for k, v in inputs.items()}
    xp = inp["x_prompt"].astype(f32, copy=False)
    meta = inp["meta_tokens"].astype(f32, copy=False)
    n = 8
    if "nc" not in _NC_CACHE:
        _NC_CACHE["nc"] = build_nc()[0]
    nc = _NC_CACHE["nc"]
    brow = np.concatenate([inp["attn_b_o"][0], inp["attn_b_o"][1], inp["b_kv"]]).astype(f32).reshape(1, 2560)
    shared = {
        "pool_w": np.ascontiguousarray(inp["pool_w"], f32), "pool_scale": np.ascontiguousarray(inp["pool_scale"], f32),
        "w_kv": np.ascontiguousarray(inp["w_kv"], f32), "w_q": np.ascontiguousarray(inp["attn_w_q"], f32),
        "w_o": np.ascontiguousarray(inp["attn_w_o"], f32), "w_in": np.ascontiguousarray(inp["ffn_w_in"], f32),
        "w_out": np.ascontiguousarray(inp["ffn_w_out"], f32),
        "lmg": np.ascontiguousarray(inp["ln_mix_g"], f32), "lmb": np.ascontiguousarray(inp["ln_mix_b"], f32),
        "lfg": np.ascontiguousarray(inp["ln_ffn_g"], f32), "lfb": np.ascontiguousarray(inp["ln_ffn_b"], f32),
        "brow": brow,
    }
    in_maps = []
    for c in range(n):
        b, qd = c // 4, c % 4
        xin = np.zeros((2, NT, 128, D), f32)
        for ch in range(2):
            for i in range(NT):
                blk = 16 * qd + 8 * ch - 1 + i
                if blk >= 1:
                    xin[ch, i] = xp[b, (blk - 1) * 128:blk * 128]
                elif blk == 0:
                    xin[ch, i, 112:128] = meta
        cst, cstb = _consts_for_core(c, inp)
        sl = slice(NS * c, NS * (c + 1))
        m = dict(shared)
        m.update({
            "xin": xin.reshape(2 * NT * 128, D),
            "xs": np.ascontiguousarray(inp["x_sample"][sl, 0, :], f32),
            "spool": np.ascontiguousarray(inp["state_pool"][:, sl], f32).reshape(2, 240, D),
            "sconv": np.ascontiguousarray(inp["state_conv"][:, sl], f32).reshape(NL, 32, DFF),
            "skw": np.ascontiguousarray(inp["state_k_win"][sl], f32).reshape(NS, 128, 256),
            "svw": np.ascontiguousarray(inp["state_v_win"][sl], f32).reshape(NS, 128, 256),
            "cst": cst, "cstb": cstb,
        })
        in_maps.append(m)
    res = run_bass_kernel_spmd(nc, in_maps, core_ids=list(range(n)))
    R = res.results
    y_prompt = np.zeros((2, 8192, D), f32)
    y_sample = np.zeros((128, 1, D), f32)
    npp = np.zeros((2, 2, 15, D), f32); nps = np.zeros((2, 128, 15, D), f32)
    ncp = np.zeros((NL, 2, 2, DFF), f32); ncs = np.zeros((NL, 128, 2, DFF), f32)
    nkp = np.zeros((2, 128, 4, 64), f32); nvp = np.zeros((2, 128, 4, 64), f32)
    nks = np.zeros((128, 128, 4, 64), f32); nvs = np.zeros((128, 128, 4, 64), f32)
    for c in range(n):
        b, qd = c // 4, c % 4
        r = R[c]
        sl = slice(NS * c, NS * (c + 1))
        y_prompt[b, 2048 * qd:2048 * (qd + 1)] = r["y_out"]
        y_sample[sl, 0] = r["ys_out"]
        nps[:, sl] = r["npool_s"]
        ncs[:, sl] = r["nconv_s"]
        nks[sl] = r["nk_s"].reshape(NS, 128, 4, 64)
        nvs[sl] = r["nv_s"].reshape(NS, 128, 4, 64)
        if qd == 3:
            npp[:, b] = r["npool_p"]
            ncp[:, b] = r["nconv_p"]
            nkp[b] = r["nk_p"].reshape(128, 4, 64)
            nvp[b] = r["nv_p"].reshape(128, 4, 64)
    return (y_prompt, y_sample, npp, nps, ncp, ncs, nkp, nvp, nks, nvs)
```

```python
import contextlib
import numpy as np
import concourse.bass as bass
import concourse.mybir as mybir
from concourse.bass_utils import run_bass_kernel_spmd

F32 = mybir.dt.float32
BF16 = mybir.dt.bfloat16
AF = mybir.ActivationFunctionType
ALU = mybir.AluOpType
AX = mybir.AxisListType


class Buf:
    __slots__ = ("name", "w", "rs", "dsem", "dcnt", "slot")

    def __init__(self, name):
        self.name = name
        self.w = None
        self.rs = {}
        self.dsem = None
        self.dcnt = 0
        self.slot = self


class Sched:
    ENG = ["pe", "act", "dve", "pool", "sp"]

    def __init__(self, nc):
        self.nc = nc
        self.ops = {e: [] for e in self.ENG}
        self.waited = {e: {} for e in self.ENG}
        self.dbufs = []

    def _deps(self, eng, reads, writes):
        best = {}
        idx = len(self.ops[eng])

        def add(tok):
            if tok is None:
                return
            if tok[0] == "e":
                _, pe, pidx = tok
                if pe == eng and eng == "pe":
                    return
                key = ("e", pe)
                v = pidx
            else:
                _, b, v = tok
                key = ("d", b)
            if best.get(key, -1) < v:
                best[key] = v

        for b in reads:
            add(b.w)
        for b in writes:
            add(b.w)
            for t in b.rs.values():
                add(t)
        waits = []
        for key, v in best.items():
            if self.waited[eng].get(key, -1) >= v:
                continue
            self.waited[eng][key] = v
            waits.append((key, v))
        return waits

    def _commit(self, tok, reads, writes):
        for b in writes:
            b.w = tok
            b.rs = {}
        for b in reads:
            if b in writes:
                continue
            if tok[0] == "e":
                b.rs[("e", tok[1])] = tok
            else:
                b.rs[("d", tok[1])] = tok

    def op(self, eng, fn, reads=(), writes=()):
        waits = self._deps(eng, reads, writes)
        idx = len(self.ops[eng])
        self.ops[eng].append(dict(fn=fn, waits=waits, sig=False, dma=None))
        tok = ("e", eng, idx)
        self._commit(tok, reads, writes)
        return tok

    def dma(self, eng, out, in_, reads=(), writes=(), **kw):
        waits = self._deps(eng, reads, writes)
        pb = (writes[0] if writes else reads[0]).slot
        if pb.dsem is None:
            pb.dsem = True
            self.dbufs.append(pb)
        pb.dcnt += 16
        self.ops[eng].append(dict(
            fn=lambda e: e.dma_start(out=out, in_=in_, **kw), waits=waits, sig=False, dma=pb))
        tok = ("d", pb, pb.dcnt)
        self._commit(tok, reads, writes)
        return tok

    def finalize(self, stack, final_bufs=()):
        nc = self.nc
        waits = self._deps("sp", list(final_bufs), list(final_bufs))
        self.ops["sp"].append(dict(fn=None, waits=waits, sig=False, dma=None))
        for e in self.ENG:
            for rec in self.ops[e]:
                for key, v in rec["waits"]:
                    if key[0] == "e":
                        self.ops[key[1]][v]["sig"] = True
        cum = {}
        for e in self.ENG:
            c = 0
            arr = []
            for rec in self.ops[e]:
                if rec["sig"]:
                    c += 1
                arr.append(c)
            cum[e] = arr
        esem = {e: stack.enter_context(nc.semaphore("s_" + e)) for e in self.ENG}
        for b in self.dbufs:
            b.dsem = stack.enter_context(nc.semaphore("d_" + b.name))
        engobj = {"pe": "tensor", "act": "scalar", "dve": "vector", "pool": "gpsimd", "sp": "sync"}

        def emit(name, e):
            for rec in self.ops[name]:
                for key, v in rec["waits"]:
                    if key[0] == "e":
                        e.wait_ge(esem[key[1]], cum[key[1]][v])
                    else:
                        e.wait_ge(key[1].dsem, v)
                if rec["fn"] is None:
                    continue
                ins = rec["fn"](e)
                if rec["dma"] is not None:
                    ins.then_inc(rec["dma"].dsem, 16)
                elif rec["sig"]:
                    ins.then_inc(esem[name], 1)

        block = stack.enter_context(nc.Block())
        for name in self.ENG:
            getattr(block, engobj[name])(lambda e, name=name: emit(name, e))
        self.stats = {e: len(self.ops[e]) for e in self.ENG}
        self.nsem = 5 + len(self.dbufs)

D = 1024; DFF = 2816; NJ = 22; NL = 4; NT = 10; TP = NT * 128; NS = 16; TC = TP + NS
ALPHA = (2.0 * 4) ** 0.25; EPS = 1e-5; SCALE = 0.125; NEG = -1e30
GROUPS = [(0, 6), (6, 12), (12, 17), (17, 22)]
C_CW = 0; C_CB = 264; C_BQT = 352; C_BQH = 368; C_BKT = 400; C_SINKB = 404; C_SINKS = 436; C_TM = 440; C_AM = 444; C_NSINKB = 1980; NCST = 2012
B_BC = 0; B_BP = 1024; B_SEL = 1536; B_CI = 1664; NCSTB = 1728
U8 = mybir.dt.uint8
PERM = [0, 2, 1, 3]
import os
KVDBG = int(os.environ.get('KVDBG', '0'))
ARENA_BYTES = 100 * 1024


def build_nc(stop=None, dbg=False):
    nc = bass.Bass("TRN2", target_bir_lowering=False)

    def din(name, shape):
        return nc.dram_tensor(name, list(shape), F32, kind="ExternalInput").ap()

    def dout(name, shape):
        return nc.dram_tensor(name, list(shape), F32, kind="ExternalOutput").ap()

    xin = din("xin", [2 * NT * 128, D]); xs = din("xs", [NS, D])
    spool = din("spool", [2, 240, D]); sconv = din("sconv", [NL, 32, DFF])
    skw = din("skw", [NS, 128, 256]); svw = din("svw", [NS, 128, 256])
    pool_w = din("pool_w", [2, 4, 256, 256]); pool_scale = din("pool_scale", [2, D])
    w_kv = din("w_kv", [D, 512])
    w_q = din("w_q", [2, D, D]); w_o = din("w_o", [2, D, D])
    w_in = din("w_in", [NL, D, 2 * DFF]); w_out = din("w_out", [NL, DFF, D])
    lmg = din("lmg", [NL, D]); lmb = din("lmb", [NL, D]); lfg = din("lfg", [NL, D]); lfb = din("lfb", [NL, D])
    cst_d = din("cst", [128, NCST]); cstb_d = din("cstb", [128, NCSTB]); brow_d = din("brow", [1, 2560])

    y_out = dout("y_out", [2 * 8 * 128, D]); ys_out = dout("ys_out", [NS, D])
    npool_p = dout("npool_p", [2, 15, D]); npool_s = dout("npool_s", [2, NS, 15, D])
    nconv_p = dout("nconv_p", [NL, 2, DFF]); nconv_s = dout("nconv_s", [NL, NS, 2, DFF])
    nk_p = dout("nk_p", [128, 256]); nv_p = dout("nv_p", [128, 256])
    nk_s = dout("nk_s", [NS, 128, 256]); nv_s = dout("nv_s", [NS, 128, 256])

    if dbg:
        dbgy = dout("dbgy", [NT + 1, 128, D]); dbgx = dout("dbgx", [128, 8 * TC])
        dbgk = dout("dbgk", [128, 4 * TP]); dbgv = dout("dbgv", [128, NT * 256])
    S = Sched(nc)
    with contextlib.ExitStack() as st:
        def sbt(name, shape, dt):
            return st.enter_context(nc.sbuf_tensor("sb_" + name, list(shape), dt))

        def A(eng, fn, r=(), w=()):
            return S.op(eng, fn, reads=list(r), writes=list(w))

        cst = sbt("cst", [128, NCST], F32); cst_b = Buf("cst")
        cstb = sbt("cstb", [128, NCSTB], BF16); cstb_b = Buf("cstb")
        Y = sbt("Y", [128, NT + 1, D], F32); yb = [Buf(f"y{i}") for i in range(NT + 1)]
        for i_ in range(1, NT):
            yb[i_].slot = yb[0]
        XT = sbt("XT", [128, 8, TC], BF16); xtb = [Buf(f"xt{i}") for i in range(NT + 1)]
        lng = sbt("lng", [128, D], F32); lng_b = Buf("lng")
        lnb = sbt("lnb", [128, D], F32); lnb_b = Buf("lnb")
        KT2 = sbt("KT2", [128, 4, TP], BF16); kt_b = Buf("kt2")
        V = sbt("V", [128, NT, 256], BF16); v_b = [Buf(f"v{i}") for i in range(NT)]
        ident = sbt("ident", [128, 128], F32); ident_b = Buf("ident")
        identb = sbt("identb", [128, 128], BF16); identb_b = Buf("identb")
        ones33 = sbt("ones33", [33, 128], BF16); ones_b = Buf("ones33")
        bb = sbt("bb", [33, 2560], BF16); bb_b = Buf("bb")
        mhalf = sbt("mhalf", [128, 1], F32); mhalf_b = Buf("mhalf")
        NSL = 4
        sm = sbt("sm", [128, NSL, 24], F32); sm_b = [Buf(f"sm{i}") for i in range(NSL)]
        sa = sbt("sa", [128, 4, 48], F32); sa_b = [Buf(f"sa{i}") for i in range(4)]
        kvo = sbt("kvo", [128, 512], F32); kvo_b = Buf("kvo")
        arena = sbt("arena", [128, ARENA_BYTES], U8)
        PS = st.enter_context(nc.psum_tensor("PS", [128, 8, 512], F32))
        pair_b = [Buf(f"pp{i}") for i in range(4)]
        bank_b = [Buf(f"pb{i}") for i in range(8)]

        def pair(p):
            return PS[:, 2 * p:2 * p + 2, :].rearrange("p a b -> p (a b)")

        def bank(bk):
            return PS[:, bk, :]

        def bank_deps(bk):
            return [pair_b[bk // 2], bank_b[bk]]

        nks_b = Buf("nks"); nvs_b = Buf("nvs"); dram_misc_b = Buf("dmisc")

        ar = {"off": 0, "bufs": [], "old": []}
        slots = {}

        def arena_reset():
            ar["old"] = ar["old"][-200:] + ar["bufs"] if False else ar["bufs"]
            ar["bufs"] = []
            ar["off"] = 0

        def take(name, shape, dt, nb=1, at=None, after=()):
            esz = 4 if dt == F32 else 2
            free = 1
            for s_ in shape[1:]:
                free *= s_
            nbytes = free * esz * nb
            nbytes = (nbytes + 63) // 64 * 64
            if at is None:
                assert ar["off"] + nbytes <= ARENA_BYTES, (name, ar["off"], nbytes)
                ar["last_off"] = ar["off"]
                v = arena[:, ar["off"]:ar["off"] + nbytes].bitcast(dt)
                ar["off"] += nbytes
            else:
                v = arena[:, at:at + nbytes].bitcast(dt)
            v = v[:, 0:free * nb]
            bufs = []
            for i in range(nb):
                b = Buf(f"{name}{i}")
                b.slot = slots.setdefault(b.name, b)
                for ob in list(ar["old"]) + list(after):
                    toks = ([ob.w] if ob.w is not None else []) + list(ob.rs.values())
                    for t in toks:
                        key = ("e", t[1]) if t[0] == "e" else ("d", t[1])
                        cur = b.rs.get(key)
                        if cur is None or cur[2] < t[2]:
                            b.rs[key] = t
                bufs.append(b)
                ar["bufs"].append(b)
            return v, bufs

        def inherit(dst, srcs):
            for ob in srcs:
                toks = ([ob.w] if ob.w is not None else []) + list(ob.rs.values())
                for t in toks:
                    key = ("e", t[1]) if t[0] == "e" else ("d", t[1])
                    cur = dst.rs.get(key)
                    if cur is None or cur[2] < t[2]:
                        dst.rs[key] = t

        def view(v, pat, **kw):
            return v.rearrange(pat, **kw)

        S.dma("sp", cst[:], cst_d, writes=[cst_b])
        S.dma("pool", cstb[:], cstb_d, writes=[cstb_b])
        A("dve", lambda e: e.memset(ident[:], 0.0), w=[ident_b])
        A("pool", lambda e: e.affine_select(out=ident[:], in_=ident[:], compare_op=ALU.not_equal, fill=1.0,
                                            base=0, pattern=[[-1, 128]], channel_multiplier=1),
          r=[ident_b], w=[ident_b])
        A("act", lambda e: e.activation(out=identb[:], in_=ident[:], func=AF.Copy), r=[ident_b], w=[identb_b])
        A("dve", lambda e: e.memset(ones33[:], 1.0), w=[ones_b])
        A("dve", lambda e: e.memset(mhalf[:], -0.5), w=[mhalf_b])
        A("dve", lambda e: e.memset(bb[:], 0.0), w=[bb_b])
        arena_reset()
        bst, (bst_b,) = take("bst", [33, 2560], F32)
        bhi, (bhi_b,) = take("bhi", [33, 2560], BF16)
        blo, (blo_b,) = take("blo", [33, 2560], F32)
        A("dve", lambda e: e.memset(bst[0:33, :], 0.0), w=[bst_b])
        S.dma("sp", bst[0:1, :], brow_d, writes=[bst_b])
        S.dma("sp", bst[32:33, :], brow_d, writes=[bst_b])
        A("act", lambda e: e.activation(out=bhi[0:33, :], in_=bst[0:33, :], func=AF.Copy), r=[bst_b], w=[bhi_b])
        A("dve", lambda e: e.tensor_tensor(out=blo[0:33, :], in0=bst[0:33, :], in1=bhi[0:33, :], op=ALU.subtract),
          r=[bst_b, bhi_b], w=[blo_b])
        A("dve", lambda e: e.tensor_copy(out=bb[0:1, :], in_=bhi[0:1, :]), r=[bhi_b, bb_b], w=[bb_b])
        A("dve", lambda e: e.tensor_copy(out=bb[32:33, :], in_=blo[32:33, :]), r=[blo_b, bb_b], w=[bb_b])

        bandc = cstb[:, B_BC:B_BC + 1024].rearrange("p (v g t) -> p v g t", v=2, g=4)
        bandp = cstb[:, B_BP:B_BP + 512].rearrange("p (g t) -> p g t", g=4)
        sel = cstb[:, B_SEL:B_SEL + 128].rearrange("p (t g i) -> p t g i", t=2, g=4)
        coefI = cstb[:, B_CI:B_CI + 64].rearrange("p (g i) -> p g i", g=4)

        cnt = {"sm": 0, "sa": 0, "pp": 0}

        def ln_A1(it):
            i, rows = it["i"], it["rows"]
            Yi = Y[0:rows, i, :]
            s_ = cnt["sm"] % NSL; cnt["sm"] += 1
            it["s"] = s_
            smb = sm_b[s_]
            t = sm[0:rows, s_, :]
            A("dve", lambda e: e.bn_stats(out=t[:, 0:6], in_=Yi[:, 0:512]), r=[yb[i]], w=[smb])
            A("dve", lambda e: e.bn_stats(out=t[:, 6:12], in_=Yi[:, 512:1024]), r=[yb[i], smb], w=[smb])
            A("dve", lambda e: e.bn_aggr(out=t[:, 12:14], in_=t[:, 0:12]), r=[smb], w=[smb])
            A("dve", lambda e: e.tensor_scalar(out=t[:, 14:15], in0=t[:, 13:14], scalar1=EPS, scalar2=None, op0=ALU.add),
              r=[smb], w=[smb])
            A("pool", lambda e: e.tensor_tensor(out=t[:, 15:16], in0=t[:, 14:15], in1=mhalf[0:rows, :], op=ALU.pow),
              r=[smb, mhalf_b], w=[smb])

        def ln_A2(it):
            i, rows, s_ = it["i"], it["rows"], it["s"]
            Yi = Y[0:rows, i, :]
            smb = sm_b[s_]
            t = sm[0:rows, s_, :]
            A("dve", lambda e: e.scalar_tensor_tensor(out=t[:, 16:17], in0=t[:, 12:13], scalar=-1.0, in1=t[:, 15:16],
                                                      op0=ALU.mult, op1=ALU.mult), r=[smb], w=[smb])
            A("act", lambda e: e.activation(out=Yi, in_=Yi, func=AF.Identity, bias=t[:, 16:17], scale=t[:, 15:16]),
              r=[yb[i], smb], w=[yb[i]])

        def ln_A3(it):
            i, rows, ch = it["i"], it["rows"], it["ch"]
            Yi = Y[0:rows, i, :]
            A("dve", lambda e: e.tensor_tensor(out=Yi, in0=Yi, in1=lng[0:rows, :], op=ALU.mult), r=[yb[i], lng_b], w=[yb[i]])
            A("dve", lambda e: e.tensor_tensor(out=Yi, in0=Yi, in1=lnb[0:rows, :], op=ALU.add), r=[yb[i], lnb_b], w=[yb[i]])
            if i < 2:
                c0 = C_TM + ch * 2 + i
                A("dve", lambda e: e.tensor_scalar(out=Yi, in0=Yi, scalar1=cst[0:rows, c0:c0 + 1], scalar2=None, op0=ALU.mult),
                  r=[yb[i], cst_b], w=[yb[i]])

        def ln_B(it):
            i, rows = it["i"], it["rows"]
            Yi = Y[0:rows, i, :]
            if it["need_xt"]:
                rpair = it["rpair"]() if callable(it["rpair"]) else it["rpair"]
                R = pair(rpair)
                for k in range(8):
                    A("pe", lambda e, k=k: e.transpose(out=R[:, k * 128:k * 128 + rows], in_=Yi[:, k * 128:(k + 1) * 128],
                                                       identity=ident[0:rows, 0:rows]),
                      r=[yb[i], ident_b], w=[pair_b[rpair]])
                cx = i * 128
                A("act", lambda e: e.activation(out=XT[:, :, cx:cx + rows],
                                                in_=R.rearrange("p (k t) -> p k t", k=8)[:, :, 0:rows], func=AF.Copy),
                  r=[pair_b[rpair]], w=[xtb[i]])
            if it.get("post") is not None:
                it["post"]()

        lnq = []

        def ln_push(ch, i, rows, need_xt, rpair, post=None):
            it = dict(ch=ch, i=i, rows=rows, need_xt=need_xt, rpair=rpair, post=post, st=1)
            ln_A1(it)
            lnq.append(it)
            if len(lnq) >= 2 and lnq[-2]["st"] == 1:
                ln_A2(lnq[-2]); lnq[-2]["st"] = 2
            if len(lnq) >= 3 and lnq[-3]["st"] == 2:
                ln_A3(lnq[-3]); lnq[-3]["st"] = 3
            if len(lnq) >= 4:
                o = lnq.pop(0)
                ln_B(o)

        def ln_flush():
            while lnq:
                for o in lnq:
                    if o["st"] == 1:
                        ln_A2(o); o["st"] = 2
                    elif o["st"] == 2:
                        ln_A3(o); o["st"] = 3
                    elif o["st"] == 3:
                        ln_B(o); o["st"] = 4
                while lnq and lnq[0]["st"] == 4:
                    lnq.pop(0)

        def ln_core(ch, i, rows, need_xt, rpair, post=None):
            ln_push(ch, i, rows, need_xt, rpair, post)

        def ln_mix(ch, i, rows, mp, need_xt, rpair):
            Yi = Y[0:rows, i, :]
            A("dve", lambda e: e.scalar_tensor_tensor(out=Yi, in0=Yi, scalar=ALPHA, in1=pair(mp)[0:rows, :],
                                                      op0=ALU.mult, op1=ALU.add), r=[yb[i], pair_b[mp]], w=[yb[i]])
            ln_core(ch, i, rows, need_xt, rpair)

        def load_ln(g_d, b_d, l):
            S.dma("sp", lng[:], g_d[l:l + 1, :].partition_broadcast(128), writes=[lng_b])
            S.dma("sp", lnb[:], b_d[l:l + 1, :].partition_broadcast(128), writes=[lnb_b])

        def pool_layer(ch, a):
            has_s = (ch == 1)
            arena_reset()
            psc, (psc_b,) = take("psc", [128, D], F32)
            wpf, (wpf_b,) = take("wpf", [128, 8, 256], F32)
            wp, (wp_b,) = take("wp", [128, 8, 256], BF16)
            ybf, ybf_b = take("ybf", [128, D], BF16, nb=3)
            dT, dT_b = take("dT", [128, 8, 128], BF16, nb=3)
            spb, (spb_b,) = take("spb", [128, 2, D], BF16)
            xnb, (xnb_b,) = take("xnb", [128, D], BF16)
            wpf4 = wpf.rearrange("p (g k e) -> p g k e", g=4, k=2)
            wp4 = wp.rearrange("p (g k e) -> p g k e", g=4, k=2)
            wp3 = wp.rearrange("p (c e) -> p c e", c=8)
            ybf3 = ybf.rearrange("p (s d) -> p s d", s=3)
            dT4 = dT.rearrange("p (s c t) -> p s c t", s=3, c=8)
            spb3 = spb.rearrange("p (t d) -> p t d", t=2)
            S.dma("sp", psc, pool_scale[a:a + 1, :].partition_broadcast(128), writes=[psc_b])
            S.dma("sp", wpf.rearrange("p (c e) -> p c e", c=8), pool_w[a].rearrange("g (k p) e -> p (g k) e", p=128), writes=[wpf_b])
            for kk in range(2):
                A("dve", lambda e, kk=kk: e.tensor_tensor(out=wp4[:, :, kk, :], in0=wpf4[:, :, kk, :],
                                                          in1=psc.rearrange("p (g e) -> p g e", g=4), op=ALU.mult),
                  r=[wpf_b, psc_b], w=[wp_b])
            load_ln(lmg, lmb, a)
            if has_s:
                for t_ in range(2):
                    S.dma("pool", spb3[0:120, t_, :], spool[a, t_ * 120:(t_ + 1) * 120, :], writes=[spb_b])
            def stageA0(i):
                sl = i % 3
                A("act", lambda e, i=i, sl=sl: e.activation(out=ybf3[:, sl, :], in_=Y[:, i, :], func=AF.Copy),
                  r=[yb[i]], w=[ybf_b[sl]])
                if ch == 1 and i == NT - 1:
                    S.dma("sp", npool_p[a], Y[113:128, i, :], reads=[yb[i]])

            def stageA(i):
                sl = i % 3
                dp = i % 2
                P = pair(dp)
                var = 0 if (ch == 0 and i == 1) else 1
                for kc in range(8):
                    g = kc // 2
                    A("pe", lambda e, kc=kc, g=g, sl=sl, var=var, P=P, i=i: e.matmul(
                        P[:, kc * 128:(kc + 1) * 128], lhsT=ybf3[:, sl, kc * 128:(kc + 1) * 128], rhs=bandc[:, var, g, :],
                        start=True, stop=(i == 0)), r=[ybf_b[sl], cstb_b], w=[pair_b[dp]])
                    if i > 0:
                        sp_ = (i - 1) % 3
                        A("pe", lambda e, kc=kc, g=g, sp_=sp_, P=P: e.matmul(
                            P[:, kc * 128:(kc + 1) * 128], lhsT=ybf3[:, sp_, kc * 128:(kc + 1) * 128], rhs=bandp[:, g, :],
                            start=False, stop=True), r=[ybf_b[sp_], cstb_b], w=[pair_b[dp]])
                ds = i % 3
                A("act", lambda e, ds=ds, P=P: e.activation(out=dT4[:, ds, :, :], in_=P.rearrange("p (c t) -> p c t", c=8),
                                                            func=AF.Copy), r=[pair_b[dp]], w=[dT_b[ds]])

            def stageB(i):
                mp = 2 + (i % 2)
                ds = i % 3
                Q = pair(mp)
                for g in range(4):
                    for kk in range(2):
                        A("pe", lambda e, g=g, kk=kk, ds=ds, Q=Q: e.matmul(
                            Q[:, g * 256:(g + 1) * 256], lhsT=dT4[:, ds, 2 * g + kk, :], rhs=wp3[:, 2 * g + kk, :],
                            start=(kk == 0), stop=(kk == 1)), r=[dT_b[ds], wp_b], w=[pair_b[mp]])
                Yi = Y[:, i, :]
                A("dve", lambda e: e.scalar_tensor_tensor(out=Yi, in0=Yi, scalar=ALPHA, in1=Q[:, :],
                                                          op0=ALU.mult, op1=ALU.add), r=[yb[i], pair_b[mp]], w=[yb[i]])
                if i + 3 < NT:
                    stageA0(i + 3)
                ln_core(ch, i, 128, True, mp)

            for i_ in range(min(3, NT)):
                stageA0(i_)
            stageA(0)
            stageA(1)
            for i in range(NT):
                if i + 2 < NT:
                    stageA(i + 2)
                stageB(i)
            if has_s:
                i = NT
                S.dma("sp", npool_s[a, :, 14, :], Y[0:NS, i, :], reads=[yb[i]])
                S.dma("sp", npool_s[a, :, 0:14, :], spool[a].rearrange("(i r) d -> i r d", r=15)[:, 1:15, :], writes=[dram_misc_b])
                A("act", lambda e: e.activation(out=xnb[0:NS, :], in_=Y[0:NS, NT, :], func=AF.Copy), r=[yb[i]], w=[xnb_b])
                dp, mp = 0, 2
                P = pair(dp)
                for kc in range(8):
                    g = kc // 2
                    for t_ in range(2):
                        A("pe", lambda e, kc=kc, g=g, t_=t_: e.matmul(
                            P[:, kc * 128:kc * 128 + NS], lhsT=spb3[0:120, t_, kc * 128:(kc + 1) * 128], rhs=sel[0:120, t_, g, :],
                            start=(t_ == 0), stop=False), r=[spb_b, cstb_b], w=[pair_b[dp]])
                    A("pe", lambda e, kc=kc, g=g: e.matmul(
                        P[:, kc * 128:kc * 128 + NS], lhsT=xnb[0:NS, kc * 128:(kc + 1) * 128], rhs=coefI[0:NS, g, :],
                        start=False, stop=True), r=[xnb_b, cstb_b], w=[pair_b[dp]])
                A("act", lambda e: e.activation(out=dT4[:, 0, :, 0:NS], in_=P.rearrange("p (c t) -> p c t", c=8)[:, :, 0:NS],
                                                func=AF.Copy), r=[pair_b[dp]], w=[dT_b[0]])
                Q = pair(mp)
                for g in range(4):
                    for kk in range(2):
                        A("pe", lambda e, g=g, kk=kk: e.matmul(
                            Q[0:NS, g * 256:(g + 1) * 256], lhsT=dT4[:, 0, 2 * g + kk, 0:NS], rhs=wp3[:, 2 * g + kk, :],
                            start=(kk == 0), stop=(kk == 1)), r=[dT_b[0], wp_b], w=[pair_b[mp]])
                ln_mix(ch, i, NS, mp, True, dp)
            ln_flush()

        def ffn_layer(ch, l):
            has_s = (ch == 1)
            ntile = NT + (1 if has_s else 0)
            arena_reset()
            HT, _ = take("HT", [128, 6, TC], BF16)
            HT3 = HT.rearrange("p (j t) -> p j t", j=6)
            ht_b = [[Buf(f"ht{j}_{t}") for t in range(3)] for j in range(6)]
            for row in ht_b:
                for b in row:
                    for ob in ar["old"]:
                        toks = ([ob.w] if ob.w is not None else []) + list(ob.rs.values())
                        for t in toks:
                            key = ("e", t[1]) if t[0] == "e" else ("d", t[1])
                            cur = b.rs.get(key)
                            if cur is None or cur[2] < t[2]:
                                b.rs[key] = t
                    ar["bufs"].append(b)
            wout, wout_b = take("wout", [128, 6, D], BF16, nb=2)
            wout4 = wout.rearrange("p (s j d) -> p s j d", s=2, j=6)
            win, win_b = take("win", [128, 8, 256], BF16, nb=4)
            win4 = win.rearrange("p (s k c) -> p s k c", s=4, k=8)
            gsb, gsb_b2 = take("gsb", [128, 2 + TC], F32, nb=2)
            gsb3 = gsb.rearrange("p (s t) -> p s t", s=2)
            gsb_b = [[Buf(f"gs{s_}_{t}") for t in range(4)] for s_ in range(2)]
            for s_ in range(2):
                for b in gsb_b[s_]:
                    b.rs = dict(gsb_b2[s_].rs)
                    ar["bufs"].append(b)
            cc, cc_b = take("cc", [128, 512], F32, nb=2)
            cc3 = cc.rearrange("p (s t) -> p s t", s=2)
            ss, ss_b = take("ss", [128, 512], F32, nb=2)
            ss3 = ss.rearrange("p (s t) -> p s t", s=2)
            us, us_b = take("us", [128, 512], F32, nb=2)
            us3 = us.rearrange("p (s t) -> p s t", s=2)
            if has_s:
                scT, (scT_b,) = take("scT", [128, NJ, 32], F32)
                scT3 = scT.rearrange("p (j c) -> p j c", j=NJ)
                cs, (cs_b,) = take("cs", [128, DFF], F32)
            load_ln(lfg, lfb, l)
            for bk_ in range(4, 8):
                inherit(bank_b[bk_], [pair_b[bk_ // 2]])
            i_lo = 1 if l >= 2 else 0
            c_lo = 128 * i_lo
            for s_ in range(2):
                A("dve", lambda e, s_=s_: e.memset(gsb3[:, s_, c_lo:c_lo + 2], 0.0), w=[gsb_b[s_][0]])
            if has_s:
                S.dma("sp", cs[0:32, :], sconv[l], writes=[cs_b])
                S.dma("sp", nconv_s[l, :, 0, :], sconv[l].rearrange("(i r) f -> i r f", r=2)[:, 1, :], writes=[dram_misc_b])
                for j0 in range(0, NJ, 8):
                    nj = min(8, NJ - j0)
                    pp = (j0 // 8) % 2
                    for jj in range(nj):
                        A("pe", lambda e, j0=j0, jj=jj, pp=pp: e.transpose(
                            out=pair(pp)[:, jj * 32:(jj + 1) * 32], in_=cs[0:32, (j0 + jj) * 128:(j0 + jj + 1) * 128],
                            identity=ident[0:32, 0:32]), r=[cs_b, ident_b], w=[pair_b[pp]])
                    A("dve", lambda e, j0=j0, nj=nj, pp=pp: e.tensor_copy(
                        out=scT3[:, j0:j0 + nj, :], in_=pair(pp)[:, 0:nj * 32].rearrange("p (j c) -> p j c", j=nj)),
                      r=[pair_b[pp]], w=[scT_b])
            if i_lo == 0:
                TT = [(0, 512), (512, 512), (1024, 256 + (NS if has_s else 0))]
            else:
                TT = [(128, 512), (640, 512), (1152, 128 + (NS if has_s else 0))]
            winv = w_in[l].rearrange("(k p) c -> p k c", p=128)
            slab = 0
            u1 = 0
            accs = {"n": 0}
            pend = {"f": None}
            exn = {"n": 0}
            pend2 = {"f": None}
            for gi, (j0, j1) in enumerate(GROUPS):
                wo = gi % 2
                ng = j1 - j0
                def load_wout(wo=wo, ng=ng, j0=j0, j1=j1):
                    S.dma("pool", wout4[:, wo, 0:ng, :], w_out[l, j0 * 128:j1 * 128, :].rearrange("(j p) d -> p j d", p=128),
                          writes=[wout_b[wo]])
                if gi > 0:
                    load_wout()
                for j in range(j0, j1):
                    s_ = slab % 4; slab += 1
                    S.dma("pool", win4[:, s_, :, 0:128], winv[:, :, j * 128:(j + 1) * 128], writes=[win_b[s_]])
                    S.dma("pool", win4[:, s_, :, 128:256], winv[:, :, DFF + j * 128:DFF + (j + 1) * 128], writes=[win_b[s_]])
                    if gi == 0 and j == j0 + 2:
                        load_wout()
                    gs = j % 2
                    cw = C_CW + (l * NJ + j) * 3
                    cbc = C_CB + l * NJ + j
                    for tt, (t0, n) in enumerate(TT):
                        set_ = u1 % 2; u1 += 1
                        bg, bu = 4 + 2 * set_, 5 + 2 * set_
                        pg, pu = bank(bg), bank(bu)
                        xr = [xtb[ii] for ii in range(t0 // 128, min(NT, (t0 + n + 127) // 128))]
                        if has_s and tt == 2:
                            xr.append(xtb[NT])
                        for k in range(8):
                            A("pe", lambda e, k=k, s_=s_, pg=pg, t0=t0, n=n: e.matmul(
                                pg[:, 0:n], lhsT=win4[:, s_, k, 0:128], rhs=XT[:, k, t0:t0 + n], start=(k == 0), stop=(k == 7)),
                              r=[win_b[s_]] + xr, w=[bank_b[bg]])
                        for k in range(8):
                            A("pe", lambda e, k=k, s_=s_, pu=pu, t0=t0, n=n: e.matmul(
                                pu[:, 0:n], lhsT=win4[:, s_, k, 128:256], rhs=XT[:, k, t0:t0 + n], start=(k == 0), stop=(k == 7)),
                              r=[win_b[s_]] + xr, w=[bank_b[bu]])
                        npr = min(n, TP - t0)
                        cs_ = u1 % 2
                        A("act", lambda e, gs=gs, t0=t0, n=n, pg=pg: e.activation(
                            out=gsb3[:, gs, 2 + t0:2 + t0 + n], in_=pg[:, 0:n], func=AF.Copy),
                          r=[bank_b[bg]], w=[gsb_b[gs][1 + tt]])
                        A("act", lambda e, cs_=cs_, n=n, pg=pg, cw=cw, cbc=cbc: e.activation(
                            out=cc3[:, cs_, 0:n], in_=pg[:, 0:n], func=AF.Identity, bias=cst[:, cbc:cbc + 1],
                            scale=cst[:, cw + 2:cw + 3]), r=[bank_b[bg]] + [cst_b], w=[cc_b[cs_]])
                        grd = [gsb_b[gs][tt], gsb_b[gs][1 + tt]]
                        A("dve", lambda e, cs_=cs_, gs=gs, t0=t0, npr=npr, cw=cw: e.scalar_tensor_tensor(
                            out=cc3[:, cs_, 0:npr], in0=gsb3[:, gs, 1 + t0:1 + t0 + npr], scalar=cst[:, cw + 1:cw + 2],
                            in1=cc3[:, cs_, 0:npr], op0=ALU.mult, op1=ALU.add), r=grd + [cc_b[cs_], cst_b], w=[cc_b[cs_]])
                        A("dve", lambda e, cs_=cs_, gs=gs, t0=t0, npr=npr, cw=cw: e.scalar_tensor_tensor(
                            out=cc3[:, cs_, 0:npr], in0=gsb3[:, gs, t0:t0 + npr], scalar=cst[:, cw:cw + 1],
                            in1=cc3[:, cs_, 0:npr], op0=ALU.mult, op1=ALU.add), r=grd + [cc_b[cs_], cst_b], w=[cc_b[cs_]])
                        if n > npr:
                            A("dve", lambda e, cs_=cs_, j=j, npr=npr, n=n, cw=cw: e.scalar_tensor_tensor(
                                out=cc3[:, cs_, npr:n], in0=scT3[:, j, 1:32:2], scalar=cst[:, cw + 1:cw + 2],
                                in1=cc3[:, cs_, npr:n], op0=ALU.mult, op1=ALU.add), r=[scT_b, cc_b[cs_], cst_b], w=[cc_b[cs_]])
                            A("dve", lambda e, cs_=cs_, j=j, npr=npr, n=n, cw=cw: e.scalar_tensor_tensor(
                                out=cc3[:, cs_, npr:n], in0=scT3[:, j, 0:32:2], scalar=cst[:, cw:cw + 1],
                                in1=cc3[:, cs_, npr:n], op0=ALU.mult, op1=ALU.add), r=[scT_b, cc_b[cs_], cst_b], w=[cc_b[cs_]])
                        A("act", lambda e, cs_=cs_, n=n, pu=pu: e.activation(out=us3[:, cs_, 0:n], in_=pu[:, 0:n], func=AF.Copy),
                          r=[bank_b[bu]], w=[us_b[cs_]])

                        def second(cs_=cs_, n=n, j=j, j0=j0, t0=t0, tt=tt):
                            A("act", lambda e: e.activation(out=ss3[:, cs_, 0:n], in_=cc3[:, cs_, 0:n], func=AF.Silu),
                              r=[cc_b[cs_]], w=[ss_b[cs_]])
                            A("dve", lambda e: e.tensor_tensor(
                                out=HT3[:, j - j0, t0:t0 + n], in0=ss3[:, cs_, 0:n], in1=us3[:, cs_, 0:n], op=ALU.mult),
                              r=[ss_b[cs_], us_b[cs_]], w=[ht_b[j - j0][tt]])
                        if pend2["f"] is not None:
                            pend2["f"]()
                        pend2["f"] = second
                        if tt == 0 and pend["f"] is not None:
                            pend["f"](); pend["f"] = None
                    if has_s:
                        def export(j=j, gs=gs):
                            eb = exn["n"] % 4; exn["n"] += 1
                            A("pe", lambda e: e.transpose(out=bank(eb)[0:18, 0:128], in_=gsb3[:, gs, TP:TP + 18],
                                                          identity=ident[:, :]),
                              r=[gsb_b[gs][3], ident_b], w=bank_deps(eb))
                            A("act", lambda e: e.activation(out=cs[0:18, j * 128:(j + 1) * 128], in_=bank(eb)[0:18, 0:128],
                                                            func=AF.Copy), r=bank_deps(eb), w=[cs_b])
                        pend["f"] = export
                if pend2["f"] is not None:
                    pend2["f"](); pend2["f"] = None
                if pend["f"] is not None:
                    pend["f"](); pend["f"] = None
                last = (gi == len(GROUPS) - 1)
                for i in range(i_lo, ntile):
                    rows = 128 if i < NT else NS
                    c0 = i * 128
                    tt = min((i - i_lo) // 4, 2)
                    ap_ = accs["n"] % 2; accs["n"] += 1
                    P = pair(ap_)
                    for jj in range(ng):
                        for half in range(2):
                            A("pe", lambda e, jj=jj, half=half, rows=rows, c0=c0, P=P, wo=wo, ng=ng: e.matmul(
                                P[0:rows, half * 512:(half + 1) * 512], lhsT=HT3[:, jj, c0:c0 + rows],
                                rhs=wout4[:, wo, jj, half * 512:(half + 1) * 512], start=(jj == 0), stop=(jj == ng - 1)),
                              r=[ht_b[jj][tt], wout_b[wo]], w=[pair_b[ap_]])
                    Yi = Y[0:rows, i, :]
                    if gi == 0:
                        A("dve", lambda e, Yi=Yi, P=P, rows=rows: e.scalar_tensor_tensor(
                            out=Yi, in0=Yi, scalar=ALPHA, in1=P[0:rows, :], op0=ALU.mult, op1=ALU.add),
                          r=[yb[i], pair_b[ap_]], w=[yb[i]])
                    else:
                        A("dve", lambda e, Yi=Yi, P=P, rows=rows: e.tensor_tensor(out=Yi, in0=Yi, in1=P[0:rows, :], op=ALU.add),
                          r=[yb[i], pair_b[ap_]], w=[yb[i]])
                    if last:
                        def rp_fn():
                            v = accs["n"] % 2; accs["n"] += 1
                            return v
                        post = None
                        if l == NL - 1 and i == NT:
                            post = lambda: S.dma("sp", ys_out, Y[0:NS, NT, :], reads=[yb[NT]])
                        elif l == NL - 1 and i >= 2:
                            def post(i=i):
                                r0 = (ch * 8 + i - 2) * 128
                                S.dma("sp", y_out[r0:r0 + 128, :], Y[:, i, :], reads=[yb[i]])
                        ln_core(ch, i, rows, l in (1, 2), rp_fn, post)
            ln_flush()
            inherit(pair_b[2], [bank_b[4], bank_b[5]])
            inherit(pair_b[3], [bank_b[6], bank_b[7]])
            if has_s:
                S.dma("sp", nconv_p[l], cs[0:2, :], reads=[cs_b])
                S.dma("sp", nconv_s[l, :, 1, :], cs[2:18, :], reads=[cs_b])

        def kv_proj(ch):
            has_s = (ch == 1)
            arena_reset()
            wkt, (wkt_b,) = take("wkt", [128, 8, 512], BF16)
            wkt3 = wkt.rearrange("p (k c) -> p k c", k=8)
            wk2, (wk2_b,) = take("wk2", [128, 8, 4, 128], BF16)
            wk24 = wk2.rearrange("p (k h c) -> p k h c", k=8, h=4)
            kvs, (kvs_b,) = take("kvs", [128, 512], F32)
            wv = w_kv.rearrange("(k p) c -> p k c", p=128)
            S.dma("pool", wkt3, wv, writes=[wkt_b])
            wksrc = wkt3[:, :, 0:256].rearrange("p k (h c) -> p k h c", h=4)
            A("act", lambda e: e.activation(out=wk24[:, :, :, 0:64], in_=wksrc, func=AF.Copy), r=[wkt_b], w=[wk2_b])
            A("dve", lambda e: e.tensor_copy(out=wk24[:, :, :, 64:128], in_=wksrc), r=[wkt_b, wk2_b], w=[wk2_b])
            u = 0
            for kh in range(4):
                for (t0, n) in [(0, 512), (512, 512), (1024, 256)]:
                    bk = 4 + (u % 4); u += 1
                    xr = [xtb[ii] for ii in range(t0 // 128, (t0 + n) // 128)]
                    for k in range(8):
                        A("pe", lambda e, k=k, kh=kh, bk=bk, t0=t0, n=n: e.matmul(
                            bank(bk)[:, 0:n], lhsT=wk24[:, k, kh, :], rhs=XT[:, k, t0:t0 + n], start=(k == 0), stop=(k == 7)),
                          r=[wk2_b] + xr, w=bank_deps(bk))
                    A("act", lambda e, kh=kh, bk=bk, t0=t0, n=n: e.activation(
                        out=KT2[:, kh, t0:t0 + n], in_=bank(bk)[:, 0:n], func=AF.Identity,
                        bias=cst[:, C_BKT + kh:C_BKT + kh + 1], scale=1.0), r=bank_deps(bk) + [cst_b], w=[kt_b])
            ntile = NT + (1 if (has_s and not (KVDBG & 2)) else 0)
            for i in range(ntile):
                rows = 128 if i < NT else NS
                c0 = i * 128
                bk = i % 4
                for k in range(8):
                    A("pe", lambda e, k=k, bk=bk, rows=rows, c0=c0: e.matmul(
                        bank(bk)[0:rows, :], lhsT=XT[:, k, c0:c0 + rows], rhs=wkt3[:, k, :], start=(k == 0), stop=False),
                      r=[wkt_b, xtb[i]], w=bank_deps(bk))
                A("pe", lambda e, bk=bk, rows=rows: e.matmul(
                    bank(bk)[0:128, :], lhsT=ones33[:, 0:128], rhs=bb[:, 2048:2560], start=False, stop=True),
                  r=[ones_b, bb_b], w=bank_deps(bk))
                if i < NT:
                    A("act", lambda e, i=i, bk=bk: e.activation(out=V[:, i, :], in_=bank(bk)[:, 256:512], func=AF.Copy),
                      r=bank_deps(bk), w=[v_b[i]])
                if ch == 1 and i == NT - 1 and not (KVDBG & 4):
                    A("act", lambda e, bk=bk: e.activation(out=kvo[:], in_=bank(bk)[:, :], func=AF.Copy), r=bank_deps(bk), w=[kvo_b])
                    S.dma("sp", nk_p, kvo[:, 0:256], reads=[kvo_b])
                    S.dma("sp", nv_p, kvo[:, 256:512], reads=[kvo_b])
                if i == NT:
                    A("dve", lambda e, bk=bk: e.tensor_copy(out=kvs[0:NS, :], in_=bank(bk)[0:NS, :]), r=bank_deps(bk), w=[kvs_b])
                    S.dma("sp", nk_s[:, 127, :], kvs[0:NS, 0:256], reads=[kvs_b], writes=[nks_b])
                    S.dma("sp", nv_s[:, 127, :], kvs[0:NS, 256:512], reads=[kvs_b], writes=[nvs_b])
                    for (a0, a1) in ([] if (KVDBG & 1) else [(1, 33), (33, 65), (65, 97), (97, 128)]):
                        S.dma("sp", nk_s[:, a0 - 1:a1 - 1, :], skw[:, a0:a1, :], writes=[nks_b])
                        S.dma("sp", nv_s[:, a0 - 1:a1 - 1, :], svw[:, a0:a1, :], writes=[nvs_b])

        def attn_layer(ch, l):
            bi = l - 2
            has_s = (ch == 1)
            arena_reset()
            wq, (wq_b,) = take("wq", [128, 8, D], BF16)
            wq_at = ar["last_off"]
            wq3 = wq.rearrange("p (k c) -> p k c", k=8)
            wo_, (wo_b,) = take("wo", [128, 8, D], BF16)
            wo3 = wo_.rearrange("p (k c) -> p k c", k=8)
            qT, (qT_b,) = take("qTa", [128, 8, TP], BF16)
            qT3 = qT.rearrange("p (m t) -> p m t", m=8)
            en, en_b = take("en", [128, 4, 256], BF16, nb=2)
            en4 = en.rearrange("p (d h s) -> p d h s", d=2, h=4)
            ssb, ssb_b = take("ssb", [128, 4, 256], F32, nb=2)
            ssb4 = ssb.rearrange("p (d h s) -> p d h s", d=2, h=4)
            ee, ee_b = take("ee", [128, 4, 256], F32, nb=2)
            ee4 = ee.rearrange("p (d h s) -> p d h s", d=2, h=4)
            PT, PT_b = take("PT", [128, 4, 2, 128], BF16, nb=2)
            PT5 = PT.rearrange("p (d h f q) -> p d h f q", d=2, h=4, f=2)
            oT, (oT_b,) = take("oT", [128, 8, 128], BF16)
            oT3 = oT.rearrange("p (m t) -> p m t", m=8)
            S.dma("pool", wq3, w_q[bi].rearrange("(k p) c -> p k c", p=128), writes=[wq_b])
            S.dma("pool", wo3, w_o[bi].rearrange("(k p) c -> p k c", p=128), writes=[wo_b])
            load_ln(lmg, lmb, l)
            if has_s:
                Ksb, (Ksb_b,) = take("Ksb", [128, NS, 256], BF16)
                Ksb3 = Ksb.rearrange("p (i d) -> p i d", i=NS)
                Vsb, (Vsb_b,) = take("Vsb", [128, NS, 256], BF16)
                Vsb3 = Vsb.rearrange("p (i d) -> p i d", i=NS)
                S.dma("pool", Ksb3, nk_s.rearrange("i s d -> s i d"), reads=[nks_b], writes=[Ksb_b])
                S.dma("pool", Vsb3, nv_s.rearrange("i s d -> s i d"), reads=[nvs_b], writes=[Vsb_b])
            sinkb = cst[:, C_SINKB + bi * 16:C_SINKB + bi * 16 + 16]
            nsinkb = cst[:, C_NSINKB + bi * 16:C_NSINKB + bi * 16 + 16]
            uq = 0
            for m in range(8):
                for (t0, n) in [(128, 512), (640, 512), (1152, 128)]:
                    bk = 4 + (uq % 4); uq += 1
                    xr = [xtb[ii] for ii in range(t0 // 128, (t0 + n) // 128)]
                    for k in range(8):
                        A("pe", lambda e, k=k, m=m, bk=bk, t0=t0, n=n: e.matmul(
                            bank(bk)[:, 0:n], lhsT=wq3[:, k, m * 128:(m + 1) * 128], rhs=XT[:, k, t0:t0 + n],
                            start=(k == 0), stop=(k == 7)), r=[wq_b] + xr, w=bank_deps(bk))
                    cq = C_BQT + bi * 8 + m
                    if uq % 2 == 0:
                        A("act", lambda e, m=m, bk=bk, t0=t0, n=n, cq=cq: e.activation(
                            out=qT3[:, m, t0:t0 + n], in_=bank(bk)[:, 0:n], func=AF.Identity, bias=cst[:, cq:cq + 1], scale=1.0),
                          r=bank_deps(bk) + [cst_b], w=[qT_b])
                    else:
                        A("dve", lambda e, m=m, bk=bk, t0=t0, n=n, cq=cq: e.tensor_scalar(
                            out=qT3[:, m, t0:t0 + n], in0=bank(bk)[:, 0:n], scalar1=cst[:, cq:cq + 1], scalar2=None, op0=ALU.add),
                          r=bank_deps(bk) + [cst_b], w=[qT_b])

            def out_proj(rows, osrc, mp):
                MP = pair(mp)
                for half in range(2):
                    for m in range(8):
                        A("pe", lambda e, half=half, m=m: e.matmul(
                            MP[0:rows, half * 512:(half + 1) * 512], lhsT=osrc(m), rhs=wo3[:, m, half * 512:(half + 1) * 512],
                            start=(m == 0), stop=False), r=[oT_b, wo_b], w=[pair_b[mp]])
                    A("pe", lambda e, half=half: e.matmul(
                        MP[0:rows, half * 512:(half + 1) * 512], lhsT=ones33[:, 0:rows],
                        rhs=bb[:, bi * 1024 + half * 512:bi * 1024 + (half + 1) * 512], start=False, stop=True),
                      r=[ones_b, bb_b], w=[pair_b[mp]])

            def make_tile(i):
                OP = pair(1)
                v_ = 0 if i == 1 else (1 if i == 2 else 2)
                am0 = C_AM + (ch * 3 + v_) * 256
                SPp = pair(2)
                TPb = pair(3).bitcast(BF16)

                def scores(kh, i=i):
                    for sl4 in range(4):
                        hh = PERM[sl4]
                        h = 4 * kh + hh; m = h // 2; po = 64 * (h % 2)
                        A("pe", lambda e, hh=sl4, m=m, po=po, kh=kh, i=i: e.matmul(
                            SPp[:, hh * 256:(hh + 1) * 256], lhsT=qT3[po:po + 64, m, i * 128:(i + 1) * 128],
                            rhs=KT2[po:po + 64, kh, (i - 1) * 128:(i + 1) * 128], start=True, stop=True),
                          r=[qT_b, kt_b], w=[pair_b[2]])

                def c1(kh, am0=am0):
                    d = kh % 2
                    s_ = cnt["sa"] % 4; cnt["sa"] += 1
                    t = sa[:, s_, :]
                    sab = sa_b[s_]
                    sS = ssb4[:, d]; eE = ee4[:, d]
                    A("dve", lambda e: e.tensor_tensor(
                        out=sS, in0=SPp.rearrange("p (h s) -> p h s", h=4),
                        in1=cst[:, am0:am0 + 256].unsqueeze(1).broadcast_to([128, 4, 256]), op=ALU.add),
                      r=[pair_b[2], cst_b], w=[ssb_b[d]])
                    A("dve", lambda e: e.tensor_reduce(out=t[:, 0:4], in_=sS, axis=AX.X, op=ALU.max), r=[ssb_b[d]], w=[sab])
                    A("dve", lambda e: e.scalar_tensor_tensor(
                        out=t[:, 8:12], in0=t[:, 0:4], scalar=-SCALE, in1=nsinkb[:, 4 * kh:4 * kh + 4], op0=ALU.mult, op1=ALU.min),
                      r=[sab, cst_b], w=[sab])
                    A("dve", lambda e: e.tensor_tensor(out=t[:, 16:20], in0=sinkb[:, 4 * kh:4 * kh + 4], in1=t[:, 8:12],
                                                       op=ALU.add), r=[sab, cst_b], w=[sab])
                    for hh in range(4):
                        A("act", lambda e, hh=hh: e.activation(
                            out=eE[:, hh, :], in_=sS[:, hh, :], func=AF.Exp, bias=t[:, 8 + hh:9 + hh], scale=SCALE,
                            accum_out=t[:, 12 + hh:13 + hh]), r=[ssb_b[d], sab], w=[ee_b[d], sab])
                    A("act", lambda e: e.activation(out=t[:, 20:24], in_=t[:, 16:20], func=AF.Exp), r=[sab], w=[sab])
                    return (t, sab)

                def c2(kh, ts):
                    d = kh % 2
                    t, sab = ts
                    eE = ee4[:, d]; eN = en4[:, d]
                    A("dve", lambda e: e.tensor_tensor(out=t[:, 24:28], in0=t[:, 12:16], in1=t[:, 20:24], op=ALU.add),
                      r=[sab], w=[sab])
                    A("dve", lambda e: e.reciprocal(out=t[:, 28:32], in_=t[:, 24:28]), r=[sab], w=[sab])
                    A("dve", lambda e: e.tensor_tensor(
                        out=eN, in0=eE, in1=t[:, 28:32].unsqueeze(2).broadcast_to([128, 4, 256]), op=ALU.mult),
                      r=[ee_b[d], sab], w=[en_b[d]])

                def tp(kh, i=i):
                    d = kh % 2
                    eN = en4[:, d]
                    for hh in range(4):
                        for half in range(2):
                            A("pe", lambda e, hh=hh, half=half: e.transpose(
                                out=TPb[:, (hh * 2 + half) * 128:(hh * 2 + half + 1) * 128],
                                in_=eN[:, hh, half * 128:(half + 1) * 128], identity=identb[:, :]),
                              r=[en_b[d], identb_b], w=[pair_b[3]])
                    A("act", lambda e: e.activation(out=PT5[:, d].rearrange("p h f q -> p (h f q)"), in_=TPb[:, 0:1024], func=AF.Copy),
                      r=[pair_b[3]], w=[PT_b[d]])
                    for sl4 in range(4):
                        hh = PERM[sl4]
                        h = 4 * kh + hh; m = h // 2; po = 64 * (h % 2)
                        for half in range(2):
                            A("pe", lambda e, hh=sl4, half=half, m=m, po=po, kh=kh, i=i: e.matmul(
                                OP[po:po + 64, m * 128:(m + 1) * 128], lhsT=V[:, i - 1 + half, kh * 64:(kh + 1) * 64],
                                rhs=PT5[:, d, hh, half, :], start=(half == 0), stop=(half == 1)),
                              r=[PT_b[d], v_b[i - 1], v_b[i]], w=[pair_b[1]])

                def otcopy():
                    A("act", lambda e: e.activation(out=oT, in_=OP, func=AF.Copy), r=[pair_b[1]], w=[oT_b])

                def outproj():
                    out_proj(128, lambda m: oT3[:, m, :], 0)

                def lnpart():
                    ln_mix(ch, i, 128, 0, True, 3)

                return dict(scores=scores, c1=c1, c2=c2, tp=tp, otcopy=otcopy, outproj=outproj, lnpart=lnpart, st={})

            tls = [make_tile(i) for i in range(1, NT)]
            T0 = tls[0]
            T0["scores"](0); T0["st"][0] = T0["c1"](0)
            T0["scores"](1); T0["st"][1] = T0["c1"](1)
            prev = None
            for ti, T_ in enumerate(tls):
                N_ = tls[ti + 1] if ti + 1 < len(tls) else None
                st_ = T_["st"]
                T_["scores"](2)
                if prev is not None:
                    prev["outproj"]()
                T_["c2"](0, st_[0]); T_["tp"](0)
                st_[2] = T_["c1"](2)
                T_["scores"](3)
                T_["c2"](1, st_[1])
                if prev is not None:
                    prev["lnpart"]()
                T_["tp"](1)
                st_[3] = T_["c1"](3)
                if N_ is not None:
                    N_["scores"](0)
                T_["c2"](2, st_[2]); T_["tp"](2)
                if N_ is not None:
                    N_["st"][0] = N_["c1"](0)
                    N_["scores"](1)
                T_["c2"](3, st_[3]); T_["tp"](3)
                T_["otcopy"]()
                if N_ is not None:
                    N_["st"][1] = N_["c1"](1)
                prev = T_
            prev["outproj"]()
            prev["lnpart"]()

            if has_s:
                i = NT
                KsT, (KsT_b,) = take("KsT", [128, NS, 4, 128], BF16, at=wq_at, after=[wq_b])
                KsT4 = KsT.rearrange("p (i h s) -> p i h s", i=NS, h=4)
                qsT, (qsT_b,) = take("qsT", [128, 16, NS], BF16)
                qsT3 = qsT.rearrange("p (h i) -> p h i", h=16)
                STs, (STs_b,) = take("STs", [128, 256], F32)
                es, (es_b,) = take("es", [128, 2, 128], F32)
                es3 = es.rearrange("p (f s) -> p f s", f=2)
                PTs, (PTs_b,) = take("PTs", [128, 256], BF16)
                osT, (osT_b,) = take("osT", [128, 8, NS], BF16)
                osT3 = osT.rearrange("p (m i) -> p m i", m=8)
                QS = pair(0)
                for h in range(16):
                    for k in range(8):
                        A("pe", lambda e, h=h, k=k: e.matmul(
                            QS[0:64, h * 16:(h + 1) * 16], lhsT=wq3[:, k, h * 64:(h + 1) * 64], rhs=XT[:, k, TP:TP + NS],
                            start=(k == 0), stop=(k == 7)), r=[wq_b, xtb[NT]], w=[pair_b[0]])
                c0 = C_BQH + bi * 16
                A("dve", lambda e, c0=c0: e.tensor_tensor(
                    out=qsT3[0:64, :, :], in0=QS[0:64, 0:256].rearrange("p (h i) -> p h i", h=16),
                    in1=cst[0:64, c0:c0 + 16].unsqueeze(2).broadcast_to([64, 16, NS]), op=ALU.add),
                  r=[pair_b[0], cst_b], w=[qsT_b])
                for i0 in range(0, NS, 4):
                    pp = 2 + (i0 // 4) % 2
                    Pb = pair(pp).bitcast(BF16)
                    for ii in range(4):
                        for kh in range(4):
                            A("pe", lambda e, ii=ii, kh=kh, i0=i0, Pb=Pb: e.transpose(
                                out=Pb[0:64, (ii * 4 + kh) * 128:(ii * 4 + kh + 1) * 128],
                                in_=Ksb3[:, i0 + ii, kh * 64:(kh + 1) * 64], identity=identb[:, :]),
                              r=[Ksb_b, identb_b], w=[pair_b[pp]])
                    A("act", lambda e, i0=i0, Pb=Pb: e.activation(
                        out=KsT4[0:64, i0:i0 + 4, :, :], in_=Pb[0:64, 0:2048].rearrange("p (i h s) -> p i h s", i=4, h=4),
                        func=AF.Copy), r=[pair_b[pp]], w=[KsT_b])
                ST = pair(1)
                for ii in range(NS):
                    for kh in range(4):
                        A("pe", lambda e, ii=ii, kh=kh: e.matmul(
                            ST[:, ii * 16 + 4 * kh:ii * 16 + 4 * kh + 4], lhsT=KsT4[0:64, ii, kh, :],
                            rhs=qsT3[0:64, 4 * kh:4 * kh + 4, ii], start=True, stop=True),
                          r=[KsT_b, qsT_b], w=[pair_b[1]])
                A("dve", lambda e: e.tensor_copy(out=STs, in_=ST[:, 0:256]), r=[pair_b[1]], w=[STs_b])
                S2 = pair(2)
                for hf in range(2):
                    A("pe", lambda e, hf=hf: e.transpose(out=S2[:, hf * 128:(hf + 1) * 128], in_=STs[:, hf * 128:(hf + 1) * 128],
                                                         identity=ident[:, :]), r=[STs_b, ident_b], w=[pair_b[2]])
                s_ = cnt["sa"] % 4; cnt["sa"] += 1
                t = sa[:, s_, :]
                sab = sa_b[s_]
                sk = cst[:, C_SINKS + bi * 2:C_SINKS + bi * 2 + 2]
                A("dve", lambda e: e.tensor_reduce(out=t[:, 0:2], in_=S2[:, 0:256].rearrange("p (f s) -> p f s", f=2),
                                                   axis=AX.X, op=ALU.max), r=[pair_b[2]], w=[sab])
                A("dve", lambda e: e.scalar_tensor_tensor(out=t[:, 4:6], in0=t[:, 0:2], scalar=SCALE, in1=sk,
                                                          op0=ALU.mult, op1=ALU.max), r=[sab, cst_b], w=[sab])
                A("dve", lambda e: e.tensor_scalar(out=t[:, 8:10], in0=t[:, 4:6], scalar1=-1.0, scalar2=None, op0=ALU.mult),
                  r=[sab], w=[sab])
                for hf in range(2):
                    A("act", lambda e, hf=hf: e.activation(
                        out=es3[:, hf, :], in_=S2[:, hf * 128:(hf + 1) * 128], func=AF.Exp, bias=t[:, 8 + hf:9 + hf], scale=SCALE,
                        accum_out=t[:, 12 + hf:13 + hf]), r=[pair_b[2], sab], w=[es_b, sab])
                A("dve", lambda e: e.tensor_tensor(out=t[:, 16:18], in0=sk, in1=t[:, 4:6], op=ALU.subtract), r=[sab, cst_b], w=[sab])
                A("act", lambda e: e.activation(out=t[:, 20:22], in_=t[:, 16:18], func=AF.Exp), r=[sab], w=[sab])
                A("dve", lambda e: e.tensor_tensor(out=t[:, 24:26], in0=t[:, 12:14], in1=t[:, 20:22], op=ALU.add), r=[sab], w=[sab])
                A("dve", lambda e: e.reciprocal(out=t[:, 28:30], in_=t[:, 24:26]), r=[sab], w=[sab])
                A("dve", lambda e: e.tensor_tensor(out=es3, in0=es3, in1=t[:, 28:30].unsqueeze(2).broadcast_to([128, 2, 128]),
                                                   op=ALU.mult), r=[es_b, sab], w=[es_b])
                P2 = pair(3)
                for hf in range(2):
                    A("pe", lambda e, hf=hf: e.transpose(out=P2[:, hf * 128:(hf + 1) * 128], in_=es3[:, hf, :], identity=ident[:, :]),
                      r=[es_b, ident_b], w=[pair_b[3]])
                A("act", lambda e: e.activation(out=PTs, in_=P2[:, 0:256], func=AF.Copy), r=[pair_b[3]], w=[PTs_b])
                OS = pair(1)
                OS3 = OS[:, 0:128].rearrange("p (m i) -> p m i", m=8)
                for ii in range(NS):
                    for kh in range(4):
                        for par in range(2):
                            A("pe", lambda e, ii=ii, kh=kh, par=par: e.matmul(
                                OS3[64 * par:64 * par + 64, 2 * kh:2 * kh + 2, ii], lhsT=Vsb3[:, ii, kh * 64:(kh + 1) * 64],
                                rhs=PTs[:, ii * 16 + 4 * kh + par:ii * 16 + 4 * kh + 4:2], start=True, stop=True),
                              r=[Vsb_b, PTs_b, STs_b], w=[pair_b[1]])
                A("act", lambda e: e.activation(out=osT, in_=OS[:, 0:128], func=AF.Copy), r=[pair_b[1]], w=[osT_b])
                oT_b_save = oT_b
                MP = pair(0)
                for half in range(2):
                    for m in range(8):
                        A("pe", lambda e, half=half, m=m: e.matmul(
                            MP[0:NS, half * 512:(half + 1) * 512], lhsT=osT3[:, m, :], rhs=wo3[:, m, half * 512:(half + 1) * 512],
                            start=(m == 0), stop=False), r=[osT_b, wo_b], w=[pair_b[0]])
                    A("pe", lambda e, half=half: e.matmul(
                        MP[0:128, half * 512:(half + 1) * 512], lhsT=ones33[:, 0:128],
                        rhs=bb[:, bi * 1024 + half * 512:bi * 1024 + (half + 1) * 512], start=False, stop=True),
                      r=[ones_b, bb_b], w=[pair_b[0]])
                ln_mix(ch, i, NS, 0, True, 3)
            ln_flush()

        stage = {"n": 0}

        def go():
            stage["n"] += 1
            return stop is None or stage["n"] <= stop

        for ch in range(2):
            if not go():
                break
            S.dma("sp", Y[:, 0:NT, :], xin[ch * TP:(ch + 1) * TP, :].rearrange("(t p) d -> p t d", p=128), writes=yb[0:NT])
            if ch == 1:
                S.dma("sp", Y[0:NS, NT, :], xs, writes=[yb[NT]])
            for l in range(NL):
                if go():
                    if l < 2:
                        pool_layer(ch, l)
                    else:
                        attn_layer(ch, l)
                if go():
                    ffn_layer(ch, l)
                if l == 1 and go():
                    kv_proj(ch)
        fin = yb + [kvo_b, nks_b, nvs_b, dram_misc_b] + ar["bufs"]
        if dbg:
            dby_b = Buf("dbgyb")
            S.dma("sp", dbgy.rearrange("t p d -> p t d"), Y[:, :, :], reads=yb, writes=[dby_b])
            S.dma("pool", dbgx, XT[:, :, :].rearrange("p k t -> p (k t)"), reads=xtb, writes=[dby_b])
            S.dma("pool", dbgk, KT2[:, :, :].rearrange("p k t -> p (k t)"), reads=[kt_b], writes=[dby_b])
            S.dma("pool", dbgv, V[:, :, :].rearrange("p k t -> p (k t)"), reads=v_b, writes=[dby_b])
            fin = fin + [dby_b]
        S.finalize(st, fin)
    return nc, S


POOL_WINDOWS = (2, 4, 8, 16)


def _consts_for_core(c, inp):
    qd = c % 4
    f32 = np.float32
    cst = np.zeros((128, NCST), f32)
    cw = np.asarray(inp["ffn_conv_w"], f32)
    cbv = np.asarray(inp["ffn_conv_b"], f32)
    cst[:, C_CW:C_CW + 264] = cw.reshape(NL, 3, NJ, 128).transpose(3, 0, 2, 1).reshape(128, 264)
    cst[:, C_CB:C_CB + 88] = cbv.reshape(NL, NJ, 128).transpose(2, 0, 1).reshape(128, 88)
    bq = np.asarray(inp["attn_b_q"], f32)
    cst[:, C_BQT:C_BQT + 16] = bq.reshape(2, 8, 128).transpose(2, 0, 1).reshape(128, 16)
    cst[0:64, C_BQH:C_BQH + 32] = bq.reshape(2, 16, 64).transpose(2, 0, 1).reshape(64, 32)
    bkv = np.asarray(inp["b_kv"], f32)
    bk = bkv[:256].reshape(4, 64)
    cst[:, C_BKT:C_BKT + 4] = np.concatenate([bk.T, bk.T], axis=0)
    sinks = np.asarray(inp["attn_sinks"], f32)
    sperm = sinks.reshape(2, 4, 4)[:, :, PERM].reshape(1, 32)
    cst[:, C_SINKB:C_SINKB + 32] = np.broadcast_to(sperm, (128, 32))
    cst[:, C_NSINKB:C_NSINKB + 32] = np.broadcast_to(-sperm, (128, 32))
    pidx = np.arange(128)
    for bi in range(2):
        for hf in range(2):
            cst[:, C_SINKS + bi * 2 + hf] = sinks[bi, pidx % 16]

    def real(ch, t, r):
        blk = 16 * qd + 8 * ch - 1 + t
        return (blk * 128 + r) >= 112

    r = np.arange(128)
    for ch in range(2):
        for i in range(2):
            cst[:, C_TM + ch * 2 + i] = real(ch, i, r).astype(f32)
    q = np.arange(128)[:, None]
    j = np.arange(256)[None, :]
    band = (q < j) & (j <= q + 128)
    for ch in range(2):
        for v in range(3):
            if v == 2:
                ok = band
            else:
                ti = 1 + v
                kr = np.where(j < 128, real(ch, ti - 1, j % 128), real(ch, ti, j % 128))
                ok = band & kr
            cst[:, C_AM + (ch * 3 + v) * 256:C_AM + (ch * 3 + v + 1) * 256] = np.where(ok, 0.0, NEG).astype(f32)

    cstb = np.zeros((128, NCSTB), f32)
    s = np.arange(128)[:, None]
    t = np.arange(128)[None, :]
    bc = np.zeros((128, 2, 4, 128), f32)
    bp = np.zeros((128, 4, 128), f32)
    for g, w in enumerate(POOL_WINDOWS):
        inwin = (s > t - w) & (s <= t)
        gen = inwin.astype(f32) / w - (s == t).astype(f32)
        bc[:, 1, g, :] = gen
        if qd == 0:
            tseq = t - 112
            cnt = np.where(tseq >= 0, np.minimum(w, tseq + 1), w).astype(f32)
            bc[:, 0, g, :] = inwin.astype(f32) / cnt - (s == t).astype(f32)
        else:
            bc[:, 0, g, :] = gen
        bp[:, g, :] = ((s > 128 + t - w).astype(f32)) / w
    cstb[:, B_BC:B_BC + 1024] = bc.reshape(128, 1024)
    cstb[:, B_BP:B_BP + 512] = bp.reshape(128, 512)
    sel = np.zeros((128, 2, 4, 16), f32)
    ci = np.zeros((128, 4, 16), f32)
    for g, w in enumerate(POOL_WINDOWS):
        for p in range(120):
            rr = p % 15
            if rr >= 16 - w:
                for t_ in range(2):
                    sel[p, t_, g, t_ * 8 + p // 15] = 1.0 / w
        for p in range(16):
            ci[p, g, p] = 1.0 / w - 1.0
    cstb[:, B_SEL:B_SEL + 128] = sel.reshape(128, 128)
    cstb[:, B_CI:B_CI + 64] = ci.reshape(128, 64)
    return cst, cstb


_NC_CACHE = {}


def kernel(**inputs):
    f32 = np.float32
    inp = {k: np.asarray(v) for k, v in inputs.items()}
    xp = inp["x_prompt"].astype(f32, copy=False)
    meta = inp["meta_tokens"].astype(f32, copy=False)
    n = 8
    if "nc" not in _NC_CACHE:
        _NC_CACHE["nc"] = build_nc()[0]
    nc = _NC_CACHE["nc"]
    brow = np.concatenate([inp["attn_b_o"][0], inp["attn_b_o"][1], inp["b_kv"]]).astype(f32).reshape(1, 2560)
    shared = {
        "pool_w": np.ascontiguousarray(inp["pool_w"], f32), "pool_scale": np.ascontiguousarray(inp["pool_scale"], f32),
        "w_kv": np.ascontiguousarray(inp["w_kv"], f32), "w_q": np.ascontiguousarray(inp["attn_w_q"], f32),
        "w_o": np.ascontiguousarray(inp["attn_w_o"], f32), "w_in": np.ascontiguousarray(inp["ffn_w_in"], f32),
        "w_out": np.ascontiguousarray(inp["ffn_w_out"], f32),
        "lmg": np.ascontiguousarray(inp["ln_mix_g"], f32), "lmb": np.ascontiguousarray(inp["ln_mix_b"], f32),
        "lfg": np.ascontiguousarray(inp["ln_ffn_g"], f32), "lfb": np.ascontiguousarray(inp["ln_ffn_b"], f32),
        "brow": brow,
    }
    in_maps = []
    for c in range(n):
        b, qd = c // 4, c % 4
        xin = np.zeros((2, NT, 128, D), f32)
        for ch in range(2):
            for i in range(NT):
                blk = 16 * qd + 8 * ch - 1 + i
                if blk >= 1:
                    xin[ch, i] = xp[b, (blk - 1) * 128:blk * 128]
                elif blk == 0:
                    xin[ch, i, 112:128] = meta
        cst, cstb = _consts_for_core(c, inp)
        sl = slice(NS * c, NS * (c + 1))
        m = dict(shared)
        m.update({
            "xin": xin.reshape(2 * NT * 128, D),
            "xs": np.ascontiguousarray(inp["x_sample"][sl, 0, :], f32),
            "spool": np.ascontiguousarray(inp["state_pool"][:, sl], f32).reshape(2, 240, D),
            "sconv": np.ascontiguousarray(inp["state_conv"][:, sl], f32).reshape(NL, 32, DFF),
            "skw": np.ascontiguousarray(inp["state_k_win"][sl], f32).reshape(NS, 128, 256),
            "svw": np.ascontiguousarray(inp["state_v_win"][sl], f32).reshape(NS, 128, 256),
            "cst": cst, "cstb": cstb,
        })
        in_maps.append(m)
    res = run_bass_kernel_spmd(nc, in_maps, core_ids=list(range(n)))
    R = res.results
    y_prompt = np.zeros((2, 8192, D), f32)
    y_sample = np.zeros((128, 1, D), f32)
    npp = np.zeros((2, 2, 15, D), f32); nps = np.zeros((2, 128, 15, D), f32)
    ncp = np.zeros((NL, 2, 2, DFF), f32); ncs = np.zeros((NL, 128, 2, DFF), f32)
    nkp = np.zeros((2, 128, 4, 64), f32); nvp = np.zeros((2, 128, 4, 64), f32)
    nks = np.zeros((128, 128, 4, 64), f32); nvs = np.zeros((128, 128, 4, 64), f32)
    for c in range(n):
        b, qd = c // 4, c % 4
        r = R[c]
        sl = slice(NS * c, NS * (c + 1))
        y_prompt[b, 2048 * qd:2048 * (qd + 1)] = r["y_out"]
        y_sample[sl, 0] = r["ys_out"]
        nps[:, sl] = r["npool_s"]
        ncs[:, sl] = r["nconv_s"]
        nks[sl] = r["nk_s"].reshape(NS, 128, 4, 64)
        nvs[sl] = r["nv_s"].reshape(NS, 128, 4, 64)
        if qd == 3:
            npp[:, b] = r["npool_p"]
            ncp[:, b] = r["nconv_p"]
            nkp[b] = r["nk_p"].reshape(128, 4, 64)
            nvp[b] = r["nv_p"].reshape(128, 4, 64)
    return (y_prompt, y_sample, npp, nps, ncp, ncs, nkp, nvp, nks, nvs)
```
